# Optimizing a Trainium2 kernel written in Bass

```python
import jax, jax.numpy as jnp
from jax import lax
import numpy as np

D_MODEL = 1024
BATCH = 16
SEQ = 2048
DEPTH = 1

M_HEADS = 4
M_QK_DIM = 128
M_V_DIM = 256
M_QK_WIDTH = M_HEADS * M_QK_DIM
M_WIDTH = M_HEADS * M_V_DIM
G_HEADS = 8
G_QK_DIM = 128
G_V_DIM = 128
G_WIDTH = G_HEADS * G_V_DIM
CONV_WIDTH = 4
MIX_WIDTH = M_WIDTH + G_WIDTH
CHUNK = 64
EPS = 1e-6
IN_SPLITS = (M_QK_WIDTH, M_QK_WIDTH, M_WIDTH, M_WIDTH, M_WIDTH, M_HEADS, M_HEADS,
             3 * G_WIDTH, G_WIDTH, G_HEADS, G_HEADS)
IN_COLS = sum(IN_SPLITS)

kernel_name = "hymba_style_mlstm_gdn_hybrid"


def rms_norm(x, g):
    xf = x.astype(jnp.float32)
    y = xf * lax.rsqrt(jnp.mean(xf * xf, axis=-1, keepdims=True) + EPS)
    return (y * g.astype(jnp.float32)).astype(x.dtype)


def l2_norm(x):
    return x * lax.rsqrt(jnp.sum(x * x, axis=-1, keepdims=True) + EPS)


def split_heads(a, n_heads):
    b, s, w = a.shape
    return a.reshape(b, s, n_heads, w // n_heads).transpose(0, 2, 1, 3)


def merge_heads(a):
    b, h, s, d = a.shape
    return a.transpose(0, 2, 1, 3).reshape(b, s, h * d)


def to_chunks(a):
    b, h, s = a.shape[:3]
    n = s // CHUNK
    return jnp.moveaxis(a.reshape(b, h, n, CHUNK, *a.shape[3:]), 2, 0)


def from_chunks(a):
    n, b, h, l, d = a.shape
    return jnp.moveaxis(a, 0, 2).reshape(b, h, n * l, d)


def causal_depthwise_conv(x, w):
    k, c = w.shape
    return lax.conv_general_dilated(
        x, w[:, None, :].astype(x.dtype), window_strides=(1,), padding=[(k - 1, 0)],
        dimension_numbers=('NWC', 'WIO', 'NWC'), feature_group_count=c)


def mlstm_chunkwise(q, k, v, i_pre, f_pre):
    b, h, s, dk = q.shape
    dv = v.shape[-1]
    q = q * (dk ** -0.5)
    log_f = jax.nn.log_sigmoid(f_pre)
    causal = jnp.tril(jnp.ones((CHUNK, CHUNK), dtype=bool))

    def step(carry, inp):
        C, n, m = carry
        qi, ki, vi, ii, lf = inp
        bcum = jnp.cumsum(lf, axis=-1)
        D = jnp.where(causal, bcum[..., :, None] - bcum[..., None, :] + ii[..., None, :], -jnp.inf)
        inter = bcum + m[..., None]
        m_t = jnp.maximum(inter, jnp.max(D, axis=-1))
        w_inter = jnp.exp(inter - m_t)
        P = jnp.exp(D - m_t[..., None]) * jnp.einsum('bhld,bhsd->bhls', qi, ki)
        num = (w_inter[..., None] * jnp.einsum('bhld,bhde->bhle', qi, C)
               + jnp.einsum('bhls,bhse->bhle', P, vi))
        den = w_inter * jnp.einsum('bhld,bhd->bhl', qi, n) + jnp.sum(P, axis=-1)
        h_out = num / jnp.maximum(jnp.abs(den), jnp.exp(-m_t))[..., None]
        b_last = bcum[..., -1]
        a_s = b_last[..., None] - bcum + ii
        m_new = jnp.maximum(b_last + m, jnp.max(a_s, axis=-1))
        decay = jnp.exp(b_last + m - m_new)
        ws = jnp.exp(a_s - m_new[..., None])
        C = decay[..., None, None] * C + jnp.einsum('bhs,bhsd,bhse->bhde', ws, ki, vi)
        n = decay[..., None] * n + jnp.einsum('bhs,bhsd->bhd', ws, ki)
        return (C, n, m_new), h_out

    init = (jnp.zeros((b, h, dk, dv), jnp.float32), jnp.zeros((b, h, dk), jnp.float32),
            jnp.zeros((b, h), jnp.float32))
    xs = (to_chunks(q), to_chunks(k), to_chunks(v), to_chunks(i_pre), to_chunks(log_f))
    _, hs = lax.scan(step, init, xs)
    return from_chunks(hs)


def gated_delta_chunkwise(q, k, v, log_alpha, beta):
    b, h, s, dk = q.shape
    dv = v.shape[-1]
    q = q * (dk ** -0.5)
    qc, kc, vc = to_chunks(q), to_chunks(k), to_chunks(v)
    lac, bc = to_chunks(log_alpha), to_chunks(beta)
    g = jnp.cumsum(lac, axis=-1)
    incl = jnp.tril(jnp.ones((CHUNK, CHUNK), dtype=bool))
    strict = jnp.tril(jnp.ones((CHUNK, CHUNK), dtype=bool), k=-1)
    gamma = jnp.exp(jnp.where(incl, g[..., :, None] - g[..., None, :], -jnp.inf))
    kb = kc * bc[..., None]
    vb = vc * bc[..., None]
    lower = jnp.where(strict, jnp.einsum('nbhid,nbhjd->nbhij', kb, kc) * gamma, 0.0)
    a_mat = lower + jnp.eye(CHUNK, dtype=lower.dtype)
    rhs = jnp.concatenate([vb, kb * jnp.exp(g)[..., None]], axis=-1)
    sol = lax.linalg.triangular_solve(a_mat, rhs, left_side=True, lower=True, unit_diagonal=True)
    U, W = sol[..., :dv], sol[..., dv:]
    A_qk = jnp.where(incl, jnp.einsum('nbhid,nbhjd->nbhij', qc, kc) * gamma, 0.0)

    def step(S, inp):
        qi, ki, Ui, Wi, Ai, gi = inp
        v_new = Ui - jnp.einsum('bhld,bhde->bhle', Wi, S)
        o = (jnp.einsum('bhld,bhde->bhle', qi * jnp.exp(gi)[..., None], S)
             + jnp.einsum('bhls,bhse->bhle', Ai, v_new))
        g_last = gi[..., -1]
        S = (jnp.exp(g_last)[..., None, None] * S
             + jnp.einsum('bhld,bhle->bhde', ki * jnp.exp(g_last[..., None] - gi)[..., None], v_new))
        return S, o

    S0 = jnp.zeros((b, h, dk, dv), jnp.float32)
    _, os_ = lax.scan(step, S0, (qc, kc, U, W, A_qk, g))
    return from_chunks(os_)


def setup_inputs(seed: int = 0) -> dict:
    key = jax.random.key(seed)
    ks = jax.random.split(key, 13)
    f32 = jnp.float32
    x = jax.random.normal(ks[0], (BATCH, SEQ, D_MODEL), f32)
    attn_norm = 1.0 + 0.02 * jax.random.normal(ks[1], (DEPTH, D_MODEL), f32)
    w_in = jax.random.normal(ks[2], (DEPTH, D_MODEL, IN_COLS), f32) * (D_MODEL ** -0.5)
    m_i_bias = 0.1 * jax.random.normal(ks[3], (DEPTH, M_HEADS), f32)
    m_f_bias = (jnp.linspace(3.0, 6.0, M_HEADS, dtype=f32)[None, :]
                + 0.1 * jax.random.normal(ks[4], (DEPTH, M_HEADS), f32))
    m_out_norm = 1.0 + 0.02 * jax.random.normal(ks[5], (DEPTH, M_WIDTH), f32)
    g_conv = jax.random.normal(ks[6], (DEPTH, CONV_WIDTH, 3 * G_WIDTH), f32) * (CONV_WIDTH ** -0.5)
    g_a_log = jnp.log(jax.random.uniform(ks[7], (DEPTH, G_HEADS), f32, 1.0, 16.0))
    dt = jnp.exp(jax.random.uniform(ks[8], (DEPTH, G_HEADS), f32, np.log(1e-3), np.log(1e-1)))
    g_dt_bias = dt + jnp.log(-jnp.expm1(-dt))
    g_out_norm = 1.0 + 0.02 * jax.random.normal(ks[9], (DEPTH, G_V_DIM), f32)
    w_out = jax.random.normal(ks[10], (DEPTH, MIX_WIDTH, D_MODEL), f32) * (MIX_WIDTH ** -0.5)
    final_norm = 1.0 + 0.02 * jax.random.normal(ks[11], (D_MODEL,), f32)
    return {"x": x, "attn_norm": attn_norm, "w_in": w_in, "m_i_bias": m_i_bias,
            "m_f_bias": m_f_bias, "m_out_norm": m_out_norm, "g_conv": g_conv,
            "g_a_log": g_a_log, "g_dt_bias": g_dt_bias, "g_out_norm": g_out_norm,
            "w_out": w_out, "final_norm": final_norm}


def reference(x, attn_norm, w_in, m_i_bias, m_f_bias, m_out_norm, g_conv, g_a_log,
              g_dt_bias, g_out_norm, w_out, final_norm):
    f32 = jnp.float32
    split_points = [int(p) for p in np.cumsum(IN_SPLITS)[:-1]]
    for l in range(DEPTH):
        h = rms_norm(x, attn_norm[l])
        proj = jnp.einsum('bsd,dp->bsp', h, w_in[l])
        (mq, mk, mv, mo, mz, mi, mf, gqkv, gz, gb, ga) = jnp.split(proj, split_points, axis=-1)

        q_m = split_heads(mq, M_HEADS).astype(f32)
        k_m = split_heads(mk, M_HEADS).astype(f32)
        v_m = split_heads(mv, M_HEADS).astype(f32)
        i_pre = (mi.astype(f32) + m_i_bias[l].astype(f32)).transpose(0, 2, 1)
        f_pre = (mf.astype(f32) + m_f_bias[l].astype(f32)).transpose(0, 2, 1)
        h_m = mlstm_chunkwise(q_m, k_m, v_m, i_pre, f_pre)
        h_m = h_m * lax.rsqrt(jnp.mean(h_m * h_m, axis=-1, keepdims=True) + EPS)
        h_m = merge_heads(h_m) * m_out_norm[l].astype(f32)
        y_m = h_m * jax.nn.sigmoid(mo.astype(f32)) * jax.nn.silu(mz.astype(f32))

        qkv = jax.nn.silu(causal_depthwise_conv(gqkv, g_conv[l]).astype(f32))
        gq, gk, gv = jnp.split(qkv, [G_HEADS * G_QK_DIM, 2 * G_HEADS * G_QK_DIM], axis=-1)
        q_g = l2_norm(split_heads(gq, G_HEADS))
        k_g = l2_norm(split_heads(gk, G_HEADS))
        v_g = split_heads(gv, G_HEADS)
        beta = jax.nn.sigmoid(gb.astype(f32)).transpose(0, 2, 1)
        log_alpha = (-jnp.exp(g_a_log[l].astype(f32))
                     * jax.nn.softplus(ga.astype(f32) + g_dt_bias[l].astype(f32))).transpose(0, 2, 1)
        o_g = gated_delta_chunkwise(q_g, k_g, v_g, log_alpha, beta)
        o_g = (o_g * lax.rsqrt(jnp.mean(o_g * o_g, axis=-1, keepdims=True) + EPS)
               * g_out_norm[l].astype(f32))
        y_g = merge_heads(o_g) * jax.nn.silu(gz.astype(f32))

        mix = jnp.concatenate([y_m, y_g], axis=-1).astype(x.dtype)
        x = x + jnp.einsum('bsm,md->bsd', mix, w_out[l])
    return rms_norm(x, final_norm)
```

```python
import numpy as np
from contextlib import ExitStack
import concourse.bass as bass
import concourse.mybir as mybir
from concourse.bass_utils import run_bass_kernel_spmd

F32 = mybir.dt.float32
BF = mybir.dt.bfloat16
AF = mybir.ActivationFunctionType
ALU = mybir.AluOpType
AX = mybir.AxisListType

NCORES = 8
SEQ = 2048
NSEQ = 2
TOK = NSEQ * SEQ
L = 128
NT = SEQ // L
D = 1024
WC = 4112
EPS = 1e-6
GEN = 3000
NDMA = 24
C_MLE, C_MGT, C_MGE, C_ONE, C_ID, C_LEV = 0, 128, 256, 384, 512, 640
NCONST = 640 + 7 * 128


def make_consts():
    idx = np.arange(L)
    c = np.zeros((L, NCONST), np.float32)
    c[:, C_MLE:C_MLE + L] = idx[:, None] <= idx[None, :]
    c[:, C_MGT:C_MGT + L] = idx[:, None] > idx[None, :]
    c[:, C_MGE:C_MGE + L] = idx[:, None] >= idx[None, :]
    c[:, C_ONE:C_ONE + L] = 1.0
    c[:, C_ID:C_ID + L] = np.eye(L)
    for k in range(7):
        b2 = 2 << k
        m = np.where((idx[:, None] // b2) == (idx[None, :] // b2), -1.0, 0.0)
        m[idx, idx] = 1.0
        c[:, C_LEV + k * L:C_LEV + (k + 1) * L] = m
    return c


class Prog:
    def __init__(self):
        self.ops = []
        self.lastw = {}
        self.readers = {}
        self.defer = None
        self.efree = {}
        self.wdone = {}
        self.rdone = {}

    def op(self, eng, fn, reads=(), writes=(), dma=False, dur=300.0):
        if self.defer is not None:
            h = [None]
            self.defer.append(dict(eng=eng, fn=fn, reads=list(reads), writes=list(writes), dma=dma, dur=dur, h=h))
            return h
        self._sim(eng, reads, writes, dma, dur)
        return [self._op(eng, fn, reads, writes, dma)]

    def _est(self, eng, reads, writes):
        t = self.efree.get(eng, 0.0)
        for r in reads:
            t = max(t, self.wdone.get(r, 0.0))
        for w in writes:
            t = max(t, self.wdone.get(w, 0.0), self.rdone.get(w, 0.0))
        return t

    def _sim(self, eng, reads, writes, dma, dur):
        st = self._est(eng, reads, writes)
        if dma:
            self.efree[eng] = st + 100.0
            end = st + 3000.0
        else:
            end = st + dur
            self.efree[eng] = end
        for r in reads:
            self.rdone[r] = max(self.rdone.get(r, 0.0), end)
        for w in writes:
            self.wdone[w] = end + 150.0
        return st

    def merge(self, lists):
        lists = [l for l in lists if l]
        n = len(lists)
        accs = []
        for l in lists:
            wl, rl = {}, {}
            for i, d in enumerate(l):
                for r in d["reads"]:
                    rl[r] = i
                for w in d["writes"]:
                    wl[w] = i
            accs.append((wl, rl))
        pos = [0] * n

        def blocked(j, d):
            for i in range(j):
                pi = pos[i]
                if pi >= len(lists[i]):
                    continue
                wl, rl = accs[i]
                for r in d["reads"]:
                    if wl.get(r, -1) >= pi:
                        return True
                for w in d["writes"]:
                    if wl.get(w, -1) >= pi or rl.get(w, -1) >= pi:
                        return True
            return False

        while True:
            best = None
            for i, l in enumerate(lists):
                if pos[i] < len(l):
                    d = l[pos[i]]
                    if blocked(i, d):
                        continue
                    st = self._est(d["eng"], d["reads"], d["writes"])
                    if best is None or st < best[0]:
                        best = (st, i)
            if best is None:
                break
            i = best[1]
            d = lists[i][pos[i]]
            pos[i] += 1
            self._sim(d["eng"], d["reads"], d["writes"], d["dma"], d["dur"])
            d["h"][0] = self._op(d["eng"], d["fn"], d["reads"], d["writes"], d["dma"])
        assert all(pos[i] == len(lists[i]) for i in range(n))

    def _op(self, eng, fn, reads=(), writes=(), dma=False):
        idx = len(self.ops)
        deps = {}
        for r in reads:
            if r in self.lastw:
                deps[self.lastw[r]] = True
        for w in writes:
            if w in self.lastw:
                deps.setdefault(self.lastw[w], False)
            for rd in self.readers.get(w, ()):
                deps.setdefault(rd, False)
        self.ops.append(dict(eng=eng, fn=fn, deps=deps, dma=dma))
        for r in reads:
            self.readers.setdefault(r, []).append(idx)
        for w in writes:
            self.lastw[w] = idx
            self.readers[w] = []
        return idx

    def finalize(self):
        ops = self.ops
        last_dma_on_sem = {}
        ndma = 0
        for i, o in enumerate(ops):
            nd = {}
            for d, raw in o["deps"].items():
                od = ops[d]
                if od["dma"]:
                    nd[d] = raw
                elif od["eng"] == o["eng"]:
                    if o["eng"] != "pe" and raw and not o["dma"]:
                        nd[d] = raw
                    elif o["dma"]:
                        nd[d] = raw
                else:
                    nd[d] = raw
            if o["dma"]:
                s = ndma % NDMA
                ndma += 1
                if s in last_dma_on_sem:
                    nd[last_dma_on_sem[s]] = False
                last_dma_on_sem[s] = i
                o["dsem"] = s
            o["deps"] = nd
        needed = set()
        for o in ops:
            needed.update(o["deps"].keys())
        cnt = {}
        dcnt = {}
        for i, o in enumerate(ops):
            if o["dma"]:
                dcnt[o["dsem"]] = dcnt.get(o["dsem"], 0) + 16
                o["tok"] = (("dma", o["dsem"]), dcnt[o["dsem"]])
            elif i in needed:
                c = cnt.get(o["eng"], 0)
                cnt[o["eng"]] = c + 1
                o["tok"] = ((o["eng"], c // GEN), c % GEN + 1)
            else:
                o["tok"] = None
        self.ngen = {e: (c + GEN - 1) // GEN for e, c in cnt.items()}

    def emit(self, eng_name, eng, sems):
        waited = {}
        for o in self.ops:
            if o["eng"] != eng_name:
                continue
            for d in sorted(o["deps"].keys()):
                key, val = self.ops[d]["tok"]
                if waited.get(key, 0) >= val:
                    continue
                waited[key] = val
                eng.wait_ge(sems[key], val)
            if o["fn"] is None:
                continue
            ins = o["fn"](eng)
            if o["tok"] is not None:
                key, _ = o["tok"]
                ins.then_inc(sems[key], 16 if o["dma"] else 1)


def build_program():
    nc = bass.Bass("TRN2", target_bir_lowering=False)
    dt_in = lambda n, s: nc.dram_tensor(n, list(s), F32, kind="ExternalInput")
    x_d = dt_in("x", [TOK, D])
    win_d = dt_in("w_in", [D, 8216])
    wout_d = dt_in("w_out", [2048, D])
    an_d = dt_in("attn_norm", [8, 128])
    mib_d = dt_in("m_i_bias", [1, 4])
    mfb_d = dt_in("m_f_bias", [1, 4])
    mon_d = dt_in("m_out_norm", [1, 1024])
    gcv_d = dt_in("g_conv", [96, 128])
    gal_d = dt_in("g_a_log", [1, 8])
    gdt_d = dt_in("g_dt_bias", [1, 8])
    gon_d = dt_in("g_out_norm", [1, 128])
    fn_d = dt_in("final_norm", [1, 1024])
    cst_d = dt_in("consts", [L, NCONST])
    out_d = nc.dram_tensor("out", [TOK, D], F32, kind="ExternalOutput")
    part_d = nc.dram_tensor("partial", [TOK, D], F32, kind="Internal")

    P = Prog()
    es = ExitStack()
    T = {}

    def sb(name, cols, dt, parts=128):
        T[name] = es.enter_context(nc.sbuf_tensor(name, [parts, cols], dt))
        return T[name]

    sb("WIN", 8 * WC, BF)
    sb("WOUT", 8 * 1024, BF)
    sb("CONST", 640, F32)
    sb("IDB", 128, BF)
    sb("LEVB", 7 * 128, BF)
    sb("GAINM", 1024, F32)
    sb("GAING", 128, F32)
    sb("CW", 96, F32)
    sb("CWROW", 128, F32, parts=96)
    sb("AN8", 128, F32, parts=8)
    sb("GW", 8, F32)
    sb("BIASM", 8, F32)
    sb("DTB", 8, F32)
    sb("NEGA", 8, F32)
    sb("X0", 1024, F32)
    sb("X1", 1024, F32)
    sb("JUNK", 1028, F32)
    sb("HT0", 1024, BF)
    sb("HT1", 1024, BF)
    sb("MIX", 1024, BF)
    sb("MIXT", 1024, BF)
    sb("RES", 1028, F32)
    for n in ["SS", "RSTD", "SS2", "RSTD2"]:
        sb(n, 1, F32)
    for n in ["G8", "E1", "LFP", "LA", "CRT", "ECRT0", "ECRT1", "TA", "TB", "EIC", "EIR", "QSC", "RR", "SSH", "RS",
              "BETA0", "BETA1", "BE", "SSQ", "RQ", "SSQK", "RQK", "QS1", "QS2", "KSB", "KSR", "SSO", "RSO"]:
        sb(n, 24 if n in ("CRT", "ECRT0", "ECRT1") else 16, F32)
    sb("CONVIN0", 8 * 131, F32)
    sb("CONVIN1", 8 * 131, F32)
    sb("CONVIN2", 8 * 131, F32)
    sb("TMPC", 1024, F32)
    sb("HALO", 24 * 3, F32)
    sb("ACC", 1024, F32)
    sb("CV", 1024, BF)
    sb("QN", 1024, BF)
    sb("QE2", 1024, BF)
    sb("KN", 1024, BF)
    sb("KBEG0", 1040, BF)
    sb("KBEG1", 1040, BF)
    sb("KREV0", 1024, BF)
    sb("KREV1", 1024, BF)
    sb("VB0", 1024, BF)
    sb("VB1", 1024, BF)
    sb("QNT", 1024, BF)
    sb("QET0", 1024, BF)
    sb("QET1", 1024, BF)
    sb("KNT", 1024, BF)
    sb("GAM", 1024, F32)
    sb("GS", 1024, F32)
    sb("MM0", 1024, BF)
    sb("MM1", 1024, BF)
    sb("AQK", 1024, BF)
    sb("AQKT0", 1024, BF)
    sb("AQKT1", 1024, BF)
    sb("Z", 1024, BF)
    sb("TT", 1024, BF)
    sb("RP", 1024, BF)
    sb("NWT", 1024, BF)
    sb("VNEW", 1024, BF)
    sb("GG20", 1024, F32)
    sb("GG21", 1024, F32)
    sb("SG", 1028, F32)
    sb("SGB", 1040, BF)
    PS = es.enter_context(nc.psum_tensor("ps", [128, 4096], F32))
    PSB = PS.bitcast(BF)

    def A(name, off=0, *dims, p=128):
        t = T[name]
        return bass.AP(t, off, [[t.shape[1], p]] + [list(d) for d in dims])

    def PA(b, off=0, *dims, p=128):
        return bass.AP(PS, b * 512 + off, [[4096, p]] + [list(d) for d in dims])

    def PB(b, off=0, *dims, p=128):
        return bass.AP(PSB, b * 1024 + off, [[8192, p]] + [list(d) for d in dims])

    def CA(off, *dims, p=128):
        return A("CONST", off, *dims, p=p)

    bank = [0]

    bset = [list(range(8))]
    bctr = {}

    def nb():
        key = tuple(bset[0])
        c = bctr.get(key, 0)
        bctr[key] = c + 1
        return bset[0][c % len(bset[0])]

    def nel(ap):
        n = 1
        for st_, cn in list(ap.ap)[1:]:
            n *= cn
        return n

    def bk(b):
        return ("ps", b)

    dq = [0]

    def dma(out, in_, reads, writes, q=None, slow=False):
        if q is None:
            q = "sp"
        if slow:
            fn = lambda e: e.dma_start(out=out, in_=in_, allow_slow_non_contiguous=True)
        else:
            fn = lambda e: e.dma_start(out=out, in_=in_)
        return P.op(q, fn, reads, writes, dma=True)

    def mmg(out, pairs, reads, b, f32=False):
        n = len(pairs)
        for i, (l, r) in enumerate(pairs):
            d = max(60.0, nel(r) * 0.45) * (4.0 if f32 else 1.0)
            P.op("pe", lambda e, l=l, r=r, i=i: e.matmul(out, l, r, start=(i == 0), stop=(i == n - 1)),
                 reads, [bk(b)], dur=d)

    def tr(out, in_, ident, reads, b):
        P.op("pe", lambda e: e.transpose(out, in_, ident), reads, [bk(b)], dur=60.0)

    def act(out, in_, func, reads, writes, scale=None, bias=None):
        kw = {}
        if scale is not None:
            kw["scale"] = scale
        if bias is not None:
            kw["bias"] = bias
        P.op("act", lambda e: e.activation(out, in_, func, **kw), reads, writes, dur=250.0 + 0.85 * nel(out))

    def edur(eng, out):
        return (150.0 + 2.1 * nel(out)) if eng == "pool" else (160.0 + 1.02 * nel(out))

    def tt(eng, out, in0, in1, op, reads, writes):
        P.op(eng, lambda e: e.tensor_tensor(out, in0, in1, op), reads, writes, dur=edur(eng, out))

    def ts(eng, out, in0, s1, s2, op0, op1, reads, writes):
        if op1 is None:
            P.op(eng, lambda e: e.tensor_scalar(out, in0, s1, None, op0), reads, writes, dur=edur(eng, out))
        else:
            P.op(eng, lambda e: e.tensor_scalar(out, in0, s1, s2, op0, op1), reads, writes, dur=edur(eng, out))

    def stt(out, in0, scalar, in1, op0, op1, reads, writes):
        P.op("dve", lambda e: e.scalar_tensor_tensor(out, in0, scalar, in1, op0, op1), reads, writes,
             dur=edur("dve", out))

    def cp(eng, out, in_, reads, writes):
        if eng == "act":
            act(out, in_, AF.Copy, reads, writes)
        else:
            P.op(eng, lambda e: e.tensor_copy(out, in_), reads, writes, dur=edur(eng, out))

    def mset(eng, ap, val, writes):
        P.op(eng, lambda e: e.memset(ap, val), [], writes)

    def rsq(out, in_, mul, reads, w):
        act(out, in_, AF.Ln, reads, [w], scale=float(mul), bias=EPS)
        act(out, out, AF.Exp, [w], [w], scale=-0.5)

    IDB = lambda n=128: A("IDB", 0, [1, n], p=n)

    dma(A("CONST", 0, [1, 640]), cst_d.ap()[:, 0:640], [], ["CONST"])
    dma(A("JUNK", 0, [1, 896]), cst_d.ap()[:, 640:NCONST], [], ["JUNK"])
    cp("dve", A("IDB", 0, [1, 128]), CA(C_ID, [1, 128]), ["CONST"], ["IDB"])
    cp("dve", A("LEVB", 0, [1, 896]), A("JUNK", 0, [1, 896]), ["JUNK"], ["LEVB"])
    bc = lambda d, n: bass.AP(d, 0, [[0, 128], [1, n]])
    dma(A("GAINM", 0, [1, 1024]), bc(mon_d, 1024), [], ["GAINM"])
    dma(A("GAING", 0, [1, 128]), bc(gon_d, 128), [], ["GAING"])
    ts("pool", A("GAINM", 0, [1, 1024]), A("GAINM", 0, [1, 1024]), 0.5, None, ALU.mult, None, ["GAINM"], ["GAINM"])
    dma(A("BIASM", 0, [1, 4]), bc(mib_d, 4), [], ["BIASM"])
    dma(A("BIASM", 4, [1, 4]), bc(mfb_d, 4), [], ["BIASM"])
    dma(A("DTB", 0, [1, 8]), bc(gdt_d, 8), [], ["DTB"])
    dma(A("NEGA", 0, [1, 8]), bc(gal_d, 8), [], ["NEGA"])
    act(A("NEGA", 0, [1, 8]), A("NEGA", 0, [1, 8]), AF.Exp, ["NEGA"], ["NEGA"])
    ts("dve", A("NEGA", 0, [1, 8]), A("NEGA", 0, [1, 8]), -1.0, None, ALU.mult, None, ["NEGA"], ["NEGA"])
    dma(A("AN8", 0, [1, 128], p=8), an_d.ap(), [], ["AN8"])
    dma(A("CWROW", 0, [1, 128], p=96), gcv_d.ap(), [], ["CWROW"])
    b = nb()
    tr(PA(b, 0, [1, 8]), A("AN8", 0, [1, 128], p=8), CA(C_ID, [1, 8], p=8), ["AN8", "CONST"], b)
    cp("dve", A("GW", 0, [1, 8]), PA(b, 0, [1, 8]), [bk(b)], ["GW"])
    b = nb()
    tr(PA(b, 0, [1, 96]), A("CWROW", 0, [1, 128], p=96), CA(C_ID, [1, 96], p=96), ["CWROW", "CONST"], b)
    cp("dve", A("CW", 0, [1, 96]), PA(b, 0, [1, 96]), [bk(b)], ["CW"])

    cvt_i = [0]

    STGS = ["RES", "JUNK", "ACC", "TMPC", "X0", "X1", "GAM", "GS"]

    def load_weights(ps_):
        c0 = 0 if ps_ == 0 else 4104
        wtot = 4104 if ps_ == 0 else 4112
        chunks = [(j * 1024, 1024) for j in range(4)] + [(4096, wtot - 4096)]
        for kc in range(8):
            for (cj, w) in chunks:
                i = cvt_i[0]
                cvt_i[0] += 1
                stg = STGS[i % len(STGS)]
                dma(A(stg, 0, [1, w]), win_d.ap()[kc * 128:(kc + 1) * 128, c0 + cj:c0 + cj + w], [], [stg])
                dst = A("WIN", kc * WC + cj, [1, w])
                if i % 2 == 0:
                    act(dst, A(stg, 0, [1, w]), AF.Copy, [stg, "GW"], ["WIN"], scale=A("GW", kc, [1, 1]))
                else:
                    ts("dve", dst, A(stg, 0, [1, w]), A("GW", kc, [1, 1]), None, ALU.mult, None, [stg, "GW"], ["WIN"])
        r0 = ps_ * 1024
        for kc in range(8):
            i = cvt_i[0]
            cvt_i[0] += 1
            stg = STGS[i % len(STGS)]
            dma(A(stg, 0, [1, 1024]), wout_d.ap()[r0 + kc * 128:r0 + (kc + 1) * 128, :], [], [stg])
            cp(("act", "dve")[i % 2], A("WOUT", kc * 1024, [1, 1024]), A(stg, 0, [1, 1024]), [stg], ["WOUT"])

    ctx = {"HT": "HT0", "BETA": "BETA0"}

    def setp(p):
        ctx["HT"] = "HT%d" % p
        ctx["BETA"] = "BETA%d" % p

    def HTk(kc):
        return A(ctx["HT"], kc * 128, [1, 128])

    def proj_tm(b, c0, n):
        mmg(PA(b, 0, [1, n]), [(HTk(kc), A("WIN", kc * WC + c0, [1, n])) for kc in range(8)], [ctx["HT"], "WIN"], b)

    store_ops = []
    RUN = lambda g: [None for _ in g]
    hk = lambda nm: [(nm, 0), (nm, 1)]
    H = lambda nm, h: A(nm, h * 128, [1, 128])

    def phase_a(s, t, p, X=None):
        X = X or ("X%d" % p)
        r0 = s * SEQ + t * L
        dma(A(X, 0, [1, 1024]), x_d.ap()[r0:r0 + L, :], [], [X])
        act(A("JUNK", 0, [1, 1024]), A(X, 0, [1, 1024]), AF.Square, [X], ["JUNK"])
        P.op("dve", lambda e: e.tensor_reduce(A("SS", 0, [1, 1]), A("JUNK", 0, [1, 1024]), AX.X, ALU.add),
             ["JUNK"], ["SS"], dur=1200.0)
        rsq(A("RSTD", 0, [1, 1]), A("SS", 0, [1, 1]), 1.0 / D, ["SS"], "RSTD")
        ts("dve", A("AQK", 0, [1, 1024]), A(X, 0, [1, 1024]), A("RSTD", 0, [1, 1]), None, ALU.mult, None,
           [X, "RSTD"], ["AQK"])
        yield
        b = nb()
        for kc in range(8):
            tr(PB(b, kc * 128, [1, 128]), A("AQK", kc * 128, [1, 128]), IDB(), ["AQK", "IDB"], b)
        cp("act", A(ctx["HT"], 0, [1, 1024]), PB(b, 0, [1, 1024]), [bk(b)], [ctx["HT"]])
        yield

    def phase_d(ps_, s, t, p):
        X = "X%d" % p
        r0 = s * SEQ + t * L
        b = nb()
        for kc in range(8):
            tr(PB(b, kc * 128, [1, 128]), A("MIX", kc * 128, [1, 128]), IDB(), ["MIX", "IDB"], b)
        cp("act", A("MIXT", 0, [1, 1024]), PB(b, 0, [1, 1024]), [bk(b)], ["MIXT"])
        yield
        src = X if ps_ == 0 else "RES"
        for n in range(2):
            b = nb()
            mmg(PA(b, 0, [1, 512]),
                [(A("MIXT", kc * 128, [1, 128]), A("WOUT", kc * 1024 + n * 512, [1, 512])) for kc in range(8)],
                ["MIXT", "WOUT"], b)
            tt("dve", A("RES", n * 512, [1, 512]), PA(b, 0, [1, 512]), A(src, n * 512, [1, 512]), ALU.add,
               [bk(b), src], ["RES"])
            yield
        if ps_ == 0:
            store_ops.append(dma(part_d.ap()[r0:r0 + L, :], A("RES", 0, [1, 1024]), ["RES"], [("pd", r0)],
                                 q="pool"))
        else:
            act(A("RP", 0, [1, 1024]), A("RES", 0, [1, 1024]), AF.Square, ["RES"], hk("RP"))
            P.op("dve", lambda e: e.tensor_reduce(A("SS2", 0, [1, 1]), A("RP", 0, [1, 1024]), AX.X, ALU.add),
                 hk("RP"), ["SS2"], dur=1200.0)
            rsq(A("RSTD2", 0, [1, 1]), A("SS2", 0, [1, 1]), 1.0 / D, ["SS2"], "RSTD2")
            stt(A("RES", 0, [1, 1024]), A("RES", 0, [1, 1024]), A("RSTD2", 0, [1, 1]), A("GAINM", 0, [1, 1024]),
                ALU.mult, ALU.mult, ["RES", "RSTD2", "GAINM"], ["RES"])
            store_ops.append(dma(out_d.ap()[r0:r0 + L, :], A("RES", 0, [1, 1024]), ["RES"], ["out_dram"], q="pool"))
        yield

    def ml_stage1(s, t, p, first):
        QE = ("QN", "Z")[p]
        K2 = ("QE2", "TT")[p]
        K3 = ("KN", "RP")[p]
        VA = "KBEG%d" % p
        GG = "GG2%d" % p
        EC = "ECRT%d" % p
        setp(p)
        yield from phase_a(s, t, p)
        b = nb()
        mmg(PA(b, 0, [1, 8]), [(HTk(kc), A("WIN", kc * WC + 4096, [1, 8])) for kc in range(8)], [ctx["HT"], "WIN"], b)
        tt("dve", A("G8", 0, [1, 8]), PA(b, 0, [1, 8]), A("BIASM", 0, [1, 8]), ALU.add, [bk(b), "BIASM"], ["G8"])
        act(A("E1", 0, [1, 4]), A("G8", 4, [1, 4]), AF.Exp, ["G8"], ["E1"], scale=-1.0)
        act(A("LFP", 0, [1, 4]), A("E1", 0, [1, 4]), AF.Ln, ["E1"], ["LFP"], bias=1.0)
        ts("dve", A("LA", 0, [1, 4]), A("LFP", 0, [1, 4]), -1.0, None, ALU.mult, None, ["LFP"], ["LA"])
        yield
        b = nb()
        for i, cm in enumerate([C_MLE, C_MGT, C_ONE]):
            mmg(PA(b, i * 4, [1, 4]), [(CA(cm, [1, 128]), A("LA", 0, [1, 4]))], ["CONST", "LA"], b, f32=True)
        cp("dve", A("CRT", 0, [1, 12]), PA(b, 0, [1, 12]), [bk(b)], ["CRT"])
        act(A(EC, 0, [1, 12]), A("CRT", 0, [1, 12]), AF.Exp, ["CRT"], [EC])
        tt("dve", A("TA", 0, [1, 4]), A("G8", 0, [1, 4]), A("CRT", 0, [1, 4]), ALU.subtract, ["G8", "CRT"], ["TA"])
        tt("dve", A("TA", 4, [1, 4]), A("G8", 0, [1, 4]), A("CRT", 4, [1, 4]), ALU.add, ["G8", "CRT"], ["TA"])
        act(A("EIC", 0, [1, 8]), A("TA", 0, [1, 8]), AF.Exp, ["TA"], ["EIC"])
        ts("dve", A("QSC", 0, [1, 4]), A(EC, 0, [1, 4]), 128.0 ** -0.5, None, ALU.mult, None, [EC], ["QSC"])
        yield
        b = nb()
        proj_tm(b, 0, 512)
        tt("dve", A(QE, 0, [128, 4], [1, 128]), PA(b, 0, [128, 4], [1, 128]), A("QSC", 0, [1, 4], [0, 128]),
           ALU.mult, [bk(b), "QSC"], hk(QE))
        yield
        b = nb()
        proj_tm(b, 512, 512)
        tt("dve", A(K2, 0, [128, 4], [1, 128]), PA(b, 0, [128, 4], [1, 128]), A("EIC", 0, [1, 4], [0, 128]),
           ALU.mult, [bk(b), "EIC"], hk(K2))
        tt("dve", A(K3, 0, [128, 4], [1, 128]), PA(b, 0, [128, 4], [1, 128]), A("EIC", 4, [1, 4], [0, 128]),
           ALU.mult, [bk(b), "EIC"], hk(K3))
        yield
        for n in range(2):
            b = nb()
            proj_tm(b, 1024 + n * 512, 512)
            cp("act", A(VA, n * 514, [257, 2], [1, 256]), PA(b, 0, [256, 2], [1, 256]), [bk(b)], [VA])
            yield
        for n in range(2):
            b = nb()
            proj_tm(b, 3072 + n * 512, 512)
            act(A("GS", n * 512, [1, 512]), PA(b, 0, [1, 512]), AF.Silu, [bk(b)], ["GS"])
            yield
        tt("pool", A("GS", 0, [1, 1024]), A("GS", 0, [1, 1024]), A("GAINM", 0, [1, 1024]), ALU.mult,
           ["GS", "GAINM"], ["GS"])
        for n in range(2):
            b = nb()
            proj_tm(b, 2048 + n * 512, 512)
            act(A(GG, n * 512, [1, 512]), PA(b, 0, [1, 512]), AF.Tanh, [bk(b)], [GG], scale=0.5)
            yield
        stt(A(GG, 0, [1, 1024]), A(GG, 0, [1, 1024]), 1.0, A("GS", 0, [1, 1024]), ALU.add, ALU.mult,
            [GG, "GS"], [GG])
        yield

    def ml_stage2(s, t, p, first):
        QE = ("QN", "Z")[p]
        K2 = ("QE2", "TT")[p]
        K3 = ("KN", "RP")[p]
        VA = "KBEG%d" % p
        GG = "GG2%d" % p
        EC = "ECRT%d" % p
        if first:
            mset("pool", A("SG", 0, [1, 1028]), 0.0, ["SG"])
            mset("pool", A("SGB", 0, [1, 1040]), 0.0, ["SGB"])
        b = nb()
        for h in range(4):
            tr(PB(b, h * 128, [1, 128]), A(QE, h * 128, [1, 128]), IDB(), hk(QE) + ["IDB"], b)
        for h in range(4):
            tr(PB(b, 512 + h * 128, [1, 128]), A(K2, h * 128, [1, 128]), IDB(), hk(K2) + ["IDB"], b)
        cp("act", A("QNT", 0, [1, 1024]), PB(b, 0, [1, 1024]), [bk(b)], ["QNT"])
        yield
        b = nb()
        for h in range(4):
            mmg(PA(b, h * 128, [1, 128]), [(A("QNT", 512 + h * 128, [1, 128]), A("QNT", h * 128, [1, 128]))],
                ["QNT"], b)
        tt("dve", A("QET0", 0, [128, 4], [1, 128]), PA(b, 0, [128, 4], [1, 128]), CA(C_MLE, [0, 4], [1, 128]),
           ALU.mult, [bk(b), "CONST"], ["QET0"])
        yield
        for h in range(4):
            b = nb()
            mmg(PA(b, 0, [1, 257]),
                [(A("QNT", h * 128, [1, 128]), A("SGB", h * 257, [1, 257])),
                 (A("QET0", h * 128, [1, 128]), A(VA, h * 257, [1, 257]))], ["QNT", "SGB", "QET0", VA], b)
            b2 = nb()
            mmg(PA(b2, 0, [1, 257]), [(A(K3, h * 128, [1, 128]), A(VA, h * 257, [1, 257]))], hk(K3) + [VA], b2)
            rr = A("RR", h, [1, 1])
            act(rr, PA(b, 256, [1, 1]), AF.Abs, [bk(b)], ["RR"])
            ts("dve", rr, rr, 1.0, None, ALU.max, None, ["RR"], ["RR"])
            P.op("dve", lambda e, rr=rr: e.reciprocal(rr, rr), ["RR"], ["RR"])
            act(A("RES", 0, [1, 256]), PA(b, 0, [1, 256]), AF.Square, [bk(b), "RR"], ["RES"], scale=rr)
            P.op("dve", lambda e, h=h: e.tensor_reduce(A("SSH", h, [1, 1]), A("RES", 0, [1, 256]), AX.X, ALU.add),
                 ["RES"], ["SSH"])
            rsq(A("RS", h, [1, 1]), A("SSH", h, [1, 1]), 1.0 / 256, ["SSH"], "RS")
            tt("dve", A("RS", h, [1, 1]), A("RS", h, [1, 1]), rr, ALU.mult, ["RS", "RR"], ["RS"])
            stt(A("MIX", h * 256, [1, 256]), PA(b, 0, [1, 256]), A("RS", h, [1, 1]), A(GG, h * 256, [1, 256]),
                ALU.mult, ALU.mult, [bk(b), "RS", GG], ["MIX"])
            stt(A("SG", h * 257, [1, 257]), A("SG", h * 257, [1, 257]), A(EC, 8 + h, [1, 1]),
                PA(b2, 0, [1, 257]), ALU.mult, ALU.add, ["SG", EC, bk(b2)], ["SG"])
            yield
        cp("act", A("SGB", 0, [1, 1028]), A("SG", 0, [1, 1028]), ["SG"], ["SGB"])
        yield
        yield from phase_d(0, s, t, p)

    def gdn_group(g, p, first):
        KB, KR, VBn, QE_, MMn = "KBEG%d" % p, "KREV%d" % p, "VB%d" % p, "QET%d" % p, "MM%d" % p
        EC = "ECRT%d" % p
        CI, ACn, TMn, ce = {0: ("CONVIN1", "ACC", "TMPC", "pool"), 1: ("CONVIN2", "X1", "JUNK", "dve"),
                            2: ("CONVIN0", "ACC", "TMPC", "pool")}[g]
        HL = "HALO%d" % g
        if first:
            mset("pool", A("HALO", g * 24, [1, 24]), 0.0, [HL])
        cp("pool", A(CI, 0, [131, 8], [1, 3]), A("HALO", g * 24, [3, 8], [1, 3]), [HL], [CI])
        for n in range(2):
            b = nb()
            for c in range(n * 4, n * 4 + 4):
                mmg(PA(b, (c % 4) * 128, [1, 128]),
                    [(A("WIN", kc * WC + g * 1024 + c * 128, [1, 128]), HTk(kc)) for kc in range(8)],
                    ["WIN", ctx["HT"]], b)
            cp("act", A(CI, n * 4 * 131 + 3, [131, 4], [1, 128]), PA(b, 0, [128, 4], [1, 128]), [bk(b)], [CI])
            yield
        cp("pool", A("HALO", g * 24, [3, 8], [1, 3]), A(CI, 128, [131, 8], [1, 3]), [CI], [HL])
        acc = A(ACn, 0, [128, 8], [1, 128])
        tmp = A(TMn, 0, [128, 8], [1, 128])
        for j in (3, 2, 1, 0):
            xw = A(CI, j, [131, 8], [1, 128])
            wj = A("CW", j * 24 + g * 8, [1, 8], [0, 128])
            if j == 3:
                tt(ce, acc, xw, wj, ALU.mult, [CI, "CW"], [ACn])
            else:
                tt(ce, tmp, xw, wj, ALU.mult, [CI, "CW"], [TMn])
                tt(ce, acc, acc, tmp, ALU.add, [ACn, TMn], [ACn])
            yield
        act(A("CV", 0, [1, 1024]), A(ACn, 0, [1, 1024]), AF.Silu, [ACn], ["CV"])
        b = nb()
        for c in range(8):
            tr(PB(b, c * 128, [1, 128]), A("CV", c * 128, [1, 128]), IDB(), ["CV", "IDB"], b)
        yield
        pall = PB(b, 0, [128, 8], [1, 128])
        if g < 2:
            SQ, RQ = ("SSQ", "RQ") if g == 0 else ("SSQK", "RQK")
            act(A(TMn, 0, [1, 1024]), PB(b, 0, [1, 1024]), AF.Square, [bk(b)], [TMn])
            P.op("dve", lambda e: e.tensor_reduce(A(SQ, 0, [1, 8]), A(TMn, 0, [128, 8], [1, 128]), AX.X,
                                                  ALU.add), [TMn], [SQ], dur=1200.0)
            rsq(A(RQ, 0, [1, 8]), A(SQ, 0, [1, 8]), 1.0, [SQ], RQ)
        if g == 0:
            ts("dve", A("QS1", 0, [1, 8]), A("RQ", 0, [1, 8]), 128.0 ** -0.5, None, ALU.mult, None, ["RQ"], ["QS1"])
            tt("dve", A("QS2", 0, [1, 8]), A("QS1", 0, [1, 8]), A(EC, 0, [1, 8]), ALU.mult, ["QS1", EC], ["QS2"])
            tt("dve", A("QN", 0, [128, 8], [1, 128]), pall, A("QS1", 0, [1, 8], [0, 128]), ALU.mult,
               [bk(b), "QS1"], hk("QN"))
            tt("dve", A("QE2", 0, [128, 8], [1, 128]), pall, A("QS2", 0, [1, 8], [0, 128]), ALU.mult,
               [bk(b), "QS2"], hk("QE2"))
            yield
            for src, dst in (("QN", "QNT"), ("QE2", QE_)):
                b2 = nb()
                for c in range(8):
                    tr(PB(b2, c * 128, [1, 128]), A(src, c * 128, [1, 128]), IDB(), hk(src) + ["IDB"], b2)
                cp("act", A(dst, 0, [1, 1024]), PB(b2, 0, [1, 1024]), [bk(b2)], [dst])
                yield
        elif g == 1:
            tt("dve", A("KSB", 0, [1, 8]), A("RQK", 0, [1, 8]), A("BE", 0, [1, 8]), ALU.mult, ["RQK", "BE"], ["KSB"])
            tt("dve", A("KSR", 0, [1, 8]), A("RQK", 0, [1, 8]), A(EC, 8, [1, 8]), ALU.mult, ["RQK", EC], ["KSR"])
            tt("dve", A("KN", 0, [128, 8], [1, 128]), pall, A("RQK", 0, [1, 8], [0, 128]), ALU.mult,
               [bk(b), "RQK"], hk("KN"))
            tt("dve", A(KB, 0, [128, 8], [1, 128]), pall, A("KSB", 0, [1, 8], [0, 128]), ALU.mult,
               [bk(b), "KSB"], [KB])
            tt("dve", A(KR, 0, [128, 8], [1, 128]), pall, A("KSR", 0, [1, 8], [0, 128]), ALU.mult,
               [bk(b), "KSR"], [KR])
            yield
            b2 = nb()
            for c in range(8):
                tr(PB(b2, c * 128, [1, 128]), A("KN", c * 128, [1, 128]), IDB(), hk("KN") + ["IDB"], b2)
            cp("act", A("KNT", 0, [1, 1024]), PB(b2, 0, [1, 1024]), [bk(b2)], ["KNT"])
            yield
        else:
            tt("dve", A(VBn, 0, [128, 8], [1, 128]), pall, A(ctx["BETA"], 0, [1, 8], [0, 128]), ALU.mult,
               [bk(b), ctx["BETA"]], [VBn])
            yield

    def per_head(pairs_fn, reads, evac):
        for n in range(2):
            b = nb()
            for hh in range(4):
                h = n * 4 + hh
                mmg(PA(b, hh * 128, [1, 128]), pairs_fn(h), reads(n) if callable(reads) else reads, b)
            evac(n, b)
            yield

    def gdn_a(s, t, p, first):
        KB, KR, VBn, QE_, MMn, AQ = "KBEG%d" % p, "KREV%d" % p, "VB%d" % p, "QET%d" % p, "MM%d" % p, "AQKT%d" % p
        EC = "ECRT%d" % p
        GG = "GG2%d" % p
        setp(p)
        yield from phase_a(s, t, p, X="X0")
        b = nb()
        mmg(PA(b, 0, [1, 16]), [(HTk(kc), A("WIN", kc * WC + 4096, [1, 16])) for kc in range(8)], [ctx["HT"], "WIN"], b)
        act(A(ctx["BETA"], 0, [1, 8]), PA(b, 0, [1, 8]), AF.Exp, [bk(b)], [ctx["BETA"]], scale=-1.0)
        ts("dve", A(ctx["BETA"], 0, [1, 8]), A(ctx["BETA"], 0, [1, 8]), 1.0, None, ALU.add, None, [ctx["BETA"]], [ctx["BETA"]])
        bt_ = A(ctx["BETA"], 0, [1, 8])
        P.op("dve", lambda e, bt_=bt_: e.reciprocal(bt_, bt_), [ctx["BETA"]], [ctx["BETA"]])
        tt("dve", A("TA", 0, [1, 8]), PA(b, 8, [1, 8]), A("DTB", 0, [1, 8]), ALU.add, [bk(b), "DTB"], ["TA"])
        act(A("E1", 0, [1, 8]), A("TA", 0, [1, 8]), AF.Exp, ["TA"], ["E1"])
        act(A("LFP", 0, [1, 8]), A("E1", 0, [1, 8]), AF.Ln, ["E1"], ["LFP"], bias=1.0)
        tt("dve", A("LA", 0, [1, 8]), A("LFP", 0, [1, 8]), A("NEGA", 0, [1, 8]), ALU.mult, ["LFP", "NEGA"], ["LA"])
        yield
        b = nb()
        for i, cm in enumerate([C_MLE, C_MGT, C_ONE]):
            mmg(PA(b, i * 8, [1, 8]), [(CA(cm, [1, 128]), A("LA", 0, [1, 8]))], ["CONST", "LA"], b, f32=True)
        cp("dve", A("CRT", 0, [1, 24]), PA(b, 0, [1, 24]), [bk(b)], ["CRT"])
        act(A(EC, 0, [1, 24]), A("CRT", 0, [1, 24]), AF.Exp, ["CRT"], [EC])
        tt("dve", A("BE", 0, [1, 8]), A(ctx["BETA"], 0, [1, 8]), A(EC, 0, [1, 8]), ALU.mult, [ctx["BETA"], EC], ["BE"])
        tt("dve", A("JUNK", 0, [128, 8], [1, 128]), CA(C_MGT, [0, 8], [1, 128]), A("LA", 0, [1, 8], [0, 128]),
           ALU.mult, ["CONST", "LA"], ["JUNK"])
        yield
        for n in range(2):
            b = nb()
            mmg(PA(b, 0, [1, 512]), [(CA(C_MLE, [1, 128]), A("JUNK", n * 512, [1, 512]))], ["CONST", "JUNK"], b, f32=True)
            act(A("GAM", n * 512, [1, 512]), PA(b, 0, [1, 512]), AF.Exp, [bk(b)], ["GAM"])
            yield
        tt("pool", A("GS", 0, [128, 8], [1, 128]), A("GAM", 0, [128, 8], [1, 128]), CA(C_MGT, [0, 8], [1, 128]),
           ALU.mult, ["GAM", "CONST"], ["GS"])
        tt("pool", A("GS", 0, [128, 8], [1, 128]), A("GS", 0, [128, 8], [1, 128]), A(ctx["BETA"], 0, [1, 8], [0, 128]),
           ALU.mult, ["GS", ctx["BETA"]], ["GS"])
        tt("pool", A("GAM", 0, [128, 8], [1, 128]), A("GAM", 0, [128, 8], [1, 128]), CA(C_MGE, [0, 8], [1, 128]),
           ALU.mult, ["GAM", "CONST"], ["GAM"])
        yield

    def gdn_k(s, t, p, first):
        KB, KR, VBn, QE_, MMn, AQ = "KBEG%d" % p, "KREV%d" % p, "VB%d" % p, "QET%d" % p, "MM%d" % p, "AQKT%d" % p
        EC = "ECRT%d" % p
        GG = "GG2%d" % p
        setp(p)
        yield from gdn_group(1, p, first)
        yield from per_head(lambda h: [(H("KNT", h), H("KNT", h))], ["KNT"],
                            lambda n, b: tt("dve", A(MMn, n * 512, [1, 512]), PA(b, 0, [1, 512]),
                                            A("GS", n * 512, [1, 512]), ALU.mult, [bk(b), "GS"], [(MMn, n)]))
        for n in range(2):
            tt("pool", A(MMn, n * 512, [128, 4], [1, 128]), A(MMn, n * 512, [128, 4], [1, 128]),
               A("IDB", 0, [0, 4], [1, 128]), ALU.add, [(MMn, n), "IDB"], [(MMn, n)])
        yield

    def gdn_q(s, t, p, first):
        KB, KR, VBn, QE_, MMn, AQ = "KBEG%d" % p, "KREV%d" % p, "VB%d" % p, "QET%d" % p, "MM%d" % p, "AQKT%d" % p
        EC = "ECRT%d" % p
        GG = "GG2%d" % p
        setp(p)
        for n in range(2):
            b = nb()
            proj_tm(b, 3072 + n * 512, 512)
            act(A(GG, n * 512, [1, 512]), PA(b, 0, [1, 512]), AF.Silu, [bk(b)], [GG])
            yield
        tt("pool", A(GG, 0, [128, 8], [1, 128]), A(GG, 0, [128, 8], [1, 128]), A("GAING", 0, [0, 8], [1, 128]),
           ALU.mult, [GG, "GAING"], [GG])
        yield from gdn_group(0, p, first)
        yield from per_head(lambda h: [(H("QNT", h), H("KNT", h))], ["QNT", "KNT"],
                            lambda n, b: tt("dve", A("AQK", n * 512, [1, 512]), PA(b, 0, [1, 512]),
                                            A("GAM", n * 512, [1, 512]), ALU.mult, [bk(b), "GAM"], ["AQK"]))
        b = nb()
        for c in range(8):
            tr(PB(b, c * 128, [1, 128]), A("AQK", c * 128, [1, 128]), IDB(), ["AQK", "IDB"], b)
        cp("act", A(AQ, 0, [1, 1024]), PB(b, 0, [1, 1024]), [bk(b)], [AQ])
        yield

    def gdn_v(s, t, p, first):
        setp(p)
        yield from gdn_group(2, p, first)

    def gdn_stage2(s, t, p, first):
        KB, KR, VBn, QE_, MMn, AQ = "KBEG%d" % p, "KREV%d" % p, "VB%d" % p, "QET%d" % p, "MM%d" % p, "AQKT%d" % p
        EC = "ECRT%d" % p
        GG = "GG2%d" % p
        r0 = s * SEQ + t * L
        if first:
            mset("pool", A("SG", 0, [1, 1028]), 0.0, ["SG"])
            mset("pool", A("SGB", 0, [1, 1040]), 0.0, ["SGB"])
        dma(A("RES", 0, [1, 1024]), part_d.ap()[r0:r0 + L, :], [("pd", r0)], ["RES"])
        for n in range(2):
            b = nb()
            for hh in range(4):
                mmg(PA(b, hh * 128, [1, 128]), [(H(MMn, n * 4 + hh), IDB())], [(MMn, n), "IDB"], b)
            tt("dve", A("Z", n * 512, [128, 4], [1, 128]), PA(b, 0, [128, 4], [1, 128]),
               A("LEVB", 0, [0, 4], [1, 128]), ALU.mult, [bk(b), "LEVB"], [("Z", n)])
            tt("pool", A("TT", n * 512, [128, 4], [1, 128]), A(MMn, n * 512, [128, 4], [1, 128]),
               A("LEVB", 0, [0, 4], [1, 128]), ALU.mult, [(MMn, n), "LEVB"], [("TT", n)])
            yield
        for k in range(1, 7):
            for n in range(2):
                b = nb()
                for hh in range(4):
                    h = n * 4 + hh
                    mmg(PA(b, hh * 128, [1, 128]), [(H(MMn, h), H("Z", h))], [(MMn, n), ("Z", n)], b)
                tt("dve", A("RP", n * 512, [128, 4], [1, 128]), PA(b, 0, [128, 4], [1, 128]),
                   A("LEVB", k * 128, [0, 4], [1, 128]), ALU.mult, [bk(b), "LEVB"], [("RP", n)])
                yield
            for n in range(2):
                zb_ = nb()
                for hh in range(4):
                    h = n * 4 + hh
                    mmg(PA(zb_, hh * 128, [1, 128]), [(H("TT", h), H("RP", h))], [("TT", n), ("RP", n)], zb_)
                tb_ = None
                if k < 6:
                    tb_ = nb()
                    for hh in range(4):
                        h = n * 4 + hh
                        mmg(PA(tb_, hh * 128, [1, 128]), [(H("RP", h), H("TT", h))], [("TT", n), ("RP", n)], tb_)
                cp("act", A("Z", n * 512, [1, 512]), PA(zb_, 0, [1, 512]), [bk(zb_)], [("Z", n)])
                if k < 6:
                    cp("act" if n == 0 else "dve", A("TT", n * 512, [1, 512]), PA(tb_, 0, [1, 512]),
                       [bk(tb_)], [("TT", n)])
                yield
        yield from per_head(lambda h: [(H(KB, h), H("Z", h))], lambda n: [KB, ("Z", n)],
                            lambda n, b: act(A("NWT", n * 512, [1, 512]), PA(b, 0, [1, 512]), AF.Copy, [bk(b)],
                                             [("NWT", n)], scale=-1.0))
        yield from per_head(lambda h: [(H("Z", h), H(VBn, h)), (H("NWT", h), H("SGB", h))],
                            lambda n: [("Z", n), VBn, ("NWT", n), "SGB"],
                            lambda n, b: cp("act", A("VNEW", n * 512, [1, 512]), PA(b, 0, [1, 512]), [bk(b)],
                                            [("VNEW", n)]))
        ob = []

        def o_evac(n, b):
            ob.append(b)
            act(A("RP", n * 512, [1, 512]), PA(b, 0, [1, 512]), AF.Square, [bk(b)], [("RP", n)])
        yield from per_head(lambda h: [(H(QE_, h), H("SGB", h)), (H(AQ, h), H("VNEW", h))],
                            lambda n: [QE_, "SGB", AQ, ("VNEW", n)], o_evac)
        P.op("dve", lambda e: e.tensor_reduce(A("SSO", 0, [1, 8]), A("RP", 0, [128, 8], [1, 128]), AX.X, ALU.add),
             hk("RP"), ["SSO"], dur=1200.0)
        rsq(A("RSO", 0, [1, 8]), A("SSO", 0, [1, 8]), 1.0 / 128, ["SSO"], "RSO")
        tt("pool", A(GG, 0, [128, 8], [1, 128]), A(GG, 0, [128, 8], [1, 128]), A("RSO", 0, [1, 8], [0, 128]),
           ALU.mult, [GG, "RSO"], [GG])
        for n in range(2):
            tt("dve", A("MIX", n * 512, [1, 512]), PA(ob[n], 0, [1, 512]), A(GG, n * 512, [1, 512]), ALU.mult,
               [bk(ob[n]), GG], ["MIX"])
        yield
        tt("pool", A("SG", 0, [128, 8], [1, 128]), A("SG", 0, [128, 8], [1, 128]), A(EC, 16, [1, 8], [0, 128]),
           ALU.mult, ["SG", EC], ["SG"])
        yield from per_head(lambda h: [(H(KR, h), H("VNEW", h))], lambda n: [KR, ("VNEW", n)],
                            lambda n, b: tt("dve", A("SG", n * 512, [1, 512]), PA(b, 0, [1, 512]),
                                            A("SG", n * 512, [1, 512]), ALU.add, [bk(b), "SG"], ["SG"]))
        cp("act", A("SGB", 0, [1, 1024]), A("SG", 0, [1, 1024]), ["SG"], ["SGB"])
        yield
        yield from phase_d(1, s, t, p)

    def collect(gen, banks):
        if gen is None:
            return []
        P.defer = []
        bset[0] = banks
        for _ in gen:
            pass
        l = P.defer
        P.defer = None
        bset[0] = list(range(8))
        return l

    def collect(gen, banks):
        P.defer = []
        bset[0] = banks
        for _ in gen:
            pass
        l = P.defer
        P.defer = None
        bset[0] = list(range(8))
        return l

    for ps_ in range(2):
        load_weights(ps_)
        if ps_ == 1:
            dma(A("GAINM", 0, [1, 1024]), bc(fn_d, 1024), [], ["GAINM"])
        if ps_ == 0:
            mset("pool", A("KBEG0", 0, [1, 1040]), 1.0, ["KBEG0"])
            mset("pool", A("KBEG1", 0, [1, 1040]), 1.0, ["KBEG1"])
        tiles = [(s, t) for s in range(NSEQ) for t in range(NT_RUN)]
        prev = None
        for i, cur in enumerate(tiles + [None]):
            lists = []
            if prev is not None:
                pa = (prev[0], prev[1], (i - 1) % 2, prev[1] == 0)
                if ps_ == 0:
                    lists.append(collect(ml_stage2(*pa), [0, 1, 2, 3]))
                else:
                    lists.append(collect(gdn_v(*pa), [3]))
                    lists.append(collect(gdn_stage2(*pa), [0, 1, 2]))
            if cur is not None:
                ca = (cur[0], cur[1], i % 2, cur[1] == 0)
                if ps_ == 0:
                    lists.append(collect(ml_stage1(*ca), [4, 5, 6, 7]))
                else:
                    lists.append(collect(gdn_a(*ca), [4, 5]))
                    lists.append(collect(gdn_k(*ca), [4, 5]))
                    lists.append(collect(gdn_q(*ca), [6, 7]))
            P.merge(lists)
            prev = cur
    P.op("pool", None, ["out_dram"], [])
    fin = P.ops[-1]
    for so in store_ops:
        fin["deps"][so[0]] = True


    P.finalize()
    sems = {}
    for e, ng in P.ngen.items():
        for g in range(ng):
            sems[(e, g)] = es.enter_context(nc.semaphore("s_%s_%d" % (e, g)))
    for i in range(NDMA):
        sems[("dma", i)] = es.enter_context(nc.semaphore("s_dma_%d" % i))
    with nc.Block() as block:
        @block.sync
        def _(e):
            P.emit("sp", e, sems)

        @block.tensor
        def _(e):
            P.emit("pe", e, sems)

        @block.scalar
        def _(e):
            P.emit("act", e, sems)

        @block.vector
        def _(e):
            P.emit("dve", e, sems)

        @block.gpsimd
        def _(e):
            P.emit("pool", e, sems)
    es.close()
    return nc


NT_RUN = NT
_CACHE = {}


def kernel(x, attn_norm, w_in, m_i_bias, m_f_bias, m_out_norm, g_conv, g_a_log, g_dt_bias, g_out_norm, w_out,
           final_norm):
    f = lambda a: np.ascontiguousarray(np.asarray(a, dtype=np.float32))
    if "nc" not in _CACHE:
        _CACHE["nc"] = build_program()
    nc = _CACHE["nc"]
    x = f(x)
    shared = {
        "w_in": f(w_in).reshape(D, 8216),
        "w_out": f(w_out).reshape(2048, D),
        "attn_norm": f(attn_norm).reshape(8, 128),
        "m_i_bias": f(m_i_bias).reshape(1, 4),
        "m_f_bias": f(m_f_bias).reshape(1, 4),
        "m_out_norm": f(m_out_norm).reshape(1, 1024),
        "g_conv": f(g_conv).reshape(96, 128),
        "g_a_log": f(g_a_log).reshape(1, 8),
        "g_dt_bias": f(g_dt_bias).reshape(1, 8),
        "g_out_norm": f(g_out_norm).reshape(1, 128),
        "final_norm": f(final_norm).reshape(1, 1024),
        "consts": make_consts(),
    }
    in_maps = []
    for c in range(NCORES):
        m = dict(shared)
        m["x"] = x[c * NSEQ:(c + 1) * NSEQ].reshape(TOK, D)
        in_maps.append(m)
    res = run_bass_kernel_spmd(nc, in_maps, core_ids=list(range(NCORES)))
    outs = [np.asarray(r["out"]).reshape(NSEQ, SEQ, D) for r in res.results]
    return np.concatenate(outs, axis=0).astype(np.float32)
```

```python
import numpy as np
from contextlib import ExitStack
import concourse.bass as bass
import concourse.mybir as mybir
from concourse.bass_utils import run_bass_kernel_spmd

F32 = mybir.dt.float32
BF = mybir.dt.bfloat16
AF = mybir.ActivationFunctionType
ALU = mybir.AluOpType
AX = mybir.AxisListType

NCORES = 8
SEQ = 2048
NSEQ = 2
TOK = NSEQ * SEQ
L = 128
NT = SEQ // L
D = 1024
WC = 4112
EPS = 1e-6
GEN = 3000
NDMA = 24
C_MLE, C_MGT, C_MGE, C_ONE, C_ID, C_LEV = 0, 128, 256, 384, 512, 640
NCONST = 640 + 7 * 128


def make_consts():
    idx = np.arange(L)
    c = np.zeros((L, NCONST), np.float32)
    c[:, C_MLE:C_MLE + L] = idx[:, None] <= idx[None, :]
    c[:, C_MGT:C_MGT + L] = idx[:, None] > idx[None, :]
    c[:, C_MGE:C_MGE + L] = idx[:, None] >= idx[None, :]
    c[:, C_ONE:C_ONE + L] = 1.0
    c[:, C_ID:C_ID + L] = np.eye(L)
    for k in range(7):
        b2 = 2 << k
        m = np.where((idx[:, None] // b2) == (idx[None, :] // b2), -1.0, 0.0)
        m[idx, idx] = 1.0
        c[:, C_LEV + k * L:C_LEV + (k + 1) * L] = m
    return c


class Prog:
    def __init__(self):
        self.ops = []
        self.lastw = {}
        self.readers = {}
        self.defer = None
        self.efree = {}
        self.wdone = {}
        self.rdone = {}

    def op(self, eng, fn, reads=(), writes=(), dma=False, dur=300.0):
        if self.defer is not None:
            h = [None]
            self.defer.append(dict(eng=eng, fn=fn, reads=list(reads), writes=list(writes), dma=dma, dur=dur, h=h))
            return h
        self._sim(eng, reads, writes, dma, dur)
        return [self._op(eng, fn, reads, writes, dma)]

    def _est(self, eng, reads, writes):
        t = self.efree.get(eng, 0.0)
        for r in reads:
            t = max(t, self.wdone.get(r, 0.0))
        for w in writes:
            t = max(t, self.wdone.get(w, 0.0), self.rdone.get(w, 0.0))
        return t

    def _sim(self, eng, reads, writes, dma, dur):
        st = self._est(eng, reads, writes)
        if dma:
            self.efree[eng] = st + 100.0
            end = st + 3000.0
        else:
            end = st + dur
            self.efree[eng] = end
        for r in reads:
            self.rdone[r] = max(self.rdone.get(r, 0.0), end)
        for w in writes:
            self.wdone[w] = end + 150.0
        return st

    def merge(self, lists):
        lists = [l for l in lists if l]
        n = len(lists)
        accs = []
        for l in lists:
            wl, rl = {}, {}
            for i, d in enumerate(l):
                for r in d["reads"]:
                    rl[r] = i
                for w in d["writes"]:
                    wl[w] = i
            accs.append((wl, rl))
        pos = [0] * n

        def blocked(j, d):
            for i in range(j):
                pi = pos[i]
                if pi >= len(lists[i]):
                    continue
                wl, rl = accs[i]
                for r in d["reads"]:
                    if wl.get(r, -1) >= pi:
                        return True
                for w in d["writes"]:
                    if wl.get(w, -1) >= pi or rl.get(w, -1) >= pi:
                        return True
            return False

        while True:
            best = None
            for i, l in enumerate(lists):
                if pos[i] < len(l):
                    d = l[pos[i]]
                    if blocked(i, d):
                        continue
                    st = self._est(d["eng"], d["reads"], d["writes"])
                    if best is None or st < best[0]:
                        best = (st, i)
            if best is None:
                break
            i = best[1]
            d = lists[i][pos[i]]
            pos[i] += 1
            self._sim(d["eng"], d["reads"], d["writes"], d["dma"], d["dur"])
            d["h"][0] = self._op(d["eng"], d["fn"], d["reads"], d["writes"], d["dma"])
        assert all(pos[i] == len(lists[i]) for i in range(n))

    def _op(self, eng, fn, reads=(), writes=(), dma=False):
        idx = len(self.ops)
        deps = {}
        for r in reads:
            if r in self.lastw:
                deps[self.lastw[r]] = True
        for w in writes:
            if w in self.lastw:
                deps.setdefault(self.lastw[w], False)
            for rd in self.readers.get(w, ()):
                deps.setdefault(rd, False)
        self.ops.append(dict(eng=eng, fn=fn, deps=deps, dma=dma))
        for r in reads:
            self.readers.setdefault(r, []).append(idx)
        for w in writes:
            self.lastw[w] = idx
            self.readers[w] = []
        return idx

    def finalize(self):
        ops = self.ops
        last_dma_on_sem = {}
        ndma = 0
        for i, o in enumerate(ops):
            nd = {}
            for d, raw in o["deps"].items():
                od = ops[d]
                if od["dma"]:
                    nd[d] = raw
                elif od["eng"] == o["eng"]:
                    if o["eng"] != "pe" and raw and not o["dma"]:
                        nd[d] = raw
                    elif o["dma"]:
                        nd[d] = raw
                else:
                    nd[d] = raw
            if o["dma"]:
                s = ndma % NDMA
                ndma += 1
                if s in last_dma_on_sem:
                    nd[last_dma_on_sem[s]] = False
                last_dma_on_sem[s] = i
                o["dsem"] = s
            o["deps"] = nd
        needed = set()
        for o in ops:
            needed.update(o["deps"].keys())
        cnt = {}
        dcnt = {}
        for i, o in enumerate(ops):
            if o["dma"]:
                dcnt[o["dsem"]] = dcnt.get(o["dsem"], 0) + 16
                o["tok"] = (("dma", o["dsem"]), dcnt[o["dsem"]])
            elif i in needed:
                c = cnt.get(o["eng"], 0)
                cnt[o["eng"]] = c + 1
                o["tok"] = ((o["eng"], c // GEN), c % GEN + 1)
            else:
                o["tok"] = None
        self.ngen = {e: (c + GEN - 1) // GEN for e, c in cnt.items()}

    def emit(self, eng_name, eng, sems):
        waited = {}
        for o in self.ops:
            if o["eng"] != eng_name:
                continue
            for d in sorted(o["deps"].keys()):
                key, val = self.ops[d]["tok"]
                if waited.get(key, 0) >= val:
                    continue
                waited[key] = val
                eng.wait_ge(sems[key], val)
            if o["fn"] is None:
                continue
            ins = o["fn"](eng)
            if o["tok"] is not None:
                key, _ = o["tok"]
                ins.then_inc(sems[key], 16 if o["dma"] else 1)


def build_program():
    nc = bass.Bass("TRN2", target_bir_lowering=False)
    dt_in = lambda n, s: nc.dram_tensor(n, list(s), F32, kind="ExternalInput")
    x_d = dt_in("x", [TOK, D])
    win_d = dt_in("w_in", [D, 8216])
    wout_d = dt_in("w_out", [2048, D])
    an_d = dt_in("attn_norm", [8, 128])
    mib_d = dt_in("m_i_bias", [1, 4])
    mfb_d = dt_in("m_f_bias", [1, 4])
    mon_d = dt_in("m_out_norm", [1, 1024])
    gcv_d = dt_in("g_conv", [96, 128])
    gal_d = dt_in("g_a_log", [1, 8])
    gdt_d = dt_in("g_dt_bias", [1, 8])
    gon_d = dt_in("g_out_norm", [1, 128])
    fn_d = dt_in("final_norm", [1, 1024])
    cst_d = dt_in("consts", [L, NCONST])
    out_d = nc.dram_tensor("out", [TOK, D], F32, kind="ExternalOutput")
    part_d = nc.dram_tensor("partial", [TOK, D], F32, kind="Internal")

    P = Prog()
    es = ExitStack()
    T = {}

    def sb(name, cols, dt, parts=128):
        T[name] = es.enter_context(nc.sbuf_tensor(name, [parts, cols], dt))
        return T[name]

    sb("WIN", 8 * WC, BF)
    sb("WOUT", 8 * 1024, BF)
    sb("CONST", 640, F32)
    sb("IDB", 128, BF)
    sb("LEVB", 7 * 128, BF)
    sb("GAINM", 1024, F32)
    sb("GAING", 128, F32)
    sb("CW", 96, F32)
    sb("CWROW", 128, F32, parts=96)
    sb("AN8", 128, F32, parts=8)
    sb("GW", 8, F32)
    sb("BIASM", 8, F32)
    sb("DTB", 8, F32)
    sb("NEGA", 8, F32)
    sb("X0", 1024, F32)
    sb("X1", 1024, F32)
    sb("JUNK", 1028, F32)
    sb("HT0", 1024, BF)
    sb("HT1", 1024, BF)
    sb("MIX", 1024, BF)
    sb("MIXT", 1024, BF)
    sb("RES", 1028, F32)
    for n in ["SS", "RSTD", "SS2", "RSTD2"]:
        sb(n, 1, F32)
    for n in ["G8", "E1", "LFP", "LA", "CRT", "ECRT0", "ECRT1", "TA", "TB", "EIC", "EIR", "QSC", "RR", "SSH", "RS",
              "BETA0", "BETA1", "BE", "SSQ", "RQ", "SSQK", "RQK", "QS1", "QS2", "KSB", "KSR", "SSO", "RSO"]:
        sb(n, 24 if n in ("CRT", "ECRT0", "ECRT1") else 16, F32)
    sb("CONVIN0", 8 * 131, F32)
    sb("CONVIN1", 8 * 131, F32)
    sb("CONVIN2", 8 * 131, F32)
    sb("TMPC", 1024, F32)
    sb("HALO", 24 * 3, F32)
    sb("ACC", 1024, F32)
    sb("CV", 1024, BF)
    sb("QN", 1024, BF)
    sb("QE2", 1024, BF)
    sb("KN", 1024, BF)
    sb("KBEG0", 1040, BF)
    sb("KBEG1", 1040, BF)
    sb("KREV0", 1024, BF)
    sb("KREV1", 1024, BF)
    sb("VB0", 1024, BF)
    sb("VB1", 1024, BF)
    sb("QNT", 1024, BF)
    sb("QET0", 1024, BF)
    sb("QET1", 1024, BF)
    sb("KNT", 1024, BF)
    sb("GAM", 1024, F32)
    sb("GS", 1024, F32)
    sb("MM0", 1024, BF)
    sb("MM1", 1024, BF)
    sb("AQK", 1024, BF)
    sb("AQKT0", 1024, BF)
    sb("AQKT1", 1024, BF)
    sb("Z", 1024, BF)
    sb("TT", 1024, BF)
    sb("RP", 1024, BF)
    sb("NWT", 1024, BF)
    sb("VNEW", 1024, BF)
    sb("GG20", 1024, F32)
    sb("GG21", 1024, F32)
    sb("SG", 1028, F32)
    sb("SGB", 1040, BF)
    PS = es.enter_context(nc.psum_tensor("ps", [128, 4096], F32))
    PSB = PS.bitcast(BF)

    def A(name, off=0, *dims, p=128):
        t = T[name]
        return bass.AP(t, off, [[t.shape[1], p]] + [list(d) for d in dims])

    def PA(b, off=0, *dims, p=128):
        return bass.AP(PS, b * 512 + off, [[4096, p]] + [list(d) for d in dims])

    def PB(b, off=0, *dims, p=128):
        return bass.AP(PSB, b * 1024 + off, [[8192, p]] + [list(d) for d in dims])

    def CA(off, *dims, p=128):
        return A("CONST", off, *dims, p=p)

    bank = [0]

    bset = [list(range(8))]
    bctr = {}

    def nb():
        key = tuple(bset[0])
        c = bctr.get(key, 0)
        bctr[key] = c + 1
        return bset[0][c % len(bset[0])]

    def nel(ap):
        n = 1
        for st_, cn in list(ap.ap)[1:]:
            n *= cn
        return n

    def bk(b):
        return ("ps", b)

    dq = [0]

    def dma(out, in_, reads, writes, q=None, slow=False):
        if q is None:
            q = "sp"
        if slow:
            fn = lambda e: e.dma_start(out=out, in_=in_, allow_slow_non_contiguous=True)
        else:
            fn = lambda e: e.dma_start(out=out, in_=in_)
        return P.op(q, fn, reads, writes, dma=True)

    def mmg(out, pairs, reads, b, f32=False):
        n = len(pairs)
        for i, (l, r) in enumerate(pairs):
            d = max(60.0, nel(r) * 0.45) * (4.0 if f32 else 1.0)
            P.op("pe", lambda e, l=l, r=r, i=i: e.matmul(out, l, r, start=(i == 0), stop=(i == n - 1)),
                 reads, [bk(b)], dur=d)

    def tr(out, in_, ident, reads, b):
        P.op("pe", lambda e: e.transpose(out, in_, ident), reads, [bk(b)], dur=60.0)

    def act(out, in_, func, reads, writes, scale=None, bias=None):
        kw = {}
        if scale is not None:
            kw["scale"] = scale
        if bias is not None:
            kw["bias"] = bias
        P.op("act", lambda e: e.activation(out, in_, func, **kw), reads, writes, dur=250.0 + 0.85 * nel(out))

    def edur(eng, out):
        return (150.0 + 2.1 * nel(out)) if eng == "pool" else (160.0 + 1.02 * nel(out))

    def tt(eng, out, in0, in1, op, reads, writes):
        P.op(eng, lambda e: e.tensor_tensor(out, in0, in1, op), reads, writes, dur=edur(eng, out))

    def ts(eng, out, in0, s1, s2, op0, op1, reads, writes):
        if op1 is None:
            P.op(eng, lambda e: e.tensor_scalar(out, in0, s1, None, op0), reads, writes, dur=edur(eng, out))
        else:
            P.op(eng, lambda e: e.tensor_scalar(out, in0, s1, s2, op0, op1), reads, writes, dur=edur(eng, out))

    def stt(out, in0, scalar, in1, op0, op1, reads, writes):
        P.op("dve", lambda e: e.scalar_tensor_tensor(out, in0, scalar, in1, op0, op1), reads, writes,
             dur=edur("dve", out))

    def cp(eng, out, in_, reads, writes):
        if eng == "act":
            act(out, in_, AF.Copy, reads, writes)
        else:
            P.op(eng, lambda e: e.tensor_copy(out, in_), reads, writes, dur=edur(eng, out))

    def mset(eng, ap, val, writes):
        P.op(eng, lambda e: e.memset(ap, val), [], writes)

    def rsq(out, in_, mul, reads, w):
        act(out, in_, AF.Ln, reads, [w], scale=float(mul), bias=EPS)
        act(out, out, AF.Exp, [w], [w], scale=-0.5)

    IDB = lambda n=128: A("IDB", 0, [1, n], p=n)

    dma(A("CONST", 0, [1, 640]), cst_d.ap()[:, 0:640], [], ["CONST"])
    dma(A("JUNK", 0, [1, 896]), cst_d.ap()[:, 640:NCONST], [], ["JUNK"])
    cp("dve", A("IDB", 0, [1, 128]), CA(C_ID, [1, 128]), ["CONST"], ["IDB"])
    cp("dve", A("LEVB", 0, [1, 896]), A("JUNK", 0, [1, 896]), ["JUNK"], ["LEVB"])
    bc = lambda d, n: bass.AP(d, 0, [[0, 128], [1, n]])
    dma(A("GAINM", 0, [1, 1024]), bc(mon_d, 1024), [], ["GAINM"])
    dma(A("GAING", 0, [1, 128]), bc(gon_d, 128), [], ["GAING"])
    ts("pool", A("GAINM", 0, [1, 1024]), A("GAINM", 0, [1, 1024]), 0.5, None, ALU.mult, None, ["GAINM"], ["GAINM"])
    dma(A("BIASM", 0, [1, 4]), bc(mib_d, 4), [], ["BIASM"])
    dma(A("BIASM", 4, [1, 4]), bc(mfb_d, 4), [], ["BIASM"])
    dma(A("DTB", 0, [1, 8]), bc(gdt_d, 8), [], ["DTB"])
    dma(A("NEGA", 0, [1, 8]), bc(gal_d, 8), [], ["NEGA"])
    act(A("NEGA", 0, [1, 8]), A("NEGA", 0, [1, 8]), AF.Exp, ["NEGA"], ["NEGA"])
    ts("dve", A("NEGA", 0, [1, 8]), A("NEGA", 0, [1, 8]), -1.0, None, ALU.mult, None, ["NEGA"], ["NEGA"])
    dma(A("AN8", 0, [1, 128], p=8), an_d.ap(), [], ["AN8"])
    dma(A("CWROW", 0, [1, 128], p=96), gcv_d.ap(), [], ["CWROW"])
    b = nb()
    tr(PA(b, 0, [1, 8]), A("AN8", 0, [1, 128], p=8), CA(C_ID, [1, 8], p=8), ["AN8", "CONST"], b)
    cp("dve", A("GW", 0, [1, 8]), PA(b, 0, [1, 8]), [bk(b)], ["GW"])
    b = nb()
    tr(PA(b, 0, [1, 96]), A("CWROW", 0, [1, 128], p=96), CA(C_ID, [1, 96], p=96), ["CWROW", "CONST"], b)
    cp("dve", A("CW", 0, [1, 96]), PA(b, 0, [1, 96]), [bk(b)], ["CW"])

    cvt_i = [0]

    STGS = ["RES", "JUNK", "ACC", "TMPC", "X0", "X1", "GAM", "GS"]

    def load_weights(ps_):
        c0 = 0 if ps_ == 0 else 4104
        wtot = 4104 if ps_ == 0 else 4112
        chunks = [(j * 1024, 1024) for j in range(4)] + [(4096, wtot - 4096)]
        for kc in range(8):
            for (cj, w) in chunks:
                i = cvt_i[0]
                cvt_i[0] += 1
                stg = STGS[i % len(STGS)]
                dma(A(stg, 0, [1, w]), win_d.ap()[kc * 128:(kc + 1) * 128, c0 + cj:c0 + cj + w], [], [stg])
                dst = A("WIN", kc * WC + cj, [1, w])
                if i % 2 == 0:
                    act(dst, A(stg, 0, [1, w]), AF.Copy, [stg, "GW"], ["WIN"], scale=A("GW", kc, [1, 1]))
                else:
                    ts("dve", dst, A(stg, 0, [1, w]), A("GW", kc, [1, 1]), None, ALU.mult, None, [stg, "GW"], ["WIN"])
        r0 = ps_ * 1024
        for kc in range(8):
            i = cvt_i[0]
            cvt_i[0] += 1
            stg = STGS[i % len(STGS)]
            dma(A(stg, 0, [1, 1024]), wout_d.ap()[r0 + kc * 128:r0 + (kc + 1) * 128, :], [], [stg])
            cp(("act", "dve")[i % 2], A("WOUT", kc * 1024, [1, 1024]), A(stg, 0, [1, 1024]), [stg], ["WOUT"])

    ctx = {"HT": "HT0", "BETA": "BETA0"}

    def setp(p):
        ctx["HT"] = "HT%d" % p
        ctx["BETA"] = "BETA%d" % p

    def HTk(kc):
        return A(ctx["HT"], kc * 128, [1, 128])

    def proj_tm(b, c0, n):
        mmg(PA(b, 0, [1, n]), [(HTk(kc), A("WIN", kc * WC + c0, [1, n])) for kc in range(8)], [ctx["HT"], "WIN"], b)

    store_ops = []
    RUN = lambda g: [None for _ in g]
    hk = lambda nm: [(nm, 0), (nm, 1)]
    H = lambda nm, h: A(nm, h * 128, [1, 128])

    def phase_a(s, t, p, X=None):
        X = X or ("X%d" % p)
        r0 = s * SEQ + t * L
        dma(A(X, 0, [1, 1024]), x_d.ap()[r0:r0 + L, :], [], [X])
        act(A("JUNK", 0, [1, 1024]), A(X, 0, [1, 1024]), AF.Square, [X], ["JUNK"])
        P.op("dve", lambda e: e.tensor_reduce(A("SS", 0, [1, 1]), A("JUNK", 0, [1, 1024]), AX.X, ALU.add),
             ["JUNK"], ["SS"], dur=1200.0)
        rsq(A("RSTD", 0, [1, 1]), A("SS", 0, [1, 1]), 1.0 / D, ["SS"], "RSTD")
        ts("dve", A("AQK", 0, [1, 1024]), A(X, 0, [1, 1024]), A("RSTD", 0, [1, 1]), None, ALU.mult, None,
           [X, "RSTD"], ["AQK"])
        yield
        b = nb()
        for kc in range(8):
            tr(PB(b, kc * 128, [1, 128]), A("AQK", kc * 128, [1, 128]), IDB(), ["AQK", "IDB"], b)
        cp("act", A(ctx["HT"], 0, [1, 1024]), PB(b, 0, [1, 1024]), [bk(b)], [ctx["HT"]])
        yield

    def phase_d(ps_, s, t, p):
        X = "X%d" % p
        r0 = s * SEQ + t * L
        b = nb()
        for kc in range(8):
            tr(PB(b, kc * 128, [1, 128]), A("MIX", kc * 128, [1, 128]), IDB(), ["MIX", "IDB"], b)
        cp("act", A("MIXT", 0, [1, 1024]), PB(b, 0, [1, 1024]), [bk(b)], ["MIXT"])
        yield
        src = X if ps_ == 0 else "RES"
        for n in range(2):
            b = nb()
            mmg(PA(b, 0, [1, 512]),
                [(A("MIXT", kc * 128, [1, 128]), A("WOUT", kc * 1024 + n * 512, [1, 512])) for kc in range(8)],
                ["MIXT", "WOUT"], b)
            tt("dve", A("RES", n * 512, [1, 512]), PA(b, 0, [1, 512]), A(src, n * 512, [1, 512]), ALU.add,
               [bk(b), src], ["RES"])
            yield
        if ps_ == 0:
            store_ops.append(dma(part_d.ap()[r0:r0 + L, :], A("RES", 0, [1, 1024]), ["RES"], [("pd", r0)],
                                 q="pool"))
        else:
            act(A("RP", 0, [1, 1024]), A("RES", 0, [1, 1024]), AF.Square, ["RES"], hk("RP"))
            P.op("dve", lambda e: e.tensor_reduce(A("SS2", 0, [1, 1]), A("RP", 0, [1, 1024]), AX.X, ALU.add),
                 hk("RP"), ["SS2"], dur=1200.0)
            rsq(A("RSTD2", 0, [1, 1]), A("SS2", 0, [1, 1]), 1.0 / D, ["SS2"], "RSTD2")
            stt(A("RES", 0, [1, 1024]), A("RES", 0, [1, 1024]), A("RSTD2", 0, [1, 1]), A("GAINM", 0, [1, 1024]),
                ALU.mult, ALU.mult, ["RES", "RSTD2", "GAINM"], ["RES"])
            store_ops.append(dma(out_d.ap()[r0:r0 + L, :], A("RES", 0, [1, 1024]), ["RES"], ["out_dram"], q="pool"))
        yield

    def ml_stage1(s, t, p, first):
        QE = ("QN", "Z")[p]
        K2 = ("QE2", "TT")[p]
        K3 = ("KN", "RP")[p]
        VA = "KBEG%d" % p
        GG = "GG2%d" % p
        EC = "ECRT%d" % p
        setp(p)
        yield from phase_a(s, t, p)
        b = nb()
        mmg(PA(b, 0, [1, 8]), [(HTk(kc), A("WIN", kc * WC + 4096, [1, 8])) for kc in range(8)], [ctx["HT"], "WIN"], b)
        tt("dve", A("G8", 0, [1, 8]), PA(b, 0, [1, 8]), A("BIASM", 0, [1, 8]), ALU.add, [bk(b), "BIASM"], ["G8"])
        act(A("E1", 0, [1, 4]), A("G8", 4, [1, 4]), AF.Exp, ["G8"], ["E1"], scale=-1.0)
        act(A("LFP", 0, [1, 4]), A("E1", 0, [1, 4]), AF.Ln, ["E1"], ["LFP"], bias=1.0)
        ts("dve", A("LA", 0, [1, 4]), A("LFP", 0, [1, 4]), -1.0, None, ALU.mult, None, ["LFP"], ["LA"])
        yield
        b = nb()
        for i, cm in enumerate([C_MLE, C_MGT, C_ONE]):
            mmg(PA(b, i * 4, [1, 4]), [(CA(cm, [1, 128]), A("LA", 0, [1, 4]))], ["CONST", "LA"], b, f32=True)
        cp("dve", A("CRT", 0, [1, 12]), PA(b, 0, [1, 12]), [bk(b)], ["CRT"])
        act(A(EC, 0, [1, 12]), A("CRT", 0, [1, 12]), AF.Exp, ["CRT"], [EC])
        tt("dve", A("TA", 0, [1, 4]), A("G8", 0, [1, 4]), A("CRT", 0, [1, 4]), ALU.subtract, ["G8", "CRT"], ["TA"])
        tt("dve", A("TA", 4, [1, 4]), A("G8", 0, [1, 4]), A("CRT", 4, [1, 4]), ALU.add, ["G8", "CRT"], ["TA"])
        act(A("EIC", 0, [1, 8]), A("TA", 0, [1, 8]), AF.Exp, ["TA"], ["EIC"])
        ts("dve", A("QSC", 0, [1, 4]), A(EC, 0, [1, 4]), 128.0 ** -0.5, None, ALU.mult, None, [EC], ["QSC"])
        yield

    def ml_qk(s, t, p, first):
        QE = ("QN", "Z")[p]
        K2 = ("QE2", "TT")[p]
        K3 = ("KN", "RP")[p]
        VA = "KBEG%d" % p
        GG = "GG2%d" % p
        EC = "ECRT%d" % p
        setp(p)
        b = nb()
        proj_tm(b, 0, 512)
        tt("dve", A(QE, 0, [128, 4], [1, 128]), PA(b, 0, [128, 4], [1, 128]), A("QSC", 0, [1, 4], [0, 128]),
           ALU.mult, [bk(b), "QSC"], hk(QE))
        yield
        b = nb()
        proj_tm(b, 512, 512)
        tt("dve", A(K2, 0, [128, 4], [1, 128]), PA(b, 0, [128, 4], [1, 128]), A("EIC", 0, [1, 4], [0, 128]),
           ALU.mult, [bk(b), "EIC"], hk(K2))
        tt("dve", A(K3, 0, [128, 4], [1, 128]), PA(b, 0, [128, 4], [1, 128]), A("EIC", 4, [1, 4], [0, 128]),
           ALU.mult, [bk(b), "EIC"], hk(K3))
        yield

    def ml_voz(s, t, p, first):
        QE = ("QN", "Z")[p]
        K2 = ("QE2", "TT")[p]
        K3 = ("KN", "RP")[p]
        VA = "KBEG%d" % p
        GG = "GG2%d" % p
        EC = "ECRT%d" % p
        setp(p)
        for n in range(2):
            b = nb()
            proj_tm(b, 1024 + n * 512, 512)
            cp("act", A(VA, n * 514, [257, 2], [1, 256]), PA(b, 0, [256, 2], [1, 256]), [bk(b)], [VA])
            yield
        for n in range(2):
            b = nb()
            proj_tm(b, 3072 + n * 512, 512)
            act(A("GS", n * 512, [1, 512]), PA(b, 0, [1, 512]), AF.Silu, [bk(b)], ["GS"])
            yield
        tt("pool", A("GS", 0, [1, 1024]), A("GS", 0, [1, 1024]), A("GAINM", 0, [1, 1024]), ALU.mult,
           ["GS", "GAINM"], ["GS"])
        for n in range(2):
            b = nb()
            proj_tm(b, 2048 + n * 512, 512)
            act(A(GG, n * 512, [1, 512]), PA(b, 0, [1, 512]), AF.Tanh, [bk(b)], [GG], scale=0.5)
            yield
        stt(A(GG, 0, [1, 1024]), A(GG, 0, [1, 1024]), 1.0, A("GS", 0, [1, 1024]), ALU.add, ALU.mult,
            [GG, "GS"], [GG])
        yield

    def ml_stage2(s, t, p, first):
        QE = ("QN", "Z")[p]
        K2 = ("QE2", "TT")[p]
        K3 = ("KN", "RP")[p]
        VA = "KBEG%d" % p
        GG = "GG2%d" % p
        EC = "ECRT%d" % p
        if first:
            mset("pool", A("SG", 0, [1, 1028]), 0.0, ["SG"])
            mset("pool", A("SGB", 0, [1, 1040]), 0.0, ["SGB"])
        b = 0
        for h in range(4):
            tr(PB(b, h * 128, [1, 128]), A(QE, h * 128, [1, 128]), IDB(), hk(QE) + ["IDB"], b)
        for h in range(4):
            tr(PB(b, 512 + h * 128, [1, 128]), A(K2, h * 128, [1, 128]), IDB(), hk(K2) + ["IDB"], b)
        cp("act", A("QNT", 0, [1, 1024]), PB(b, 0, [1, 1024]), [bk(b)], ["QNT"])
        yield
        b = 1
        for h in range(4):
            mmg(PA(b, h * 128, [1, 128]), [(A("QNT", 512 + h * 128, [1, 128]), A("QNT", h * 128, [1, 128]))],
                ["QNT"], b)
        tt("dve", A("QET0", 0, [128, 4], [1, 128]), PA(b, 0, [128, 4], [1, 128]), CA(C_MLE, [0, 4], [1, 128]),
           ALU.mult, [bk(b), "CONST"], ["QET0"])
        yield
        for h in range(4):
            mmg(PA(h, 0, [1, 257]),
                [(A("QNT", h * 128, [1, 128]), A("SGB", h * 257, [1, 257])),
                 (A("QET0", h * 128, [1, 128]), A(VA, h * 257, [1, 257]))], ["QNT", "SGB", "QET0", VA], h)
        allb = [bk(h) for h in range(4)]
        rr4 = A("RR", 0, [1, 4])
        act(rr4, PA(0, 256, [512, 4]), AF.Abs, allb, ["RR"])
        ts("dve", rr4, rr4, 1.0, None, ALU.max, None, ["RR"], ["RR"])
        P.op("dve", lambda e, rr4=rr4: e.reciprocal(rr4, rr4), ["RR"], ["RR"])
        yield
        for h in range(4):
            act(A("RES", h * 256, [1, 256]), PA(h, 0, [1, 256]), AF.Square, [bk(h), "RR"], ["RES"],
                scale=A("RR", h, [1, 1]))
        P.op("dve", lambda e: e.tensor_reduce(A("SSH", 0, [1, 4]), A("RES", 0, [256, 4], [1, 256]), AX.X, ALU.add),
             ["RES"], ["SSH"], dur=1200.0)
        rsq(A("RS", 0, [1, 4]), A("SSH", 0, [1, 4]), 1.0 / 256, ["SSH"], "RS")
        tt("dve", A("RS", 0, [1, 4]), A("RS", 0, [1, 4]), rr4, ALU.mult, ["RS", "RR"], ["RS"])
        yield
        for h in range(4):
            stt(A("MIX", h * 256, [1, 256]), PA(h, 0, [1, 256]), A("RS", h, [1, 1]), A(GG, h * 256, [1, 256]),
                ALU.mult, ALU.mult, [bk(h), "RS", GG], ["MIX"])
        yield
        for h in range(4):
            mmg(PA(h, 0, [1, 257]), [(A(K3, h * 128, [1, 128]), A(VA, h * 257, [1, 257]))], hk(K3) + [VA], h)
            stt(A("SG", h * 257, [1, 257]), A("SG", h * 257, [1, 257]), A(EC, 8 + h, [1, 1]),
                PA(h, 0, [1, 257]), ALU.mult, ALU.add, ["SG", EC, bk(h)], ["SG"])
        cp("act", A("SGB", 0, [1, 1028]), A("SG", 0, [1, 1028]), ["SG"], ["SGB"])
        yield
        yield from phase_d(0, s, t, p)

    def gdn_group(g, p, first):
        KB, KR, VBn, QE_, MMn = "KBEG%d" % p, "KREV%d" % p, "VB%d" % p, "QET%d" % p, "MM%d" % p
        EC = "ECRT%d" % p
        CI, ACn, TMn, ce = {0: ("CONVIN1", "ACC", "TMPC", "dve"), 1: ("CONVIN2", "X1", "JUNK", "dve"),
                            2: ("CONVIN0", "ACC", "TMPC", "dve")}[g]
        HL = "HALO%d" % g
        if first:
            mset("pool", A("HALO", g * 24, [1, 24]), 0.0, [HL])
        cp("pool", A(CI, 0, [131, 8], [1, 3]), A("HALO", g * 24, [3, 8], [1, 3]), [HL], [CI])
        for n in range(2):
            b = nb()
            for c in range(n * 4, n * 4 + 4):
                mmg(PA(b, (c % 4) * 128, [1, 128]),
                    [(A("WIN", kc * WC + g * 1024 + c * 128, [1, 128]), HTk(kc)) for kc in range(8)],
                    ["WIN", ctx["HT"]], b)
            cp("act", A(CI, n * 4 * 131 + 3, [131, 4], [1, 128]), PA(b, 0, [128, 4], [1, 128]), [bk(b)], [CI])
            yield
        cp("pool", A("HALO", g * 24, [3, 8], [1, 3]), A(CI, 128, [131, 8], [1, 3]), [CI], [HL])
        acc = A(ACn, 0, [128, 8], [1, 128])
        tmp = A(TMn, 0, [128, 8], [1, 128])
        for j in (3, 2, 1, 0):
            xw = A(CI, j, [131, 8], [1, 128])
            wj = A("CW", j * 24 + g * 8, [1, 8], [0, 128])
            if j == 3:
                tt(ce, acc, xw, wj, ALU.mult, [CI, "CW"], [ACn])
            else:
                tt(ce, tmp, xw, wj, ALU.mult, [CI, "CW"], [TMn])
                tt(ce, acc, acc, tmp, ALU.add, [ACn, TMn], [ACn])
            yield
        act(A("CV", 0, [1, 1024]), A(ACn, 0, [1, 1024]), AF.Silu, [ACn], ["CV"])
        b = nb()
        for c in range(8):
            tr(PB(b, c * 128, [1, 128]), A("CV", c * 128, [1, 128]), IDB(), ["CV", "IDB"], b)
        yield
        pall = PB(b, 0, [128, 8], [1, 128])
        if g < 2:
            SQ, RQ = ("SSQ", "RQ") if g == 0 else ("SSQK", "RQK")
            act(A(TMn, 0, [1, 1024]), PB(b, 0, [1, 1024]), AF.Square, [bk(b)], [TMn])
            P.op("dve", lambda e: e.tensor_reduce(A(SQ, 0, [1, 8]), A(TMn, 0, [128, 8], [1, 128]), AX.X,
                                                  ALU.add), [TMn], [SQ], dur=1200.0)
            rsq(A(RQ, 0, [1, 8]), A(SQ, 0, [1, 8]), 1.0, [SQ], RQ)
        if g == 0:
            ts("dve", A("QS1", 0, [1, 8]), A("RQ", 0, [1, 8]), 128.0 ** -0.5, None, ALU.mult, None, ["RQ"], ["QS1"])
            tt("dve", A("QS2", 0, [1, 8]), A("QS1", 0, [1, 8]), A(EC, 0, [1, 8]), ALU.mult, ["QS1", EC], ["QS2"])
            tt("dve", A("QN", 0, [128, 8], [1, 128]), pall, A("QS1", 0, [1, 8], [0, 128]), ALU.mult,
               [bk(b), "QS1"], hk("QN"))
            tt("dve", A("QE2", 0, [128, 8], [1, 128]), pall, A("QS2", 0, [1, 8], [0, 128]), ALU.mult,
               [bk(b), "QS2"], hk("QE2"))
            yield
            for src, dst in (("QN", "QNT"), ("QE2", QE_)):
                b2 = nb()
                for c in range(8):
                    tr(PB(b2, c * 128, [1, 128]), A(src, c * 128, [1, 128]), IDB(), hk(src) + ["IDB"], b2)
                cp("act", A(dst, 0, [1, 1024]), PB(b2, 0, [1, 1024]), [bk(b2)], [dst])
                yield
        elif g == 1:
            tt("dve", A("KSB", 0, [1, 8]), A("RQK", 0, [1, 8]), A("BE", 0, [1, 8]), ALU.mult, ["RQK", "BE"], ["KSB"])
            tt("dve", A("KSR", 0, [1, 8]), A("RQK", 0, [1, 8]), A(EC, 8, [1, 8]), ALU.mult, ["RQK", EC], ["KSR"])
            tt("dve", A("KN", 0, [128, 8], [1, 128]), pall, A("RQK", 0, [1, 8], [0, 128]), ALU.mult,
               [bk(b), "RQK"], hk("KN"))
            tt("dve", A(KB, 0, [128, 8], [1, 128]), pall, A("KSB", 0, [1, 8], [0, 128]), ALU.mult,
               [bk(b), "KSB"], [KB])
            tt("dve", A(KR, 0, [128, 8], [1, 128]), pall, A("KSR", 0, [1, 8], [0, 128]), ALU.mult,
               [bk(b), "KSR"], [KR])
            yield
            b2 = nb()
            for c in range(8):
                tr(PB(b2, c * 128, [1, 128]), A("KN", c * 128, [1, 128]), IDB(), hk("KN") + ["IDB"], b2)
            cp("act", A("KNT", 0, [1, 1024]), PB(b2, 0, [1, 1024]), [bk(b2)], ["KNT"])
            yield
        else:
            tt("dve", A(VBn, 0, [128, 8], [1, 128]), pall, A(ctx["BETA"], 0, [1, 8], [0, 128]), ALU.mult,
               [bk(b), ctx["BETA"]], [VBn])
            yield

    def per_head(pairs_fn, reads, evac):
        for n in range(2):
            b = nb()
            for hh in range(4):
                h = n * 4 + hh
                mmg(PA(b, hh * 128, [1, 128]), pairs_fn(h), reads(n) if callable(reads) else reads, b)
            evac(n, b)
            yield

    def gdn_a(s, t, p, first):
        KB, KR, VBn, QE_, MMn, AQ = "KBEG%d" % p, "KREV%d" % p, "VB%d" % p, "QET%d" % p, "MM%d" % p, "AQKT%d" % p
        EC = "ECRT%d" % p
        GG = "GG2%d" % p
        setp(p)
        yield from phase_a(s, t, p, X="X0")
        b = nb()
        mmg(PA(b, 0, [1, 16]), [(HTk(kc), A("WIN", kc * WC + 4096, [1, 16])) for kc in range(8)], [ctx["HT"], "WIN"], b)
        act(A(ctx["BETA"], 0, [1, 8]), PA(b, 0, [1, 8]), AF.Exp, [bk(b)], [ctx["BETA"]], scale=-1.0)
        ts("dve", A(ctx["BETA"], 0, [1, 8]), A(ctx["BETA"], 0, [1, 8]), 1.0, None, ALU.add, None, [ctx["BETA"]], [ctx["BETA"]])
        bt_ = A(ctx["BETA"], 0, [1, 8])
        P.op("dve", lambda e, bt_=bt_: e.reciprocal(bt_, bt_), [ctx["BETA"]], [ctx["BETA"]])
        tt("dve", A("TA", 0, [1, 8]), PA(b, 8, [1, 8]), A("DTB", 0, [1, 8]), ALU.add, [bk(b), "DTB"], ["TA"])
        act(A("E1", 0, [1, 8]), A("TA", 0, [1, 8]), AF.Exp, ["TA"], ["E1"])
        act(A("LFP", 0, [1, 8]), A("E1", 0, [1, 8]), AF.Ln, ["E1"], ["LFP"], bias=1.0)
        tt("dve", A("LA", 0, [1, 8]), A("LFP", 0, [1, 8]), A("NEGA", 0, [1, 8]), ALU.mult, ["LFP", "NEGA"], ["LA"])
        yield
        b = nb()
        for i, cm in enumerate([C_MLE, C_MGT, C_ONE]):
            mmg(PA(b, i * 8, [1, 8]), [(CA(cm, [1, 128]), A("LA", 0, [1, 8]))], ["CONST", "LA"], b, f32=True)
        cp("dve", A("CRT", 0, [1, 24]), PA(b, 0, [1, 24]), [bk(b)], ["CRT"])
        act(A(EC, 0, [1, 24]), A("CRT", 0, [1, 24]), AF.Exp, ["CRT"], [EC])
        tt("dve", A("BE", 0, [1, 8]), A(ctx["BETA"], 0, [1, 8]), A(EC, 0, [1, 8]), ALU.mult, [ctx["BETA"], EC], ["BE"])
        tt("dve", A("JUNK", 0, [128, 8], [1, 128]), CA(C_MGT, [0, 8], [1, 128]), A("LA", 0, [1, 8], [0, 128]),
           ALU.mult, ["CONST", "LA"], ["JUNK"])
        yield
        for n in range(2):
            b = nb()
            mmg(PA(b, 0, [1, 512]), [(CA(C_MLE, [1, 128]), A("JUNK", n * 512, [1, 512]))], ["CONST", "JUNK"], b, f32=True)
            act(A("GAM", n * 512, [1, 512]), PA(b, 0, [1, 512]), AF.Exp, [bk(b)], ["GAM"])
            yield
        tt("pool", A("GS", 0, [128, 8], [1, 128]), A("GAM", 0, [128, 8], [1, 128]), CA(C_MGT, [0, 8], [1, 128]),
           ALU.mult, ["GAM", "CONST"], ["GS"])
        tt("pool", A("GS", 0, [128, 8], [1, 128]), A("GS", 0, [128, 8], [1, 128]), A(ctx["BETA"], 0, [1, 8], [0, 128]),
           ALU.mult, ["GS", ctx["BETA"]], ["GS"])
        tt("pool", A("GAM", 0, [128, 8], [1, 128]), A("GAM", 0, [128, 8], [1, 128]), CA(C_MGE, [0, 8], [1, 128]),
           ALU.mult, ["GAM", "CONST"], ["GAM"])
        yield

    def gdn_k(s, t, p, first):
        KB, KR, VBn, QE_, MMn, AQ = "KBEG%d" % p, "KREV%d" % p, "VB%d" % p, "QET%d" % p, "MM%d" % p, "AQKT%d" % p
        EC = "ECRT%d" % p
        GG = "GG2%d" % p
        setp(p)
        yield from gdn_group(1, p, first)
        yield from per_head(lambda h: [(H("KNT", h), H("KNT", h))], ["KNT"],
                            lambda n, b: tt("dve", A(MMn, n * 512, [1, 512]), PA(b, 0, [1, 512]),
                                            A("GS", n * 512, [1, 512]), ALU.mult, [bk(b), "GS"], [(MMn, n)]))
        for n in range(2):
            tt("pool", A(MMn, n * 512, [128, 4], [1, 128]), A(MMn, n * 512, [128, 4], [1, 128]),
               A("IDB", 0, [0, 4], [1, 128]), ALU.add, [(MMn, n), "IDB"], [(MMn, n)])
        yield

    def gdn_q(s, t, p, first):
        KB, KR, VBn, QE_, MMn, AQ = "KBEG%d" % p, "KREV%d" % p, "VB%d" % p, "QET%d" % p, "MM%d" % p, "AQKT%d" % p
        EC = "ECRT%d" % p
        GG = "GG2%d" % p
        setp(p)
        for n in range(2):
            b = nb()
            proj_tm(b, 3072 + n * 512, 512)
            act(A(GG, n * 512, [1, 512]), PA(b, 0, [1, 512]), AF.Silu, [bk(b)], [GG])
            yield
        tt("pool", A(GG, 0, [128, 8], [1, 128]), A(GG, 0, [128, 8], [1, 128]), A("GAING", 0, [0, 8], [1, 128]),
           ALU.mult, [GG, "GAING"], [GG])
        yield from gdn_group(0, p, first)
        yield from per_head(lambda h: [(H("QNT", h), H("KNT", h))], ["QNT", "KNT"],
                            lambda n, b: tt("dve", A("AQK", n * 512, [1, 512]), PA(b, 0, [1, 512]),
                                            A("GAM", n * 512, [1, 512]), ALU.mult, [bk(b), "GAM"], ["AQK"]))
        b = nb()
        for c in range(8):
            tr(PB(b, c * 128, [1, 128]), A("AQK", c * 128, [1, 128]), IDB(), ["AQK", "IDB"], b)
        cp("act", A(AQ, 0, [1, 1024]), PB(b, 0, [1, 1024]), [bk(b)], [AQ])
        yield

    def gdn_v(s, t, p, first):
        setp(p)
        yield from gdn_group(2, p, first)

    def gdn_stage2(s, t, p, first):
        KB, KR, VBn, QE_, MMn, AQ = "KBEG%d" % p, "KREV%d" % p, "VB%d" % p, "QET%d" % p, "MM%d" % p, "AQKT%d" % p
        EC = "ECRT%d" % p
        GG = "GG2%d" % p
        r0 = s * SEQ + t * L
        if first:
            mset("pool", A("SG", 0, [1, 1028]), 0.0, ["SG"])
            mset("pool", A("SGB", 0, [1, 1040]), 0.0, ["SGB"])
        dma(A("RES", 0, [1, 1024]), part_d.ap()[r0:r0 + L, :], [("pd", r0)], ["RES"])
        for n in range(2):
            b = nb()
            for hh in range(4):
                mmg(PA(b, hh * 128, [1, 128]), [(H(MMn, n * 4 + hh), IDB())], [(MMn, n), "IDB"], b)
            tt("dve", A("Z", n * 512, [128, 4], [1, 128]), PA(b, 0, [128, 4], [1, 128]),
               A("LEVB", 0, [0, 4], [1, 128]), ALU.mult, [bk(b), "LEVB"], [("Z", n)])
            tt("pool", A("TT", n * 512, [128, 4], [1, 128]), A(MMn, n * 512, [128, 4], [1, 128]),
               A("LEVB", 0, [0, 4], [1, 128]), ALU.mult, [(MMn, n), "LEVB"], [("TT", n)])
            yield
        for k in range(1, 7):
            for n in range(2):
                b = nb()
                for hh in range(4):
                    h = n * 4 + hh
                    mmg(PA(b, hh * 128, [1, 128]), [(H(MMn, h), H("Z", h))], [(MMn, n), ("Z", n)], b)
                tt("dve", A("RP", n * 512, [128, 4], [1, 128]), PA(b, 0, [128, 4], [1, 128]),
                   A("LEVB", k * 128, [0, 4], [1, 128]), ALU.mult, [bk(b), "LEVB"], [("RP", n)])
                yield
            for n in range(2):
                zb_ = nb()
                for hh in range(4):
                    h = n * 4 + hh
                    mmg(PA(zb_, hh * 128, [1, 128]), [(H("TT", h), H("RP", h))], [("TT", n), ("RP", n)], zb_)
                tb_ = None
                if k < 6:
                    tb_ = nb()
                    for hh in range(4):
                        h = n * 4 + hh
                        mmg(PA(tb_, hh * 128, [1, 128]), [(H("RP", h), H("TT", h))], [("TT", n), ("RP", n)], tb_)
                cp("act", A("Z", n * 512, [1, 512]), PA(zb_, 0, [1, 512]), [bk(zb_)], [("Z", n)])
                if k < 6:
                    cp("act" if n == 0 else "dve", A("TT", n * 512, [1, 512]), PA(tb_, 0, [1, 512]),
                       [bk(tb_)], [("TT", n)])
                yield
        yield from per_head(lambda h: [(H(KB, h), H("Z", h))], lambda n: [KB, ("Z", n)],
                            lambda n, b: act(A("NWT", n * 512, [1, 512]), PA(b, 0, [1, 512]), AF.Copy, [bk(b)],
                                             [("NWT", n)], scale=-1.0))
        yield from per_head(lambda h: [(H("Z", h), H(VBn, h)), (H("NWT", h), H("SGB", h))],
                            lambda n: [("Z", n), VBn, ("NWT", n), "SGB"],
                            lambda n, b: cp("act", A("VNEW", n * 512, [1, 512]), PA(b, 0, [1, 512]), [bk(b)],
                                            [("VNEW", n)]))
        ob = []

        def o_evac(n, b):
            ob.append(b)
            act(A("RP", n * 512, [1, 512]), PA(b, 0, [1, 512]), AF.Square, [bk(b)], [("RP", n)])
        yield from per_head(lambda h: [(H(QE_, h), H("SGB", h)), (H(AQ, h), H("VNEW", h))],
                            lambda n: [QE_, "SGB", AQ, ("VNEW", n)], o_evac)
        P.op("dve", lambda e: e.tensor_reduce(A("SSO", 0, [1, 8]), A("RP", 0, [128, 8], [1, 128]), AX.X, ALU.add),
             hk("RP"), ["SSO"], dur=1200.0)
        rsq(A("RSO", 0, [1, 8]), A("SSO", 0, [1, 8]), 1.0 / 128, ["SSO"], "RSO")
        tt("pool", A(GG, 0, [128, 8], [1, 128]), A(GG, 0, [128, 8], [1, 128]), A("RSO", 0, [1, 8], [0, 128]),
           ALU.mult, [GG, "RSO"], [GG])
        for n in range(2):
            tt("dve", A("MIX", n * 512, [1, 512]), PA(ob[n], 0, [1, 512]), A(GG, n * 512, [1, 512]), ALU.mult,
               [bk(ob[n]), GG], ["MIX"])
        yield
        tt("pool", A("SG", 0, [128, 8], [1, 128]), A("SG", 0, [128, 8], [1, 128]), A(EC, 16, [1, 8], [0, 128]),
           ALU.mult, ["SG", EC], ["SG"])
        yield from per_head(lambda h: [(H(KR, h), H("VNEW", h))], lambda n: [KR, ("VNEW", n)],
                            lambda n, b: tt("dve", A("SG", n * 512, [1, 512]), PA(b, 0, [1, 512]),
                                            A("SG", n * 512, [1, 512]), ALU.add, [bk(b), "SG"], ["SG"]))
        cp("act", A("SGB", 0, [1, 1024]), A("SG", 0, [1, 1024]), ["SG"], ["SGB"])
        yield
        yield from phase_d(1, s, t, p)

    def collect(gen, banks):
        if gen is None:
            return []
        P.defer = []
        bset[0] = banks
        for _ in gen:
            pass
        l = P.defer
        P.defer = None
        bset[0] = list(range(8))
        return l

    def collect(gen, banks):
        P.defer = []
        bset[0] = banks
        for _ in gen:
            pass
        l = P.defer
        P.defer = None
        bset[0] = list(range(8))
        return l

    for ps_ in range(2):
        load_weights(ps_)
        if ps_ == 1:
            dma(A("GAINM", 0, [1, 1024]), bc(fn_d, 1024), [], ["GAINM"])
        if ps_ == 0:
            mset("pool", A("KBEG0", 0, [1, 1040]), 1.0, ["KBEG0"])
            mset("pool", A("KBEG1", 0, [1, 1040]), 1.0, ["KBEG1"])
        tiles = [(s, t) for s in range(NSEQ) for t in range(NT_RUN)]
        prev = None
        for i, cur in enumerate(tiles + [None]):
            lists = []
            if prev is not None:
                pa = (prev[0], prev[1], (i - 1) % 2, prev[1] == 0)
                if ps_ == 0:
                    lists.append(collect(ml_stage2(*pa), [0, 1, 2, 3]))
                else:
                    lists.append(collect(gdn_v(*pa), [3]))
                    lists.append(collect(gdn_stage2(*pa), [0, 1, 2]))
            if cur is not None:
                ca = (cur[0], cur[1], i % 2, cur[1] == 0)
                if ps_ == 0:
                    lists.append(collect(ml_stage1(*ca), [4]))
                    lists.append(collect(ml_qk(*ca), [5]))
                    lists.append(collect(ml_voz(*ca), [6, 7]))
                else:
                    lists.append(collect(gdn_a(*ca), [4, 5]))
                    lists.append(collect(gdn_k(*ca), [4, 5]))
                    lists.append(collect(gdn_q(*ca), [6, 7]))
            P.merge(lists)
            prev = cur
    P.op("pool", None, ["out_dram"], [])
    fin = P.ops[-1]
    for so in store_ops:
        fin["deps"][so[0]] = True


    P.finalize()
    sems = {}
    for e, ng in P.ngen.items():
        for g in range(ng):
            sems[(e, g)] = es.enter_context(nc.semaphore("s_%s_%d" % (e, g)))
    for i in range(NDMA):
        sems[("dma", i)] = es.enter_context(nc.semaphore("s_dma_%d" % i))
    with nc.Block() as block:
        @block.sync
        def _(e):
            P.emit("sp", e, sems)

        @block.tensor
        def _(e):
            P.emit("pe", e, sems)

        @block.scalar
        def _(e):
            P.emit("act", e, sems)

        @block.vector
        def _(e):
            P.emit("dve", e, sems)

        @block.gpsimd
        def _(e):
            P.emit("pool", e, sems)
    es.close()
    return nc


NT_RUN = NT
_CACHE = {}


def kernel(x, attn_norm, w_in, m_i_bias, m_f_bias, m_out_norm, g_conv, g_a_log, g_dt_bias, g_out_norm, w_out,
           final_norm):
    f = lambda a: np.ascontiguousarray(np.asarray(a, dtype=np.float32))
    if "nc" not in _CACHE:
        _CACHE["nc"] = build_program()
    nc = _CACHE["nc"]
    x = f(x)
    shared = {
        "w_in": f(w_in).reshape(D, 8216),
        "w_out": f(w_out).reshape(2048, D),
        "attn_norm": f(attn_norm).reshape(8, 128),
        "m_i_bias": f(m_i_bias).reshape(1, 4),
        "m_f_bias": f(m_f_bias).reshape(1, 4),
        "m_out_norm": f(m_out_norm).reshape(1, 1024),
        "g_conv": f(g_conv).reshape(96, 128),
        "g_a_log": f(g_a_log).reshape(1, 8),
        "g_dt_bias": f(g_dt_bias).reshape(1, 8),
        "g_out_norm": f(g_out_norm).reshape(1, 128),
        "final_norm": f(final_norm).reshape(1, 1024),
        "consts": make_consts(),
    }
    in_maps = []
    for c in range(NCORES):
        m = dict(shared)
        m["x"] = x[c * NSEQ:(c + 1) * NSEQ].reshape(TOK, D)
        in_maps.append(m)
    res = run_bass_kernel_spmd(nc, in_maps, core_ids=list(range(NCORES)))
    outs = [np.asarray(r["out"]).reshape(NSEQ, SEQ, D) for r in res.results]
    return np.concatenate(outs, axis=0).astype(np.float32)
```

```python
import numpy as np
from contextlib import ExitStack
import concourse.bass as bass
import concourse.mybir as mybir
from concourse.bass_utils import run_bass_kernel_spmd

F32 = mybir.dt.float32
BF = mybir.dt.bfloat16
AF = mybir.ActivationFunctionType
ALU = mybir.AluOpType
AX = mybir.AxisListType

NCORES = 8
SEQ = 2048
NSEQ = 2
TOK = NSEQ * SEQ
L = 128
NT = SEQ // L
D = 1024
WC = 4112
EPS = 1e-6
GEN = 3000
NDMA = 24
C_MLE, C_MGT, C_MGE, C_ONE, C_ID, C_LEV = 0, 128, 256, 384, 512, 640
NCONST = 640 + 7 * 128


def make_consts():
    idx = np.arange(L)
    c = np.zeros((L, NCONST), np.float32)
    c[:, C_MLE:C_MLE + L] = idx[:, None] <= idx[None, :]
    c[:, C_MGT:C_MGT + L] = idx[:, None] > idx[None, :]
    c[:, C_MGE:C_MGE + L] = idx[:, None] >= idx[None, :]
    c[:, C_ONE:C_ONE + L] = 1.0
    c[:, C_ID:C_ID + L] = np.eye(L)
    for k in range(7):
        b2 = 2 << k
        m = np.where((idx[:, None] // b2) == (idx[None, :] // b2), -1.0, 0.0)
        m[idx, idx] = 1.0
        c[:, C_LEV + k * L:C_LEV + (k + 1) * L] = m
    return c


class Prog:
    def __init__(self):
        self.ops = []
        self.lastw = {}
        self.readers = {}
        self.defer = None
        self.cur_tbl = None
        self.efree = {}
        self.wdone = {}
        self.rdone = {}

    def _norm(self, keys):
        return ["JUNK" if k == "GS" else k for k in keys]

    def op(self, eng, fn, reads=(), writes=(), dma=False, dur=300.0, tbl=None):
        reads = self._norm(reads)
        writes = self._norm(writes)
        if self.defer is not None:
            h = [None]
            self.defer.append(dict(eng=eng, fn=fn, reads=list(reads), writes=list(writes), dma=dma, dur=dur, h=h,
                                   tbl=tbl))
            return h
        self._sim(eng, reads, writes, dma, dur, tbl)
        return [self._op(eng, fn, reads, writes, dma)]

    def _est(self, eng, reads, writes, tbl=None, pen=None):
        t = self.efree.get(eng, 0.0)
        if tbl is not None and tbl != self.cur_tbl:
            t += TBL_NS if pen is None else pen
        for r in reads:
            t = max(t, self.wdone.get(r, 0.0))
        for w in writes:
            t = max(t, self.wdone.get(w, 0.0), self.rdone.get(w, 0.0))
        return t

    def _sim(self, eng, reads, writes, dma, dur, tbl=None):
        st = self._est(eng, reads, writes, tbl)
        if tbl is not None:
            self.cur_tbl = tbl
        if dma:
            self.efree[eng] = st + 100.0
            end = st + 3000.0
        else:
            end = st + dur
            self.efree[eng] = end
        for r in reads:
            self.rdone[r] = max(self.rdone.get(r, 0.0), end)
        for w in writes:
            self.wdone[w] = end + 150.0
        return st

    def merge(self, lists):
        lists = [l for l in lists if l]
        n = len(lists)
        accs = []
        for l in lists:
            wl, rl = {}, {}
            for i, d in enumerate(l):
                for r in d["reads"]:
                    rl[r] = i
                for w in d["writes"]:
                    wl[w] = i
            accs.append((wl, rl))
        pos = [0] * n
        rems = []
        for l in lists:
            r_ = [0.0] * (len(l) + 1)
            for i in range(len(l) - 1, -1, -1):
                r_[i] = r_[i + 1] + l[i]["dur"]
            rems.append(r_)

        def blocked(j, d):
            for i in range(j):
                pi = pos[i]
                if pi >= len(lists[i]):
                    continue
                wl, rl = accs[i]
                for r in d["reads"]:
                    if wl.get(r, -1) >= pi:
                        return True
                for w in d["writes"]:
                    if wl.get(w, -1) >= pi or rl.get(w, -1) >= pi:
                        return True
            return False

        while True:
            best = None
            for i, l in enumerate(lists):
                if pos[i] < len(l):
                    d = l[pos[i]]
                    if blocked(i, d):
                        continue
                    st = self._est(d["eng"], d["reads"], d["writes"], d.get("tbl"), TBL_PEN) - ALPHA * rems[i][pos[i]]
                    if best is None or st < best[0]:
                        best = (st, i)
            if best is None:
                break
            i = best[1]
            d = lists[i][pos[i]]
            pos[i] += 1
            self._sim(d["eng"], d["reads"], d["writes"], d["dma"], d["dur"], d.get("tbl"))
            d["h"][0] = self._op(d["eng"], d["fn"], d["reads"], d["writes"], d["dma"])
        assert all(pos[i] == len(lists[i]) for i in range(n))

    def _op(self, eng, fn, reads=(), writes=(), dma=False):
        idx = len(self.ops)
        deps = {}
        for r in reads:
            if r in self.lastw:
                deps[self.lastw[r]] = True
        for w in writes:
            if w in self.lastw:
                deps.setdefault(self.lastw[w], False)
            for rd in self.readers.get(w, ()):
                deps.setdefault(rd, False)
        self.ops.append(dict(eng=eng, fn=fn, deps=deps, dma=dma))
        for r in reads:
            self.readers.setdefault(r, []).append(idx)
        for w in writes:
            self.lastw[w] = idx
            self.readers[w] = []
        return idx

    def finalize(self):
        ops = self.ops
        last_dma_on_sem = {}
        ndma = 0
        for i, o in enumerate(ops):
            nd = {}
            for d, raw in o["deps"].items():
                od = ops[d]
                if od["dma"]:
                    nd[d] = raw
                elif od["eng"] == o["eng"]:
                    if o["eng"] != "pe" and raw and not o["dma"]:
                        nd[d] = raw
                    elif o["dma"]:
                        nd[d] = raw
                else:
                    nd[d] = raw
            if o["dma"]:
                s = ndma % NDMA
                ndma += 1
                if s in last_dma_on_sem:
                    nd[last_dma_on_sem[s]] = False
                last_dma_on_sem[s] = i
                o["dsem"] = s
            o["deps"] = nd
        needed = set()
        for o in ops:
            needed.update(o["deps"].keys())
        cnt = {}
        dcnt = {}
        for i, o in enumerate(ops):
            if o["dma"]:
                dcnt[o["dsem"]] = dcnt.get(o["dsem"], 0) + 16
                o["tok"] = (("dma", o["dsem"]), dcnt[o["dsem"]])
            elif i in needed:
                c = cnt.get(o["eng"], 0)
                cnt[o["eng"]] = c + 1
                o["tok"] = ((o["eng"], c // GEN), c % GEN + 1)
            else:
                o["tok"] = None
        self.ngen = {e: (c + GEN - 1) // GEN for e, c in cnt.items()}

    def emit(self, eng_name, eng, sems):
        waited = {}
        for o in self.ops:
            if o["eng"] != eng_name:
                continue
            for d in sorted(o["deps"].keys()):
                key, val = self.ops[d]["tok"]
                if waited.get(key, 0) >= val:
                    continue
                waited[key] = val
                eng.wait_ge(sems[key], val)
            if o["fn"] is None:
                continue
            ins = o["fn"](eng)
            if o["tok"] is not None:
                key, _ = o["tok"]
                ins.then_inc(sems[key], 16 if o["dma"] else 1)


def build_program():
    nc = bass.Bass("TRN2", target_bir_lowering=False)
    dt_in = lambda n, s: nc.dram_tensor(n, list(s), F32, kind="ExternalInput")
    x_d = dt_in("x", [TOK, D])
    win_d = dt_in("w_in", [D, 8216])
    wout_d = dt_in("w_out", [2048, D])
    an_d = dt_in("attn_norm", [8, 128])
    mib_d = dt_in("m_i_bias", [1, 4])
    mfb_d = dt_in("m_f_bias", [1, 4])
    mon_d = dt_in("m_out_norm", [1, 1024])
    gcv_d = dt_in("g_conv", [96, 128])
    gal_d = dt_in("g_a_log", [1, 8])
    gdt_d = dt_in("g_dt_bias", [1, 8])
    gon_d = dt_in("g_out_norm", [1, 128])
    fn_d = dt_in("final_norm", [1, 1024])
    cst_d = dt_in("consts", [L, NCONST])
    out_d = nc.dram_tensor("out", [TOK, D], F32, kind="ExternalOutput")
    part_d = nc.dram_tensor("partial", [TOK, D], F32, kind="Internal")

    P = Prog()
    es = ExitStack()
    T = {}

    def sb(name, cols, dt, parts=128):
        T[name] = es.enter_context(nc.sbuf_tensor(name, [parts, cols], dt))
        return T[name]

    sb("WIN", 8 * WC, BF)
    sb("WOUT", 8 * 1024, BF)
    sb("CONST", 640, F32)
    sb("IDB", 128, BF)
    sb("LEVB", 7 * 128, BF)
    sb("GAINM", 1024, F32)
    sb("GAING", 128, F32)
    sb("CW", 96, F32)
    sb("CWROW", 128, F32, parts=96)
    sb("AN8", 128, F32, parts=8)
    sb("GW", 8, F32)
    sb("BIASM", 8, F32)
    sb("DTB", 8, F32)
    sb("NEGA", 8, F32)
    sb("X0", 1024, F32)
    sb("X1", 1024, F32)
    sb("JUNK", 1028, F32)
    sb("HT0", 1024, BF)
    sb("HT1", 1024, BF)
    sb("HT2", 1024, BF)
    sb("MIX0", 1024, BF)
    sb("MIX1", 1024, BF)
    sb("MIXT", 1024, BF)
    sb("RES", 1028, F32)
    for n in ["SS", "RSTD", "SS2", "RSTD2"]:
        sb(n, 1, F32)
    for n in ["G8", "E1", "LFP", "LA", "CRT", "ECRT0", "ECRT1", "TA", "TB", "EIC", "EIR", "QSC", "RR", "SSH", "RS",
              "BETA0", "BETA1", "BE", "SSQ", "RQ", "SSQK", "RQK", "QS1", "QS2", "KSB", "KSR", "SSO", "RSO"]:
        sb(n, 24 if n in ("CRT", "ECRT0", "ECRT1") else 16, F32)
    sb("TMPC", 1024, F32)
    sb("HALO", 24 * 3, BF)
    sb("ACC", 1024, F32)
    sb("XB0", 8 * 131, BF)
    sb("XB1", 8 * 131, BF)
    sb("XB2", 8 * 131, BF)
    sb("DWA", 6 * 1024, BF)
    sb("QN", 1024, BF)
    sb("QE2", 1024, BF)
    sb("KN", 1024, BF)
    sb("KBEG0", 1040, BF)
    sb("KBEG1", 1040, BF)
    sb("KREV0", 1024, BF)
    sb("KREV1", 1024, BF)
    sb("VB0", 1024, BF)
    sb("VB1", 1024, BF)
    sb("QNT", 1024, BF)
    sb("QET0", 1024, BF)
    sb("QET1", 1024, BF)
    sb("KNT", 1024, BF)
    sb("GAM", 1024, F32)
    sb("MM0", 1024, BF)
    sb("MM1", 1024, BF)
    sb("AQK", 1024, BF)
    sb("AQKT0", 1024, BF)
    sb("AQKT1", 1024, BF)
    sb("Z", 1024, BF)
    sb("TT", 1024, BF)
    sb("RP", 1024, BF)
    sb("NWT", 1024, BF)
    sb("VNEW", 1024, BF)
    sb("GG20", 1024, BF)
    sb("GG21", 1024, BF)
    sb("SG", 1028, F32)
    sb("SGB", 1040, BF)
    PS = es.enter_context(nc.psum_tensor("ps", [128, 4096], F32))
    for nm_ in ("ACC", "TMPC", "X1"):
        T[nm_ + "_bf"] = T[nm_].bitcast(BF)
    PSB = PS.bitcast(BF)

    def A(name, off=0, *dims, p=128):
        if name == "GS":
            name = "JUNK"
        t = T[name]
        return bass.AP(t, off, [[t.shape[1], p]] + [list(d) for d in dims])

    def PA(b, off=0, *dims, p=128):
        return bass.AP(PS, b * 512 + off, [[4096, p]] + [list(d) for d in dims])

    def PB(b, off=0, *dims, p=128):
        return bass.AP(PSB, b * 1024 + off, [[8192, p]] + [list(d) for d in dims])

    def CA(off, *dims, p=128):
        return A("CONST", off, *dims, p=p)

    bank = [0]

    bset = [list(range(8))]
    bctr = {}

    def nb():
        key = tuple(bset[0])
        c = bctr.get(key, 0)
        bctr[key] = c + 1
        return bset[0][c % len(bset[0])]

    def nel(ap):
        n = 1
        for st_, cn in list(ap.ap)[1:]:
            n *= cn
        return n

    def bk(b):
        return ("ps", b)

    dq = [0]

    def dma(out, in_, reads, writes, q=None, slow=False):
        if q is None:
            q = "sp"
        if slow:
            fn = lambda e: e.dma_start(out=out, in_=in_, allow_slow_non_contiguous=True)
        else:
            fn = lambda e: e.dma_start(out=out, in_=in_)
        return P.op(q, fn, reads, writes, dma=True)

    def mmg(out, pairs, reads, b, f32=False):
        n = len(pairs)
        for i, (l, r) in enumerate(pairs):
            d = max(60.0, nel(r) * 0.45) * (4.0 if f32 else 1.0)
            P.op("pe", lambda e, l=l, r=r, i=i: e.matmul(out, l, r, start=(i == 0), stop=(i == n - 1)),
                 reads, [bk(b)], dur=d)

    def tr(out, in_, ident, reads, b):
        P.op("pe", lambda e: e.transpose(out, in_, ident), reads, [bk(b)], dur=60.0)

    def act(out, in_, func, reads, writes, scale=None, bias=None):
        kw = {}
        if scale is not None:
            kw["scale"] = scale
        if bias is not None:
            kw["bias"] = bias
        tbl = "silu" if func in (AF.Silu, AF.Tanh) else ("exp" if func in (AF.Exp, AF.Ln) else None)
        P.op("act", lambda e: e.activation(out, in_, func, **kw), reads, writes, dur=250.0 + 0.85 * nel(out), tbl=tbl)

    def edur(eng, out):
        return (150.0 + 2.1 * nel(out)) if eng == "pool" else (160.0 + 1.02 * nel(out))

    def tt(eng, out, in0, in1, op, reads, writes):
        P.op(eng, lambda e: e.tensor_tensor(out, in0, in1, op), reads, writes, dur=edur(eng, out))

    def ts(eng, out, in0, s1, s2, op0, op1, reads, writes):
        if op1 is None:
            P.op(eng, lambda e: e.tensor_scalar(out, in0, s1, None, op0), reads, writes, dur=edur(eng, out))
        else:
            P.op(eng, lambda e: e.tensor_scalar(out, in0, s1, s2, op0, op1), reads, writes, dur=edur(eng, out))

    def stt(out, in0, scalar, in1, op0, op1, reads, writes):
        P.op("dve", lambda e: e.scalar_tensor_tensor(out, in0, scalar, in1, op0, op1), reads, writes,
             dur=edur("dve", out))

    def cp(eng, out, in_, reads, writes):
        if eng == "act":
            act(out, in_, AF.Copy, reads, writes)
        else:
            P.op(eng, lambda e: e.tensor_copy(out, in_), reads, writes, dur=edur(eng, out))

    def mset(eng, ap, val, writes):
        P.op(eng, lambda e: e.memset(ap, val), [], writes)

    def rsq(out, in_, mul, reads, w):
        act(out, in_, AF.Ln, reads, [w], scale=float(mul), bias=EPS)
        act(out, out, AF.Exp, [w], [w], scale=-0.5)

    IDB = lambda n=128: A("IDB", 0, [1, n], p=n)

    dma(A("CONST", 0, [1, 640]), cst_d.ap()[:, 0:640], [], ["CONST"])
    dma(A("JUNK", 0, [1, 896]), cst_d.ap()[:, 640:NCONST], [], ["JUNK"])
    cp("dve", A("IDB", 0, [1, 128]), CA(C_ID, [1, 128]), ["CONST"], ["IDB"])
    cp("dve", A("LEVB", 0, [1, 896]), A("JUNK", 0, [1, 896]), ["JUNK"], ["LEVB"])
    bc = lambda d, n: bass.AP(d, 0, [[0, 128], [1, n]])
    dma(A("GAINM", 0, [1, 1024]), bc(mon_d, 1024), [], ["GAINM"])
    dma(A("GAING", 0, [1, 128]), bc(gon_d, 128), [], ["GAING"])
    ts("pool", A("GAINM", 0, [1, 1024]), A("GAINM", 0, [1, 1024]), 0.5, None, ALU.mult, None, ["GAINM"], ["GAINM"])
    dma(A("BIASM", 0, [1, 4]), bc(mib_d, 4), [], ["BIASM"])
    dma(A("BIASM", 4, [1, 4]), bc(mfb_d, 4), [], ["BIASM"])
    dma(A("DTB", 0, [1, 8]), bc(gdt_d, 8), [], ["DTB"])
    dma(A("NEGA", 0, [1, 8]), bc(gal_d, 8), [], ["NEGA"])
    act(A("NEGA", 0, [1, 8]), A("NEGA", 0, [1, 8]), AF.Exp, ["NEGA"], ["NEGA"])
    ts("dve", A("NEGA", 0, [1, 8]), A("NEGA", 0, [1, 8]), -1.0, None, ALU.mult, None, ["NEGA"], ["NEGA"])
    dma(A("AN8", 0, [1, 128], p=8), an_d.ap(), [], ["AN8"])
    dma(A("CWROW", 0, [1, 128], p=96), gcv_d.ap(), [], ["CWROW"])
    b = nb()
    tr(PA(b, 0, [1, 8]), A("AN8", 0, [1, 128], p=8), CA(C_ID, [1, 8], p=8), ["AN8", "CONST"], b)
    cp("dve", A("GW", 0, [1, 8]), PA(b, 0, [1, 8]), [bk(b)], ["GW"])
    b = nb()
    tr(PA(b, 0, [1, 96]), A("CWROW", 0, [1, 128], p=96), CA(C_ID, [1, 96], p=96), ["CWROW", "CONST"], b)
    cp("dve", A("CW", 0, [1, 96]), PA(b, 0, [1, 96]), [bk(b)], ["CW"])

    cvt_i = [0]

    STGS = ["RES", "JUNK", "ACC", "TMPC", "X0", "X1", "GAM"]

    def load_weights(ps_):
        c0 = 0 if ps_ == 0 else 4104
        wtot = 4104 if ps_ == 0 else 4112
        chunks = [(j * 1024, 1024) for j in range(4)] + [(4096, wtot - 4096)]
        for kc in range(8):
            for (cj, w) in chunks:
                i = cvt_i[0]
                cvt_i[0] += 1
                stg = STGS[i % len(STGS)]
                dma(A(stg, 0, [1, w]), win_d.ap()[kc * 128:(kc + 1) * 128, c0 + cj:c0 + cj + w], [], [stg])
                dst = A("WIN", kc * WC + cj, [1, w])
                if i % 2 == 0:
                    act(dst, A(stg, 0, [1, w]), AF.Copy, [stg, "GW"], ["WIN"], scale=A("GW", kc, [1, 1]))
                else:
                    ts("dve", dst, A(stg, 0, [1, w]), A("GW", kc, [1, 1]), None, ALU.mult, None, [stg, "GW"], ["WIN"])
        r0 = ps_ * 1024
        for kc in range(8):
            i = cvt_i[0]
            cvt_i[0] += 1
            stg = STGS[i % len(STGS)]
            dma(A(stg, 0, [1, 1024]), wout_d.ap()[r0 + kc * 128:r0 + (kc + 1) * 128, :], [], [stg])
            cp(("act", "dve")[i % 2], A("WOUT", kc * 1024, [1, 1024]), A(stg, 0, [1, 1024]), [stg], ["WOUT"])

    ctx = {"HT": "HT0", "BETA": "BETA0"}

    def setp(p):
        ctx["BETA"] = "BETA%d" % p

    def HTk(kc):
        return A(ctx["HT"], kc * 128, [1, 128])

    def proj_tm(b, c0, n):
        mmg(PA(b, 0, [1, n]), [(HTk(kc), A("WIN", kc * WC + c0, [1, n])) for kc in range(8)], [ctx["HT"], "WIN"], b)

    store_ops = []
    RUN = lambda g: [None for _ in g]
    hk = lambda nm: [(nm, 0), (nm, 1)]
    H = lambda nm, h: A(nm, h * 128, [1, 128])

    def phase_a(s, t, p, X=None):
        X = X or ("X%d" % p)
        r0 = s * SEQ + t * L
        dma(A(X, 0, [1, 1024]), x_d.ap()[r0:r0 + L, :], [], [X])
        act(A("JUNK", 0, [1, 1024]), A(X, 0, [1, 1024]), AF.Square, [X], ["JUNK"])
        P.op("dve", lambda e: e.tensor_reduce(A("SS", 0, [1, 1]), A("JUNK", 0, [1, 1024]), AX.X, ALU.add),
             ["JUNK"], ["SS"], dur=1200.0)
        rsq(A("RSTD", 0, [1, 1]), A("SS", 0, [1, 1]), 1.0 / D, ["SS"], "RSTD")
        ts("dve", A("AQK", 0, [1, 1024]), A(X, 0, [1, 1024]), A("RSTD", 0, [1, 1]), None, ALU.mult, None,
           [X, "RSTD"], ["AQK"])
        yield
        b = nb()
        for kc in range(8):
            tr(PB(b, kc * 128, [1, 128]), A("AQK", kc * 128, [1, 128]), IDB(), ["AQK", "IDB"], b)
        cp("act", A(ctx["HT"], 0, [1, 1024]), PB(b, 0, [1, 1024]), [bk(b)], [ctx["HT"]])
        yield

    def phase_d(ps_, s, t, p):
        MX = "MIX%d" % p
        r0 = s * SEQ + t * L
        src_d = x_d if ps_ == 0 else part_d
        dma(A("RES", 0, [1, 1024]), src_d.ap()[r0:r0 + L, :], [("pd", r0)] if ps_ == 1 else [], ["RES"])
        b = nb()
        for kc in range(8):
            tr(PB(b, kc * 128, [1, 128]), A(MX, kc * 128, [1, 128]), IDB(), [MX, "IDB"], b)
        cp("act", A("MIXT", 0, [1, 1024]), PB(b, 0, [1, 1024]), [bk(b)], ["MIXT"])
        yield
        for n in range(2):
            b = nb()
            mmg(PA(b, 0, [1, 512]),
                [(A("MIXT", kc * 128, [1, 128]), A("WOUT", kc * 1024 + n * 512, [1, 512])) for kc in range(8)],
                ["MIXT", "WOUT"], b)
            tt("dve", A("RES", n * 512, [1, 512]), PA(b, 0, [1, 512]), A("RES", n * 512, [1, 512]), ALU.add,
               [bk(b), "RES"], ["RES"])
            yield
        if ps_ == 0:
            store_ops.append(dma(part_d.ap()[r0:r0 + L, :], A("RES", 0, [1, 1024]), ["RES"], [("pd", r0)],
                                 q="pool"))
        else:
            act(A("MIXT", 0, [1, 1024]), A("RES", 0, [1, 1024]), AF.Square, ["RES"], ["MIXT"])
            P.op("dve", lambda e: e.tensor_reduce(A("SS2", 0, [1, 1]), A("MIXT", 0, [1, 1024]), AX.X, ALU.add),
                 ["MIXT"], ["SS2"], dur=1200.0)
            rsq(A("RSTD2", 0, [1, 1]), A("SS2", 0, [1, 1]), 1.0 / D, ["SS2"], "RSTD2")
            stt(A("RES", 0, [1, 1024]), A("RES", 0, [1, 1024]), A("RSTD2", 0, [1, 1]), A("GAINM", 0, [1, 1024]),
                ALU.mult, ALU.mult, ["RES", "RSTD2", "GAINM"], ["RES"])
            store_ops.append(dma(out_d.ap()[r0:r0 + L, :], A("RES", 0, [1, 1024]), ["RES"], ["out_dram"], q="pool"))
        yield

    def ml_stage1(s, t, p, first):
        QE = ("QN", "Z")[p]
        K2 = ("QE2", "TT")[p]
        K3 = ("KN", "RP")[p]
        VA = "KBEG%d" % p
        GG = "GG2%d" % p
        EC = "ECRT%d" % p
        setp(p)
        b = nb()
        mmg(PA(b, 0, [1, 8]), [(HTk(kc), A("WIN", kc * WC + 4096, [1, 8])) for kc in range(8)], [ctx["HT"], "WIN"], b)
        tt("dve", A("G8", 0, [1, 8]), PA(b, 0, [1, 8]), A("BIASM", 0, [1, 8]), ALU.add, [bk(b), "BIASM"], ["G8"])
        act(A("E1", 0, [1, 4]), A("G8", 4, [1, 4]), AF.Exp, ["G8"], ["E1"], scale=-1.0)
        act(A("LFP", 0, [1, 4]), A("E1", 0, [1, 4]), AF.Ln, ["E1"], ["LFP"], bias=1.0)
        ts("dve", A("LA", 0, [1, 4]), A("LFP", 0, [1, 4]), -1.0, None, ALU.mult, None, ["LFP"], ["LA"])
        yield
        b = nb()
        for i, cm in enumerate([C_MLE, C_MGT, C_ONE]):
            mmg(PA(b, i * 4, [1, 4]), [(CA(cm, [1, 128]), A("LA", 0, [1, 4]))], ["CONST", "LA"], b, f32=True)
        cp("dve", A("CRT", 0, [1, 12]), PA(b, 0, [1, 12]), [bk(b)], ["CRT"])
        act(A(EC, 0, [1, 12]), A("CRT", 0, [1, 12]), AF.Exp, ["CRT"], [EC])
        tt("dve", A("TA", 0, [1, 4]), A("G8", 0, [1, 4]), A("CRT", 0, [1, 4]), ALU.subtract, ["G8", "CRT"], ["TA"])
        tt("dve", A("TA", 4, [1, 4]), A("G8", 0, [1, 4]), A("CRT", 4, [1, 4]), ALU.add, ["G8", "CRT"], ["TA"])
        act(A("EIC", 0, [1, 8]), A("TA", 0, [1, 8]), AF.Exp, ["TA"], ["EIC"])
        ts("dve", A("QSC", 0, [1, 4]), A(EC, 0, [1, 4]), 128.0 ** -0.5, None, ALU.mult, None, [EC], ["QSC"])
        yield

    def ml_qk(s, t, p, first):
        QE = ("QN", "Z")[p]
        K2 = ("QE2", "TT")[p]
        K3 = ("KN", "RP")[p]
        VA = "KBEG%d" % p
        GG = "GG2%d" % p
        EC = "ECRT%d" % p
        setp(p)
        b = nb()
        proj_tm(b, 0, 512)
        tt("dve", A(QE, 0, [128, 4], [1, 128]), PA(b, 0, [128, 4], [1, 128]), A("QSC", 0, [1, 4], [0, 128]),
           ALU.mult, [bk(b), "QSC"], hk(QE))
        yield
        b = nb()
        proj_tm(b, 512, 512)
        tt("dve", A(K2, 0, [128, 4], [1, 128]), PA(b, 0, [128, 4], [1, 128]), A("EIC", 0, [1, 4], [0, 128]),
           ALU.mult, [bk(b), "EIC"], hk(K2))
        tt("dve", A(K3, 0, [128, 4], [1, 128]), PA(b, 0, [128, 4], [1, 128]), A("EIC", 4, [1, 4], [0, 128]),
           ALU.mult, [bk(b), "EIC"], hk(K3))
        yield

    def ml_voz(s, t, p, first):
        QE = ("QN", "Z")[p]
        K2 = ("QE2", "TT")[p]
        K3 = ("KN", "RP")[p]
        VA = "KBEG%d" % p
        GG = "GG2%d" % p
        EC = "ECRT%d" % p
        setp(p)
        for n in range(2):
            b = nb()
            proj_tm(b, 1024 + n * 512, 512)
            cp("act", A(VA, n * 514, [257, 2], [1, 256]), PA(b, 0, [256, 2], [1, 256]), [bk(b)], [VA])
            yield
        for n in range(2):
            b = nb()
            proj_tm(b, 3072 + n * 512, 512)
            act(A("GS", n * 512, [1, 512]), PA(b, 0, [1, 512]), AF.Silu, [bk(b)], ["GS"])
            yield
        tt("pool", A("GS", 0, [1, 1024]), A("GS", 0, [1, 1024]), A("GAINM", 0, [1, 1024]), ALU.mult,
           ["GS", "GAINM"], ["GS"])
        for n in range(2):
            b = nb()
            proj_tm(b, 2048 + n * 512, 512)
            act(A(GG, n * 512, [1, 512]), PA(b, 0, [1, 512]), AF.Tanh, [bk(b)], [GG], scale=0.5)
            yield
        stt(A(GG, 0, [1, 1024]), A(GG, 0, [1, 1024]), 1.0, A("GS", 0, [1, 1024]), ALU.add, ALU.mult,
            [GG, "GS"], [GG])
        yield

    def ml_stage2(s, t, p, first):
        QE = ("QN", "Z")[p]
        K2 = ("QE2", "TT")[p]
        K3 = ("KN", "RP")[p]
        VA = "KBEG%d" % p
        GG = "GG2%d" % p
        EC = "ECRT%d" % p
        if first:
            mset("pool", A("SG", 0, [1, 1028]), 0.0, ["SG"])
            mset("pool", A("SGB", 0, [1, 1040]), 0.0, ["SGB"])
        b = 0
        for h in range(4):
            tr(PB(b, h * 128, [1, 128]), A(QE, h * 128, [1, 128]), IDB(), hk(QE) + ["IDB"], b)
        for h in range(4):
            tr(PB(b, 512 + h * 128, [1, 128]), A(K2, h * 128, [1, 128]), IDB(), hk(K2) + ["IDB"], b)
        cp("act", A("QNT", 0, [1, 1024]), PB(b, 0, [1, 1024]), [bk(b)], ["QNT"])
        yield
        b = 1
        for h in range(4):
            mmg(PA(b, h * 128, [1, 128]), [(A("QNT", 512 + h * 128, [1, 128]), A("QNT", h * 128, [1, 128]))],
                ["QNT"], b)
        tt("dve", A("QET0", 0, [128, 4], [1, 128]), PA(b, 0, [128, 4], [1, 128]), CA(C_MLE, [0, 4], [1, 128]),
           ALU.mult, [bk(b), "CONST"], ["QET0"])
        yield
        for h in range(4):
            mmg(PA(h, 0, [1, 257]),
                [(A("QNT", h * 128, [1, 128]), A("SGB", h * 257, [1, 257])),
                 (A("QET0", h * 128, [1, 128]), A(VA, h * 257, [1, 257]))], ["QNT", "SGB", "QET0", VA], h)
        allb = [bk(h) for h in range(4)]
        rr4 = A("RR", 0, [1, 4])
        act(rr4, PA(0, 256, [512, 4]), AF.Abs, allb, ["RR"])
        ts("dve", rr4, rr4, 1.0, None, ALU.max, None, ["RR"], ["RR"])
        P.op("dve", lambda e, rr4=rr4: e.reciprocal(rr4, rr4), ["RR"], ["RR"])
        yield
        for h in range(4):
            act(A("GAM", h * 256, [1, 256]), PA(h, 0, [1, 256]), AF.Square, [bk(h), "RR"], ["GAM"],
                scale=A("RR", h, [1, 1]))
        P.op("dve", lambda e: e.tensor_reduce(A("SSH", 0, [1, 4]), A("GAM", 0, [256, 4], [1, 256]), AX.X, ALU.add),
             ["GAM"], ["SSH"], dur=1200.0)
        rsq(A("RS", 0, [1, 4]), A("SSH", 0, [1, 4]), 1.0 / 256, ["SSH"], "RS")
        tt("dve", A("RS", 0, [1, 4]), A("RS", 0, [1, 4]), rr4, ALU.mult, ["RS", "RR"], ["RS"])
        yield
        for h in range(4):
            stt(A("MIX%d" % p, h * 256, [1, 256]), PA(h, 0, [1, 256]), A("RS", h, [1, 1]), A(GG, h * 256, [1, 256]),
                ALU.mult, ALU.mult, [bk(h), "RS", GG], ["MIX%d" % p])
        yield
        for h in range(4):
            mmg(PA(h, 0, [1, 257]), [(A(K3, h * 128, [1, 128]), A(VA, h * 257, [1, 257]))], hk(K3) + [VA], h)
            stt(A("SG", h * 257, [1, 257]), A("SG", h * 257, [1, 257]), A(EC, 8 + h, [1, 1]),
                PA(h, 0, [1, 257]), ALU.mult, ALU.add, ["SG", EC, bk(h)], ["SG"])
        cp("act", A("SGB", 0, [1, 1028]), A("SG", 0, [1, 1028]), ["SG"], ["SGB"])
        yield

    def dwslot(j, g):
        i = j * 3 + g
        if i < 6:
            nm = ("ACC", "TMPC", "X1")[i // 2]
            return nm, (i % 2) * 1024, T[nm + "_bf"], 2048
        return "DWA", (i - 6) * 1024, T["DWA"], 6 * 1024

    def dwap(j, g, c):
        nm, off, th, rs = dwslot(j, g)
        return nm, bass.AP(th, off + c * 128, [[rs, 128], [1, 128]])

    def build_dw():
        i = 0
        for j in range(4):
            for g in range(3):
                for c in range(8):
                    nm, ap = dwap(j, g, c)
                    ts(("dve", "pool")[i % 2], ap, IDB(), A("CW", j * 24 + g * 8 + c, [1, 1]), None, ALU.mult, None,
                       ["IDB", "CW"], [nm])
                    i += 1

    def gdn_group(g, p, first):
        KB, KR, VBn, QE_, MMn = "KBEG%d" % p, "KREV%d" % p, "VB%d" % p, "QET%d" % p, "MM%d" % p
        EC = "ECRT%d" % p
        XB = ("XB1", "XB2", "XB0")[g]
        S = ("QN", "KN", VBn)[g]
        SK = hk(S) if g < 2 else [S]
        HL = "HALO%d" % g
        if first:
            mset("pool", A("HALO", g * 24, [1, 24]), 0.0, [HL])
        cp("pool", A(XB, 0, [131, 8], [1, 3]), A("HALO", g * 24, [3, 8], [1, 3]), [HL], [XB])
        dwn = sorted(set(dwslot(j, g)[0] for j in range(4)))
        for n in range(2):
            b = nb()
            for c in range(n * 4, n * 4 + 4):
                mmg(PA(b, (c % 4) * 128, [1, 128]),
                    [(A("WIN", kc * WC + g * 1024 + c * 128, [1, 128]), HTk(kc)) for kc in range(8)],
                    ["WIN", ctx["HT"]], b)
            cp("act", A(XB, n * 4 * 131 + 3, [131, 4], [1, 128]), PA(b, 0, [128, 4], [1, 128]), [bk(b)], [XB])
            yield
            if n == 1:
                cp("pool", A("HALO", g * 24, [3, 8], [1, 3]), A(XB, 128, [131, 8], [1, 3]), [XB], [HL])
            b2 = nb()
            for c in range(n * 4, n * 4 + 4):
                mmg(PA(b2, (c % 4) * 128, [1, 128]),
                    [(A(XB, c * 131 + j, [1, 128]), dwap(j, g, c)[1]) for j in range(4)], [XB] + dwn, b2)
            act(A(S, n * 512, [1, 512]), PA(b2, 0, [1, 512]), AF.Silu, [bk(b2)], [SK[n]] if g < 2 else [S])
            yield
        sall = A(S, 0, [128, 8], [1, 128])
        if g < 2:
            SQ, RQ = ("SSQ", "RQ") if g == 0 else ("SSQK", "RQK")
            for n in range(2):
                b = nb()
                act(PA(b, 0, [1, 512]), A(S, n * 512, [1, 512]), AF.Square, [SK[n]], [bk(b)])
                P.op("dve", lambda e, b=b, n=n: e.tensor_reduce(A(SQ, n * 4, [1, 4]), PA(b, 0, [128, 4], [1, 128]),
                                                                AX.X, ALU.add), [bk(b)], [SQ], dur=700.0)
            rsq(A(RQ, 0, [1, 8]), A(SQ, 0, [1, 8]), 1.0, [SQ], RQ)
        if g == 0:
            ts("dve", A("QS1", 0, [1, 8]), A("RQ", 0, [1, 8]), 128.0 ** -0.5, None, ALU.mult, None, ["RQ"], ["QS1"])
            tt("dve", A("QS2", 0, [1, 8]), A("QS1", 0, [1, 8]), A(EC, 0, [1, 8]), ALU.mult, ["QS1", EC], ["QS2"])
            tt("dve", A("QE2", 0, [128, 8], [1, 128]), sall, A("QS2", 0, [1, 8], [0, 128]), ALU.mult,
               SK + ["QS2"], hk("QE2"))
            tt("dve", sall, sall, A("QS1", 0, [1, 8], [0, 128]), ALU.mult, SK + ["QS1"], SK)
            yield
            for src, dst in (("QN", "QNT"), ("QE2", QE_)):
                b2 = nb()
                for c in range(8):
                    tr(PB(b2, c * 128, [1, 128]), A(src, c * 128, [1, 128]), IDB(), hk(src) + ["IDB"], b2)
                cp("act", A(dst, 0, [1, 1024]), PB(b2, 0, [1, 1024]), [bk(b2)], [dst])
                yield
        elif g == 1:
            tt("dve", A("KSB", 0, [1, 8]), A("RQK", 0, [1, 8]), A("BE", 0, [1, 8]), ALU.mult, ["RQK", "BE"], ["KSB"])
            tt("dve", A("KSR", 0, [1, 8]), A("RQK", 0, [1, 8]), A(EC, 8, [1, 8]), ALU.mult, ["RQK", EC], ["KSR"])
            tt("dve", A(KB, 0, [128, 8], [1, 128]), sall, A("KSB", 0, [1, 8], [0, 128]), ALU.mult,
               SK + ["KSB"], [KB])
            tt("dve", A(KR, 0, [128, 8], [1, 128]), sall, A("KSR", 0, [1, 8], [0, 128]), ALU.mult,
               SK + ["KSR"], [KR])
            tt("dve", sall, sall, A("RQK", 0, [1, 8], [0, 128]), ALU.mult, SK + ["RQK"], SK)
            yield
            b2 = nb()
            for c in range(8):
                tr(PB(b2, c * 128, [1, 128]), A("KN", c * 128, [1, 128]), IDB(), hk("KN") + ["IDB"], b2)
            cp("act", A("KNT", 0, [1, 1024]), PB(b2, 0, [1, 1024]), [bk(b2)], ["KNT"])
            yield
        else:
            tt("dve", sall, sall, A(ctx["BETA"], 0, [1, 8], [0, 128]), ALU.mult, [S, ctx["BETA"]], [S])
            yield

    def per_head(pairs_fn, reads, evac):
        for n in range(2):
            b = nb()
            for hh in range(4):
                h = n * 4 + hh
                mmg(PA(b, hh * 128, [1, 128]), pairs_fn(h), reads(n) if callable(reads) else reads, b)
            evac(n, b)
            yield

    def gdn_a(s, t, p, first):
        KB, KR, VBn, QE_, MMn, AQ = "KBEG%d" % p, "KREV%d" % p, "VB%d" % p, "QET%d" % p, "MM%d" % p, "AQKT%d" % p
        EC = "ECRT%d" % p
        GG = "GG2%d" % p
        setp(p)
        b = nb()
        mmg(PA(b, 0, [1, 16]), [(HTk(kc), A("WIN", kc * WC + 4096, [1, 16])) for kc in range(8)], [ctx["HT"], "WIN"], b)
        act(A(ctx["BETA"], 0, [1, 8]), PA(b, 0, [1, 8]), AF.Exp, [bk(b)], [ctx["BETA"]], scale=-1.0)
        ts("dve", A(ctx["BETA"], 0, [1, 8]), A(ctx["BETA"], 0, [1, 8]), 1.0, None, ALU.add, None, [ctx["BETA"]], [ctx["BETA"]])
        bt_ = A(ctx["BETA"], 0, [1, 8])
        P.op("dve", lambda e, bt_=bt_: e.reciprocal(bt_, bt_), [ctx["BETA"]], [ctx["BETA"]])
        tt("dve", A("TA", 0, [1, 8]), PA(b, 8, [1, 8]), A("DTB", 0, [1, 8]), ALU.add, [bk(b), "DTB"], ["TA"])
        act(A("E1", 0, [1, 8]), A("TA", 0, [1, 8]), AF.Exp, ["TA"], ["E1"])
        act(A("LFP", 0, [1, 8]), A("E1", 0, [1, 8]), AF.Ln, ["E1"], ["LFP"], bias=1.0)
        tt("dve", A("LA", 0, [1, 8]), A("LFP", 0, [1, 8]), A("NEGA", 0, [1, 8]), ALU.mult, ["LFP", "NEGA"], ["LA"])
        yield
        b = nb()
        for i, cm in enumerate([C_MLE, C_MGT, C_ONE]):
            mmg(PA(b, i * 8, [1, 8]), [(CA(cm, [1, 128]), A("LA", 0, [1, 8]))], ["CONST", "LA"], b, f32=True)
        cp("dve", A("CRT", 0, [1, 24]), PA(b, 0, [1, 24]), [bk(b)], ["CRT"])
        act(A(EC, 0, [1, 24]), A("CRT", 0, [1, 24]), AF.Exp, ["CRT"], [EC])
        tt("dve", A("BE", 0, [1, 8]), A(ctx["BETA"], 0, [1, 8]), A(EC, 0, [1, 8]), ALU.mult, [ctx["BETA"], EC], ["BE"])
        tt("dve", A("JUNK", 0, [128, 8], [1, 128]), CA(C_MGT, [0, 8], [1, 128]), A("LA", 0, [1, 8], [0, 128]),
           ALU.mult, ["CONST", "LA"], ["JUNK"])
        yield
        for n in range(2):
            b = nb()
            mmg(PA(b, 0, [1, 512]), [(CA(C_MLE, [1, 128]), A("JUNK", n * 512, [1, 512]))], ["CONST", "JUNK"], b, f32=True)
            act(A("GAM", n * 512, [1, 512]), PA(b, 0, [1, 512]), AF.Exp, [bk(b)], ["GAM"])
            yield
        tt("pool", A("GS", 0, [128, 8], [1, 128]), A("GAM", 0, [128, 8], [1, 128]), CA(C_MGT, [0, 8], [1, 128]),
           ALU.mult, ["GAM", "CONST"], ["GS"])
        tt("pool", A("GS", 0, [128, 8], [1, 128]), A("GS", 0, [128, 8], [1, 128]), A(ctx["BETA"], 0, [1, 8], [0, 128]),
           ALU.mult, ["GS", ctx["BETA"]], ["GS"])
        tt("pool", A("GAM", 0, [128, 8], [1, 128]), A("GAM", 0, [128, 8], [1, 128]), CA(C_MGE, [0, 8], [1, 128]),
           ALU.mult, ["GAM", "CONST"], ["GAM"])
        yield

    def gdn_k(s, t, p, first):
        KB, KR, VBn, QE_, MMn, AQ = "KBEG%d" % p, "KREV%d" % p, "VB%d" % p, "QET%d" % p, "MM%d" % p, "AQKT%d" % p
        EC = "ECRT%d" % p
        GG = "GG2%d" % p
        setp(p)
        yield from gdn_group(1, p, first)
        yield from per_head(lambda h: [(H("KNT", h), H("KNT", h))], ["KNT"],
                            lambda n, b: tt("dve", A(MMn, n * 512, [1, 512]), PA(b, 0, [1, 512]),
                                            A("GS", n * 512, [1, 512]), ALU.mult, [bk(b), "GS"], [(MMn, n)]))
        for n in range(2):
            tt("pool", A(MMn, n * 512, [128, 4], [1, 128]), A(MMn, n * 512, [128, 4], [1, 128]),
               A("IDB", 0, [0, 4], [1, 128]), ALU.add, [(MMn, n), "IDB"], [(MMn, n)])
        yield

    def gdn_q(s, t, p, first):
        KB, KR, VBn, QE_, MMn, AQ = "KBEG%d" % p, "KREV%d" % p, "VB%d" % p, "QET%d" % p, "MM%d" % p, "AQKT%d" % p
        EC = "ECRT%d" % p
        GG = "GG2%d" % p
        setp(p)
        for n in range(2):
            b = nb()
            proj_tm(b, 3072 + n * 512, 512)
            act(A(GG, n * 512, [1, 512]), PA(b, 0, [1, 512]), AF.Silu, [bk(b)], [GG])
            yield
        tt("pool", A(GG, 0, [128, 8], [1, 128]), A(GG, 0, [128, 8], [1, 128]), A("GAING", 0, [0, 8], [1, 128]),
           ALU.mult, [GG, "GAING"], [GG])
        yield from gdn_group(0, p, first)
        yield from per_head(lambda h: [(H("QNT", h), H("KNT", h))], ["QNT", "KNT"],
                            lambda n, b: tt("dve", A("AQK", n * 512, [1, 512]), PA(b, 0, [1, 512]),
                                            A("GAM", n * 512, [1, 512]), ALU.mult, [bk(b), "GAM"], ["AQK"]))
        b = nb()
        for c in range(8):
            tr(PB(b, c * 128, [1, 128]), A("AQK", c * 128, [1, 128]), IDB(), ["AQK", "IDB"], b)
        cp("act", A(AQ, 0, [1, 1024]), PB(b, 0, [1, 1024]), [bk(b)], [AQ])
        yield

    def gdn_v(s, t, p, first):
        setp(p)
        yield from gdn_group(2, p, first)

    def gdn_stage2(s, t, p, first):
        KB, KR, VBn, QE_, MMn, AQ = "KBEG%d" % p, "KREV%d" % p, "VB%d" % p, "QET%d" % p, "MM%d" % p, "AQKT%d" % p
        EC = "ECRT%d" % p
        GG = "GG2%d" % p
        r0 = s * SEQ + t * L
        if first:
            mset("pool", A("SG", 0, [1, 1028]), 0.0, ["SG"])
            mset("pool", A("SGB", 0, [1, 1040]), 0.0, ["SGB"])
        for n in range(2):
            b = nb()
            for hh in range(4):
                mmg(PA(b, hh * 128, [1, 128]), [(H(MMn, n * 4 + hh), IDB())], [(MMn, n), "IDB"], b)
            tt("dve", A("Z", n * 512, [128, 4], [1, 128]), PA(b, 0, [128, 4], [1, 128]),
               A("LEVB", 0, [0, 4], [1, 128]), ALU.mult, [bk(b), "LEVB"], [("Z", n)])
            tt("pool", A("TT", n * 512, [128, 4], [1, 128]), A(MMn, n * 512, [128, 4], [1, 128]),
               A("LEVB", 0, [0, 4], [1, 128]), ALU.mult, [(MMn, n), "LEVB"], [("TT", n)])
            yield
        for k in range(1, 7):
            for n in range(2):
                b = nb()
                for hh in range(4):
                    h = n * 4 + hh
                    mmg(PA(b, hh * 128, [1, 128]), [(H(MMn, h), H("Z", h))], [(MMn, n), ("Z", n)], b)
                tt("dve", A("RP", n * 512, [128, 4], [1, 128]), PA(b, 0, [128, 4], [1, 128]),
                   A("LEVB", k * 128, [0, 4], [1, 128]), ALU.mult, [bk(b), "LEVB"], [("RP", n)])
                yield
            for n in range(2):
                zb_ = nb()
                for hh in range(4):
                    h = n * 4 + hh
                    mmg(PA(zb_, hh * 128, [1, 128]), [(H("TT", h), H("RP", h))], [("TT", n), ("RP", n)], zb_)
                tb_ = None
                if k < 6:
                    tb_ = nb()
                    for hh in range(4):
                        h = n * 4 + hh
                        mmg(PA(tb_, hh * 128, [1, 128]), [(H("RP", h), H("TT", h))], [("TT", n), ("RP", n)], tb_)
                cp("act", A("Z", n * 512, [1, 512]), PA(zb_, 0, [1, 512]), [bk(zb_)], [("Z", n)])
                if k < 6:
                    cp("act" if n == 0 else "dve", A("TT", n * 512, [1, 512]), PA(tb_, 0, [1, 512]),
                       [bk(tb_)], [("TT", n)])
                yield
        yield from per_head(lambda h: [(H(KB, h), H("Z", h))], lambda n: [KB, ("Z", n)],
                            lambda n, b: act(A("NWT", n * 512, [1, 512]), PA(b, 0, [1, 512]), AF.Copy, [bk(b)],
                                             [("NWT", n)], scale=-1.0))
        yield from per_head(lambda h: [(H("Z", h), H(VBn, h)), (H("NWT", h), H("SGB", h))],
                            lambda n: [("Z", n), VBn, ("NWT", n), "SGB"],
                            lambda n, b: cp("act", A("VNEW", n * 512, [1, 512]), PA(b, 0, [1, 512]), [bk(b)],
                                            [("VNEW", n)]))
        ob = []

        def o_evac(n, b):
            ob.append(b)
            act(A("RP", n * 512, [1, 512]), PA(b, 0, [1, 512]), AF.Square, [bk(b)], [("RP", n)])
        yield from per_head(lambda h: [(H(QE_, h), H("SGB", h)), (H(AQ, h), H("VNEW", h))],
                            lambda n: [QE_, "SGB", AQ, ("VNEW", n)], o_evac)
        P.op("dve", lambda e: e.tensor_reduce(A("SSO", 0, [1, 8]), A("RP", 0, [128, 8], [1, 128]), AX.X, ALU.add),
             hk("RP"), ["SSO"], dur=1200.0)
        rsq(A("RSO", 0, [1, 8]), A("SSO", 0, [1, 8]), 1.0 / 128, ["SSO"], "RSO")
        tt("dve", A(GG, 0, [128, 8], [1, 128]), A(GG, 0, [128, 8], [1, 128]), A("RSO", 0, [1, 8], [0, 128]),
           ALU.mult, [GG, "RSO"], [GG])
        for n in range(2):
            tt("dve", A("MIX%d" % p, n * 512, [1, 512]), PA(ob[n], 0, [1, 512]), A(GG, n * 512, [1, 512]), ALU.mult,
               [bk(ob[n]), GG], ["MIX%d" % p])
        yield
        tt("pool", A("SG", 0, [128, 8], [1, 128]), A("SG", 0, [128, 8], [1, 128]), A(EC, 16, [1, 8], [0, 128]),
           ALU.mult, ["SG", EC], ["SG"])
        yield from per_head(lambda h: [(H(KR, h), H("VNEW", h))], lambda n: [KR, ("VNEW", n)],
                            lambda n, b: tt("dve", A("SG", n * 512, [1, 512]), PA(b, 0, [1, 512]),
                                            A("SG", n * 512, [1, 512]), ALU.add, [bk(b), "SG"], ["SG"]))
        cp("act", A("SGB", 0, [1, 1024]), A("SG", 0, [1, 1024]), ["SG"], ["SGB"])
        yield

    def collect(gen, banks):
        if gen is None:
            return []
        P.defer = []
        bset[0] = banks
        for _ in gen:
            pass
        l = P.defer
        P.defer = None
        bset[0] = list(range(8))
        return l

    def collect(gen, banks):
        P.defer = []
        bset[0] = banks
        for _ in gen:
            pass
        l = P.defer
        P.defer = None
        bset[0] = list(range(8))
        return l

    for ps_ in range(2):
        load_weights(ps_)
        if ps_ == 1:
            dma(A("GAINM", 0, [1, 1024]), bc(fn_d, 1024), [], ["GAINM"])
            build_dw()
        if ps_ == 0:
            mset("pool", A("KBEG0", 0, [1, 1040]), 1.0, ["KBEG0"])
            mset("pool", A("KBEG1", 0, [1, 1040]), 1.0, ["KBEG1"])
        tiles = [(s, t) for s in range(NSEQ) for t in range(NT_RUN)]
        prev = None
        prev2 = None
        hts = lambda j: "HT%d" % (j % 3)
        ctx["HT"] = hts(0)
        P.merge([collect(phase_a(tiles[0][0], tiles[0][1], 0, X="X0"), [4])])
        for i, cur in enumerate(tiles + [None, None]):
            lists = []
            pd = None
            if prev2 is not None:
                pd = (ps_, prev2[0], prev2[1], (i - 2) % 2)
            if prev is not None:
                pa = (prev[0], prev[1], (i - 1) % 2, prev[1] == 0)
                ctx["HT"] = hts(i - 1)
                if ps_ == 0:
                    lists.append(collect(ml_stage2(*pa), [0, 1, 2, 3]))
                else:
                    lists.append(collect(gdn_v(*pa), [3]))
                    lists.append(collect(gdn_stage2(*pa), [0, 1, 2]))
            if ps_ == 1 and pd is not None:
                lists.append(collect(phase_d(*pd), [3]))
            if cur is not None:
                ca = (cur[0], cur[1], i % 2, cur[1] == 0)
                ctx["HT"] = hts(i)
                if ps_ == 0:
                    lists.append(collect(ml_stage1(*ca), [4]))
                    lists.append(collect(ml_qk(*ca), [5]))
                    lists.append(collect(ml_voz(*ca), [6, 7]))
                else:
                    lists.append(collect(gdn_a(*ca), [4, 5]))
                    lists.append(collect(gdn_k(*ca), [4, 5]))
                    lists.append(collect(gdn_q(*ca), [6, 7]))
            if ps_ == 0 and pd is not None:
                lists.append(collect(phase_d(*pd), [4]))
            if i + 1 < len(tiles):
                nx = tiles[i + 1]
                ctx["HT"] = hts(i + 1)
                lists.insert(0, collect(phase_a(nx[0], nx[1], 0, X="X0"), [5] if ps_ == 0 else [3]))
            P.merge(lists)
            prev2 = prev
            prev = cur
    P.op("pool", None, ["out_dram"], [])
    fin = P.ops[-1]
    for so in store_ops:
        fin["deps"][so[0]] = True


    P.finalize()
    sems = {}
    for e, ng in P.ngen.items():
        for g in range(ng):
            sems[(e, g)] = es.enter_context(nc.semaphore("s_%s_%d" % (e, g)))
    for i in range(NDMA):
        sems[("dma", i)] = es.enter_context(nc.semaphore("s_dma_%d" % i))
    with nc.Block() as block:
        @block.sync
        def _(e):
            P.emit("sp", e, sems)

        @block.tensor
        def _(e):
            P.emit("pe", e, sems)

        @block.scalar
        def _(e):
            P.emit("act", e, sems)

        @block.vector
        def _(e):
            P.emit("dve", e, sems)

        @block.gpsimd
        def _(e):
            P.emit("pool", e, sems)
    es.close()
    return nc


NT_RUN = NT
ALPHA = 0.0
TBL_NS = 1300.0
TBL_PEN = 1300.0
_CACHE = {}


def kernel(x, attn_norm, w_in, m_i_bias, m_f_bias, m_out_norm, g_conv, g_a_log, g_dt_bias, g_out_norm, w_out,
           final_norm):
    f = lambda a: np.ascontiguousarray(np.asarray(a, dtype=np.float32))
    if "nc" not in _CACHE:
        _CACHE["nc"] = build_program()
    nc = _CACHE["nc"]
    x = f(x)
    shared = {
        "w_in": f(w_in).reshape(D, 8216),
        "w_out": f(w_out).reshape(2048, D),
        "attn_norm": f(attn_norm).reshape(8, 128),
        "m_i_bias": f(m_i_bias).reshape(1, 4),
        "m_f_bias": f(m_f_bias).reshape(1, 4),
        "m_out_norm": f(m_out_norm).reshape(1, 1024),
        "g_conv": f(g_conv).reshape(96, 128),
        "g_a_log": f(g_a_log).reshape(1, 8),
        "g_dt_bias": f(g_dt_bias).reshape(1, 8),
        "g_out_norm": f(g_out_norm).reshape(1, 128),
        "final_norm": f(final_norm).reshape(1, 1024),
        "consts": make_consts(),
    }
    in_maps = []
    for c in range(NCORES):
        m = dict(shared)
        m["x"] = x[c * NSEQ:(c + 1) * NSEQ].reshape(TOK, D)
        in_maps.append(m)
    res = run_bass_kernel_spmd(nc, in_maps, core_ids=list(range(NCORES)))
    outs = [np.asarray(r["out"]).reshape(NSEQ, SEQ, D) for r in res.results]
    return np.concatenate(outs, axis=0).astype(np.float32)
```

```python
import numpy as np
from contextlib import ExitStack
import concourse.bass as bass
import concourse.mybir as mybir
from concourse.bass_utils import run_bass_kernel_spmd

F32 = mybir.dt.float32
BF = mybir.dt.bfloat16
AF = mybir.ActivationFunctionType
ALU = mybir.AluOpType
AX = mybir.AxisListType

NCORES = 8
SEQ = 2048
NSEQ = 2
TOK = NSEQ * SEQ
L = 128
NT = SEQ // L
D = 1024
WC = 4112
EPS = 1e-6
GEN = 3000
NDMA = 24
NDMAP = 8
C_MLE, C_MGT, C_MGE, C_ONE, C_ID, C_LEV = 0, 128, 256, 384, 512, 640
NCONST = 640 + 7 * 128


def make_consts():
    idx = np.arange(L)
    c = np.zeros((L, NCONST), np.float32)
    c[:, C_MLE:C_MLE + L] = idx[:, None] <= idx[None, :]
    c[:, C_MGT:C_MGT + L] = idx[:, None] > idx[None, :]
    c[:, C_MGE:C_MGE + L] = idx[:, None] >= idx[None, :]
    c[:, C_ONE:C_ONE + L] = 1.0
    c[:, C_ID:C_ID + L] = np.eye(L)
    for k in range(7):
        b2 = 2 << k
        m = np.where((idx[:, None] // b2) == (idx[None, :] // b2), -1.0, 0.0)
        m[idx, idx] = 1.0
        c[:, C_LEV + k * L:C_LEV + (k + 1) * L] = m
    return c


class Prog:
    def __init__(self):
        self.ops = []
        self.lastw = {}
        self.readers = {}
        self.defer = None
        self.cur_tbl = None
        self.efree = {}
        self.wdone = {}
        self.rdone = {}

    def _norm(self, keys):
        return ["JUNK" if k == "GS" else k for k in keys]

    def op(self, eng, fn, reads=(), writes=(), dma=False, dur=300.0, tbl=None):
        reads = self._norm(reads)
        writes = self._norm(writes)
        if self.defer is not None:
            h = [None]
            self.defer.append(dict(eng=eng, fn=fn, reads=list(reads), writes=list(writes), dma=dma, dur=dur, h=h,
                                   tbl=tbl))
            return h
        self._sim(eng, reads, writes, dma, dur, tbl)
        return [self._op(eng, fn, reads, writes, dma)]

    def _est(self, eng, reads, writes, tbl=None, pen=None):
        t = self.efree.get(eng, 0.0)
        if tbl is not None and tbl != self.cur_tbl:
            t += TBL_NS if pen is None else pen
        for r in reads:
            t = max(t, self.wdone.get(r, 0.0))
        for w in writes:
            t = max(t, self.wdone.get(w, 0.0), self.rdone.get(w, 0.0))
        return t

    def _sim(self, eng, reads, writes, dma, dur, tbl=None):
        st = self._est(eng, reads, writes, tbl)
        if tbl is not None:
            self.cur_tbl = tbl
        if dma:
            self.efree[eng] = st + 100.0
            end = st + 3000.0
        else:
            end = st + dur
            self.efree[eng] = end
        for r in reads:
            self.rdone[r] = max(self.rdone.get(r, 0.0), end)
        for w in writes:
            self.wdone[w] = end + 150.0
        return st

    def merge(self, lists):
        lists = [l for l in lists if l]
        n = len(lists)
        accs = []
        for l in lists:
            wl, rl = {}, {}
            for i, d in enumerate(l):
                for r in d["reads"]:
                    rl[r] = i
                for w in d["writes"]:
                    wl[w] = i
            accs.append((wl, rl))
        pos = [0] * n
        rems = []
        for l in lists:
            r_ = [0.0] * (len(l) + 1)
            for i in range(len(l) - 1, -1, -1):
                r_[i] = r_[i + 1] + l[i]["dur"]
            rems.append(r_)

        def blocked(j, d):
            for i in range(j):
                pi = pos[i]
                if pi >= len(lists[i]):
                    continue
                wl, rl = accs[i]
                for r in d["reads"]:
                    if wl.get(r, -1) >= pi:
                        return True
                for w in d["writes"]:
                    if wl.get(w, -1) >= pi or rl.get(w, -1) >= pi:
                        return True
            return False

        while True:
            best = None
            for i, l in enumerate(lists):
                if pos[i] < len(l):
                    d = l[pos[i]]
                    if blocked(i, d):
                        continue
                    st = self._est(d["eng"], d["reads"], d["writes"], d.get("tbl"), TBL_PEN) - ALPHA * rems[i][pos[i]]
                    if best is None or st < best[0]:
                        best = (st, i)
            if best is None:
                break
            i = best[1]
            d = lists[i][pos[i]]
            pos[i] += 1
            self._sim(d["eng"], d["reads"], d["writes"], d["dma"], d["dur"], d.get("tbl"))
            d["h"][0] = self._op(d["eng"], d["fn"], d["reads"], d["writes"], d["dma"])
        assert all(pos[i] == len(lists[i]) for i in range(n))

    def _op(self, eng, fn, reads=(), writes=(), dma=False):
        idx = len(self.ops)
        deps = {}
        for r in reads:
            if r in self.lastw:
                deps[self.lastw[r]] = True
        for w in writes:
            if w in self.lastw:
                deps.setdefault(self.lastw[w], False)
            for rd in self.readers.get(w, ()):
                deps.setdefault(rd, False)
        self.ops.append(dict(eng=eng, fn=fn, deps=deps, dma=dma))
        for r in reads:
            self.readers.setdefault(r, []).append(idx)
        for w in writes:
            self.lastw[w] = idx
            self.readers[w] = []
        return idx

    def finalize(self):
        ops = self.ops
        last_dma_on_sem = {}
        ndma = 0
        npool = 0
        for i, o in enumerate(ops):
            nd = {}
            for d, raw in o["deps"].items():
                od = ops[d]
                if od["dma"]:
                    nd[d] = raw
                elif od["eng"] == o["eng"]:
                    if o["eng"] != "pe" and raw and not o["dma"]:
                        nd[d] = raw
                    elif o["dma"]:
                        nd[d] = raw
                else:
                    nd[d] = raw
            if o["dma"]:
                if o["eng"] == "pool":
                    s = ("dmap", npool % NDMAP)
                    npool += 1
                else:
                    s = ("dma", ndma % NDMA)
                    ndma += 1
                if s in last_dma_on_sem:
                    nd[last_dma_on_sem[s]] = False
                last_dma_on_sem[s] = i
                o["dsem"] = s
            o["deps"] = nd
        needed = set()
        for o in ops:
            needed.update(o["deps"].keys())
        cnt = {}
        dcnt = {}
        for i, o in enumerate(ops):
            if o["dma"]:
                dcnt[o["dsem"]] = dcnt.get(o["dsem"], 0) + 16
                o["tok"] = (o["dsem"], dcnt[o["dsem"]])
            elif i in needed:
                c = cnt.get(o["eng"], 0)
                cnt[o["eng"]] = c + 1
                o["tok"] = ((o["eng"], c // GEN), c % GEN + 1)
            else:
                o["tok"] = None
        self.ngen = {e: (c + GEN - 1) // GEN for e, c in cnt.items()}

    def emit(self, eng_name, eng, sems):
        waited = {}
        for o in self.ops:
            if o["eng"] != eng_name:
                continue
            for d in sorted(o["deps"].keys()):
                key, val = self.ops[d]["tok"]
                if waited.get(key, 0) >= val:
                    continue
                waited[key] = val
                eng.wait_ge(sems[key], val)
            if o["fn"] is None:
                continue
            ins = o["fn"](eng)
            if o["tok"] is not None:
                key, _ = o["tok"]
                ins.then_inc(sems[key], 16 if o["dma"] else 1)


def build_program():
    nc = bass.Bass("TRN2", target_bir_lowering=False)
    dt_in = lambda n, s: nc.dram_tensor(n, list(s), F32, kind="ExternalInput")
    x_d = dt_in("x", [TOK, D])
    win_d = dt_in("w_in", [D, 8216])
    wout_d = dt_in("w_out", [2048, D])
    an_d = dt_in("attn_norm", [8, 128])
    mib_d = dt_in("m_i_bias", [1, 4])
    mfb_d = dt_in("m_f_bias", [1, 4])
    mon_d = dt_in("m_out_norm", [1, 1024])
    gcv_d = dt_in("g_conv", [96, 128])
    gal_d = dt_in("g_a_log", [1, 8])
    gdt_d = dt_in("g_dt_bias", [1, 8])
    gon_d = dt_in("g_out_norm", [1, 128])
    fn_d = dt_in("final_norm", [1, 1024])
    cst_d = dt_in("consts", [L, NCONST])
    out_d = nc.dram_tensor("out", [TOK, D], F32, kind="ExternalOutput")
    part_d = nc.dram_tensor("partial", [TOK, D], F32, kind="Internal")

    P = Prog()
    es = ExitStack()
    T = {}

    def sb(name, cols, dt, parts=128):
        T[name] = es.enter_context(nc.sbuf_tensor(name, [parts, cols], dt))
        return T[name]

    sb("WIN", 8 * WC, BF)
    sb("WOUT", 8 * 1024, BF)
    sb("CONST", 640, F32)
    sb("IDB", 128, BF)
    sb("LEVB", 7 * 128, BF)
    sb("GAINM", 1024, F32)
    sb("GAING", 128, F32)
    sb("CW", 96, F32)
    sb("CWROW", 128, F32, parts=96)
    sb("AN8", 128, F32, parts=8)
    sb("GW", 8, F32)
    sb("BIASM", 8, F32)
    sb("DTB", 8, F32)
    sb("NEGA", 8, F32)
    sb("X0", 1024, F32)
    sb("X1", 1024, F32)
    sb("JUNK", 1028, F32)
    sb("HT0", 1024, BF)
    sb("HT1", 1024, BF)
    sb("HT2", 1024, BF)
    sb("MIX0", 1024, BF)
    sb("MIX1", 1024, BF)
    sb("MIXT", 1024, BF)
    sb("RES", 1028, F32)
    for n in ["SS", "RSTD", "SS2", "RSTD2"]:
        sb(n, 1, F32)
    for n in ["G8", "E1", "LFP", "LA", "CRT", "ECRT0", "ECRT1", "TA", "TB", "EIC", "EIR", "QSC", "RR", "SSH", "RS",
              "BETA0", "BETA1", "BE", "SSQ", "RQ", "SSQK", "RQK", "QS1", "QS2", "KSB", "KSR", "SSO", "RSO"]:
        sb(n, 24 if n in ("CRT", "ECRT0", "ECRT1") else 16, F32)
    sb("TMPC", 1024, F32)
    sb("HALO", 24 * 3, BF)
    sb("ACC", 1024, F32)
    sb("XB0", 8 * 131, BF)
    sb("XB1", 8 * 131, BF)
    sb("XB2", 8 * 131, BF)
    sb("DWA", 6 * 1024, BF)
    sb("QN", 1024, BF)
    sb("QE2", 1024, BF)
    sb("KN", 1024, BF)
    sb("KBEG0", 1040, BF)
    sb("KBEG1", 1040, BF)
    sb("KREV0", 1024, BF)
    sb("KREV1", 1024, BF)
    sb("VB0", 1024, BF)
    sb("VB1", 1024, BF)
    sb("QNT", 1024, BF)
    sb("QET0", 1024, BF)
    sb("QET1", 1024, BF)
    sb("KNT", 1024, BF)
    sb("GAM", 1024, F32)
    sb("MM0", 1024, BF)
    sb("MM1", 1024, BF)
    sb("AQK", 1024, BF)
    sb("AQKT0", 1024, BF)
    sb("AQKT1", 1024, BF)
    sb("Z", 1024, BF)
    sb("TT", 1024, BF)
    sb("RP", 1024, BF)
    sb("NWT", 1024, BF)
    sb("VNEW", 1024, BF)
    sb("GG20", 1024, BF)
    sb("GG21", 1024, BF)
    sb("SG", 1028, F32)
    sb("SGB", 1040, BF)
    PS = es.enter_context(nc.psum_tensor("ps", [128, 4096], F32))
    for nm_ in ("ACC", "TMPC", "X1"):
        T[nm_ + "_bf"] = T[nm_].bitcast(BF)
    PSB = PS.bitcast(BF)

    def A(name, off=0, *dims, p=128):
        if name == "GS":
            name = "JUNK"
        t = T[name]
        return bass.AP(t, off, [[t.shape[1], p]] + [list(d) for d in dims])

    def PA(b, off=0, *dims, p=128):
        return bass.AP(PS, b * 512 + off, [[4096, p]] + [list(d) for d in dims])

    def PB(b, off=0, *dims, p=128):
        return bass.AP(PSB, b * 1024 + off, [[8192, p]] + [list(d) for d in dims])

    def CA(off, *dims, p=128):
        return A("CONST", off, *dims, p=p)

    bank = [0]

    bset = [list(range(8))]
    bctr = {}

    def nb():
        key = tuple(bset[0])
        c = bctr.get(key, 0)
        bctr[key] = c + 1
        return bset[0][c % len(bset[0])]

    def nel(ap):
        n = 1
        for st_, cn in list(ap.ap)[1:]:
            n *= cn
        return n

    def bk(b):
        return ("ps", b)

    dq = [0]

    def dma(out, in_, reads, writes, q=None, slow=False):
        if q is None:
            q = "sp"
        if slow:
            fn = lambda e: e.dma_start(out=out, in_=in_, allow_slow_non_contiguous=True)
        else:
            fn = lambda e: e.dma_start(out=out, in_=in_)
        return P.op(q, fn, reads, writes, dma=True)

    def mmg(out, pairs, reads, b, f32=False):
        n = len(pairs)
        for i, (l, r) in enumerate(pairs):
            d = max(60.0, nel(r) * 0.45) * (4.0 if f32 else 1.0)
            P.op("pe", lambda e, l=l, r=r, i=i: e.matmul(out, l, r, start=(i == 0), stop=(i == n - 1)),
                 reads, [bk(b)], dur=d)

    def tr(out, in_, ident, reads, b):
        P.op("pe", lambda e: e.transpose(out, in_, ident), reads, [bk(b)], dur=60.0)

    def act(out, in_, func, reads, writes, scale=None, bias=None):
        kw = {}
        if scale is not None:
            kw["scale"] = scale
        if bias is not None:
            kw["bias"] = bias
        tbl = "silu" if func in (AF.Silu, AF.Tanh) else ("exp" if func in (AF.Exp, AF.Ln) else None)
        P.op("act", lambda e: e.activation(out, in_, func, **kw), reads, writes, dur=250.0 + 0.85 * nel(out), tbl=tbl)

    def edur(eng, out):
        return (150.0 + 2.1 * nel(out)) if eng == "pool" else (160.0 + 1.02 * nel(out))

    def tt(eng, out, in0, in1, op, reads, writes):
        P.op(eng, lambda e: e.tensor_tensor(out, in0, in1, op), reads, writes, dur=edur(eng, out))

    def ts(eng, out, in0, s1, s2, op0, op1, reads, writes):
        if op1 is None:
            P.op(eng, lambda e: e.tensor_scalar(out, in0, s1, None, op0), reads, writes, dur=edur(eng, out))
        else:
            P.op(eng, lambda e: e.tensor_scalar(out, in0, s1, s2, op0, op1), reads, writes, dur=edur(eng, out))

    def stt(out, in0, scalar, in1, op0, op1, reads, writes):
        P.op("dve", lambda e: e.scalar_tensor_tensor(out, in0, scalar, in1, op0, op1), reads, writes,
             dur=edur("dve", out))

    def cp(eng, out, in_, reads, writes):
        if eng == "act":
            act(out, in_, AF.Copy, reads, writes)
        else:
            P.op(eng, lambda e: e.tensor_copy(out, in_), reads, writes, dur=edur(eng, out))

    def mset(eng, ap, val, writes):
        P.op(eng, lambda e: e.memset(ap, val), [], writes)

    def rsq(out, in_, mul, reads, w):
        act(out, in_, AF.Ln, reads, [w], scale=float(mul), bias=EPS)
        act(out, out, AF.Exp, [w], [w], scale=-0.5)

    IDB = lambda n=128: A("IDB", 0, [1, n], p=n)

    dma(A("CONST", 0, [1, 640]), cst_d.ap()[:, 0:640], [], ["CONST"])
    dma(A("JUNK", 0, [1, 896]), cst_d.ap()[:, 640:NCONST], [], ["JUNK"])
    cp("dve", A("IDB", 0, [1, 128]), CA(C_ID, [1, 128]), ["CONST"], ["IDB"])
    cp("dve", A("LEVB", 0, [1, 896]), A("JUNK", 0, [1, 896]), ["JUNK"], ["LEVB"])
    bc = lambda d, n: bass.AP(d, 0, [[0, 128], [1, n]])
    dma(A("GAINM", 0, [1, 1024]), bc(mon_d, 1024), [], ["GAINM"])
    dma(A("GAING", 0, [1, 128]), bc(gon_d, 128), [], ["GAING"])
    ts("pool", A("GAINM", 0, [1, 1024]), A("GAINM", 0, [1, 1024]), 0.5, None, ALU.mult, None, ["GAINM"], ["GAINM"])
    dma(A("BIASM", 0, [1, 4]), bc(mib_d, 4), [], ["BIASM"])
    dma(A("BIASM", 4, [1, 4]), bc(mfb_d, 4), [], ["BIASM"])
    dma(A("DTB", 0, [1, 8]), bc(gdt_d, 8), [], ["DTB"])
    dma(A("NEGA", 0, [1, 8]), bc(gal_d, 8), [], ["NEGA"])
    act(A("NEGA", 0, [1, 8]), A("NEGA", 0, [1, 8]), AF.Exp, ["NEGA"], ["NEGA"])
    ts("dve", A("NEGA", 0, [1, 8]), A("NEGA", 0, [1, 8]), -1.0, None, ALU.mult, None, ["NEGA"], ["NEGA"])
    dma(A("AN8", 0, [1, 128], p=8), an_d.ap(), [], ["AN8"])
    dma(A("CWROW", 0, [1, 128], p=96), gcv_d.ap(), [], ["CWROW"])
    b = nb()
    tr(PA(b, 0, [1, 8]), A("AN8", 0, [1, 128], p=8), CA(C_ID, [1, 8], p=8), ["AN8", "CONST"], b)
    cp("dve", A("GW", 0, [1, 8]), PA(b, 0, [1, 8]), [bk(b)], ["GW"])
    b = nb()
    tr(PA(b, 0, [1, 96]), A("CWROW", 0, [1, 128], p=96), CA(C_ID, [1, 96], p=96), ["CWROW", "CONST"], b)
    cp("dve", A("CW", 0, [1, 96]), PA(b, 0, [1, 96]), [bk(b)], ["CW"])

    cvt_i = [0]

    STGS = ["RES", "JUNK", "ACC", "TMPC", "X0", "X1", "GAM"]

    def load_weights(ps_):
        c0 = 0 if ps_ == 0 else 4104
        wtot = 4104 if ps_ == 0 else 4112
        chunks = [(j * 1024, 1024) for j in range(4)] + [(4096, wtot - 4096)]
        for kc in range(8):
            for (cj, w) in chunks:
                i = cvt_i[0]
                cvt_i[0] += 1
                stg = STGS[i % len(STGS)]
                dma(A(stg, 0, [1, w]), win_d.ap()[kc * 128:(kc + 1) * 128, c0 + cj:c0 + cj + w], [], [stg])
                dst = A("WIN", kc * WC + cj, [1, w])
                if i % 2 == 0:
                    act(dst, A(stg, 0, [1, w]), AF.Copy, [stg, "GW"], ["WIN"], scale=A("GW", kc, [1, 1]))
                else:
                    ts("dve", dst, A(stg, 0, [1, w]), A("GW", kc, [1, 1]), None, ALU.mult, None, [stg, "GW"], ["WIN"])
        r0 = ps_ * 1024
        for kc in range(8):
            i = cvt_i[0]
            cvt_i[0] += 1
            stg = STGS[i % len(STGS)]
            dma(A(stg, 0, [1, 1024]), wout_d.ap()[r0 + kc * 128:r0 + (kc + 1) * 128, :], [], [stg])
            cp(("act", "dve")[i % 2], A("WOUT", kc * 1024, [1, 1024]), A(stg, 0, [1, 1024]), [stg], ["WOUT"])

    ctx = {"HT": "HT0", "BETA": "BETA0"}

    def setp(p):
        ctx["BETA"] = "BETA%d" % p

    def HTk(kc):
        return A(ctx["HT"], kc * 128, [1, 128])

    def proj_tm(b, c0, n):
        mmg(PA(b, 0, [1, n]), [(HTk(kc), A("WIN", kc * WC + c0, [1, n])) for kc in range(8)], [ctx["HT"], "WIN"], b)

    store_ops = []
    RUN = lambda g: [None for _ in g]
    hk = lambda nm: [(nm, 0), (nm, 1)]
    H = lambda nm, h: A(nm, h * 128, [1, 128])

    def phase_a(s, t, p, X=None):
        X = X or ("X%d" % p)
        r0 = s * SEQ + t * L
        dma(A(X, 0, [1, 1024]), x_d.ap()[r0:r0 + L, :], [], [X])
        act(A("JUNK", 0, [1, 1024]), A(X, 0, [1, 1024]), AF.Square, [X], ["JUNK"])
        P.op("dve", lambda e: e.tensor_reduce(A("SS", 0, [1, 1]), A("JUNK", 0, [1, 1024]), AX.X, ALU.add),
             ["JUNK"], ["SS"], dur=1200.0)
        rsq(A("RSTD", 0, [1, 1]), A("SS", 0, [1, 1]), 1.0 / D, ["SS"], "RSTD")
        ts("dve", A("AQK", 0, [1, 1024]), A(X, 0, [1, 1024]), A("RSTD", 0, [1, 1]), None, ALU.mult, None,
           [X, "RSTD"], ["AQK"])
        yield
        b = nb()
        for kc in range(8):
            tr(PB(b, kc * 128, [1, 128]), A("AQK", kc * 128, [1, 128]), IDB(), ["AQK", "IDB"], b)
        cp("act", A(ctx["HT"], 0, [1, 1024]), PB(b, 0, [1, 1024]), [bk(b)], [ctx["HT"]])
        yield

    def phase_d(ps_, s, t, p):
        MX = "MIX%d" % p
        r0 = s * SEQ + t * L
        src_d = x_d if ps_ == 0 else part_d
        dma(A("RES", 0, [1, 1024]), src_d.ap()[r0:r0 + L, :], [("pd", r0)] if ps_ == 1 else [], ["RES"])
        b = nb()
        for kc in range(8):
            tr(PB(b, kc * 128, [1, 128]), A(MX, kc * 128, [1, 128]), IDB(), [MX, "IDB"], b)
        cp("act", A("MIXT", 0, [1, 1024]), PB(b, 0, [1, 1024]), [bk(b)], ["MIXT"])
        yield
        for n in range(2):
            b = nb()
            mmg(PA(b, 0, [1, 512]),
                [(A("MIXT", kc * 128, [1, 128]), A("WOUT", kc * 1024 + n * 512, [1, 512])) for kc in range(8)],
                ["MIXT", "WOUT"], b)
            tt("dve", A("RES", n * 512, [1, 512]), PA(b, 0, [1, 512]), A("RES", n * 512, [1, 512]), ALU.add,
               [bk(b), "RES"], ["RES"])
            yield
        if ps_ == 0:
            store_ops.append(dma(part_d.ap()[r0:r0 + L, :], A("RES", 0, [1, 1024]), ["RES"], [("pd", r0)],
                                 q="pool"))
        else:
            act(A("MIXT", 0, [1, 1024]), A("RES", 0, [1, 1024]), AF.Square, ["RES"], ["MIXT"])
            P.op("dve", lambda e: e.tensor_reduce(A("SS2", 0, [1, 1]), A("MIXT", 0, [1, 1024]), AX.X, ALU.add),
                 ["MIXT"], ["SS2"], dur=1200.0)
            rsq(A("RSTD2", 0, [1, 1]), A("SS2", 0, [1, 1]), 1.0 / D, ["SS2"], "RSTD2")
            stt(A("RES", 0, [1, 1024]), A("RES", 0, [1, 1024]), A("RSTD2", 0, [1, 1]), A("GAINM", 0, [1, 1024]),
                ALU.mult, ALU.mult, ["RES", "RSTD2", "GAINM"], ["RES"])
            store_ops.append(dma(out_d.ap()[r0:r0 + L, :], A("RES", 0, [1, 1024]), ["RES"], ["out_dram"], q="pool"))
        yield

    def ml_stage1(s, t, p, first):
        QE = ("QN", "Z")[p]
        K2 = ("QE2", "TT")[p]
        K3 = ("KN", "RP")[p]
        VA = "KBEG%d" % p
        GG = "GG2%d" % p
        EC = "ECRT%d" % p
        setp(p)
        b = nb()
        mmg(PA(b, 0, [1, 8]), [(HTk(kc), A("WIN", kc * WC + 4096, [1, 8])) for kc in range(8)], [ctx["HT"], "WIN"], b)
        tt("dve", A("G8", 0, [1, 8]), PA(b, 0, [1, 8]), A("BIASM", 0, [1, 8]), ALU.add, [bk(b), "BIASM"], ["G8"])
        act(A("E1", 0, [1, 4]), A("G8", 4, [1, 4]), AF.Exp, ["G8"], ["E1"], scale=-1.0)
        act(A("LFP", 0, [1, 4]), A("E1", 0, [1, 4]), AF.Ln, ["E1"], ["LFP"], bias=1.0)
        ts("dve", A("LA", 0, [1, 4]), A("LFP", 0, [1, 4]), -1.0, None, ALU.mult, None, ["LFP"], ["LA"])
        yield
        b = nb()
        for i, cm in enumerate([C_MLE, C_MGT, C_ONE]):
            mmg(PA(b, i * 4, [1, 4]), [(CA(cm, [1, 128]), A("LA", 0, [1, 4]))], ["CONST", "LA"], b, f32=True)
        cp("dve", A("CRT", 0, [1, 12]), PA(b, 0, [1, 12]), [bk(b)], ["CRT"])
        act(A(EC, 0, [1, 12]), A("CRT", 0, [1, 12]), AF.Exp, ["CRT"], [EC])
        tt("dve", A("TA", 0, [1, 4]), A("G8", 0, [1, 4]), A("CRT", 0, [1, 4]), ALU.subtract, ["G8", "CRT"], ["TA"])
        tt("dve", A("TA", 4, [1, 4]), A("G8", 0, [1, 4]), A("CRT", 4, [1, 4]), ALU.add, ["G8", "CRT"], ["TA"])
        act(A("EIC", 0, [1, 8]), A("TA", 0, [1, 8]), AF.Exp, ["TA"], ["EIC"])
        ts("dve", A("QSC", 0, [1, 4]), A(EC, 0, [1, 4]), 128.0 ** -0.5, None, ALU.mult, None, [EC], ["QSC"])
        yield

    def ml_qk(s, t, p, first):
        QE = ("QN", "Z")[p]
        K2 = ("QE2", "TT")[p]
        K3 = ("KN", "RP")[p]
        VA = "KBEG%d" % p
        GG = "GG2%d" % p
        EC = "ECRT%d" % p
        setp(p)
        b = nb()
        proj_tm(b, 0, 512)
        tt("dve", A(QE, 0, [128, 4], [1, 128]), PA(b, 0, [128, 4], [1, 128]), A("QSC", 0, [1, 4], [0, 128]),
           ALU.mult, [bk(b), "QSC"], hk(QE))
        yield
        b = nb()
        proj_tm(b, 512, 512)
        tt("dve", A(K2, 0, [128, 4], [1, 128]), PA(b, 0, [128, 4], [1, 128]), A("EIC", 0, [1, 4], [0, 128]),
           ALU.mult, [bk(b), "EIC"], hk(K2))
        tt("dve", A(K3, 0, [128, 4], [1, 128]), PA(b, 0, [128, 4], [1, 128]), A("EIC", 4, [1, 4], [0, 128]),
           ALU.mult, [bk(b), "EIC"], hk(K3))
        yield

    def ml_voz(s, t, p, first):
        QE = ("QN", "Z")[p]
        K2 = ("QE2", "TT")[p]
        K3 = ("KN", "RP")[p]
        VA = "KBEG%d" % p
        GG = "GG2%d" % p
        EC = "ECRT%d" % p
        setp(p)
        for n in range(2):
            b = nb()
            proj_tm(b, 1024 + n * 512, 512)
            cp("act", A(VA, n * 514, [257, 2], [1, 256]), PA(b, 0, [256, 2], [1, 256]), [bk(b)], [VA])
            yield
        for n in range(2):
            b = nb()
            proj_tm(b, 3072 + n * 512, 512)
            act(A("GS", n * 512, [1, 512]), PA(b, 0, [1, 512]), AF.Silu, [bk(b)], ["GS"])
            yield
        tt("pool", A("GS", 0, [1, 1024]), A("GS", 0, [1, 1024]), A("GAINM", 0, [1, 1024]), ALU.mult,
           ["GS", "GAINM"], ["GS"])
        for n in range(2):
            b = nb()
            proj_tm(b, 2048 + n * 512, 512)
            act(A(GG, n * 512, [1, 512]), PA(b, 0, [1, 512]), AF.Tanh, [bk(b)], [GG], scale=0.5)
            yield
        stt(A(GG, 0, [1, 1024]), A(GG, 0, [1, 1024]), 1.0, A("GS", 0, [1, 1024]), ALU.add, ALU.mult,
            [GG, "GS"], [GG])
        yield

    def ml_stage2(s, t, p, first):
        QE = ("QN", "Z")[p]
        K2 = ("QE2", "TT")[p]
        K3 = ("KN", "RP")[p]
        VA = "KBEG%d" % p
        GG = "GG2%d" % p
        EC = "ECRT%d" % p
        if first:
            mset("pool", A("SG", 0, [1, 1028]), 0.0, ["SG"])
            mset("pool", A("SGB", 0, [1, 1040]), 0.0, ["SGB"])
        b = 0
        for h in range(4):
            tr(PB(b, h * 128, [1, 128]), A(QE, h * 128, [1, 128]), IDB(), hk(QE) + ["IDB"], b)
        for h in range(4):
            tr(PB(b, 512 + h * 128, [1, 128]), A(K2, h * 128, [1, 128]), IDB(), hk(K2) + ["IDB"], b)
        cp("act", A("QNT", 0, [1, 1024]), PB(b, 0, [1, 1024]), [bk(b)], ["QNT"])
        yield
        b = 1
        for h in range(4):
            mmg(PA(b, h * 128, [1, 128]), [(A("QNT", 512 + h * 128, [1, 128]), A("QNT", h * 128, [1, 128]))],
                ["QNT"], b)
        tt("dve", A("QET0", 0, [128, 4], [1, 128]), PA(b, 0, [128, 4], [1, 128]), CA(C_MLE, [0, 4], [1, 128]),
           ALU.mult, [bk(b), "CONST"], ["QET0"])
        yield
        for h in range(4):
            mmg(PA(h, 0, [1, 257]),
                [(A("QNT", h * 128, [1, 128]), A("SGB", h * 257, [1, 257])),
                 (A("QET0", h * 128, [1, 128]), A(VA, h * 257, [1, 257]))], ["QNT", "SGB", "QET0", VA], h)
        allb = [bk(h) for h in range(4)]
        rr4 = A("RR", 0, [1, 4])
        act(rr4, PA(0, 256, [512, 4]), AF.Abs, allb, ["RR"])
        ts("dve", rr4, rr4, 1.0, None, ALU.max, None, ["RR"], ["RR"])
        P.op("dve", lambda e, rr4=rr4: e.reciprocal(rr4, rr4), ["RR"], ["RR"])
        yield
        for h in range(4):
            act(A("GAM", h * 256, [1, 256]), PA(h, 0, [1, 256]), AF.Square, [bk(h), "RR"], ["GAM"],
                scale=A("RR", h, [1, 1]))
        P.op("dve", lambda e: e.tensor_reduce(A("SSH", 0, [1, 4]), A("GAM", 0, [256, 4], [1, 256]), AX.X, ALU.add),
             ["GAM"], ["SSH"], dur=1200.0)
        rsq(A("RS", 0, [1, 4]), A("SSH", 0, [1, 4]), 1.0 / 256, ["SSH"], "RS")
        tt("dve", A("RS", 0, [1, 4]), A("RS", 0, [1, 4]), rr4, ALU.mult, ["RS", "RR"], ["RS"])
        yield
        for h in range(4):
            stt(A("MIX%d" % p, h * 256, [1, 256]), PA(h, 0, [1, 256]), A("RS", h, [1, 1]), A(GG, h * 256, [1, 256]),
                ALU.mult, ALU.mult, [bk(h), "RS", GG], ["MIX%d" % p])
        yield
        for h in range(4):
            mmg(PA(h, 0, [1, 257]), [(A(K3, h * 128, [1, 128]), A(VA, h * 257, [1, 257]))], hk(K3) + [VA], h)
            stt(A("SG", h * 257, [1, 257]), A("SG", h * 257, [1, 257]), A(EC, 8 + h, [1, 1]),
                PA(h, 0, [1, 257]), ALU.mult, ALU.add, ["SG", EC, bk(h)], ["SG"])
        cp("act", A("SGB", 0, [1, 1028]), A("SG", 0, [1, 1028]), ["SG"], ["SGB"])
        yield

    def dwslot(j, g):
        i = j * 3 + g
        if i < 6:
            nm = ("ACC", "TMPC", "X1")[i // 2]
            return nm, (i % 2) * 1024, T[nm + "_bf"], 2048
        return "DWA", (i - 6) * 1024, T["DWA"], 6 * 1024

    def dwap(j, g, c):
        nm, off, th, rs = dwslot(j, g)
        return nm, bass.AP(th, off + c * 128, [[rs, 128], [1, 128]])

    def build_dw():
        i = 0
        for j in range(4):
            for g in range(3):
                for c in range(8):
                    nm, ap = dwap(j, g, c)
                    ts(("dve", "pool")[i % 2], ap, IDB(), A("CW", j * 24 + g * 8 + c, [1, 1]), None, ALU.mult, None,
                       ["IDB", "CW"], [nm])
                    i += 1

    def gdn_group(g, p, first):
        KB, KR, VBn, QE_, MMn = "KBEG%d" % p, "KREV%d" % p, "VB%d" % p, "QET%d" % p, "MM%d" % p
        EC = "ECRT%d" % p
        XB = ("XB1", "XB2", "XB0")[g]
        S = ("QN", "KN", VBn)[g]
        SK = hk(S) if g < 2 else [S]
        HL = "HALO%d" % g
        if first:
            mset("pool", A("HALO", g * 24, [1, 24]), 0.0, [HL])
        cp("pool", A(XB, 0, [131, 8], [1, 3]), A("HALO", g * 24, [3, 8], [1, 3]), [HL], [XB])
        dwn = sorted(set(dwslot(j, g)[0] for j in range(4)))
        for n in range(2):
            b = nb()
            for c in range(n * 4, n * 4 + 4):
                mmg(PA(b, (c % 4) * 128, [1, 128]),
                    [(A("WIN", kc * WC + g * 1024 + c * 128, [1, 128]), HTk(kc)) for kc in range(8)],
                    ["WIN", ctx["HT"]], b)
            cp("act", A(XB, n * 4 * 131 + 3, [131, 4], [1, 128]), PA(b, 0, [128, 4], [1, 128]), [bk(b)], [XB])
            yield
            if n == 1:
                cp("pool", A("HALO", g * 24, [3, 8], [1, 3]), A(XB, 128, [131, 8], [1, 3]), [XB], [HL])
            b2 = nb()
            for c in range(n * 4, n * 4 + 4):
                mmg(PA(b2, (c % 4) * 128, [1, 128]),
                    [(A(XB, c * 131 + j, [1, 128]), dwap(j, g, c)[1]) for j in range(4)], [XB] + dwn, b2)
            act(A(S, n * 512, [1, 512]), PA(b2, 0, [1, 512]), AF.Silu, [bk(b2)], [SK[n]] if g < 2 else [S])
            yield
        sall = A(S, 0, [128, 8], [1, 128])
        if g < 2:
            SQ, RQ = ("SSQ", "RQ") if g == 0 else ("SSQK", "RQK")
            for n in range(2):
                b = nb()
                act(PA(b, 0, [1, 512]), A(S, n * 512, [1, 512]), AF.Square, [SK[n]], [bk(b)])
                P.op("dve", lambda e, b=b, n=n: e.tensor_reduce(A(SQ, n * 4, [1, 4]), PA(b, 0, [128, 4], [1, 128]),
                                                                AX.X, ALU.add), [bk(b)], [SQ], dur=700.0)
            rsq(A(RQ, 0, [1, 8]), A(SQ, 0, [1, 8]), 1.0, [SQ], RQ)
        if g == 0:
            ts("dve", A("QS1", 0, [1, 8]), A("RQ", 0, [1, 8]), 128.0 ** -0.5, None, ALU.mult, None, ["RQ"], ["QS1"])
            tt("dve", A("QS2", 0, [1, 8]), A("QS1", 0, [1, 8]), A(EC, 0, [1, 8]), ALU.mult, ["QS1", EC], ["QS2"])
            tt("dve", A("QE2", 0, [128, 8], [1, 128]), sall, A("QS2", 0, [1, 8], [0, 128]), ALU.mult,
               SK + ["QS2"], hk("QE2"))
            tt("dve", sall, sall, A("QS1", 0, [1, 8], [0, 128]), ALU.mult, SK + ["QS1"], SK)
            yield
            for src, dst in (("QN", "QNT"), ("QE2", QE_)):
                b2 = nb()
                for c in range(8):
                    tr(PB(b2, c * 128, [1, 128]), A(src, c * 128, [1, 128]), IDB(), hk(src) + ["IDB"], b2)
                cp("act", A(dst, 0, [1, 1024]), PB(b2, 0, [1, 1024]), [bk(b2)], [dst])
                yield
        elif g == 1:
            tt("dve", A("KSB", 0, [1, 8]), A("RQK", 0, [1, 8]), A("BE", 0, [1, 8]), ALU.mult, ["RQK", "BE"], ["KSB"])
            tt("dve", A("KSR", 0, [1, 8]), A("RQK", 0, [1, 8]), A(EC, 8, [1, 8]), ALU.mult, ["RQK", EC], ["KSR"])
            tt("dve", A(KB, 0, [128, 8], [1, 128]), sall, A("KSB", 0, [1, 8], [0, 128]), ALU.mult,
               SK + ["KSB"], [KB])
            tt("dve", A(KR, 0, [128, 8], [1, 128]), sall, A("KSR", 0, [1, 8], [0, 128]), ALU.mult,
               SK + ["KSR"], [KR])
            tt("dve", sall, sall, A("RQK", 0, [1, 8], [0, 128]), ALU.mult, SK + ["RQK"], SK)
            yield
            b2 = nb()
            for c in range(8):
                tr(PB(b2, c * 128, [1, 128]), A("KN", c * 128, [1, 128]), IDB(), hk("KN") + ["IDB"], b2)
            cp("act", A("KNT", 0, [1, 1024]), PB(b2, 0, [1, 1024]), [bk(b2)], ["KNT"])
            yield
        else:
            tt("dve", sall, sall, A(ctx["BETA"], 0, [1, 8], [0, 128]), ALU.mult, [S, ctx["BETA"]], [S])
            yield

    def per_head(pairs_fn, reads, evac):
        for n in range(2):
            b = nb()
            for hh in range(4):
                h = n * 4 + hh
                mmg(PA(b, hh * 128, [1, 128]), pairs_fn(h), reads(n) if callable(reads) else reads, b)
            evac(n, b)
            yield

    def gdn_a(s, t, p, first):
        KB, KR, VBn, QE_, MMn, AQ = "KBEG%d" % p, "KREV%d" % p, "VB%d" % p, "QET%d" % p, "MM%d" % p, "AQKT%d" % p
        EC = "ECRT%d" % p
        GG = "GG2%d" % p
        setp(p)
        b = nb()
        mmg(PA(b, 0, [1, 16]), [(HTk(kc), A("WIN", kc * WC + 4096, [1, 16])) for kc in range(8)], [ctx["HT"], "WIN"], b)
        act(A(ctx["BETA"], 0, [1, 8]), PA(b, 0, [1, 8]), AF.Exp, [bk(b)], [ctx["BETA"]], scale=-1.0)
        ts("dve", A(ctx["BETA"], 0, [1, 8]), A(ctx["BETA"], 0, [1, 8]), 1.0, None, ALU.add, None, [ctx["BETA"]], [ctx["BETA"]])
        bt_ = A(ctx["BETA"], 0, [1, 8])
        P.op("dve", lambda e, bt_=bt_: e.reciprocal(bt_, bt_), [ctx["BETA"]], [ctx["BETA"]])
        tt("dve", A("TA", 0, [1, 8]), PA(b, 8, [1, 8]), A("DTB", 0, [1, 8]), ALU.add, [bk(b), "DTB"], ["TA"])
        act(A("E1", 0, [1, 8]), A("TA", 0, [1, 8]), AF.Exp, ["TA"], ["E1"])
        act(A("LFP", 0, [1, 8]), A("E1", 0, [1, 8]), AF.Ln, ["E1"], ["LFP"], bias=1.0)
        tt("dve", A("LA", 0, [1, 8]), A("LFP", 0, [1, 8]), A("NEGA", 0, [1, 8]), ALU.mult, ["LFP", "NEGA"], ["LA"])
        yield
        b = nb()
        for i, cm in enumerate([C_MLE, C_MGT, C_ONE]):
            mmg(PA(b, i * 8, [1, 8]), [(CA(cm, [1, 128]), A("LA", 0, [1, 8]))], ["CONST", "LA"], b, f32=True)
        cp("dve", A("CRT", 0, [1, 24]), PA(b, 0, [1, 24]), [bk(b)], ["CRT"])
        act(A(EC, 0, [1, 24]), A("CRT", 0, [1, 24]), AF.Exp, ["CRT"], [EC])
        tt("dve", A("BE", 0, [1, 8]), A(ctx["BETA"], 0, [1, 8]), A(EC, 0, [1, 8]), ALU.mult, [ctx["BETA"], EC], ["BE"])
        tt("dve", A("JUNK", 0, [128, 8], [1, 128]), CA(C_MGT, [0, 8], [1, 128]), A("LA", 0, [1, 8], [0, 128]),
           ALU.mult, ["CONST", "LA"], ["JUNK"])
        yield
        for n in range(2):
            b = nb()
            mmg(PA(b, 0, [1, 512]), [(CA(C_MLE, [1, 128]), A("JUNK", n * 512, [1, 512]))], ["CONST", "JUNK"], b, f32=True)
            act(A("GAM", n * 512, [1, 512]), PA(b, 0, [1, 512]), AF.Exp, [bk(b)], ["GAM"])
            yield
        tt("pool", A("GS", 0, [128, 8], [1, 128]), A("GAM", 0, [128, 8], [1, 128]), CA(C_MGT, [0, 8], [1, 128]),
           ALU.mult, ["GAM", "CONST"], ["GS"])
        tt("pool", A("GS", 0, [128, 8], [1, 128]), A("GS", 0, [128, 8], [1, 128]), A(ctx["BETA"], 0, [1, 8], [0, 128]),
           ALU.mult, ["GS", ctx["BETA"]], ["GS"])
        tt("pool", A("GAM", 0, [128, 8], [1, 128]), A("GAM", 0, [128, 8], [1, 128]), CA(C_MGE, [0, 8], [1, 128]),
           ALU.mult, ["GAM", "CONST"], ["GAM"])
        yield

    def gdn_k(s, t, p, first):
        KB, KR, VBn, QE_, MMn, AQ = "KBEG%d" % p, "KREV%d" % p, "VB%d" % p, "QET%d" % p, "MM%d" % p, "AQKT%d" % p
        EC = "ECRT%d" % p
        GG = "GG2%d" % p
        setp(p)
        yield from gdn_group(1, p, first)
        yield from per_head(lambda h: [(H("KNT", h), H("KNT", h))], ["KNT"],
                            lambda n, b: tt("dve", A(MMn, n * 512, [1, 512]), PA(b, 0, [1, 512]),
                                            A("GS", n * 512, [1, 512]), ALU.mult, [bk(b), "GS"], [(MMn, n)]))
        for n in range(2):
            tt("pool", A(MMn, n * 512, [128, 4], [1, 128]), A(MMn, n * 512, [128, 4], [1, 128]),
               A("IDB", 0, [0, 4], [1, 128]), ALU.add, [(MMn, n), "IDB"], [(MMn, n)])
        yield

    def gdn_q(s, t, p, first):
        KB, KR, VBn, QE_, MMn, AQ = "KBEG%d" % p, "KREV%d" % p, "VB%d" % p, "QET%d" % p, "MM%d" % p, "AQKT%d" % p
        EC = "ECRT%d" % p
        GG = "GG2%d" % p
        setp(p)
        for n in range(2):
            b = nb()
            proj_tm(b, 3072 + n * 512, 512)
            act(A(GG, n * 512, [1, 512]), PA(b, 0, [1, 512]), AF.Silu, [bk(b)], [GG])
            yield
        tt("pool", A(GG, 0, [128, 8], [1, 128]), A(GG, 0, [128, 8], [1, 128]), A("GAING", 0, [0, 8], [1, 128]),
           ALU.mult, [GG, "GAING"], [GG])
        yield from gdn_group(0, p, first)
        yield from per_head(lambda h: [(H("QNT", h), H("KNT", h))], ["QNT", "KNT"],
                            lambda n, b: tt("dve", A("AQK", n * 512, [1, 512]), PA(b, 0, [1, 512]),
                                            A("GAM", n * 512, [1, 512]), ALU.mult, [bk(b), "GAM"], ["AQK"]))
        b = nb()
        for c in range(8):
            tr(PB(b, c * 128, [1, 128]), A("AQK", c * 128, [1, 128]), IDB(), ["AQK", "IDB"], b)
        cp("act", A(AQ, 0, [1, 1024]), PB(b, 0, [1, 1024]), [bk(b)], [AQ])
        yield

    def gdn_v(s, t, p, first):
        setp(p)
        yield from gdn_group(2, p, first)

    def gdn_stage2(s, t, p, first):
        KB, KR, VBn, QE_, MMn, AQ = "KBEG%d" % p, "KREV%d" % p, "VB%d" % p, "QET%d" % p, "MM%d" % p, "AQKT%d" % p
        EC = "ECRT%d" % p
        GG = "GG2%d" % p
        r0 = s * SEQ + t * L
        if first:
            mset("pool", A("SG", 0, [1, 1028]), 0.0, ["SG"])
            mset("pool", A("SGB", 0, [1, 1040]), 0.0, ["SGB"])
        for n in range(2):
            b = nb()
            for hh in range(4):
                mmg(PA(b, hh * 128, [1, 128]), [(H(MMn, n * 4 + hh), IDB())], [(MMn, n), "IDB"], b)
            tt("dve", A("Z", n * 512, [128, 4], [1, 128]), PA(b, 0, [128, 4], [1, 128]),
               A("LEVB", 0, [0, 4], [1, 128]), ALU.mult, [bk(b), "LEVB"], [("Z", n)])
            tt("pool", A("TT", n * 512, [128, 4], [1, 128]), A(MMn, n * 512, [128, 4], [1, 128]),
               A("LEVB", 0, [0, 4], [1, 128]), ALU.mult, [(MMn, n), "LEVB"], [("TT", n)])
            yield
        for k in range(1, 7):
            for n in range(2):
                b = nb()
                for hh in range(4):
                    h = n * 4 + hh
                    mmg(PA(b, hh * 128, [1, 128]), [(H(MMn, h), H("Z", h))], [(MMn, n), ("Z", n)], b)
                tt("dve", A("RP", n * 512, [128, 4], [1, 128]), PA(b, 0, [128, 4], [1, 128]),
                   A("LEVB", k * 128, [0, 4], [1, 128]), ALU.mult, [bk(b), "LEVB"], [("RP", n)])
                yield
            for n in range(2):
                zb_ = nb()
                for hh in range(4):
                    h = n * 4 + hh
                    mmg(PA(zb_, hh * 128, [1, 128]), [(H("TT", h), H("RP", h))], [("TT", n), ("RP", n)], zb_)
                tb_ = None
                if k < 6:
                    tb_ = nb()
                    for hh in range(4):
                        h = n * 4 + hh
                        mmg(PA(tb_, hh * 128, [1, 128]), [(H("RP", h), H("TT", h))], [("TT", n), ("RP", n)], tb_)
                cp("act", A("Z", n * 512, [1, 512]), PA(zb_, 0, [1, 512]), [bk(zb_)], [("Z", n)])
                if k < 6:
                    cp("act" if n == 0 else "dve", A("TT", n * 512, [1, 512]), PA(tb_, 0, [1, 512]),
                       [bk(tb_)], [("TT", n)])
                yield
        yield from per_head(lambda h: [(H(KB, h), H("Z", h))], lambda n: [KB, ("Z", n)],
                            lambda n, b: act(A("NWT", n * 512, [1, 512]), PA(b, 0, [1, 512]), AF.Copy, [bk(b)],
                                             [("NWT", n)], scale=-1.0))
        yield from per_head(lambda h: [(H("Z", h), H(VBn, h)), (H("NWT", h), H("SGB", h))],
                            lambda n: [("Z", n), VBn, ("NWT", n), "SGB"],
                            lambda n, b: cp("act", A("VNEW", n * 512, [1, 512]), PA(b, 0, [1, 512]), [bk(b)],
                                            [("VNEW", n)]))
        ob = []

        def o_evac(n, b):
            ob.append(b)
            act(A("RP", n * 512, [1, 512]), PA(b, 0, [1, 512]), AF.Square, [bk(b)], [("RP", n)])
        yield from per_head(lambda h: [(H(QE_, h), H("SGB", h)), (H(AQ, h), H("VNEW", h))],
                            lambda n: [QE_, "SGB", AQ, ("VNEW", n)], o_evac)
        P.op("dve", lambda e: e.tensor_reduce(A("SSO", 0, [1, 8]), A("RP", 0, [128, 8], [1, 128]), AX.X, ALU.add),
             hk("RP"), ["SSO"], dur=1200.0)
        rsq(A("RSO", 0, [1, 8]), A("SSO", 0, [1, 8]), 1.0 / 128, ["SSO"], "RSO")
        tt("dve", A(GG, 0, [128, 8], [1, 128]), A(GG, 0, [128, 8], [1, 128]), A("RSO", 0, [1, 8], [0, 128]),
           ALU.mult, [GG, "RSO"], [GG])
        for n in range(2):
            tt("dve", A("MIX%d" % p, n * 512, [1, 512]), PA(ob[n], 0, [1, 512]), A(GG, n * 512, [1, 512]), ALU.mult,
               [bk(ob[n]), GG], ["MIX%d" % p])
        yield
        tt("pool", A("SG", 0, [128, 8], [1, 128]), A("SG", 0, [128, 8], [1, 128]), A(EC, 16, [1, 8], [0, 128]),
           ALU.mult, ["SG", EC], ["SG"])
        yield from per_head(lambda h: [(H(KR, h), H("VNEW", h))], lambda n: [KR, ("VNEW", n)],
                            lambda n, b: tt("dve", A("SG", n * 512, [1, 512]), PA(b, 0, [1, 512]),
                                            A("SG", n * 512, [1, 512]), ALU.add, [bk(b), "SG"], ["SG"]))
        cp("act", A("SGB", 0, [1, 1024]), A("SG", 0, [1, 1024]), ["SG"], ["SGB"])
        yield

    def collect(gen, banks):
        if gen is None:
            return []
        P.defer = []
        bset[0] = banks
        for _ in gen:
            pass
        l = P.defer
        P.defer = None
        bset[0] = list(range(8))
        return l

    def collect(gen, banks):
        P.defer = []
        bset[0] = banks
        for _ in gen:
            pass
        l = P.defer
        P.defer = None
        bset[0] = list(range(8))
        return l

    for ps_ in range(2):
        load_weights(ps_)
        if ps_ == 1:
            dma(A("GAINM", 0, [1, 1024]), bc(fn_d, 1024), [], ["GAINM"])
            build_dw()
        if ps_ == 0:
            mset("pool", A("KBEG0", 0, [1, 1040]), 1.0, ["KBEG0"])
            mset("pool", A("KBEG1", 0, [1, 1040]), 1.0, ["KBEG1"])
        tiles = [(s, t) for s in range(NSEQ) for t in range(NT_RUN)]
        prev = None
        prev2 = None
        hts = lambda j: "HT%d" % (j % 3)
        ctx["HT"] = hts(0)
        P.merge([collect(phase_a(tiles[0][0], tiles[0][1], 0, X="X0"), [4])])
        for i, cur in enumerate(tiles + [None, None]):
            lists = []
            pd = None
            if prev2 is not None:
                pd = (ps_, prev2[0], prev2[1], (i - 2) % 2)
            if prev is not None:
                pa = (prev[0], prev[1], (i - 1) % 2, prev[1] == 0)
                ctx["HT"] = hts(i - 1)
                if ps_ == 0:
                    lists.append(collect(ml_stage2(*pa), [0, 1, 2, 3]))
                else:
                    lists.append(collect(gdn_v(*pa), [3]))
                    lists.append(collect(gdn_stage2(*pa), [0, 1, 2]))
            if ps_ == 1 and pd is not None:
                lists.append(collect(phase_d(*pd), [3]))
            if cur is not None:
                ca = (cur[0], cur[1], i % 2, cur[1] == 0)
                ctx["HT"] = hts(i)
                if ps_ == 0:
                    lists.append(collect(ml_stage1(*ca), [4]))
                    lists.append(collect(ml_qk(*ca), [5]))
                    lists.append(collect(ml_voz(*ca), [6, 7]))
                else:
                    lists.append(collect(gdn_a(*ca), [4, 5]))
                    lists.append(collect(gdn_k(*ca), [4, 5]))
                    lists.append(collect(gdn_q(*ca), [6, 7]))
            if ps_ == 0 and pd is not None:
                lists.append(collect(phase_d(*pd), [4]))
            if i + 1 < len(tiles):
                nx = tiles[i + 1]
                ctx["HT"] = hts(i + 1)
                lists.insert(0, collect(phase_a(nx[0], nx[1], 0, X="X0"), [5] if ps_ == 0 else [3]))
            P.merge(lists)
            prev2 = prev
            prev = cur
    P.op("pool", None, ["out_dram"], [])
    fin = P.ops[-1]
    for so in store_ops:
        fin["deps"][so[0]] = True


    P.finalize()
    sems = {}
    for e, ng in P.ngen.items():
        for g in range(ng):
            sems[(e, g)] = es.enter_context(nc.semaphore("s_%s_%d" % (e, g)))
    for i in range(NDMA):
        sems[("dma", i)] = es.enter_context(nc.semaphore("s_dma_%d" % i))
    for i in range(NDMAP):
        sems[("dmap", i)] = es.enter_context(nc.semaphore("s_dmap_%d" % i))
    with nc.Block() as block:
        @block.sync
        def _(e):
            P.emit("sp", e, sems)

        @block.tensor
        def _(e):
            P.emit("pe", e, sems)

        @block.scalar
        def _(e):
            P.emit("act", e, sems)

        @block.vector
        def _(e):
            P.emit("dve", e, sems)

        @block.gpsimd
        def _(e):
            P.emit("pool", e, sems)
    es.close()
    return nc


NT_RUN = NT
ALPHA = 0.0
TBL_NS = 1300.0
TBL_PEN = 1300.0
_CACHE = {}


def kernel(x, attn_norm, w_in, m_i_bias, m_f_bias, m_out_norm, g_conv, g_a_log, g_dt_bias, g_out_norm, w_out,
           final_norm):
    f = lambda a: np.ascontiguousarray(np.asarray(a, dtype=np.float32))
    if "nc" not in _CACHE:
        _CACHE["nc"] = build_program()
    nc = _CACHE["nc"]
    x = f(x)
    shared = {
        "w_in": f(w_in).reshape(D, 8216),
        "w_out": f(w_out).reshape(2048, D),
        "attn_norm": f(attn_norm).reshape(8, 128),
        "m_i_bias": f(m_i_bias).reshape(1, 4),
        "m_f_bias": f(m_f_bias).reshape(1, 4),
        "m_out_norm": f(m_out_norm).reshape(1, 1024),
        "g_conv": f(g_conv).reshape(96, 128),
        "g_a_log": f(g_a_log).reshape(1, 8),
        "g_dt_bias": f(g_dt_bias).reshape(1, 8),
        "g_out_norm": f(g_out_norm).reshape(1, 128),
        "final_norm": f(final_norm).reshape(1, 1024),
        "consts": make_consts(),
    }
    in_maps = []
    for c in range(NCORES):
        m = dict(shared)
        m["x"] = x[c * NSEQ:(c + 1) * NSEQ].reshape(TOK, D)
        in_maps.append(m)
    res = run_bass_kernel_spmd(nc, in_maps, core_ids=list(range(NCORES)))
    outs = [np.asarray(r["out"]).reshape(NSEQ, SEQ, D) for r in res.results]
    return np.concatenate(outs, axis=0).astype(np.float32)
```

```python
import numpy as np
from contextlib import ExitStack
import concourse.bass as bass
import concourse.mybir as mybir
from concourse.bass_utils import run_bass_kernel_spmd

F32 = mybir.dt.float32
BF = mybir.dt.bfloat16
AF = mybir.ActivationFunctionType
ALU = mybir.AluOpType
AX = mybir.AxisListType

NCORES = 8
SEQ = 2048
NSEQ = 2
TOK = NSEQ * SEQ
L = 128
NT = SEQ // L
D = 1024
WC = 4112
EPS = 1e-6
GEN = 3000
NDMA = 24
NDMAP = 8
C_MLE, C_MGT, C_MGE, C_ONE, C_ID, C_LEV = 0, 128, 256, 384, 512, 640
NCONST = 640 + 7 * 128


def make_consts():
    idx = np.arange(L)
    c = np.zeros((L, NCONST), np.float32)
    c[:, C_MLE:C_MLE + L] = idx[:, None] <= idx[None, :]
    c[:, C_MGT:C_MGT + L] = idx[:, None] > idx[None, :]
    c[:, C_MGE:C_MGE + L] = idx[:, None] >= idx[None, :]
    c[:, C_ONE:C_ONE + L] = 1.0
    c[:, C_ID:C_ID + L] = np.eye(L)
    for k in range(7):
        b2 = 2 << k
        m = np.where((idx[:, None] // b2) == (idx[None, :] // b2), -1.0, 0.0)
        m[idx, idx] = 1.0
        c[:, C_LEV + k * L:C_LEV + (k + 1) * L] = m
    return c


class Prog:
    def __init__(self):
        self.ops = []
        self.lastw = {}
        self.readers = {}
        self.defer = None
        self.cur_tbl = None
        self.efree = {}
        self.wdone = {}
        self.rdone = {}

    def _norm(self, keys):
        return ["JUNK" if k == "GS" else k for k in keys]

    def op(self, eng, fn, reads=(), writes=(), dma=False, dur=300.0, tbl=None):
        reads = self._norm(reads)
        writes = self._norm(writes)
        if self.defer is not None:
            h = [None]
            self.defer.append(dict(eng=eng, fn=fn, reads=list(reads), writes=list(writes), dma=dma, dur=dur, h=h,
                                   tbl=tbl))
            return h
        self._sim(eng, reads, writes, dma, dur, tbl)
        return [self._op(eng, fn, reads, writes, dma)]

    def _est(self, eng, reads, writes, tbl=None, pen=None):
        t = self.efree.get(eng, 0.0)
        if tbl is not None and tbl != self.cur_tbl:
            t += TBL_NS if pen is None else pen
        for r in reads:
            t = max(t, self.wdone.get(r, 0.0))
        for w in writes:
            t = max(t, self.wdone.get(w, 0.0), self.rdone.get(w, 0.0))
        return t

    def _sim(self, eng, reads, writes, dma, dur, tbl=None):
        st = self._est(eng, reads, writes, tbl)
        if tbl is not None:
            self.cur_tbl = tbl
        if dma:
            self.efree[eng] = st + 100.0
            end = st + 3000.0
        else:
            end = st + dur
            self.efree[eng] = end
        for r in reads:
            self.rdone[r] = max(self.rdone.get(r, 0.0), end)
        for w in writes:
            self.wdone[w] = end + 150.0
        return st

    def merge(self, lists):
        lists = [l for l in lists if l]
        n = len(lists)
        accs = []
        for l in lists:
            wl, rl = {}, {}
            for i, d in enumerate(l):
                for r in d["reads"]:
                    rl[r] = i
                for w in d["writes"]:
                    wl[w] = i
            accs.append((wl, rl))
        pos = [0] * n
        rems = []
        for l in lists:
            r_ = [0.0] * (len(l) + 1)
            for i in range(len(l) - 1, -1, -1):
                r_[i] = r_[i + 1] + l[i]["dur"]
            rems.append(r_)

        def blocked(j, d):
            for i in range(j):
                pi = pos[i]
                if pi >= len(lists[i]):
                    continue
                wl, rl = accs[i]
                for r in d["reads"]:
                    if wl.get(r, -1) >= pi:
                        return True
                for w in d["writes"]:
                    if wl.get(w, -1) >= pi or rl.get(w, -1) >= pi:
                        return True
            return False

        while True:
            best = None
            for i, l in enumerate(lists):
                if pos[i] < len(l):
                    d = l[pos[i]]
                    if blocked(i, d):
                        continue
                    st = self._est(d["eng"], d["reads"], d["writes"], d.get("tbl"), TBL_PEN) - ALPHA * rems[i][pos[i]]
                    if best is None or st < best[0]:
                        best = (st, i)
            if best is None:
                break
            i = best[1]
            d = lists[i][pos[i]]
            pos[i] += 1
            self._sim(d["eng"], d["reads"], d["writes"], d["dma"], d["dur"], d.get("tbl"))
            d["h"][0] = self._op(d["eng"], d["fn"], d["reads"], d["writes"], d["dma"])
        assert all(pos[i] == len(lists[i]) for i in range(n))

    def _op(self, eng, fn, reads=(), writes=(), dma=False):
        idx = len(self.ops)
        deps = {}
        for r in reads:
            if r in self.lastw:
                deps[self.lastw[r]] = True
        for w in writes:
            if w in self.lastw:
                deps.setdefault(self.lastw[w], False)
            for rd in self.readers.get(w, ()):
                deps.setdefault(rd, False)
        self.ops.append(dict(eng=eng, fn=fn, deps=deps, dma=dma))
        for r in reads:
            self.readers.setdefault(r, []).append(idx)
        for w in writes:
            self.lastw[w] = idx
            self.readers[w] = []
        return idx

    def finalize(self):
        ops = self.ops
        last_dma_on_sem = {}
        ndma = 0
        npool = 0
        for i, o in enumerate(ops):
            nd = {}
            for d, raw in o["deps"].items():
                od = ops[d]
                if od["dma"]:
                    nd[d] = raw
                elif od["eng"] == o["eng"]:
                    if o["eng"] != "pe" and raw and not o["dma"]:
                        nd[d] = raw
                    elif o["dma"]:
                        nd[d] = raw
                else:
                    nd[d] = raw
            if o["dma"]:
                if o["eng"] == "pool":
                    s = ("dmap", npool % NDMAP)
                    npool += 1
                else:
                    s = ("dma", ndma % NDMA)
                    ndma += 1
                if s in last_dma_on_sem:
                    nd[last_dma_on_sem[s]] = False
                last_dma_on_sem[s] = i
                o["dsem"] = s
            o["deps"] = nd
        needed = set()
        for o in ops:
            needed.update(o["deps"].keys())
        cnt = {}
        dcnt = {}
        for i, o in enumerate(ops):
            if o["dma"]:
                dcnt[o["dsem"]] = dcnt.get(o["dsem"], 0) + 16
                o["tok"] = (o["dsem"], dcnt[o["dsem"]])
            elif i in needed:
                c = cnt.get(o["eng"], 0)
                cnt[o["eng"]] = c + 1
                o["tok"] = ((o["eng"], c // GEN), c % GEN + 1)
            else:
                o["tok"] = None
        self.ngen = {e: (c + GEN - 1) // GEN for e, c in cnt.items()}

    def emit(self, eng_name, eng, sems):
        waited = {}
        for o in self.ops:
            if o["eng"] != eng_name:
                continue
            for d in sorted(o["deps"].keys()):
                key, val = self.ops[d]["tok"]
                if waited.get(key, 0) >= val:
                    continue
                waited[key] = val
                eng.wait_ge(sems[key], val)
            if o["fn"] is None:
                continue
            ins = o["fn"](eng)
            if o["tok"] is not None:
                key, _ = o["tok"]
                ins.then_inc(sems[key], 16 if o["dma"] else 1)


def build_program():
    nc = bass.Bass("TRN2", target_bir_lowering=False)
    dt_in = lambda n, s: nc.dram_tensor(n, list(s), F32, kind="ExternalInput")
    x_d = dt_in("x", [TOK, D])
    win_d = dt_in("w_in", [D, 8216])
    wout_d = dt_in("w_out", [2048, D])
    an_d = dt_in("attn_norm", [8, 128])
    mib_d = dt_in("m_i_bias", [1, 4])
    mfb_d = dt_in("m_f_bias", [1, 4])
    mon_d = dt_in("m_out_norm", [1, 1024])
    gcv_d = dt_in("g_conv", [96, 128])
    gal_d = dt_in("g_a_log", [1, 8])
    gdt_d = dt_in("g_dt_bias", [1, 8])
    gon_d = dt_in("g_out_norm", [1, 128])
    fn_d = dt_in("final_norm", [1, 1024])
    cst_d = dt_in("consts", [L, NCONST])
    out_d = nc.dram_tensor("out", [TOK, D], F32, kind="ExternalOutput")
    part_d = nc.dram_tensor("partial", [TOK, D], F32, kind="Internal")

    P = Prog()
    es = ExitStack()
    T = {}

    def sb(name, cols, dt, parts=128):
        T[name] = es.enter_context(nc.sbuf_tensor(name, [parts, cols], dt))
        return T[name]

    sb("WIN", 8 * WC, BF)
    sb("WOUT", 8 * 1024, BF)
    sb("CONST", 640, F32)
    sb("IDB", 128, BF)
    sb("LEVB", 7 * 128, BF)
    sb("GAINM", 1024, F32)
    sb("GAING", 128, F32)
    sb("CW", 96, F32)
    sb("CWROW", 128, F32, parts=96)
    sb("AN8", 128, F32, parts=8)
    sb("GW", 8, F32)
    sb("BIASM", 8, F32)
    sb("DTB", 8, F32)
    sb("NEGA", 8, F32)
    sb("X0", 1024, F32)
    sb("X1", 1024, F32)
    sb("JUNK", 1028, F32)
    sb("HT0", 1024, BF)
    sb("HT1", 1024, BF)
    sb("HT2", 1024, BF)
    sb("MIX0", 1024, BF)
    sb("MIX1", 1024, BF)
    sb("MIXT", 1024, BF)
    sb("RES", 1028, F32)
    for n in ["SS", "RSTD", "SS2", "RSTD2"]:
        sb(n, 1, F32)
    for n in ["G8", "E1", "LFP", "LA", "CRT", "ECRT0", "ECRT1", "TA", "TB", "EIC", "EIR", "QSC", "RR", "SSH", "RS",
              "BETA0", "BETA1", "BE", "SSQ", "RQ", "SSQK", "RQK", "QS1", "QS2", "KSB", "KSR", "SSO", "RSO"]:
        sb(n, 24 if n in ("CRT", "ECRT0", "ECRT1") else 16, F32)
    sb("TMPC", 1024, F32)
    sb("HALO", 24 * 3, BF)
    sb("ACC", 1024, F32)
    sb("XB0", 8 * 131, BF)
    sb("XB1", 8 * 131, BF)
    sb("XB2", 8 * 131, BF)
    sb("DWA", 6 * 1024, BF)
    sb("QN", 1024, BF)
    sb("QE2", 1024, BF)
    sb("KN", 1024, BF)
    sb("KBEG0", 1040, BF)
    sb("KBEG1", 1040, BF)
    sb("KREV0", 1024, BF)
    sb("KREV1", 1024, BF)
    sb("VB0", 1024, BF)
    sb("VB1", 1024, BF)
    sb("QNT", 1024, BF)
    sb("QET0", 1024, BF)
    sb("QET1", 1024, BF)
    sb("KNT", 1024, BF)
    sb("GAM", 1024, F32)
    sb("MM0", 1024, BF)
    sb("MM1", 1024, BF)
    sb("AQK", 1024, BF)
    sb("AQKT0", 1024, BF)
    sb("AQKT1", 1024, BF)
    sb("Z", 1024, BF)
    sb("TT", 1024, BF)
    sb("RP", 1024, BF)
    sb("NWT", 1024, BF)
    sb("VNEW", 1024, BF)
    sb("GG20", 1024, BF)
    sb("GG21", 1024, BF)
    sb("SG", 1028, F32)
    sb("SGB", 1040, BF)
    PS = es.enter_context(nc.psum_tensor("ps", [128, 4096], F32))
    for nm_ in ("ACC", "TMPC", "X1"):
        T[nm_ + "_bf"] = T[nm_].bitcast(BF)
    PSB = PS.bitcast(BF)

    def A(name, off=0, *dims, p=128):
        if name == "GS":
            name = "JUNK"
        t = T[name]
        return bass.AP(t, off, [[t.shape[1], p]] + [list(d) for d in dims])

    def PA(b, off=0, *dims, p=128):
        return bass.AP(PS, b * 512 + off, [[4096, p]] + [list(d) for d in dims])

    def PB(b, off=0, *dims, p=128):
        return bass.AP(PSB, b * 1024 + off, [[8192, p]] + [list(d) for d in dims])

    def CA(off, *dims, p=128):
        return A("CONST", off, *dims, p=p)

    bank = [0]

    bset = [list(range(8))]
    bctr = {}

    def nb():
        key = tuple(bset[0])
        c = bctr.get(key, 0)
        bctr[key] = c + 1
        return bset[0][c % len(bset[0])]

    def nel(ap):
        n = 1
        for st_, cn in list(ap.ap)[1:]:
            n *= cn
        return n

    def bk(b):
        return ("ps", b)

    dq = [0]

    def dma(out, in_, reads, writes, q=None, slow=False):
        if q is None:
            q = "sp"
        if slow:
            fn = lambda e: e.dma_start(out=out, in_=in_, allow_slow_non_contiguous=True)
        else:
            fn = lambda e: e.dma_start(out=out, in_=in_)
        return P.op(q, fn, reads, writes, dma=True)

    def mmg(out, pairs, reads, b, f32=False):
        n = len(pairs)
        for i, (l, r) in enumerate(pairs):
            d = max(60.0, nel(r) * 0.45) * (4.0 if f32 else 1.0)
            P.op("pe", lambda e, l=l, r=r, i=i: e.matmul(out, l, r, start=(i == 0), stop=(i == n - 1)),
                 reads, [bk(b)], dur=d)

    def tr(out, in_, ident, reads, b):
        P.op("pe", lambda e: e.transpose(out, in_, ident), reads, [bk(b)], dur=60.0)

    def act(out, in_, func, reads, writes, scale=None, bias=None):
        kw = {}
        if scale is not None:
            kw["scale"] = scale
        if bias is not None:
            kw["bias"] = bias
        tbl = "silu" if func in (AF.Silu, AF.Tanh) else ("exp" if func in (AF.Exp, AF.Ln) else None)
        P.op("act", lambda e: e.activation(out, in_, func, **kw), reads, writes, dur=250.0 + 0.85 * nel(out), tbl=tbl)

    def edur(eng, out):
        return (150.0 + 2.1 * nel(out)) if eng == "pool" else (160.0 + 1.02 * nel(out))

    def tt(eng, out, in0, in1, op, reads, writes):
        P.op(eng, lambda e: e.tensor_tensor(out, in0, in1, op), reads, writes, dur=edur(eng, out))

    def ts(eng, out, in0, s1, s2, op0, op1, reads, writes):
        if op1 is None:
            P.op(eng, lambda e: e.tensor_scalar(out, in0, s1, None, op0), reads, writes, dur=edur(eng, out))
        else:
            P.op(eng, lambda e: e.tensor_scalar(out, in0, s1, s2, op0, op1), reads, writes, dur=edur(eng, out))

    def stt(out, in0, scalar, in1, op0, op1, reads, writes):
        P.op("dve", lambda e: e.scalar_tensor_tensor(out, in0, scalar, in1, op0, op1), reads, writes,
             dur=edur("dve", out))

    def cp(eng, out, in_, reads, writes):
        if eng == "act":
            act(out, in_, AF.Copy, reads, writes)
        else:
            P.op(eng, lambda e: e.tensor_copy(out, in_), reads, writes, dur=edur(eng, out))

    def mset(eng, ap, val, writes):
        P.op(eng, lambda e: e.memset(ap, val), [], writes)

    def rsq(out, in_, mul, reads, w):
        act(out, in_, AF.Ln, reads, [w], scale=float(mul), bias=EPS)
        act(out, out, AF.Exp, [w], [w], scale=-0.5)

    IDB = lambda n=128: A("IDB", 0, [1, n], p=n)

    dma(A("CONST", 0, [1, 640]), cst_d.ap()[:, 0:640], [], ["CONST"])
    dma(A("JUNK", 0, [1, 896]), cst_d.ap()[:, 640:NCONST], [], ["JUNK"])
    cp("dve", A("IDB", 0, [1, 128]), CA(C_ID, [1, 128]), ["CONST"], ["IDB"])
    cp("dve", A("LEVB", 0, [1, 896]), A("JUNK", 0, [1, 896]), ["JUNK"], ["LEVB"])
    bc = lambda d, n: bass.AP(d, 0, [[0, 128], [1, n]])
    dma(A("GAINM", 0, [1, 1024]), bc(mon_d, 1024), [], ["GAINM"])
    dma(A("GAING", 0, [1, 128]), bc(gon_d, 128), [], ["GAING"])
    ts("pool", A("GAINM", 0, [1, 1024]), A("GAINM", 0, [1, 1024]), 0.5, None, ALU.mult, None, ["GAINM"], ["GAINM"])
    dma(A("BIASM", 0, [1, 4]), bc(mib_d, 4), [], ["BIASM"])
    dma(A("BIASM", 4, [1, 4]), bc(mfb_d, 4), [], ["BIASM"])
    dma(A("DTB", 0, [1, 8]), bc(gdt_d, 8), [], ["DTB"])
    dma(A("NEGA", 0, [1, 8]), bc(gal_d, 8), [], ["NEGA"])
    act(A("NEGA", 0, [1, 8]), A("NEGA", 0, [1, 8]), AF.Exp, ["NEGA"], ["NEGA"])
    ts("dve", A("NEGA", 0, [1, 8]), A("NEGA", 0, [1, 8]), -1.0, None, ALU.mult, None, ["NEGA"], ["NEGA"])
    dma(A("AN8", 0, [1, 128], p=8), an_d.ap(), [], ["AN8"])
    dma(A("CWROW", 0, [1, 128], p=96), gcv_d.ap(), [], ["CWROW"])
    b = nb()
    tr(PA(b, 0, [1, 8]), A("AN8", 0, [1, 128], p=8), CA(C_ID, [1, 8], p=8), ["AN8", "CONST"], b)
    cp("dve", A("GW", 0, [1, 8]), PA(b, 0, [1, 8]), [bk(b)], ["GW"])
    b = nb()
    tr(PA(b, 0, [1, 96]), A("CWROW", 0, [1, 128], p=96), CA(C_ID, [1, 96], p=96), ["CWROW", "CONST"], b)
    cp("dve", A("CW", 0, [1, 96]), PA(b, 0, [1, 96]), [bk(b)], ["CW"])

    cvt_i = [0]

    STGS = ["RES", "JUNK", "ACC", "TMPC", "X0", "X1", "GAM"]

    def load_weights(ps_, STGS=STGS, do_in=True, do_out=True):
        c0 = 0 if ps_ == 0 else 4104
        wtot = 4104 if ps_ == 0 else 4112
        chunks = [(j * 1024, 1024) for j in range(4)] + [(4096, wtot - 4096)]
        for kc in (range(8) if do_in else []):
            for (cj, w) in chunks:
                i = cvt_i[0]
                cvt_i[0] += 1
                stg = STGS[i % len(STGS)]
                dma(A(stg, 0, [1, w]), win_d.ap()[kc * 128:(kc + 1) * 128, c0 + cj:c0 + cj + w], [], [stg])
                dst = A("WIN", kc * WC + cj, [1, w])
                if i % 2 == 0:
                    act(dst, A(stg, 0, [1, w]), AF.Copy, [stg, "GW"], ["WIN"], scale=A("GW", kc, [1, 1]))
                else:
                    ts("dve", dst, A(stg, 0, [1, w]), A("GW", kc, [1, 1]), None, ALU.mult, None, [stg, "GW"], ["WIN"])
        if not do_out:
            return
        r0 = ps_ * 1024
        for kc in range(8):
            i = cvt_i[0]
            cvt_i[0] += 1
            stg = STGS[i % len(STGS)]
            dma(A(stg, 0, [1, 1024]), wout_d.ap()[r0 + kc * 128:r0 + (kc + 1) * 128, :], [], [stg])
            cp(("act", "dve")[i % 2], A("WOUT", kc * 1024, [1, 1024]), A(stg, 0, [1, 1024]), [stg], ["WOUT"])

    ctx = {"HT": "HT0", "BETA": "BETA0"}

    def setp(p):
        ctx["BETA"] = "BETA%d" % p

    def HTk(kc):
        return A(ctx["HT"], kc * 128, [1, 128])

    def proj_tm(b, c0, n):
        mmg(PA(b, 0, [1, n]), [(HTk(kc), A("WIN", kc * WC + c0, [1, n])) for kc in range(8)], [ctx["HT"], "WIN"], b)

    store_ops = []
    RUN = lambda g: [None for _ in g]
    hk = lambda nm: [(nm, 0), (nm, 1)]
    H = lambda nm, h: A(nm, h * 128, [1, 128])

    def phase_a(s, t, p, X=None):
        X = X or ("X%d" % p)
        r0 = s * SEQ + t * L
        dma(A(X, 0, [1, 1024]), x_d.ap()[r0:r0 + L, :], [], [X])
        act(A("JUNK", 0, [1, 1024]), A(X, 0, [1, 1024]), AF.Square, [X], ["JUNK"])
        P.op("dve", lambda e: e.tensor_reduce(A("SS", 0, [1, 1]), A("JUNK", 0, [1, 1024]), AX.X, ALU.add),
             ["JUNK"], ["SS"], dur=1200.0)
        rsq(A("RSTD", 0, [1, 1]), A("SS", 0, [1, 1]), 1.0 / D, ["SS"], "RSTD")
        ts("dve", A("AQK", 0, [1, 1024]), A(X, 0, [1, 1024]), A("RSTD", 0, [1, 1]), None, ALU.mult, None,
           [X, "RSTD"], ["AQK"])
        yield
        b = nb()
        for kc in range(8):
            tr(PB(b, kc * 128, [1, 128]), A("AQK", kc * 128, [1, 128]), IDB(), ["AQK", "IDB"], b)
        cp("act", A(ctx["HT"], 0, [1, 1024]), PB(b, 0, [1, 1024]), [bk(b)], [ctx["HT"]])
        yield

    def phase_d(ps_, s, t, p):
        MX = "MIX%d" % p
        r0 = s * SEQ + t * L
        src_d = x_d if ps_ == 0 else part_d
        dma(A("RES", 0, [1, 1024]), src_d.ap()[r0:r0 + L, :], [("pd", r0)] if ps_ == 1 else [], ["RES"])
        b = nb()
        for kc in range(8):
            tr(PB(b, kc * 128, [1, 128]), A(MX, kc * 128, [1, 128]), IDB(), [MX, "IDB"], b)
        cp("act", A("MIXT", 0, [1, 1024]), PB(b, 0, [1, 1024]), [bk(b)], ["MIXT"])
        yield
        for n in range(2):
            b = nb()
            mmg(PA(b, 0, [1, 512]),
                [(A("MIXT", kc * 128, [1, 128]), A("WOUT", kc * 1024 + n * 512, [1, 512])) for kc in range(8)],
                ["MIXT", "WOUT"], b)
            tt("dve", A("RES", n * 512, [1, 512]), PA(b, 0, [1, 512]), A("RES", n * 512, [1, 512]), ALU.add,
               [bk(b), "RES"], ["RES"])
            yield
        if ps_ == 0:
            store_ops.append(dma(part_d.ap()[r0:r0 + L, :], A("RES", 0, [1, 1024]), ["RES"], [("pd", r0)],
                                 q="pool"))
        else:
            act(A("MIXT", 0, [1, 1024]), A("RES", 0, [1, 1024]), AF.Square, ["RES"], ["MIXT"])
            P.op("dve", lambda e: e.tensor_reduce(A("SS2", 0, [1, 1]), A("MIXT", 0, [1, 1024]), AX.X, ALU.add),
                 ["MIXT"], ["SS2"], dur=1200.0)
            rsq(A("RSTD2", 0, [1, 1]), A("SS2", 0, [1, 1]), 1.0 / D, ["SS2"], "RSTD2")
            stt(A("RES", 0, [1, 1024]), A("RES", 0, [1, 1024]), A("RSTD2", 0, [1, 1]), A("GAINM", 0, [1, 1024]),
                ALU.mult, ALU.mult, ["RES", "RSTD2", "GAINM"], ["RES"])
            store_ops.append(dma(out_d.ap()[r0:r0 + L, :], A("RES", 0, [1, 1024]), ["RES"], ["out_dram"], q="pool"))
        yield

    def ml_stage1(s, t, p, first):
        QE = ("QN", "Z")[p]
        K2 = ("QE2", "TT")[p]
        K3 = ("KN", "RP")[p]
        VA = "KBEG%d" % p
        GG = "GG2%d" % p
        EC = "ECRT%d" % p
        setp(p)
        b = nb()
        mmg(PA(b, 0, [1, 8]), [(HTk(kc), A("WIN", kc * WC + 4096, [1, 8])) for kc in range(8)], [ctx["HT"], "WIN"], b)
        tt("dve", A("G8", 0, [1, 8]), PA(b, 0, [1, 8]), A("BIASM", 0, [1, 8]), ALU.add, [bk(b), "BIASM"], ["G8"])
        act(A("E1", 0, [1, 4]), A("G8", 4, [1, 4]), AF.Exp, ["G8"], ["E1"], scale=-1.0)
        act(A("LFP", 0, [1, 4]), A("E1", 0, [1, 4]), AF.Ln, ["E1"], ["LFP"], bias=1.0)
        ts("dve", A("LA", 0, [1, 4]), A("LFP", 0, [1, 4]), -1.0, None, ALU.mult, None, ["LFP"], ["LA"])
        yield
        b = nb()
        for i, cm in enumerate([C_MLE, C_MGT, C_ONE]):
            mmg(PA(b, i * 4, [1, 4]), [(CA(cm, [1, 128]), A("LA", 0, [1, 4]))], ["CONST", "LA"], b, f32=True)
        cp("dve", A("CRT", 0, [1, 12]), PA(b, 0, [1, 12]), [bk(b)], ["CRT"])
        act(A(EC, 0, [1, 12]), A("CRT", 0, [1, 12]), AF.Exp, ["CRT"], [EC])
        tt("dve", A("TA", 0, [1, 4]), A("G8", 0, [1, 4]), A("CRT", 0, [1, 4]), ALU.subtract, ["G8", "CRT"], ["TA"])
        tt("dve", A("TA", 4, [1, 4]), A("G8", 0, [1, 4]), A("CRT", 4, [1, 4]), ALU.add, ["G8", "CRT"], ["TA"])
        act(A("EIC", 0, [1, 8]), A("TA", 0, [1, 8]), AF.Exp, ["TA"], ["EIC"])
        ts("dve", A("QSC", 0, [1, 4]), A(EC, 0, [1, 4]), 128.0 ** -0.5, None, ALU.mult, None, [EC], ["QSC"])
        yield

    def ml_qk(s, t, p, first):
        QE = ("QN", "Z")[p]
        K2 = ("QE2", "TT")[p]
        K3 = ("KN", "RP")[p]
        VA = "KBEG%d" % p
        GG = "GG2%d" % p
        EC = "ECRT%d" % p
        setp(p)
        b = nb()
        proj_tm(b, 0, 512)
        tt("dve", A(QE, 0, [128, 4], [1, 128]), PA(b, 0, [128, 4], [1, 128]), A("QSC", 0, [1, 4], [0, 128]),
           ALU.mult, [bk(b), "QSC"], hk(QE))
        yield
        b = nb()
        proj_tm(b, 512, 512)
        tt("dve", A(K2, 0, [128, 4], [1, 128]), PA(b, 0, [128, 4], [1, 128]), A("EIC", 0, [1, 4], [0, 128]),
           ALU.mult, [bk(b), "EIC"], hk(K2))
        tt("dve", A(K3, 0, [128, 4], [1, 128]), PA(b, 0, [128, 4], [1, 128]), A("EIC", 4, [1, 4], [0, 128]),
           ALU.mult, [bk(b), "EIC"], hk(K3))
        yield

    def ml_voz(s, t, p, first):
        QE = ("QN", "Z")[p]
        K2 = ("QE2", "TT")[p]
        K3 = ("KN", "RP")[p]
        VA = "KBEG%d" % p
        GG = "GG2%d" % p
        EC = "ECRT%d" % p
        setp(p)
        for n in range(2):
            b = nb()
            proj_tm(b, 1024 + n * 512, 512)
            cp("act", A(VA, n * 514, [257, 2], [1, 256]), PA(b, 0, [256, 2], [1, 256]), [bk(b)], [VA])
            yield
        for n in range(2):
            b = nb()
            proj_tm(b, 3072 + n * 512, 512)
            act(A("GS", n * 512, [1, 512]), PA(b, 0, [1, 512]), AF.Silu, [bk(b)], ["GS"])
            yield
        tt("pool", A("GS", 0, [1, 1024]), A("GS", 0, [1, 1024]), A("GAINM", 0, [1, 1024]), ALU.mult,
           ["GS", "GAINM"], ["GS"])
        for n in range(2):
            b = nb()
            proj_tm(b, 2048 + n * 512, 512)
            act(A(GG, n * 512, [1, 512]), PA(b, 0, [1, 512]), AF.Tanh, [bk(b)], [GG], scale=0.5)
            yield
        stt(A(GG, 0, [1, 1024]), A(GG, 0, [1, 1024]), 1.0, A("GS", 0, [1, 1024]), ALU.add, ALU.mult,
            [GG, "GS"], [GG])
        yield

    def ml_stage2(s, t, p, first):
        QE = ("QN", "Z")[p]
        K2 = ("QE2", "TT")[p]
        K3 = ("KN", "RP")[p]
        VA = "KBEG%d" % p
        GG = "GG2%d" % p
        EC = "ECRT%d" % p
        if first:
            mset("pool", A("SG", 0, [1, 1028]), 0.0, ["SG"])
            mset("pool", A("SGB", 0, [1, 1040]), 0.0, ["SGB"])
        b = 0
        for h in range(4):
            tr(PB(b, h * 128, [1, 128]), A(QE, h * 128, [1, 128]), IDB(), hk(QE) + ["IDB"], b)
        for h in range(4):
            tr(PB(b, 512 + h * 128, [1, 128]), A(K2, h * 128, [1, 128]), IDB(), hk(K2) + ["IDB"], b)
        cp("act", A("QNT", 0, [1, 1024]), PB(b, 0, [1, 1024]), [bk(b)], ["QNT"])
        yield
        b = 1
        for h in range(4):
            mmg(PA(b, h * 128, [1, 128]), [(A("QNT", 512 + h * 128, [1, 128]), A("QNT", h * 128, [1, 128]))],
                ["QNT"], b)
        tt("dve", A("QET0", 0, [128, 4], [1, 128]), PA(b, 0, [128, 4], [1, 128]), CA(C_MLE, [0, 4], [1, 128]),
           ALU.mult, [bk(b), "CONST"], ["QET0"])
        yield
        for h in range(4):
            mmg(PA(h, 0, [1, 257]),
                [(A("QNT", h * 128, [1, 128]), A("SGB", h * 257, [1, 257])),
                 (A("QET0", h * 128, [1, 128]), A(VA, h * 257, [1, 257]))], ["QNT", "SGB", "QET0", VA], h)
        allb = [bk(h) for h in range(4)]
        rr4 = A("RR", 0, [1, 4])
        act(rr4, PA(0, 256, [512, 4]), AF.Abs, allb, ["RR"])
        ts("dve", rr4, rr4, 1.0, None, ALU.max, None, ["RR"], ["RR"])
        P.op("dve", lambda e, rr4=rr4: e.reciprocal(rr4, rr4), ["RR"], ["RR"])
        yield
        for h in range(4):
            act(A("GAM", h * 256, [1, 256]), PA(h, 0, [1, 256]), AF.Square, [bk(h), "RR"], ["GAM"],
                scale=A("RR", h, [1, 1]))
        P.op("dve", lambda e: e.tensor_reduce(A("SSH", 0, [1, 4]), A("GAM", 0, [256, 4], [1, 256]), AX.X, ALU.add),
             ["GAM"], ["SSH"], dur=1200.0)
        rsq(A("RS", 0, [1, 4]), A("SSH", 0, [1, 4]), 1.0 / 256, ["SSH"], "RS")
        tt("dve", A("RS", 0, [1, 4]), A("RS", 0, [1, 4]), rr4, ALU.mult, ["RS", "RR"], ["RS"])
        yield
        for h in range(4):
            stt(A("MIX%d" % p, h * 256, [1, 256]), PA(h, 0, [1, 256]), A("RS", h, [1, 1]), A(GG, h * 256, [1, 256]),
                ALU.mult, ALU.mult, [bk(h), "RS", GG], ["MIX%d" % p])
        yield
        for h in range(4):
            mmg(PA(h, 0, [1, 257]), [(A(K3, h * 128, [1, 128]), A(VA, h * 257, [1, 257]))], hk(K3) + [VA], h)
            stt(A("SG", h * 257, [1, 257]), A("SG", h * 257, [1, 257]), A(EC, 8 + h, [1, 1]),
                PA(h, 0, [1, 257]), ALU.mult, ALU.add, ["SG", EC, bk(h)], ["SG"])
        cp("act", A("SGB", 0, [1, 1028]), A("SG", 0, [1, 1028]), ["SG"], ["SGB"])
        yield

    def dwslot(j, g):
        i = j * 3 + g
        if i < 6:
            nm = ("ACC", "TMPC", "X1")[i // 2]
            return nm, (i % 2) * 1024, T[nm + "_bf"], 2048
        return "DWA", (i - 6) * 1024, T["DWA"], 6 * 1024

    def dwap(j, g, c):
        nm, off, th, rs = dwslot(j, g)
        return nm, bass.AP(th, off + c * 128, [[rs, 128], [1, 128]])

    def build_dw():
        i = 0
        for j in range(4):
            for g in range(3):
                for c in range(8):
                    nm, ap = dwap(j, g, c)
                    ts(("dve", "pool")[i % 2], ap, IDB(), A("CW", j * 24 + g * 8 + c, [1, 1]), None, ALU.mult, None,
                       ["IDB", "CW"], [nm])
                    i += 1

    def gdn_group(g, p, first):
        KB, KR, VBn, QE_, MMn = "KBEG%d" % p, "KREV%d" % p, "VB%d" % p, "QET%d" % p, "MM%d" % p
        EC = "ECRT%d" % p
        XB = ("XB1", "XB2", "XB0")[g]
        S = ("QN", "KN", VBn)[g]
        SK = hk(S) if g < 2 else [S]
        HL = "HALO%d" % g
        if first:
            mset("pool", A("HALO", g * 24, [1, 24]), 0.0, [HL])
        cp("pool", A(XB, 0, [131, 8], [1, 3]), A("HALO", g * 24, [3, 8], [1, 3]), [HL], [XB])
        dwn = sorted(set(dwslot(j, g)[0] for j in range(4)))
        for n in range(2):
            b = nb()
            for c in range(n * 4, n * 4 + 4):
                mmg(PA(b, (c % 4) * 128, [1, 128]),
                    [(A("WIN", kc * WC + g * 1024 + c * 128, [1, 128]), HTk(kc)) for kc in range(8)],
                    ["WIN", ctx["HT"]], b)
            cp("act", A(XB, n * 4 * 131 + 3, [131, 4], [1, 128]), PA(b, 0, [128, 4], [1, 128]), [bk(b)], [XB])
            yield
            if n == 1:
                cp("pool", A("HALO", g * 24, [3, 8], [1, 3]), A(XB, 128, [131, 8], [1, 3]), [XB], [HL])
            b2 = nb()
            for c in range(n * 4, n * 4 + 4):
                mmg(PA(b2, (c % 4) * 128, [1, 128]),
                    [(A(XB, c * 131 + j, [1, 128]), dwap(j, g, c)[1]) for j in range(4)], [XB] + dwn, b2)
            act(A(S, n * 512, [1, 512]), PA(b2, 0, [1, 512]), AF.Silu, [bk(b2)], [SK[n]] if g < 2 else [S])
            yield
        sall = A(S, 0, [128, 8], [1, 128])
        if g < 2:
            SQ, RQ = ("SSQ", "RQ") if g == 0 else ("SSQK", "RQK")
            for n in range(2):
                b = nb()
                act(PA(b, 0, [1, 512]), A(S, n * 512, [1, 512]), AF.Square, [SK[n]], [bk(b)])
                P.op("dve", lambda e, b=b, n=n: e.tensor_reduce(A(SQ, n * 4, [1, 4]), PA(b, 0, [128, 4], [1, 128]),
                                                                AX.X, ALU.add), [bk(b)], [SQ], dur=700.0)
            rsq(A(RQ, 0, [1, 8]), A(SQ, 0, [1, 8]), 1.0, [SQ], RQ)
        if g == 0:
            ts("dve", A("QS1", 0, [1, 8]), A("RQ", 0, [1, 8]), 128.0 ** -0.5, None, ALU.mult, None, ["RQ"], ["QS1"])
            tt("dve", A("QS2", 0, [1, 8]), A("QS1", 0, [1, 8]), A(EC, 0, [1, 8]), ALU.mult, ["QS1", EC], ["QS2"])
            tt("dve", A("QE2", 0, [128, 8], [1, 128]), sall, A("QS2", 0, [1, 8], [0, 128]), ALU.mult,
               SK + ["QS2"], hk("QE2"))
            tt("dve", sall, sall, A("QS1", 0, [1, 8], [0, 128]), ALU.mult, SK + ["QS1"], SK)
            yield
            for src, dst in (("QN", "QNT"), ("QE2", QE_)):
                b2 = nb()
                for c in range(8):
                    tr(PB(b2, c * 128, [1, 128]), A(src, c * 128, [1, 128]), IDB(), hk(src) + ["IDB"], b2)
                cp("act", A(dst, 0, [1, 1024]), PB(b2, 0, [1, 1024]), [bk(b2)], [dst])
                yield
        elif g == 1:
            tt("dve", A("KSB", 0, [1, 8]), A("RQK", 0, [1, 8]), A("BE", 0, [1, 8]), ALU.mult, ["RQK", "BE"], ["KSB"])
            tt("dve", A("KSR", 0, [1, 8]), A("RQK", 0, [1, 8]), A(EC, 8, [1, 8]), ALU.mult, ["RQK", EC], ["KSR"])
            tt("dve", A(KB, 0, [128, 8], [1, 128]), sall, A("KSB", 0, [1, 8], [0, 128]), ALU.mult,
               SK + ["KSB"], [KB])
            tt("dve", A(KR, 0, [128, 8], [1, 128]), sall, A("KSR", 0, [1, 8], [0, 128]), ALU.mult,
               SK + ["KSR"], [KR])
            tt("dve", sall, sall, A("RQK", 0, [1, 8], [0, 128]), ALU.mult, SK + ["RQK"], SK)
            yield
            b2 = nb()
            for c in range(8):
                tr(PB(b2, c * 128, [1, 128]), A("KN", c * 128, [1, 128]), IDB(), hk("KN") + ["IDB"], b2)
            cp("act", A("KNT", 0, [1, 1024]), PB(b2, 0, [1, 1024]), [bk(b2)], ["KNT"])
            yield
        else:
            tt("dve", sall, sall, A(ctx["BETA"], 0, [1, 8], [0, 128]), ALU.mult, [S, ctx["BETA"]], [S])
            yield

    def per_head(pairs_fn, reads, evac):
        for n in range(2):
            b = nb()
            for hh in range(4):
                h = n * 4 + hh
                mmg(PA(b, hh * 128, [1, 128]), pairs_fn(h), reads(n) if callable(reads) else reads, b)
            evac(n, b)
            yield

    def gdn_a(s, t, p, first):
        KB, KR, VBn, QE_, MMn, AQ = "KBEG%d" % p, "KREV%d" % p, "VB%d" % p, "QET%d" % p, "MM%d" % p, "AQKT%d" % p
        EC = "ECRT%d" % p
        GG = "GG2%d" % p
        setp(p)
        b = nb()
        mmg(PA(b, 0, [1, 16]), [(HTk(kc), A("WIN", kc * WC + 4096, [1, 16])) for kc in range(8)], [ctx["HT"], "WIN"], b)
        act(A(ctx["BETA"], 0, [1, 8]), PA(b, 0, [1, 8]), AF.Exp, [bk(b)], [ctx["BETA"]], scale=-1.0)
        ts("dve", A(ctx["BETA"], 0, [1, 8]), A(ctx["BETA"], 0, [1, 8]), 1.0, None, ALU.add, None, [ctx["BETA"]], [ctx["BETA"]])
        bt_ = A(ctx["BETA"], 0, [1, 8])
        P.op("dve", lambda e, bt_=bt_: e.reciprocal(bt_, bt_), [ctx["BETA"]], [ctx["BETA"]])
        tt("dve", A("TA", 0, [1, 8]), PA(b, 8, [1, 8]), A("DTB", 0, [1, 8]), ALU.add, [bk(b), "DTB"], ["TA"])
        act(A("E1", 0, [1, 8]), A("TA", 0, [1, 8]), AF.Exp, ["TA"], ["E1"])
        act(A("LFP", 0, [1, 8]), A("E1", 0, [1, 8]), AF.Ln, ["E1"], ["LFP"], bias=1.0)
        tt("dve", A("LA", 0, [1, 8]), A("LFP", 0, [1, 8]), A("NEGA", 0, [1, 8]), ALU.mult, ["LFP", "NEGA"], ["LA"])
        yield
        b = nb()
        for i, cm in enumerate([C_MLE, C_MGT, C_ONE]):
            mmg(PA(b, i * 8, [1, 8]), [(CA(cm, [1, 128]), A("LA", 0, [1, 8]))], ["CONST", "LA"], b, f32=True)
        cp("dve", A("CRT", 0, [1, 24]), PA(b, 0, [1, 24]), [bk(b)], ["CRT"])
        act(A(EC, 0, [1, 24]), A("CRT", 0, [1, 24]), AF.Exp, ["CRT"], [EC])
        tt("dve", A("BE", 0, [1, 8]), A(ctx["BETA"], 0, [1, 8]), A(EC, 0, [1, 8]), ALU.mult, [ctx["BETA"], EC], ["BE"])
        tt("dve", A("JUNK", 0, [128, 8], [1, 128]), CA(C_MGT, [0, 8], [1, 128]), A("LA", 0, [1, 8], [0, 128]),
           ALU.mult, ["CONST", "LA"], ["JUNK"])
        yield
        for n in range(2):
            b = nb()
            mmg(PA(b, 0, [1, 512]), [(CA(C_MLE, [1, 128]), A("JUNK", n * 512, [1, 512]))], ["CONST", "JUNK"], b, f32=True)
            act(A("GAM", n * 512, [1, 512]), PA(b, 0, [1, 512]), AF.Exp, [bk(b)], ["GAM"])
            yield
        tt("pool", A("GS", 0, [128, 8], [1, 128]), A("GAM", 0, [128, 8], [1, 128]), CA(C_MGT, [0, 8], [1, 128]),
           ALU.mult, ["GAM", "CONST"], ["GS"])
        tt("pool", A("GS", 0, [128, 8], [1, 128]), A("GS", 0, [128, 8], [1, 128]), A(ctx["BETA"], 0, [1, 8], [0, 128]),
           ALU.mult, ["GS", ctx["BETA"]], ["GS"])
        tt("pool", A("GAM", 0, [128, 8], [1, 128]), A("GAM", 0, [128, 8], [1, 128]), CA(C_MGE, [0, 8], [1, 128]),
           ALU.mult, ["GAM", "CONST"], ["GAM"])
        yield

    def gdn_k(s, t, p, first):
        KB, KR, VBn, QE_, MMn, AQ = "KBEG%d" % p, "KREV%d" % p, "VB%d" % p, "QET%d" % p, "MM%d" % p, "AQKT%d" % p
        EC = "ECRT%d" % p
        GG = "GG2%d" % p
        setp(p)
        yield from gdn_group(1, p, first)
        yield from per_head(lambda h: [(H("KNT", h), H("KNT", h))], ["KNT"],
                            lambda n, b: tt("dve", A(MMn, n * 512, [1, 512]), PA(b, 0, [1, 512]),
                                            A("GS", n * 512, [1, 512]), ALU.mult, [bk(b), "GS"], [(MMn, n)]))
        for n in range(2):
            tt("pool", A(MMn, n * 512, [128, 4], [1, 128]), A(MMn, n * 512, [128, 4], [1, 128]),
               A("IDB", 0, [0, 4], [1, 128]), ALU.add, [(MMn, n), "IDB"], [(MMn, n)])
        yield

    def gdn_q(s, t, p, first):
        KB, KR, VBn, QE_, MMn, AQ = "KBEG%d" % p, "KREV%d" % p, "VB%d" % p, "QET%d" % p, "MM%d" % p, "AQKT%d" % p
        EC = "ECRT%d" % p
        GG = "GG2%d" % p
        setp(p)
        for n in range(2):
            b = nb()
            proj_tm(b, 3072 + n * 512, 512)
            act(A(GG, n * 512, [1, 512]), PA(b, 0, [1, 512]), AF.Silu, [bk(b)], [GG])
            yield
        tt("pool", A(GG, 0, [128, 8], [1, 128]), A(GG, 0, [128, 8], [1, 128]), A("GAING", 0, [0, 8], [1, 128]),
           ALU.mult, [GG, "GAING"], [GG])
        yield from gdn_group(0, p, first)
        yield from per_head(lambda h: [(H("QNT", h), H("KNT", h))], ["QNT", "KNT"],
                            lambda n, b: tt("dve", A("AQK", n * 512, [1, 512]), PA(b, 0, [1, 512]),
                                            A("GAM", n * 512, [1, 512]), ALU.mult, [bk(b), "GAM"], ["AQK"]))
        b = nb()
        for c in range(8):
            tr(PB(b, c * 128, [1, 128]), A("AQK", c * 128, [1, 128]), IDB(), ["AQK", "IDB"], b)
        cp("act", A(AQ, 0, [1, 1024]), PB(b, 0, [1, 1024]), [bk(b)], [AQ])
        yield

    def gdn_v(s, t, p, first):
        setp(p)
        yield from gdn_group(2, p, first)

    def gdn_stage2(s, t, p, first):
        KB, KR, VBn, QE_, MMn, AQ = "KBEG%d" % p, "KREV%d" % p, "VB%d" % p, "QET%d" % p, "MM%d" % p, "AQKT%d" % p
        EC = "ECRT%d" % p
        GG = "GG2%d" % p
        r0 = s * SEQ + t * L
        if first:
            mset("pool", A("SG", 0, [1, 1028]), 0.0, ["SG"])
            mset("pool", A("SGB", 0, [1, 1040]), 0.0, ["SGB"])
        for n in range(2):
            b = nb()
            for hh in range(4):
                mmg(PA(b, hh * 128, [1, 128]), [(H(MMn, n * 4 + hh), IDB())], [(MMn, n), "IDB"], b)
            tt("dve", A("Z", n * 512, [128, 4], [1, 128]), PA(b, 0, [128, 4], [1, 128]),
               A("LEVB", 0, [0, 4], [1, 128]), ALU.mult, [bk(b), "LEVB"], [("Z", n)])
            tt("pool", A("TT", n * 512, [128, 4], [1, 128]), A(MMn, n * 512, [128, 4], [1, 128]),
               A("LEVB", 0, [0, 4], [1, 128]), ALU.mult, [(MMn, n), "LEVB"], [("TT", n)])
            yield
        for k in range(1, 7):
            for n in range(2):
                b = nb()
                for hh in range(4):
                    h = n * 4 + hh
                    mmg(PA(b, hh * 128, [1, 128]), [(H(MMn, h), H("Z", h))], [(MMn, n), ("Z", n)], b)
                tt("dve", A("RP", n * 512, [128, 4], [1, 128]), PA(b, 0, [128, 4], [1, 128]),
                   A("LEVB", k * 128, [0, 4], [1, 128]), ALU.mult, [bk(b), "LEVB"], [("RP", n)])
                yield
            for n in range(2):
                zb_ = nb()
                for hh in range(4):
                    h = n * 4 + hh
                    mmg(PA(zb_, hh * 128, [1, 128]), [(H("TT", h), H("RP", h))], [("TT", n), ("RP", n)], zb_)
                tb_ = None
                if k < 6:
                    tb_ = nb()
                    for hh in range(4):
                        h = n * 4 + hh
                        mmg(PA(tb_, hh * 128, [1, 128]), [(H("RP", h), H("TT", h))], [("TT", n), ("RP", n)], tb_)
                cp(ZENG[n], A("Z", n * 512, [1, 512]), PA(zb_, 0, [1, 512]), [bk(zb_)], [("Z", n)])
                if k < 6:
                    cp(TTENG[n], A("TT", n * 512, [1, 512]), PA(tb_, 0, [1, 512]),
                       [bk(tb_)], [("TT", n)])
                yield
        yield from per_head(lambda h: [(H(KB, h), H("Z", h))], lambda n: [KB, ("Z", n)],
                            lambda n, b: act(A("NWT", n * 512, [1, 512]), PA(b, 0, [1, 512]), AF.Copy, [bk(b)],
                                             [("NWT", n)], scale=-1.0))
        yield from per_head(lambda h: [(H("Z", h), H(VBn, h)), (H("NWT", h), H("SGB", h))],
                            lambda n: [("Z", n), VBn, ("NWT", n), "SGB"],
                            lambda n, b: cp("act", A("VNEW", n * 512, [1, 512]), PA(b, 0, [1, 512]), [bk(b)],
                                            [("VNEW", n)]))
        ob = []

        def o_evac(n, b):
            ob.append(b)
            act(A("RP", n * 512, [1, 512]), PA(b, 0, [1, 512]), AF.Square, [bk(b)], [("RP", n)])
        yield from per_head(lambda h: [(H(QE_, h), H("SGB", h)), (H(AQ, h), H("VNEW", h))],
                            lambda n: [QE_, "SGB", AQ, ("VNEW", n)], o_evac)
        P.op("dve", lambda e: e.tensor_reduce(A("SSO", 0, [1, 8]), A("RP", 0, [128, 8], [1, 128]), AX.X, ALU.add),
             hk("RP"), ["SSO"], dur=1200.0)
        rsq(A("RSO", 0, [1, 8]), A("SSO", 0, [1, 8]), 1.0 / 128, ["SSO"], "RSO")
        tt("dve", A(GG, 0, [128, 8], [1, 128]), A(GG, 0, [128, 8], [1, 128]), A("RSO", 0, [1, 8], [0, 128]),
           ALU.mult, [GG, "RSO"], [GG])
        for n in range(2):
            tt("dve", A("MIX%d" % p, n * 512, [1, 512]), PA(ob[n], 0, [1, 512]), A(GG, n * 512, [1, 512]), ALU.mult,
               [bk(ob[n]), GG], ["MIX%d" % p])
        yield
        tt("pool", A("SG", 0, [128, 8], [1, 128]), A("SG", 0, [128, 8], [1, 128]), A(EC, 16, [1, 8], [0, 128]),
           ALU.mult, ["SG", EC], ["SG"])
        yield from per_head(lambda h: [(H(KR, h), H("VNEW", h))], lambda n: [KR, ("VNEW", n)],
                            lambda n, b: tt("dve", A("SG", n * 512, [1, 512]), PA(b, 0, [1, 512]),
                                            A("SG", n * 512, [1, 512]), ALU.add, [bk(b), "SG"], ["SG"]))
        cp("act", A("SGB", 0, [1, 1024]), A("SG", 0, [1, 1024]), ["SG"], ["SGB"])
        yield

    def collect(gen, banks):
        if gen is None:
            return []
        P.defer = []
        bset[0] = banks
        for _ in gen:
            pass
        l = P.defer
        P.defer = None
        bset[0] = list(range(8))
        return l

    def collect(gen, banks):
        P.defer = []
        bset[0] = banks
        for _ in gen:
            pass
        l = P.defer
        P.defer = None
        bset[0] = list(range(8))
        return l

    def prep_pass1():
        load_weights(1, ["ACC", "TMPC", "X1"], do_out=False)
        dma(A("GAINM", 0, [1, 1024]), bc(fn_d, 1024), [], ["GAINM"])
        build_dw()
        yield

    for ps_ in range(2):
        if ps_ == 0:
            load_weights(ps_)
        else:
            load_weights(1, ["RES", "JUNK", "X0", "GAM"], do_in=False)
        if ps_ == 0:
            mset("pool", A("KBEG0", 0, [1, 1040]), 1.0, ["KBEG0"])
            mset("pool", A("KBEG1", 0, [1, 1040]), 1.0, ["KBEG1"])
        tiles = [(s, t) for s in range(NSEQ) for t in range(NT_RUN)]
        prev = None
        prev2 = None
        hts = lambda j: "HT%d" % (j % 3)
        ctx["HT"] = hts(0)
        P.merge([collect(phase_a(tiles[0][0], tiles[0][1], 0, X="X0"), [4])])
        for i, cur in enumerate(tiles + [None, None]):
            lists = []
            pd = None
            if prev2 is not None:
                pd = (ps_, prev2[0], prev2[1], (i - 2) % 2)
            if prev is not None:
                pa = (prev[0], prev[1], (i - 1) % 2, prev[1] == 0)
                ctx["HT"] = hts(i - 1)
                if ps_ == 0:
                    lists.append(collect(ml_stage2(*pa), [0, 1, 2, 3]))
                else:
                    lists.append(collect(gdn_v(*pa), [3]))
                    lists.append(collect(gdn_stage2(*pa), [0, 1, 2]))
            if ps_ == 1 and pd is not None:
                lists.append(collect(phase_d(*pd), [3]))
            if cur is not None:
                ca = (cur[0], cur[1], i % 2, cur[1] == 0)
                ctx["HT"] = hts(i)
                if ps_ == 0:
                    lists.append(collect(ml_stage1(*ca), [4]))
                    lists.append(collect(ml_qk(*ca), [5]))
                    lists.append(collect(ml_voz(*ca), [6, 7]))
                else:
                    lists.append(collect(gdn_a(*ca), [4, 5]))
                    lists.append(collect(gdn_k(*ca), [4, 5]))
                    lists.append(collect(gdn_q(*ca), [6, 7]))
            if ps_ == 0 and pd is not None:
                lists.append(collect(phase_d(*pd), [4]))
            if ps_ == 0 and i == len(tiles):
                lists.append(collect(prep_pass1(), [5]))
            if i + 1 < len(tiles):
                nx = tiles[i + 1]
                ctx["HT"] = hts(i + 1)
                lists.insert(0, collect(phase_a(nx[0], nx[1], 0, X="X0"), [5] if ps_ == 0 else [3]))
            P.merge(lists)
            prev2 = prev
            prev = cur
    P.op("pool", None, ["out_dram"], [])
    fin = P.ops[-1]
    for so in store_ops:
        fin["deps"][so[0]] = True


    P.finalize()
    sems = {}
    for e, ng in P.ngen.items():
        for g in range(ng):
            sems[(e, g)] = es.enter_context(nc.semaphore("s_%s_%d" % (e, g)))
    for i in range(NDMA):
        sems[("dma", i)] = es.enter_context(nc.semaphore("s_dma_%d" % i))
    for i in range(NDMAP):
        sems[("dmap", i)] = es.enter_context(nc.semaphore("s_dmap_%d" % i))
    with nc.Block() as block:
        @block.sync
        def _(e):
            P.emit("sp", e, sems)

        @block.tensor
        def _(e):
            P.emit("pe", e, sems)

        @block.scalar
        def _(e):
            P.emit("act", e, sems)

        @block.vector
        def _(e):
            P.emit("dve", e, sems)

        @block.gpsimd
        def _(e):
            P.emit("pool", e, sems)
    es.close()
    return nc


NT_RUN = NT
ALPHA = 0.0
ZENG = ("act", "dve")
TTENG = ("act", "dve")
TBL_NS = 1300.0
TBL_PEN = 1300.0
_CACHE = {}


def kernel(x, attn_norm, w_in, m_i_bias, m_f_bias, m_out_norm, g_conv, g_a_log, g_dt_bias, g_out_norm, w_out,
           final_norm):
    f = lambda a: np.ascontiguousarray(np.asarray(a, dtype=np.float32))
    if "nc" not in _CACHE:
        _CACHE["nc"] = build_program()
    nc = _CACHE["nc"]
    x = f(x)
    shared = {
        "w_in": f(w_in).reshape(D, 8216),
        "w_out": f(w_out).reshape(2048, D),
        "attn_norm": f(attn_norm).reshape(8, 128),
        "m_i_bias": f(m_i_bias).reshape(1, 4),
        "m_f_bias": f(m_f_bias).reshape(1, 4),
        "m_out_norm": f(m_out_norm).reshape(1, 1024),
        "g_conv": f(g_conv).reshape(96, 128),
        "g_a_log": f(g_a_log).reshape(1, 8),
        "g_dt_bias": f(g_dt_bias).reshape(1, 8),
        "g_out_norm": f(g_out_norm).reshape(1, 128),
        "final_norm": f(final_norm).reshape(1, 1024),
        "consts": make_consts(),
    }
    in_maps = []
    for c in range(NCORES):
        m = dict(shared)
        m["x"] = x[c * NSEQ:(c + 1) * NSEQ].reshape(TOK, D)
        in_maps.append(m)
    res = run_bass_kernel_spmd(nc, in_maps, core_ids=list(range(NCORES)))
    outs = [np.asarray(r["out"]).reshape(NSEQ, SEQ, D) for r in res.results]
    return np.concatenate(outs, axis=0).astype(np.float32)
```

```python
import numpy as np
from contextlib import ExitStack
import concourse.bass as bass
import concourse.mybir as mybir
from concourse.bass_utils import run_bass_kernel_spmd

F32 = mybir.dt.float32
BF = mybir.dt.bfloat16
AF = mybir.ActivationFunctionType
ALU = mybir.AluOpType
AX = mybir.AxisListType

NCORES = 8
SEQ = 2048
NSEQ = 2
TOK = NSEQ * SEQ
L = 128
NT = SEQ // L
D = 1024
WC = 4112
EPS = 1e-6
GEN = 3000
NDMA = 24
NDMAP = 8
C_MLE, C_MGT, C_MGE, C_ONE, C_ID, C_LEV = 0, 128, 256, 384, 512, 640
NCONST = 640 + 7 * 128


def make_consts():
    idx = np.arange(L)
    c = np.zeros((L, NCONST), np.float32)
    c[:, C_MLE:C_MLE + L] = idx[:, None] <= idx[None, :]
    c[:, C_MGT:C_MGT + L] = idx[:, None] > idx[None, :]
    c[:, C_MGE:C_MGE + L] = idx[:, None] >= idx[None, :]
    c[:, C_ONE:C_ONE + L] = 1.0
    c[:, C_ID:C_ID + L] = np.eye(L)
    for k in range(7):
        b2 = 2 << k
        m = np.where((idx[:, None] // b2) == (idx[None, :] // b2), -1.0, 0.0)
        m[idx, idx] = 1.0
        c[:, C_LEV + k * L:C_LEV + (k + 1) * L] = m
    return c


class Prog:
    def __init__(self):
        self.ops = []
        self.lastw = {}
        self.readers = {}
        self.defer = None
        self.cur_tbl = None
        self.efree = {}
        self.wdone = {}
        self.rdone = {}

    def _norm(self, keys):
        return ["JUNK" if k == "GS" else k for k in keys]

    def op(self, eng, fn, reads=(), writes=(), dma=False, dur=300.0, tbl=None):
        reads = self._norm(reads)
        writes = self._norm(writes)
        if self.defer is not None:
            h = [None]
            self.defer.append(dict(eng=eng, fn=fn, reads=list(reads), writes=list(writes), dma=dma, dur=dur, h=h,
                                   tbl=tbl))
            return h
        self._sim(eng, reads, writes, dma, dur, tbl)
        return [self._op(eng, fn, reads, writes, dma)]

    def _est(self, eng, reads, writes, tbl=None, pen=None):
        t = self.efree.get(eng, 0.0)
        if tbl is not None and tbl != self.cur_tbl:
            t += TBL_NS if pen is None else pen
        for r in reads:
            t = max(t, self.wdone.get(r, 0.0))
        for w in writes:
            t = max(t, self.wdone.get(w, 0.0), self.rdone.get(w, 0.0))
        return t

    def _sim(self, eng, reads, writes, dma, dur, tbl=None):
        st = self._est(eng, reads, writes, tbl)
        if tbl is not None:
            self.cur_tbl = tbl
        if dma:
            self.efree[eng] = st + 100.0
            end = st + 3000.0
        else:
            end = st + dur
            self.efree[eng] = end
        for r in reads:
            self.rdone[r] = max(self.rdone.get(r, 0.0), end)
        for w in writes:
            self.wdone[w] = end + 150.0
        return st

    def merge(self, lists):
        lists = [l for l in lists if l]
        n = len(lists)
        accs = []
        for l in lists:
            wl, rl = {}, {}
            for i, d in enumerate(l):
                for r in d["reads"]:
                    rl[r] = i
                for w in d["writes"]:
                    wl[w] = i
            accs.append((wl, rl))
        pos = [0] * n
        rems = []
        for l in lists:
            r_ = [0.0] * (len(l) + 1)
            for i in range(len(l) - 1, -1, -1):
                r_[i] = r_[i + 1] + l[i]["dur"]
            rems.append(r_)

        def blocked(j, d):
            for i in range(j):
                pi = pos[i]
                if pi >= len(lists[i]):
                    continue
                wl, rl = accs[i]
                for r in d["reads"]:
                    if wl.get(r, -1) >= pi:
                        return True
                for w in d["writes"]:
                    if wl.get(w, -1) >= pi or rl.get(w, -1) >= pi:
                        return True
            return False

        while True:
            best = None
            for i, l in enumerate(lists):
                if pos[i] < len(l):
                    d = l[pos[i]]
                    if blocked(i, d):
                        continue
                    st = self._est(d["eng"], d["reads"], d["writes"], d.get("tbl"), TBL_PEN) - ALPHA * rems[i][pos[i]]
                    if best is None or st < best[0]:
                        best = (st, i)
            if best is None:
                break
            i = best[1]
            d = lists[i][pos[i]]
            pos[i] += 1
            self._sim(d["eng"], d["reads"], d["writes"], d["dma"], d["dur"], d.get("tbl"))
            d["h"][0] = self._op(d["eng"], d["fn"], d["reads"], d["writes"], d["dma"])
        assert all(pos[i] == len(lists[i]) for i in range(n))

    def _op(self, eng, fn, reads=(), writes=(), dma=False):
        idx = len(self.ops)
        deps = {}
        for r in reads:
            if r in self.lastw:
                deps[self.lastw[r]] = True
        for w in writes:
            if w in self.lastw:
                deps.setdefault(self.lastw[w], False)
            for rd in self.readers.get(w, ()):
                deps.setdefault(rd, False)
        self.ops.append(dict(eng=eng, fn=fn, deps=deps, dma=dma))
        for r in reads:
            self.readers.setdefault(r, []).append(idx)
        for w in writes:
            self.lastw[w] = idx
            self.readers[w] = []
        return idx

    def finalize(self):
        ops = self.ops
        last_dma_on_sem = {}
        ndma = 0
        npool = 0
        for i, o in enumerate(ops):
            nd = {}
            for d, raw in o["deps"].items():
                od = ops[d]
                if od["dma"]:
                    nd[d] = raw
                elif od["eng"] == o["eng"]:
                    if o["eng"] != "pe" and raw and not o["dma"]:
                        nd[d] = raw
                    elif o["dma"]:
                        nd[d] = raw
                else:
                    nd[d] = raw
            if o["dma"]:
                if o["eng"] == "pool":
                    s = ("dmap", npool % NDMAP)
                    npool += 1
                else:
                    s = ("dma", ndma % NDMA)
                    ndma += 1
                if s in last_dma_on_sem:
                    nd[last_dma_on_sem[s]] = False
                last_dma_on_sem[s] = i
                o["dsem"] = s
            o["deps"] = nd
        needed = set()
        for o in ops:
            needed.update(o["deps"].keys())
        cnt = {}
        dcnt = {}
        for i, o in enumerate(ops):
            if o["dma"]:
                dcnt[o["dsem"]] = dcnt.get(o["dsem"], 0) + 16
                o["tok"] = (o["dsem"], dcnt[o["dsem"]])
            elif i in needed:
                c = cnt.get(o["eng"], 0)
                cnt[o["eng"]] = c + 1
                o["tok"] = ((o["eng"], c // GEN), c % GEN + 1)
            else:
                o["tok"] = None
        self.ngen = {e: (c + GEN - 1) // GEN for e, c in cnt.items()}

    def emit(self, eng_name, eng, sems):
        waited = {}
        for o in self.ops:
            if o["eng"] != eng_name:
                continue
            for d in sorted(o["deps"].keys()):
                key, val = self.ops[d]["tok"]
                if waited.get(key, 0) >= val:
                    continue
                waited[key] = val
                eng.wait_ge(sems[key], val)
            if o["fn"] is None:
                continue
            ins = o["fn"](eng)
            if o["tok"] is not None:
                key, _ = o["tok"]
                ins.then_inc(sems[key], 16 if o["dma"] else 1)


def build_program():
    nc = bass.Bass("TRN2", target_bir_lowering=False)
    dt_in = lambda n, s: nc.dram_tensor(n, list(s), F32, kind="ExternalInput")
    x_d = dt_in("x", [TOK, D])
    win_d = dt_in("w_in", [D, 8216])
    wout_d = dt_in("w_out", [2048, D])
    an_d = dt_in("attn_norm", [8, 128])
    mib_d = dt_in("m_i_bias", [1, 4])
    mfb_d = dt_in("m_f_bias", [1, 4])
    mon_d = dt_in("m_out_norm", [1, 1024])
    gcv_d = dt_in("g_conv", [96, 128])
    gal_d = dt_in("g_a_log", [1, 8])
    gdt_d = dt_in("g_dt_bias", [1, 8])
    gon_d = dt_in("g_out_norm", [1, 128])
    fn_d = dt_in("final_norm", [1, 1024])
    cst_d = dt_in("consts", [L, NCONST])
    out_d = nc.dram_tensor("out", [TOK, D], F32, kind="ExternalOutput")
    part_d = nc.dram_tensor("partial", [TOK, D], F32, kind="Internal")

    P = Prog()
    es = ExitStack()
    T = {}

    def sb(name, cols, dt, parts=128):
        T[name] = es.enter_context(nc.sbuf_tensor(name, [parts, cols], dt))
        return T[name]

    sb("WIN", 8 * WC, BF)
    sb("WOUT", 8 * 1024, BF)
    sb("CONST", 640, F32)
    sb("IDB", 128, BF)
    sb("LEVB", 7 * 128, BF)
    sb("GAINM", 1024, F32)
    sb("GAING", 128, F32)
    sb("CW", 96, F32)
    sb("CWROW", 128, F32, parts=96)
    sb("AN8", 128, F32, parts=8)
    sb("GW", 8, F32)
    sb("BIASM", 8, F32)
    sb("DTB", 8, F32)
    sb("NEGA", 8, F32)
    sb("X0", 1024, F32)
    sb("X1", 1024, F32)
    sb("JUNK", 1028, F32)
    sb("HT0", 1024, BF)
    sb("HT1", 1024, BF)
    sb("HT2", 1024, BF)
    sb("MIX0", 1024, BF)
    sb("MIX1", 1024, BF)
    sb("MIXT", 1024, BF)
    sb("RES", 1028, F32)
    for n in ["SS", "RSTD", "SS2", "RSTD2"]:
        sb(n, 1, F32)
    for n in ["G8", "E1", "LFP", "LA", "CRT", "ECRT0", "ECRT1", "TA", "TB", "EIC", "EIR", "QSC", "RR", "SSH", "RS",
              "BETA0", "BETA1", "BE", "SSQ", "RQ", "SSQK", "RQK", "QS1", "QS2", "KSB", "KSR", "SSO", "RSO"]:
        sb(n, 24 if n in ("CRT", "ECRT0", "ECRT1") else 16, F32)
    sb("TMPC", 1024, F32)
    sb("HALO", 24 * 3, BF)
    sb("ACC", 1024, F32)
    sb("XB0", 8 * 131, BF)
    sb("XB1", 8 * 131, BF)
    sb("XB2", 8 * 131, BF)
    sb("DWA", 6 * 1024, BF)
    sb("QN", 1024, BF)
    sb("QE2", 1024, BF)
    sb("KN", 1024, BF)
    sb("KBEG0", 1040, BF)
    sb("KBEG1", 1040, BF)
    sb("KREV0", 1024, BF)
    sb("KREV1", 1024, BF)
    sb("VB0", 1024, BF)
    sb("VB1", 1024, BF)
    sb("QNT", 1024, BF)
    sb("QET0", 1024, BF)
    sb("QET1", 1024, BF)
    sb("KNT", 1024, BF)
    sb("GAM", 1024, F32)
    sb("MM0", 1024, BF)
    sb("MM1", 1024, BF)
    sb("AQK", 1024, BF)
    sb("AQKT0", 1024, BF)
    sb("AQKT1", 1024, BF)
    sb("Z", 1024, BF)
    sb("TT", 1024, BF)
    sb("RP", 1024, BF)
    sb("NWT", 1024, BF)
    sb("VNEW", 1024, BF)
    sb("GG20", 1024, BF)
    sb("GG21", 1024, BF)
    sb("SG", 1028, F32)
    sb("SGB", 1040, BF)
    PS = es.enter_context(nc.psum_tensor("ps", [128, 4096], F32))
    for nm_ in ("ACC", "TMPC", "X1"):
        T[nm_ + "_bf"] = T[nm_].bitcast(BF)
    PSB = PS.bitcast(BF)

    def A(name, off=0, *dims, p=128):
        if name == "GS":
            name = "JUNK"
        t = T[name]
        return bass.AP(t, off, [[t.shape[1], p]] + [list(d) for d in dims])

    def PA(b, off=0, *dims, p=128):
        return bass.AP(PS, b * 512 + off, [[4096, p]] + [list(d) for d in dims])

    def PB(b, off=0, *dims, p=128):
        return bass.AP(PSB, b * 1024 + off, [[8192, p]] + [list(d) for d in dims])

    def CA(off, *dims, p=128):
        return A("CONST", off, *dims, p=p)

    bank = [0]

    bset = [list(range(8))]
    bctr = {}

    def nb():
        key = tuple(bset[0])
        c = bctr.get(key, 0)
        bctr[key] = c + 1
        return bset[0][c % len(bset[0])]

    def nel(ap):
        n = 1
        for st_, cn in list(ap.ap)[1:]:
            n *= cn
        return n

    def bk(b):
        return ("ps", b)

    dq = [0]

    def dma(out, in_, reads, writes, q=None, slow=False):
        if q is None:
            q = "sp"
        if slow:
            fn = lambda e: e.dma_start(out=out, in_=in_, allow_slow_non_contiguous=True)
        else:
            fn = lambda e: e.dma_start(out=out, in_=in_)
        return P.op(q, fn, reads, writes, dma=True)

    def mmg(out, pairs, reads, b, f32=False):
        n = len(pairs)
        for i, (l, r) in enumerate(pairs):
            d = max(60.0, nel(r) * 0.45) * (4.0 if f32 else 1.0)
            P.op("pe", lambda e, l=l, r=r, i=i: e.matmul(out, l, r, start=(i == 0), stop=(i == n - 1)),
                 reads, [bk(b)], dur=d)

    def tr(out, in_, ident, reads, b):
        P.op("pe", lambda e: e.transpose(out, in_, ident), reads, [bk(b)], dur=60.0)

    def act(out, in_, func, reads, writes, scale=None, bias=None):
        kw = {}
        if scale is not None:
            kw["scale"] = scale
        if bias is not None:
            kw["bias"] = bias
        tbl = "silu" if func in (AF.Silu, AF.Tanh) else ("exp" if func in (AF.Exp, AF.Ln) else None)
        P.op("act", lambda e: e.activation(out, in_, func, **kw), reads, writes, dur=250.0 + 0.85 * nel(out), tbl=tbl)

    def edur(eng, out):
        return (150.0 + 2.1 * nel(out)) if eng == "pool" else (160.0 + 1.02 * nel(out))

    def tt(eng, out, in0, in1, op, reads, writes):
        P.op(eng, lambda e: e.tensor_tensor(out, in0, in1, op), reads, writes, dur=edur(eng, out))

    def ts(eng, out, in0, s1, s2, op0, op1, reads, writes):
        if op1 is None:
            P.op(eng, lambda e: e.tensor_scalar(out, in0, s1, None, op0), reads, writes, dur=edur(eng, out))
        else:
            P.op(eng, lambda e: e.tensor_scalar(out, in0, s1, s2, op0, op1), reads, writes, dur=edur(eng, out))

    def stt(out, in0, scalar, in1, op0, op1, reads, writes):
        P.op("dve", lambda e: e.scalar_tensor_tensor(out, in0, scalar, in1, op0, op1), reads, writes,
             dur=edur("dve", out))

    def cp(eng, out, in_, reads, writes):
        if eng == "act":
            act(out, in_, AF.Copy, reads, writes)
        else:
            P.op(eng, lambda e: e.tensor_copy(out, in_), reads, writes, dur=edur(eng, out))

    def mset(eng, ap, val, writes):
        P.op(eng, lambda e: e.memset(ap, val), [], writes)

    def rsq(out, in_, mul, reads, w):
        act(out, in_, AF.Ln, reads, [w], scale=float(mul), bias=EPS)
        act(out, out, AF.Exp, [w], [w], scale=-0.5)

    IDB = lambda n=128: A("IDB", 0, [1, n], p=n)

    dma(A("CONST", 0, [1, 640]), cst_d.ap()[:, 0:640], [], ["CONST"])
    dma(A("JUNK", 0, [1, 896]), cst_d.ap()[:, 640:NCONST], [], ["JUNK"])
    cp("dve", A("IDB", 0, [1, 128]), CA(C_ID, [1, 128]), ["CONST"], ["IDB"])
    cp("dve", A("LEVB", 0, [1, 896]), A("JUNK", 0, [1, 896]), ["JUNK"], ["LEVB"])
    bc = lambda d, n: bass.AP(d, 0, [[0, 128], [1, n]])
    dma(A("GAINM", 0, [1, 1024]), bc(mon_d, 1024), [], ["GAINM"])
    dma(A("GAING", 0, [1, 128]), bc(gon_d, 128), [], ["GAING"])
    ts("pool", A("GAINM", 0, [1, 1024]), A("GAINM", 0, [1, 1024]), 0.5, None, ALU.mult, None, ["GAINM"], ["GAINM"])
    dma(A("BIASM", 0, [1, 4]), bc(mib_d, 4), [], ["BIASM"])
    dma(A("BIASM", 4, [1, 4]), bc(mfb_d, 4), [], ["BIASM"])
    dma(A("DTB", 0, [1, 8]), bc(gdt_d, 8), [], ["DTB"])
    dma(A("NEGA", 0, [1, 8]), bc(gal_d, 8), [], ["NEGA"])
    act(A("NEGA", 0, [1, 8]), A("NEGA", 0, [1, 8]), AF.Exp, ["NEGA"], ["NEGA"])
    ts("dve", A("NEGA", 0, [1, 8]), A("NEGA", 0, [1, 8]), -1.0, None, ALU.mult, None, ["NEGA"], ["NEGA"])
    dma(A("AN8", 0, [1, 128], p=8), an_d.ap(), [], ["AN8"])
    dma(A("CWROW", 0, [1, 128], p=96), gcv_d.ap(), [], ["CWROW"])
    b = nb()
    tr(PA(b, 0, [1, 8]), A("AN8", 0, [1, 128], p=8), CA(C_ID, [1, 8], p=8), ["AN8", "CONST"], b)
    cp("dve", A("GW", 0, [1, 8]), PA(b, 0, [1, 8]), [bk(b)], ["GW"])
    b = nb()
    tr(PA(b, 0, [1, 96]), A("CWROW", 0, [1, 128], p=96), CA(C_ID, [1, 96], p=96), ["CWROW", "CONST"], b)
    cp("dve", A("CW", 0, [1, 96]), PA(b, 0, [1, 96]), [bk(b)], ["CW"])

    cvt_i = [0]

    STGS = ["RES", "JUNK", "ACC", "TMPC", "X0", "X1", "GAM"]

    def load_weights(ps_, STGS=STGS, do_in=True, do_out=True):
        c0 = 0 if ps_ == 0 else 4104
        wtot = 4104 if ps_ == 0 else 4112
        chunks = [(j * 1024, 1024) for j in range(4)] + [(4096, wtot - 4096)]
        for kc in (range(8) if do_in else []):
            for (cj, w) in chunks:
                i = cvt_i[0]
                cvt_i[0] += 1
                stg = STGS[i % len(STGS)]
                dma(A(stg, 0, [1, w]), win_d.ap()[kc * 128:(kc + 1) * 128, c0 + cj:c0 + cj + w], [], [stg])
                dst = A("WIN", kc * WC + cj, [1, w])
                if i % 2 == 0:
                    act(dst, A(stg, 0, [1, w]), AF.Copy, [stg, "GW"], ["WIN"], scale=A("GW", kc, [1, 1]))
                else:
                    ts("dve", dst, A(stg, 0, [1, w]), A("GW", kc, [1, 1]), None, ALU.mult, None, [stg, "GW"], ["WIN"])
        if not do_out:
            return
        r0 = ps_ * 1024
        for kc in range(8):
            i = cvt_i[0]
            cvt_i[0] += 1
            stg = STGS[i % len(STGS)]
            dma(A(stg, 0, [1, 1024]), wout_d.ap()[r0 + kc * 128:r0 + (kc + 1) * 128, :], [], [stg])
            cp(("act", "dve")[i % 2], A("WOUT", kc * 1024, [1, 1024]), A(stg, 0, [1, 1024]), [stg], ["WOUT"])

    ctx = {"HT": "HT0", "BETA": "BETA0"}

    def setp(p):
        ctx["BETA"] = "BETA%d" % p

    def HTk(kc):
        return A(ctx["HT"], kc * 128, [1, 128])

    def proj_tm(b, c0, n):
        mmg(PA(b, 0, [1, n]), [(HTk(kc), A("WIN", kc * WC + c0, [1, n])) for kc in range(8)], [ctx["HT"], "WIN"], b)

    store_ops = []
    RUN = lambda g: [None for _ in g]
    hk = lambda nm: [(nm, 0), (nm, 1)]
    H = lambda nm, h: A(nm, h * 128, [1, 128])

    def phase_a(s, t, p, X=None):
        X = X or ("X%d" % p)
        r0 = s * SEQ + t * L
        dma(A(X, 0, [1, 1024]), x_d.ap()[r0:r0 + L, :], [], [X])
        act(A("JUNK", 0, [1, 1024]), A(X, 0, [1, 1024]), AF.Square, [X], ["JUNK"])
        P.op("dve", lambda e: e.tensor_reduce(A("SS", 0, [1, 1]), A("JUNK", 0, [1, 1024]), AX.X, ALU.add),
             ["JUNK"], ["SS"], dur=1200.0)
        rsq(A("RSTD", 0, [1, 1]), A("SS", 0, [1, 1]), 1.0 / D, ["SS"], "RSTD")
        ts("dve", A("AQK", 0, [1, 1024]), A(X, 0, [1, 1024]), A("RSTD", 0, [1, 1]), None, ALU.mult, None,
           [X, "RSTD"], ["AQK"])
        yield
        b = nb()
        for kc in range(8):
            tr(PB(b, kc * 128, [1, 128]), A("AQK", kc * 128, [1, 128]), IDB(), ["AQK", "IDB"], b)
        cp("dve", A(ctx["HT"], 0, [1, 1024]), PB(b, 0, [1, 1024]), [bk(b)], [ctx["HT"]])
        yield

    def phase_d(ps_, s, t, p):
        MX = "MIX%d" % p
        r0 = s * SEQ + t * L
        src_d = x_d if ps_ == 0 else part_d
        dma(A("RES", 0, [1, 1024]), src_d.ap()[r0:r0 + L, :], [("pd", r0)] if ps_ == 1 else [], ["RES"])
        b = nb()
        for kc in range(8):
            tr(PB(b, kc * 128, [1, 128]), A(MX, kc * 128, [1, 128]), IDB(), [MX, "IDB"], b)
        cp("act", A("MIXT", 0, [1, 1024]), PB(b, 0, [1, 1024]), [bk(b)], ["MIXT"])
        yield
        for n in range(2):
            b = nb()
            mmg(PA(b, 0, [1, 512]),
                [(A("MIXT", kc * 128, [1, 128]), A("WOUT", kc * 1024 + n * 512, [1, 512])) for kc in range(8)],
                ["MIXT", "WOUT"], b)
            tt("dve", A("RES", n * 512, [1, 512]), PA(b, 0, [1, 512]), A("RES", n * 512, [1, 512]), ALU.add,
               [bk(b), "RES"], ["RES"])
            yield
        if ps_ == 0:
            store_ops.append(dma(part_d.ap()[r0:r0 + L, :], A("RES", 0, [1, 1024]), ["RES"], [("pd", r0)],
                                 q="pool"))
        else:
            act(A("MIXT", 0, [1, 1024]), A("RES", 0, [1, 1024]), AF.Square, ["RES"], ["MIXT"])
            P.op("dve", lambda e: e.tensor_reduce(A("SS2", 0, [1, 1]), A("MIXT", 0, [1, 1024]), AX.X, ALU.add),
                 ["MIXT"], ["SS2"], dur=1200.0)
            rsq(A("RSTD2", 0, [1, 1]), A("SS2", 0, [1, 1]), 1.0 / D, ["SS2"], "RSTD2")
            stt(A("RES", 0, [1, 1024]), A("RES", 0, [1, 1024]), A("RSTD2", 0, [1, 1]), A("GAINM", 0, [1, 1024]),
                ALU.mult, ALU.mult, ["RES", "RSTD2", "GAINM"], ["RES"])
            store_ops.append(dma(out_d.ap()[r0:r0 + L, :], A("RES", 0, [1, 1024]), ["RES"], ["out_dram"], q="pool"))
        yield

    def ml_stage1(s, t, p, first):
        QE = ("QN", "Z")[p]
        K2 = ("QE2", "TT")[p]
        K3 = ("KN", "RP")[p]
        VA = "KBEG%d" % p
        GG = "GG2%d" % p
        EC = "ECRT%d" % p
        setp(p)
        b = nb()
        mmg(PA(b, 0, [1, 8]), [(HTk(kc), A("WIN", kc * WC + 4096, [1, 8])) for kc in range(8)], [ctx["HT"], "WIN"], b)
        tt("dve", A("G8", 0, [1, 8]), PA(b, 0, [1, 8]), A("BIASM", 0, [1, 8]), ALU.add, [bk(b), "BIASM"], ["G8"])
        act(A("E1", 0, [1, 4]), A("G8", 4, [1, 4]), AF.Exp, ["G8"], ["E1"], scale=-1.0)
        act(A("LFP", 0, [1, 4]), A("E1", 0, [1, 4]), AF.Ln, ["E1"], ["LFP"], bias=1.0)
        ts("dve", A("LA", 0, [1, 4]), A("LFP", 0, [1, 4]), -1.0, None, ALU.mult, None, ["LFP"], ["LA"])
        yield
        b = nb()
        for i, cm in enumerate([C_MLE, C_MGT, C_ONE]):
            mmg(PA(b, i * 4, [1, 4]), [(CA(cm, [1, 128]), A("LA", 0, [1, 4]))], ["CONST", "LA"], b, f32=True)
        cp("dve", A("CRT", 0, [1, 12]), PA(b, 0, [1, 12]), [bk(b)], ["CRT"])
        act(A(EC, 0, [1, 12]), A("CRT", 0, [1, 12]), AF.Exp, ["CRT"], [EC])
        tt("dve", A("TA", 0, [1, 4]), A("G8", 0, [1, 4]), A("CRT", 0, [1, 4]), ALU.subtract, ["G8", "CRT"], ["TA"])
        tt("dve", A("TA", 4, [1, 4]), A("G8", 0, [1, 4]), A("CRT", 4, [1, 4]), ALU.add, ["G8", "CRT"], ["TA"])
        act(A("EIC", 0, [1, 8]), A("TA", 0, [1, 8]), AF.Exp, ["TA"], ["EIC"])
        ts("dve", A("QSC", 0, [1, 4]), A(EC, 0, [1, 4]), 128.0 ** -0.5, None, ALU.mult, None, [EC], ["QSC"])
        yield

    def ml_qk(s, t, p, first):
        QE = ("QN", "Z")[p]
        K2 = ("QE2", "TT")[p]
        K3 = ("KN", "RP")[p]
        VA = "KBEG%d" % p
        GG = "GG2%d" % p
        EC = "ECRT%d" % p
        setp(p)
        b = nb()
        proj_tm(b, 0, 512)
        tt("dve", A(QE, 0, [128, 4], [1, 128]), PA(b, 0, [128, 4], [1, 128]), A("QSC", 0, [1, 4], [0, 128]),
           ALU.mult, [bk(b), "QSC"], hk(QE))
        yield
        b = nb()
        proj_tm(b, 512, 512)
        tt("dve", A(K2, 0, [128, 4], [1, 128]), PA(b, 0, [128, 4], [1, 128]), A("EIC", 0, [1, 4], [0, 128]),
           ALU.mult, [bk(b), "EIC"], hk(K2))
        tt("dve", A(K3, 0, [128, 4], [1, 128]), PA(b, 0, [128, 4], [1, 128]), A("EIC", 4, [1, 4], [0, 128]),
           ALU.mult, [bk(b), "EIC"], hk(K3))
        yield

    def ml_voz(s, t, p, first):
        QE = ("QN", "Z")[p]
        K2 = ("QE2", "TT")[p]
        K3 = ("KN", "RP")[p]
        VA = "KBEG%d" % p
        GG = "GG2%d" % p
        EC = "ECRT%d" % p
        setp(p)
        for n in range(2):
            b = nb()
            proj_tm(b, 1024 + n * 512, 512)
            cp("act", A(VA, n * 514, [257, 2], [1, 256]), PA(b, 0, [256, 2], [1, 256]), [bk(b)], [VA])
            yield
        for n in range(2):
            b = nb()
            proj_tm(b, 3072 + n * 512, 512)
            act(A("GS", n * 512, [1, 512]), PA(b, 0, [1, 512]), AF.Silu, [bk(b)], ["GS"])
            yield
        tt("pool", A("GS", 0, [1, 1024]), A("GS", 0, [1, 1024]), A("GAINM", 0, [1, 1024]), ALU.mult,
           ["GS", "GAINM"], ["GS"])
        for n in range(2):
            b = nb()
            proj_tm(b, 2048 + n * 512, 512)
            act(A(GG, n * 512, [1, 512]), PA(b, 0, [1, 512]), AF.Tanh, [bk(b)], [GG], scale=0.5)
            yield
        stt(A(GG, 0, [1, 1024]), A(GG, 0, [1, 1024]), 1.0, A("GS", 0, [1, 1024]), ALU.add, ALU.mult,
            [GG, "GS"], [GG])
        yield

    def ml_stage2(s, t, p, first):
        QE = ("QN", "Z")[p]
        K2 = ("QE2", "TT")[p]
        K3 = ("KN", "RP")[p]
        VA = "KBEG%d" % p
        GG = "GG2%d" % p
        EC = "ECRT%d" % p
        if first:
            mset("pool", A("SG", 0, [1, 1028]), 0.0, ["SG"])
            mset("pool", A("SGB", 0, [1, 1040]), 0.0, ["SGB"])
        b = 0
        for h in range(4):
            tr(PB(b, h * 128, [1, 128]), A(QE, h * 128, [1, 128]), IDB(), hk(QE) + ["IDB"], b)
        for h in range(4):
            tr(PB(b, 512 + h * 128, [1, 128]), A(K2, h * 128, [1, 128]), IDB(), hk(K2) + ["IDB"], b)
        cp("act", A("QNT", 0, [1, 1024]), PB(b, 0, [1, 1024]), [bk(b)], ["QNT"])
        yield
        b = 1
        for h in range(4):
            mmg(PA(b, h * 128, [1, 128]), [(A("QNT", 512 + h * 128, [1, 128]), A("QNT", h * 128, [1, 128]))],
                ["QNT"], b)
        tt("dve", A("QET0", 0, [128, 4], [1, 128]), PA(b, 0, [128, 4], [1, 128]), CA(C_MLE, [0, 4], [1, 128]),
           ALU.mult, [bk(b), "CONST"], ["QET0"])
        yield
        for h in range(4):
            mmg(PA(h, 0, [1, 257]),
                [(A("QNT", h * 128, [1, 128]), A("SGB", h * 257, [1, 257])),
                 (A("QET0", h * 128, [1, 128]), A(VA, h * 257, [1, 257]))], ["QNT", "SGB", "QET0", VA], h)
        allb = [bk(h) for h in range(4)]
        rr4 = A("RR", 0, [1, 4])
        act(rr4, PA(0, 256, [512, 4]), AF.Abs, allb, ["RR"])
        ts("dve", rr4, rr4, 1.0, None, ALU.max, None, ["RR"], ["RR"])
        P.op("dve", lambda e, rr4=rr4: e.reciprocal(rr4, rr4), ["RR"], ["RR"])
        yield
        for h in range(4):
            act(A("GAM", h * 256, [1, 256]), PA(h, 0, [1, 256]), AF.Square, [bk(h), "RR"], ["GAM"],
                scale=A("RR", h, [1, 1]))
        P.op("dve", lambda e: e.tensor_reduce(A("SSH", 0, [1, 4]), A("GAM", 0, [256, 4], [1, 256]), AX.X, ALU.add),
             ["GAM"], ["SSH"], dur=1200.0)
        rsq(A("RS", 0, [1, 4]), A("SSH", 0, [1, 4]), 1.0 / 256, ["SSH"], "RS")
        tt("dve", A("RS", 0, [1, 4]), A("RS", 0, [1, 4]), rr4, ALU.mult, ["RS", "RR"], ["RS"])
        yield
        for h in range(4):
            stt(A("MIX%d" % p, h * 256, [1, 256]), PA(h, 0, [1, 256]), A("RS", h, [1, 1]), A(GG, h * 256, [1, 256]),
                ALU.mult, ALU.mult, [bk(h), "RS", GG], ["MIX%d" % p])
        yield
        for h in range(4):
            mmg(PA(h, 0, [1, 257]), [(A(K3, h * 128, [1, 128]), A(VA, h * 257, [1, 257]))], hk(K3) + [VA], h)
            stt(A("SG", h * 257, [1, 257]), A("SG", h * 257, [1, 257]), A(EC, 8 + h, [1, 1]),
                PA(h, 0, [1, 257]), ALU.mult, ALU.add, ["SG", EC, bk(h)], ["SG"])
        cp("act", A("SGB", 0, [1, 1028]), A("SG", 0, [1, 1028]), ["SG"], ["SGB"])
        yield

    def dwslot(j, g):
        i = j * 3 + g
        if i < 6:
            nm = ("ACC", "TMPC", "X1")[i // 2]
            return nm, (i % 2) * 1024, T[nm + "_bf"], 2048
        return "DWA", (i - 6) * 1024, T["DWA"], 6 * 1024

    def dwap(j, g, c):
        nm, off, th, rs = dwslot(j, g)
        return nm, bass.AP(th, off + c * 128, [[rs, 128], [1, 128]])

    def build_dw():
        i = 0
        for j in range(4):
            for g in range(3):
                for c in range(8):
                    nm, ap = dwap(j, g, c)
                    ts(("dve", "pool")[i % 2], ap, IDB(), A("CW", j * 24 + g * 8 + c, [1, 1]), None, ALU.mult, None,
                       ["IDB", "CW"], [nm])
                    i += 1

    def gdn_group(g, p, first):
        KB, KR, VBn, QE_, MMn = "KBEG%d" % p, "KREV%d" % p, "VB%d" % p, "QET%d" % p, "MM%d" % p
        EC = "ECRT%d" % p
        XB = ("XB1", "XB2", "XB0")[g]
        S = ("QN", "KN", VBn)[g]
        SK = hk(S) if g < 2 else [S]
        HL = "HALO%d" % g
        if first:
            mset("pool", A("HALO", g * 24, [1, 24]), 0.0, [HL])
        cp("pool", A(XB, 0, [131, 8], [1, 3]), A("HALO", g * 24, [3, 8], [1, 3]), [HL], [XB])
        dwn = sorted(set(dwslot(j, g)[0] for j in range(4)))
        for n in range(2):
            b = nb()
            for c in range(n * 4, n * 4 + 4):
                mmg(PA(b, (c % 4) * 128, [1, 128]),
                    [(A("WIN", kc * WC + g * 1024 + c * 128, [1, 128]), HTk(kc)) for kc in range(8)],
                    ["WIN", ctx["HT"]], b)
            cp("act", A(XB, n * 4 * 131 + 3, [131, 4], [1, 128]), PA(b, 0, [128, 4], [1, 128]), [bk(b)], [XB])
            yield
            if n == 1:
                cp("pool", A("HALO", g * 24, [3, 8], [1, 3]), A(XB, 128, [131, 8], [1, 3]), [XB], [HL])
            b2 = nb()
            for c in range(n * 4, n * 4 + 4):
                mmg(PA(b2, (c % 4) * 128, [1, 128]),
                    [(A(XB, c * 131 + j, [1, 128]), dwap(j, g, c)[1]) for j in range(4)], [XB] + dwn, b2)
            act(A(S, n * 512, [1, 512]), PA(b2, 0, [1, 512]), AF.Silu, [bk(b2)], [SK[n]] if g < 2 else [S])
            yield
        sall = A(S, 0, [128, 8], [1, 128])
        if g < 2:
            SQ, RQ = ("SSQ", "RQ") if g == 0 else ("SSQK", "RQK")
            for n in range(2):
                b = nb()
                act(PA(b, 0, [1, 512]), A(S, n * 512, [1, 512]), AF.Square, [SK[n]], [bk(b)])
                P.op("dve", lambda e, b=b, n=n: e.tensor_reduce(A(SQ, n * 4, [1, 4]), PA(b, 0, [128, 4], [1, 128]),
                                                                AX.X, ALU.add), [bk(b)], [SQ], dur=700.0)
            rsq(A(RQ, 0, [1, 8]), A(SQ, 0, [1, 8]), 1.0, [SQ], RQ)
        if g == 0:
            ts("dve", A("QS1", 0, [1, 8]), A("RQ", 0, [1, 8]), 128.0 ** -0.5, None, ALU.mult, None, ["RQ"], ["QS1"])
            tt("dve", A("QS2", 0, [1, 8]), A("QS1", 0, [1, 8]), A(EC, 0, [1, 8]), ALU.mult, ["QS1", EC], ["QS2"])
            tt("dve", A("QE2", 0, [128, 8], [1, 128]), sall, A("QS2", 0, [1, 8], [0, 128]), ALU.mult,
               SK + ["QS2"], hk("QE2"))
            tt("dve", sall, sall, A("QS1", 0, [1, 8], [0, 128]), ALU.mult, SK + ["QS1"], SK)
            yield
            for src, dst in (("QN", "QNT"), ("QE2", QE_)):
                b2 = nb()
                for c in range(8):
                    tr(PB(b2, c * 128, [1, 128]), A(src, c * 128, [1, 128]), IDB(), hk(src) + ["IDB"], b2)
                cp("dve", A(dst, 0, [1, 1024]), PB(b2, 0, [1, 1024]), [bk(b2)], [dst])
                yield
        elif g == 1:
            tt("dve", A("KSB", 0, [1, 8]), A("RQK", 0, [1, 8]), A("BE", 0, [1, 8]), ALU.mult, ["RQK", "BE"], ["KSB"])
            tt("dve", A("KSR", 0, [1, 8]), A("RQK", 0, [1, 8]), A(EC, 8, [1, 8]), ALU.mult, ["RQK", EC], ["KSR"])
            tt("dve", A(KB, 0, [128, 8], [1, 128]), sall, A("KSB", 0, [1, 8], [0, 128]), ALU.mult,
               SK + ["KSB"], [KB])
            tt("dve", A(KR, 0, [128, 8], [1, 128]), sall, A("KSR", 0, [1, 8], [0, 128]), ALU.mult,
               SK + ["KSR"], [KR])
            tt("dve", sall, sall, A("RQK", 0, [1, 8], [0, 128]), ALU.mult, SK + ["RQK"], SK)
            yield
            b2 = nb()
            for c in range(8):
                tr(PB(b2, c * 128, [1, 128]), A("KN", c * 128, [1, 128]), IDB(), hk("KN") + ["IDB"], b2)
            cp("act", A("KNT", 0, [1, 1024]), PB(b2, 0, [1, 1024]), [bk(b2)], ["KNT"])
            yield
        else:
            tt("dve", sall, sall, A(ctx["BETA"], 0, [1, 8], [0, 128]), ALU.mult, [S, ctx["BETA"]], [S])
            yield

    def per_head(pairs_fn, reads, evac):
        for n in range(2):
            b = nb()
            for hh in range(4):
                h = n * 4 + hh
                mmg(PA(b, hh * 128, [1, 128]), pairs_fn(h), reads(n) if callable(reads) else reads, b)
            evac(n, b)
            yield

    def gdn_a(s, t, p, first):
        KB, KR, VBn, QE_, MMn, AQ = "KBEG%d" % p, "KREV%d" % p, "VB%d" % p, "QET%d" % p, "MM%d" % p, "AQKT%d" % p
        EC = "ECRT%d" % p
        GG = "GG2%d" % p
        setp(p)
        b = nb()
        mmg(PA(b, 0, [1, 16]), [(HTk(kc), A("WIN", kc * WC + 4096, [1, 16])) for kc in range(8)], [ctx["HT"], "WIN"], b)
        act(A(ctx["BETA"], 0, [1, 8]), PA(b, 0, [1, 8]), AF.Exp, [bk(b)], [ctx["BETA"]], scale=-1.0)
        ts("dve", A(ctx["BETA"], 0, [1, 8]), A(ctx["BETA"], 0, [1, 8]), 1.0, None, ALU.add, None, [ctx["BETA"]], [ctx["BETA"]])
        bt_ = A(ctx["BETA"], 0, [1, 8])
        P.op("dve", lambda e, bt_=bt_: e.reciprocal(bt_, bt_), [ctx["BETA"]], [ctx["BETA"]])
        tt("dve", A("TA", 0, [1, 8]), PA(b, 8, [1, 8]), A("DTB", 0, [1, 8]), ALU.add, [bk(b), "DTB"], ["TA"])
        act(A("E1", 0, [1, 8]), A("TA", 0, [1, 8]), AF.Exp, ["TA"], ["E1"])
        act(A("LFP", 0, [1, 8]), A("E1", 0, [1, 8]), AF.Ln, ["E1"], ["LFP"], bias=1.0)
        tt("dve", A("LA", 0, [1, 8]), A("LFP", 0, [1, 8]), A("NEGA", 0, [1, 8]), ALU.mult, ["LFP", "NEGA"], ["LA"])
        yield
        b = nb()
        for i, cm in enumerate([C_MLE, C_MGT, C_ONE]):
            mmg(PA(b, i * 8, [1, 8]), [(CA(cm, [1, 128]), A("LA", 0, [1, 8]))], ["CONST", "LA"], b, f32=True)
        cp("dve", A("CRT", 0, [1, 24]), PA(b, 0, [1, 24]), [bk(b)], ["CRT"])
        act(A(EC, 0, [1, 24]), A("CRT", 0, [1, 24]), AF.Exp, ["CRT"], [EC])
        tt("dve", A("BE", 0, [1, 8]), A(ctx["BETA"], 0, [1, 8]), A(EC, 0, [1, 8]), ALU.mult, [ctx["BETA"], EC], ["BE"])
        tt("dve", A("JUNK", 0, [128, 8], [1, 128]), CA(C_MGT, [0, 8], [1, 128]), A("LA", 0, [1, 8], [0, 128]),
           ALU.mult, ["CONST", "LA"], ["JUNK"])
        yield
        for n in range(2):
            b = nb()
            mmg(PA(b, 0, [1, 512]), [(CA(C_MLE, [1, 128]), A("JUNK", n * 512, [1, 512]))], ["CONST", "JUNK"], b, f32=True)
            act(A("GAM", n * 512, [1, 512]), PA(b, 0, [1, 512]), AF.Exp, [bk(b)], ["GAM"])
            yield
        tt("pool", A("GS", 0, [128, 8], [1, 128]), A("GAM", 0, [128, 8], [1, 128]), CA(C_MGT, [0, 8], [1, 128]),
           ALU.mult, ["GAM", "CONST"], ["GS"])
        tt("pool", A("GS", 0, [128, 8], [1, 128]), A("GS", 0, [128, 8], [1, 128]), A(ctx["BETA"], 0, [1, 8], [0, 128]),
           ALU.mult, ["GS", ctx["BETA"]], ["GS"])
        tt("pool", A("GAM", 0, [128, 8], [1, 128]), A("GAM", 0, [128, 8], [1, 128]), CA(C_MGE, [0, 8], [1, 128]),
           ALU.mult, ["GAM", "CONST"], ["GAM"])
        yield

    def gdn_k(s, t, p, first):
        KB, KR, VBn, QE_, MMn, AQ = "KBEG%d" % p, "KREV%d" % p, "VB%d" % p, "QET%d" % p, "MM%d" % p, "AQKT%d" % p
        EC = "ECRT%d" % p
        GG = "GG2%d" % p
        setp(p)
        yield from gdn_group(1, p, first)
        yield from per_head(lambda h: [(H("KNT", h), H("KNT", h))], ["KNT"],
                            lambda n, b: tt("dve", A(MMn, n * 512, [1, 512]), PA(b, 0, [1, 512]),
                                            A("GS", n * 512, [1, 512]), ALU.mult, [bk(b), "GS"], [(MMn, n)]))
        for n in range(2):
            tt("pool", A(MMn, n * 512, [128, 4], [1, 128]), A(MMn, n * 512, [128, 4], [1, 128]),
               A("IDB", 0, [0, 4], [1, 128]), ALU.add, [(MMn, n), "IDB"], [(MMn, n)])
        yield

    def gdn_q(s, t, p, first):
        KB, KR, VBn, QE_, MMn, AQ = "KBEG%d" % p, "KREV%d" % p, "VB%d" % p, "QET%d" % p, "MM%d" % p, "AQKT%d" % p
        EC = "ECRT%d" % p
        GG = "GG2%d" % p
        setp(p)
        for n in range(2):
            b = nb()
            proj_tm(b, 3072 + n * 512, 512)
            act(A(GG, n * 512, [1, 512]), PA(b, 0, [1, 512]), AF.Silu, [bk(b)], [GG])
            yield
        tt("pool", A(GG, 0, [128, 8], [1, 128]), A(GG, 0, [128, 8], [1, 128]), A("GAING", 0, [0, 8], [1, 128]),
           ALU.mult, [GG, "GAING"], [GG])
        yield from gdn_group(0, p, first)
        yield from per_head(lambda h: [(H("QNT", h), H("KNT", h))], ["QNT", "KNT"],
                            lambda n, b: tt("dve", A("AQK", n * 512, [1, 512]), PA(b, 0, [1, 512]),
                                            A("GAM", n * 512, [1, 512]), ALU.mult, [bk(b), "GAM"], ["AQK"]))
        b = nb()
        for c in range(8):
            tr(PB(b, c * 128, [1, 128]), A("AQK", c * 128, [1, 128]), IDB(), ["AQK", "IDB"], b)
        cp("dve", A(AQ, 0, [1, 1024]), PB(b, 0, [1, 1024]), [bk(b)], [AQ])
        yield

    def gdn_v(s, t, p, first):
        setp(p)
        yield from gdn_group(2, p, first)

    def gdn_stage2(s, t, p, first):
        KB, KR, VBn, QE_, MMn, AQ = "KBEG%d" % p, "KREV%d" % p, "VB%d" % p, "QET%d" % p, "MM%d" % p, "AQKT%d" % p
        EC = "ECRT%d" % p
        GG = "GG2%d" % p
        r0 = s * SEQ + t * L
        if first:
            mset("pool", A("SG", 0, [1, 1028]), 0.0, ["SG"])
            mset("pool", A("SGB", 0, [1, 1040]), 0.0, ["SGB"])
        for n in range(2):
            b = nb()
            for hh in range(4):
                mmg(PA(b, hh * 128, [1, 128]), [(H(MMn, n * 4 + hh), IDB())], [(MMn, n), "IDB"], b)
            tt("dve", A("Z", n * 512, [128, 4], [1, 128]), PA(b, 0, [128, 4], [1, 128]),
               A("LEVB", 0, [0, 4], [1, 128]), ALU.mult, [bk(b), "LEVB"], [("Z", n)])
            tt("pool", A("TT", n * 512, [128, 4], [1, 128]), A(MMn, n * 512, [128, 4], [1, 128]),
               A("LEVB", 0, [0, 4], [1, 128]), ALU.mult, [(MMn, n), "LEVB"], [("TT", n)])
            yield
        for k in range(1, 7):
            for n in range(2):
                b = nb()
                for hh in range(4):
                    h = n * 4 + hh
                    mmg(PA(b, hh * 128, [1, 128]), [(H(MMn, h), H("Z", h))], [(MMn, n), ("Z", n)], b)
                tt("dve", A("RP", n * 512, [128, 4], [1, 128]), PA(b, 0, [128, 4], [1, 128]),
                   A("LEVB", k * 128, [0, 4], [1, 128]), ALU.mult, [bk(b), "LEVB"], [("RP", n)])
                yield
            for n in range(2):
                zb_ = nb()
                for hh in range(4):
                    h = n * 4 + hh
                    mmg(PA(zb_, hh * 128, [1, 128]), [(H("TT", h), H("RP", h))], [("TT", n), ("RP", n)], zb_)
                tb_ = None
                if k < 6:
                    tb_ = nb()
                    for hh in range(4):
                        h = n * 4 + hh
                        mmg(PA(tb_, hh * 128, [1, 128]), [(H("RP", h), H("TT", h))], [("TT", n), ("RP", n)], tb_)
                cp(ZENG[n], A("Z", n * 512, [1, 512]), PA(zb_, 0, [1, 512]), [bk(zb_)], [("Z", n)])
                if k < 6:
                    cp(TTENG[n], A("TT", n * 512, [1, 512]), PA(tb_, 0, [1, 512]),
                       [bk(tb_)], [("TT", n)])
                yield
        yield from per_head(lambda h: [(H(KB, h), H("Z", h))], lambda n: [KB, ("Z", n)],
                            lambda n, b: act(A("NWT", n * 512, [1, 512]), PA(b, 0, [1, 512]), AF.Copy, [bk(b)],
                                             [("NWT", n)], scale=-1.0))
        yield from per_head(lambda h: [(H("Z", h), H(VBn, h)), (H("NWT", h), H("SGB", h))],
                            lambda n: [("Z", n), VBn, ("NWT", n), "SGB"],
                            lambda n, b: cp("act", A("VNEW", n * 512, [1, 512]), PA(b, 0, [1, 512]), [bk(b)],
                                            [("VNEW", n)]))
        ob = []

        def o_evac(n, b):
            ob.append(b)
            act(A("RP", n * 512, [1, 512]), PA(b, 0, [1, 512]), AF.Square, [bk(b)], [("RP", n)])
        yield from per_head(lambda h: [(H(QE_, h), H("SGB", h)), (H(AQ, h), H("VNEW", h))],
                            lambda n: [QE_, "SGB", AQ, ("VNEW", n)], o_evac)
        P.op("dve", lambda e: e.tensor_reduce(A("SSO", 0, [1, 8]), A("RP", 0, [128, 8], [1, 128]), AX.X, ALU.add),
             hk("RP"), ["SSO"], dur=1200.0)
        rsq(A("RSO", 0, [1, 8]), A("SSO", 0, [1, 8]), 1.0 / 128, ["SSO"], "RSO")
        tt("dve", A(GG, 0, [128, 8], [1, 128]), A(GG, 0, [128, 8], [1, 128]), A("RSO", 0, [1, 8], [0, 128]),
           ALU.mult, [GG, "RSO"], [GG])
        for n in range(2):
            tt("dve", A("MIX%d" % p, n * 512, [1, 512]), PA(ob[n], 0, [1, 512]), A(GG, n * 512, [1, 512]), ALU.mult,
               [bk(ob[n]), GG], ["MIX%d" % p])
        yield
        tt("pool", A("SG", 0, [128, 8], [1, 128]), A("SG", 0, [128, 8], [1, 128]), A(EC, 16, [1, 8], [0, 128]),
           ALU.mult, ["SG", EC], ["SG"])
        yield from per_head(lambda h: [(H(KR, h), H("VNEW", h))], lambda n: [KR, ("VNEW", n)],
                            lambda n, b: tt("dve", A("SG", n * 512, [1, 512]), PA(b, 0, [1, 512]),
                                            A("SG", n * 512, [1, 512]), ALU.add, [bk(b), "SG"], ["SG"]))
        cp("act", A("SGB", 0, [1, 1024]), A("SG", 0, [1, 1024]), ["SG"], ["SGB"])
        yield

    def collect(gen, banks):
        if gen is None:
            return []
        P.defer = []
        bset[0] = banks
        for _ in gen:
            pass
        l = P.defer
        P.defer = None
        bset[0] = list(range(8))
        return l

    def collect(gen, banks):
        P.defer = []
        bset[0] = banks
        for _ in gen:
            pass
        l = P.defer
        P.defer = None
        bset[0] = list(range(8))
        return l

    def prep_pass1():
        load_weights(1, ["ACC", "TMPC", "X1"], do_out=False)
        dma(A("GAINM", 0, [1, 1024]), bc(fn_d, 1024), [], ["GAINM"])
        build_dw()
        yield

    for ps_ in range(2):
        if ps_ == 0:
            load_weights(ps_)
        else:
            load_weights(1, ["RES", "JUNK", "X0", "GAM"], do_in=False)
        if ps_ == 0:
            mset("pool", A("KBEG0", 0, [1, 1040]), 1.0, ["KBEG0"])
            mset("pool", A("KBEG1", 0, [1, 1040]), 1.0, ["KBEG1"])
        tiles = [(s, t) for s in range(NSEQ) for t in range(NT_RUN)]
        prev = None
        prev2 = None
        hts = lambda j: "HT%d" % (j % 3)
        ctx["HT"] = hts(0)
        P.merge([collect(phase_a(tiles[0][0], tiles[0][1], 0, X="X0"), [4])])
        for i, cur in enumerate(tiles + [None, None]):
            lists = []
            pd = None
            if prev2 is not None:
                pd = (ps_, prev2[0], prev2[1], (i - 2) % 2)
            if prev is not None:
                pa = (prev[0], prev[1], (i - 1) % 2, prev[1] == 0)
                ctx["HT"] = hts(i - 1)
                if ps_ == 0:
                    lists.append(collect(ml_stage2(*pa), [0, 1, 2, 3]))
                else:
                    lists.append(collect(gdn_v(*pa), [3]))
                    lists.append(collect(gdn_stage2(*pa), [0, 1, 2]))
            if ps_ == 1 and pd is not None:
                lists.append(collect(phase_d(*pd), [3]))
            if cur is not None:
                ca = (cur[0], cur[1], i % 2, cur[1] == 0)
                ctx["HT"] = hts(i)
                if ps_ == 0:
                    lists.append(collect(ml_stage1(*ca), [4]))
                    lists.append(collect(ml_qk(*ca), [5]))
                    lists.append(collect(ml_voz(*ca), [6, 7]))
                else:
                    lists.append(collect(gdn_a(*ca), [4, 5]))
                    lists.append(collect(gdn_k(*ca), [4, 5]))
                    lists.append(collect(gdn_q(*ca), [6, 7]))
            if ps_ == 0 and pd is not None:
                lists.append(collect(phase_d(*pd), [4]))
            if ps_ == 0 and i == len(tiles):
                lists.append(collect(prep_pass1(), [5]))
            if i + 1 < len(tiles):
                nx = tiles[i + 1]
                ctx["HT"] = hts(i + 1)
                lists.insert(0, collect(phase_a(nx[0], nx[1], 0, X="X0"), [5] if ps_ == 0 else [3]))
            P.merge(lists)
            prev2 = prev
            prev = cur
    P.op("pool", None, ["out_dram"], [])
    fin = P.ops[-1]
    for so in store_ops:
        fin["deps"][so[0]] = True


    P.finalize()
    sems = {}
    for e, ng in P.ngen.items():
        for g in range(ng):
            sems[(e, g)] = es.enter_context(nc.semaphore("s_%s_%d" % (e, g)))
    for i in range(NDMA):
        sems[("dma", i)] = es.enter_context(nc.semaphore("s_dma_%d" % i))
    for i in range(NDMAP):
        sems[("dmap", i)] = es.enter_context(nc.semaphore("s_dmap_%d" % i))
    with nc.Block() as block:
        @block.sync
        def _(e):
            P.emit("sp", e, sems)

        @block.tensor
        def _(e):
            P.emit("pe", e, sems)

        @block.scalar
        def _(e):
            P.emit("act", e, sems)

        @block.vector
        def _(e):
            P.emit("dve", e, sems)

        @block.gpsimd
        def _(e):
            P.emit("pool", e, sems)
    es.close()
    return nc


NT_RUN = NT
ALPHA = 0.0
ZENG = ("act", "dve")
TTENG = ("act", "dve")
TBL_NS = 1300.0
TBL_PEN = 1300.0
_CACHE = {}


def kernel(x, attn_norm, w_in, m_i_bias, m_f_bias, m_out_norm, g_conv, g_a_log, g_dt_bias, g_out_norm, w_out,
           final_norm):
    f = lambda a: np.ascontiguousarray(np.asarray(a, dtype=np.float32))
    if "nc" not in _CACHE:
        _CACHE["nc"] = build_program()
    nc = _CACHE["nc"]
    x = f(x)
    shared = {
        "w_in": f(w_in).reshape(D, 8216),
        "w_out": f(w_out).reshape(2048, D),
        "attn_norm": f(attn_norm).reshape(8, 128),
        "m_i_bias": f(m_i_bias).reshape(1, 4),
        "m_f_bias": f(m_f_bias).reshape(1, 4),
        "m_out_norm": f(m_out_norm).reshape(1, 1024),
        "g_conv": f(g_conv).reshape(96, 128),
        "g_a_log": f(g_a_log).reshape(1, 8),
        "g_dt_bias": f(g_dt_bias).reshape(1, 8),
        "g_out_norm": f(g_out_norm).reshape(1, 128),
        "final_norm": f(final_norm).reshape(1, 1024),
        "consts": make_consts(),
    }
    in_maps = []
    for c in range(NCORES):
        m = dict(shared)
        m["x"] = x[c * NSEQ:(c + 1) * NSEQ].reshape(TOK, D)
        in_maps.append(m)
    res = run_bass_kernel_spmd(nc, in_maps, core_ids=list(range(NCORES)))
    outs = [np.asarray(r["out"]).reshape(NSEQ, SEQ, D) for r in res.results]
    return np.concatenate(outs, axis=0).astype(np.float32)
```

```python
import numpy as np
from contextlib import ExitStack
import concourse.bass as bass
import concourse.mybir as mybir
from concourse.bass_utils import run_bass_kernel_spmd

F32 = mybir.dt.float32
BF = mybir.dt.bfloat16
AF = mybir.ActivationFunctionType
ALU = mybir.AluOpType
AX = mybir.AxisListType

NCORES = 8
SEQ = 2048
NSEQ = 2
TOK = NSEQ * SEQ
L = 128
NT = SEQ // L
D = 1024
WC = 4112
EPS = 1e-6
GEN = 3000
NDMA = 24
NDMAP = 8
C_MLE, C_MGT, C_MGE, C_ONE, C_ID, C_LEV = 0, 128, 256, 384, 512, 640
NCONST = 640 + 7 * 128


def make_consts():
    idx = np.arange(L)
    c = np.zeros((L, NCONST), np.float32)
    c[:, C_MLE:C_MLE + L] = idx[:, None] <= idx[None, :]
    c[:, C_MGT:C_MGT + L] = idx[:, None] > idx[None, :]
    c[:, C_MGE:C_MGE + L] = idx[:, None] >= idx[None, :]
    c[:, C_ONE:C_ONE + L] = 1.0
    c[:, C_ID:C_ID + L] = np.eye(L)
    for k in range(7):
        b2 = 2 << k
        m = np.where((idx[:, None] // b2) == (idx[None, :] // b2), -1.0, 0.0)
        m[idx, idx] = 1.0
        c[:, C_LEV + k * L:C_LEV + (k + 1) * L] = m
    return c


class Prog:
    def __init__(self):
        self.ops = []
        self.lastw = {}
        self.readers = {}
        self.defer = None
        self.cur_tbl = None
        self.efree = {}
        self.wdone = {}
        self.rdone = {}

    def _norm(self, keys):
        return ["JUNK" if k == "GS" else k for k in keys]

    def op(self, eng, fn, reads=(), writes=(), dma=False, dur=300.0, tbl=None):
        reads = self._norm(reads)
        writes = self._norm(writes)
        if self.defer is not None:
            h = [None]
            self.defer.append(dict(eng=eng, fn=fn, reads=list(reads), writes=list(writes), dma=dma, dur=dur, h=h,
                                   tbl=tbl))
            return h
        self._sim(eng, reads, writes, dma, dur, tbl)
        return [self._op(eng, fn, reads, writes, dma)]

    def _est(self, eng, reads, writes, tbl=None, pen=None):
        t = self.efree.get(eng, 0.0)
        if tbl is not None and tbl != self.cur_tbl:
            t += TBL_NS if pen is None else pen
        for r in reads:
            t = max(t, self.wdone.get(r, 0.0))
        for w in writes:
            t = max(t, self.wdone.get(w, 0.0), self.rdone.get(w, 0.0))
        return t

    def _sim(self, eng, reads, writes, dma, dur, tbl=None):
        st = self._est(eng, reads, writes, tbl)
        if tbl is not None:
            self.cur_tbl = tbl
        if dma:
            self.efree[eng] = st + 100.0
            end = st + 3000.0
        else:
            end = st + dur
            self.efree[eng] = end
        for r in reads:
            self.rdone[r] = max(self.rdone.get(r, 0.0), end)
        for w in writes:
            self.wdone[w] = end + 150.0
        return st

    def merge(self, lists):
        lists = [l for l in lists if l]
        n = len(lists)
        accs = []
        for l in lists:
            wl, rl = {}, {}
            for i, d in enumerate(l):
                for r in d["reads"]:
                    rl[r] = i
                for w in d["writes"]:
                    wl[w] = i
            accs.append((wl, rl))
        pos = [0] * n
        rems = []
        for l in lists:
            r_ = [0.0] * (len(l) + 1)
            for i in range(len(l) - 1, -1, -1):
                r_[i] = r_[i + 1] + l[i]["dur"]
            rems.append(r_)

        def blocked(j, d):
            for i in range(j):
                pi = pos[i]
                if pi >= len(lists[i]):
                    continue
                wl, rl = accs[i]
                for r in d["reads"]:
                    if wl.get(r, -1) >= pi:
                        return True
                for w in d["writes"]:
                    if wl.get(w, -1) >= pi or rl.get(w, -1) >= pi:
                        return True
            return False

        while True:
            best = None
            for i, l in enumerate(lists):
                if pos[i] < len(l):
                    d = l[pos[i]]
                    if blocked(i, d):
                        continue
                    st = self._est(d["eng"], d["reads"], d["writes"], d.get("tbl"), TBL_PEN) - ALPHA * rems[i][pos[i]]
                    if best is None or st < best[0]:
                        best = (st, i)
            if best is None:
                break
            i = best[1]
            d = lists[i][pos[i]]
            pos[i] += 1
            self._sim(d["eng"], d["reads"], d["writes"], d["dma"], d["dur"], d.get("tbl"))
            d["h"][0] = self._op(d["eng"], d["fn"], d["reads"], d["writes"], d["dma"])
        assert all(pos[i] == len(lists[i]) for i in range(n))

    def _op(self, eng, fn, reads=(), writes=(), dma=False):
        idx = len(self.ops)
        deps = {}
        for r in reads:
            if r in self.lastw:
                deps[self.lastw[r]] = True
        for w in writes:
            if w in self.lastw:
                deps.setdefault(self.lastw[w], False)
            for rd in self.readers.get(w, ()):
                deps.setdefault(rd, False)
        self.ops.append(dict(eng=eng, fn=fn, deps=deps, dma=dma))
        for r in reads:
            self.readers.setdefault(r, []).append(idx)
        for w in writes:
            self.lastw[w] = idx
            self.readers[w] = []
        return idx

    def finalize(self):
        ops = self.ops
        last_dma_on_sem = {}
        ndma = 0
        npool = 0
        for i, o in enumerate(ops):
            nd = {}
            for d, raw in o["deps"].items():
                od = ops[d]
                if od["dma"]:
                    nd[d] = raw
                elif od["eng"] == o["eng"]:
                    if o["eng"] != "pe" and raw and not o["dma"]:
                        nd[d] = raw
                    elif o["dma"]:
                        nd[d] = raw
                else:
                    nd[d] = raw
            if o["dma"]:
                if o["eng"] == "pool":
                    s = ("dmap", npool % NDMAP)
                    npool += 1
                else:
                    s = ("dma", ndma % NDMA)
                    ndma += 1
                if s in last_dma_on_sem:
                    nd[last_dma_on_sem[s]] = False
                last_dma_on_sem[s] = i
                o["dsem"] = s
            o["deps"] = nd
        needed = set()
        for o in ops:
            needed.update(o["deps"].keys())
        cnt = {}
        dcnt = {}
        for i, o in enumerate(ops):
            if o["dma"]:
                dcnt[o["dsem"]] = dcnt.get(o["dsem"], 0) + 16
                o["tok"] = (o["dsem"], dcnt[o["dsem"]])
            elif i in needed:
                c = cnt.get(o["eng"], 0)
                cnt[o["eng"]] = c + 1
                o["tok"] = ((o["eng"], c // GEN), c % GEN + 1)
            else:
                o["tok"] = None
        self.ngen = {e: (c + GEN - 1) // GEN for e, c in cnt.items()}

    def emit(self, eng_name, eng, sems):
        waited = {}
        for o in self.ops:
            if o["eng"] != eng_name:
                continue
            for d in sorted(o["deps"].keys()):
                key, val = self.ops[d]["tok"]
                if waited.get(key, 0) >= val:
                    continue
                waited[key] = val
                eng.wait_ge(sems[key], val)
            if o["fn"] is None:
                continue
            ins = o["fn"](eng)
            if o["tok"] is not None:
                key, _ = o["tok"]
                ins.then_inc(sems[key], 16 if o["dma"] else 1)


def build_program():
    nc = bass.Bass("TRN2", target_bir_lowering=False)
    dt_in = lambda n, s: nc.dram_tensor(n, list(s), F32, kind="ExternalInput")
    x_d = dt_in("x", [TOK, D])
    win_d = dt_in("w_in", [D, 8216])
    wout_d = dt_in("w_out", [2048, D])
    an_d = dt_in("attn_norm", [8, 128])
    mib_d = dt_in("m_i_bias", [1, 4])
    mfb_d = dt_in("m_f_bias", [1, 4])
    mon_d = dt_in("m_out_norm", [1, 1024])
    gcv_d = dt_in("g_conv", [96, 128])
    gal_d = dt_in("g_a_log", [1, 8])
    gdt_d = dt_in("g_dt_bias", [1, 8])
    gon_d = dt_in("g_out_norm", [1, 128])
    fn_d = dt_in("final_norm", [1, 1024])
    cst_d = dt_in("consts", [L, NCONST])
    out_d = nc.dram_tensor("out", [TOK, D], F32, kind="ExternalOutput")
    part_d = nc.dram_tensor("partial", [TOK, D], F32, kind="Internal")

    P = Prog()
    es = ExitStack()
    T = {}

    def sb(name, cols, dt, parts=128):
        T[name] = es.enter_context(nc.sbuf_tensor(name, [parts, cols], dt))
        return T[name]

    sb("WIN", 8 * WC, BF)
    sb("WOUT", 8 * 1024, BF)
    sb("CONST", 640, F32)
    sb("IDB", 128, BF)
    sb("LEVB", 7 * 128, BF)
    sb("GAINM", 1024, F32)
    sb("GAING", 128, F32)
    sb("CW", 96, F32)
    sb("CWROW", 128, F32, parts=96)
    sb("AN8", 128, F32, parts=8)
    sb("GW", 8, F32)
    sb("BIASM", 8, F32)
    sb("DTB", 8, F32)
    sb("NEGA", 8, F32)
    sb("X0", 1024, F32)
    sb("X1", 1024, F32)
    sb("JUNK", 1028, F32)
    sb("HT0", 1024, BF)
    sb("HT1", 1024, BF)
    sb("HT2", 1024, BF)
    sb("MIX0", 1024, BF)
    sb("MIX1", 1024, BF)
    sb("MIXT", 1024, BF)
    sb("RES", 1028, F32)
    for n in ["SS", "RSTD", "SS2", "RSTD2"]:
        sb(n, 1, F32)
    for n in ["G8", "E1", "LFP", "LA", "CRT", "ECRT0", "ECRT1", "TA", "TB", "EIC", "EIR", "QSC", "RR", "SSH", "RS",
              "BETA0", "BETA1", "BE", "SSQ", "RQ", "SSQK", "RQK", "QS1", "QS2", "KSB", "KSR", "SSO", "RSO"]:
        sb(n, 24 if n in ("CRT", "ECRT0", "ECRT1") else 16, F32)
    sb("TMPC", 1024, F32)
    sb("HALO", 24 * 3, BF)
    sb("ACC", 1024, F32)
    sb("XB0", 8 * 131, BF)
    sb("XB1", 8 * 131, BF)
    sb("XB2", 8 * 131, BF)
    sb("DWA", 6 * 1024, BF)
    sb("QN", 1024, BF)
    sb("QE2", 1024, BF)
    sb("KN", 1024, BF)
    sb("KBEG0", 1040, BF)
    sb("KBEG1", 1040, BF)
    sb("KREV0", 1024, BF)
    sb("KREV1", 1024, BF)
    sb("VB0", 1024, BF)
    sb("VB1", 1024, BF)
    sb("QNT", 1024, BF)
    sb("QET0", 1024, BF)
    sb("QET1", 1024, BF)
    sb("KNT", 1024, BF)
    sb("GAM", 1024, F32)
    sb("MM0", 1024, BF)
    sb("MM1", 1024, BF)
    sb("AQK", 1024, BF)
    sb("AQKT0", 1024, BF)
    sb("AQKT1", 1024, BF)
    sb("Z", 1024, BF)
    sb("TT", 1024, BF)
    sb("RP", 1024, BF)
    sb("NWT", 1024, BF)
    sb("VNEW", 1024, BF)
    sb("GG20", 1024, BF)
    sb("GG21", 1024, BF)
    sb("SG", 1028, F32)
    sb("SGB", 1040, BF)
    PS = es.enter_context(nc.psum_tensor("ps", [128, 4096], F32))
    for nm_ in ("ACC", "TMPC", "X1"):
        T[nm_ + "_bf"] = T[nm_].bitcast(BF)
    PSB = PS.bitcast(BF)

    def A(name, off=0, *dims, p=128):
        if name == "GS":
            name = "JUNK"
        t = T[name]
        return bass.AP(t, off, [[t.shape[1], p]] + [list(d) for d in dims])

    def PA(b, off=0, *dims, p=128):
        return bass.AP(PS, b * 512 + off, [[4096, p]] + [list(d) for d in dims])

    def PB(b, off=0, *dims, p=128):
        return bass.AP(PSB, b * 1024 + off, [[8192, p]] + [list(d) for d in dims])

    def CA(off, *dims, p=128):
        return A("CONST", off, *dims, p=p)

    bank = [0]

    bset = [list(range(8))]
    bctr = {}

    def nb():
        key = tuple(bset[0])
        c = bctr.get(key, 0)
        bctr[key] = c + 1
        return bset[0][c % len(bset[0])]

    def nel(ap):
        n = 1
        for st_, cn in list(ap.ap)[1:]:
            n *= cn
        return n

    def bk(b):
        return ("ps", b)

    dq = [0]

    def dma(out, in_, reads, writes, q=None, slow=False):
        if q is None:
            q = "sp"
        if slow:
            fn = lambda e: e.dma_start(out=out, in_=in_, allow_slow_non_contiguous=True)
        else:
            fn = lambda e: e.dma_start(out=out, in_=in_)
        return P.op(q, fn, reads, writes, dma=True)

    def mmg(out, pairs, reads, b, f32=False):
        n = len(pairs)
        for i, (l, r) in enumerate(pairs):
            d = max(60.0, nel(r) * 0.45) * (4.0 if f32 else 1.0)
            P.op("pe", lambda e, l=l, r=r, i=i: e.matmul(out, l, r, start=(i == 0), stop=(i == n - 1)),
                 reads, [bk(b)], dur=d)

    def tr(out, in_, ident, reads, b):
        P.op("pe", lambda e: e.transpose(out, in_, ident), reads, [bk(b)], dur=60.0)

    def act(out, in_, func, reads, writes, scale=None, bias=None):
        kw = {}
        if scale is not None:
            kw["scale"] = scale
        if bias is not None:
            kw["bias"] = bias
        tbl = "silu" if func in (AF.Silu, AF.Tanh) else ("exp" if func in (AF.Exp, AF.Ln) else None)
        P.op("act", lambda e: e.activation(out, in_, func, **kw), reads, writes, dur=250.0 + 0.85 * nel(out), tbl=tbl)

    def edur(eng, out):
        return (150.0 + 2.1 * nel(out)) if eng == "pool" else (160.0 + 1.02 * nel(out))

    def tt(eng, out, in0, in1, op, reads, writes):
        P.op(eng, lambda e: e.tensor_tensor(out, in0, in1, op), reads, writes, dur=edur(eng, out))

    def ts(eng, out, in0, s1, s2, op0, op1, reads, writes):
        if op1 is None:
            P.op(eng, lambda e: e.tensor_scalar(out, in0, s1, None, op0), reads, writes, dur=edur(eng, out))
        else:
            P.op(eng, lambda e: e.tensor_scalar(out, in0, s1, s2, op0, op1), reads, writes, dur=edur(eng, out))

    def stt(out, in0, scalar, in1, op0, op1, reads, writes):
        P.op("dve", lambda e: e.scalar_tensor_tensor(out, in0, scalar, in1, op0, op1), reads, writes,
             dur=edur("dve", out))

    def cp(eng, out, in_, reads, writes):
        if eng == "act":
            act(out, in_, AF.Copy, reads, writes)
        else:
            P.op(eng, lambda e: e.tensor_copy(out, in_), reads, writes, dur=edur(eng, out))

    def mset(eng, ap, val, writes):
        P.op(eng, lambda e: e.memset(ap, val), [], writes)

    def rsq(out, in_, mul, reads, w):
        act(out, in_, AF.Ln, reads, [w], scale=float(mul), bias=EPS)
        act(out, out, AF.Exp, [w], [w], scale=-0.5)

    IDB = lambda n=128: A("IDB", 0, [1, n], p=n)

    dma(A("CONST", 0, [1, 640]), cst_d.ap()[:, 0:640], [], ["CONST"])
    dma(A("JUNK", 0, [1, 896]), cst_d.ap()[:, 640:NCONST], [], ["JUNK"])
    cp("dve", A("IDB", 0, [1, 128]), CA(C_ID, [1, 128]), ["CONST"], ["IDB"])
    cp("dve", A("LEVB", 0, [1, 896]), A("JUNK", 0, [1, 896]), ["JUNK"], ["LEVB"])
    bc = lambda d, n: bass.AP(d, 0, [[0, 128], [1, n]])
    dma(A("GAINM", 0, [1, 1024]), bc(mon_d, 1024), [], ["GAINM"])
    dma(A("GAING", 0, [1, 128]), bc(gon_d, 128), [], ["GAING"])
    ts("pool", A("GAINM", 0, [1, 1024]), A("GAINM", 0, [1, 1024]), 0.5, None, ALU.mult, None, ["GAINM"], ["GAINM"])
    dma(A("BIASM", 0, [1, 4]), bc(mib_d, 4), [], ["BIASM"])
    dma(A("BIASM", 4, [1, 4]), bc(mfb_d, 4), [], ["BIASM"])
    dma(A("DTB", 0, [1, 8]), bc(gdt_d, 8), [], ["DTB"])
    dma(A("NEGA", 0, [1, 8]), bc(gal_d, 8), [], ["NEGA"])
    act(A("NEGA", 0, [1, 8]), A("NEGA", 0, [1, 8]), AF.Exp, ["NEGA"], ["NEGA"])
    ts("dve", A("NEGA", 0, [1, 8]), A("NEGA", 0, [1, 8]), -1.0, None, ALU.mult, None, ["NEGA"], ["NEGA"])
    dma(A("AN8", 0, [1, 128], p=8), an_d.ap(), [], ["AN8"])
    dma(A("CWROW", 0, [1, 128], p=96), gcv_d.ap(), [], ["CWROW"])
    b = nb()
    tr(PA(b, 0, [1, 8]), A("AN8", 0, [1, 128], p=8), CA(C_ID, [1, 8], p=8), ["AN8", "CONST"], b)
    cp("dve", A("GW", 0, [1, 8]), PA(b, 0, [1, 8]), [bk(b)], ["GW"])
    b = nb()
    tr(PA(b, 0, [1, 96]), A("CWROW", 0, [1, 128], p=96), CA(C_ID, [1, 96], p=96), ["CWROW", "CONST"], b)
    cp("dve", A("CW", 0, [1, 96]), PA(b, 0, [1, 96]), [bk(b)], ["CW"])

    cvt_i = [0]

    STGS = ["RES", "JUNK", "ACC", "TMPC", "X0", "X1", "GAM"]

    def load_weights(ps_, STGS=STGS, do_in=True, do_out=True):
        c0 = 0 if ps_ == 0 else 4104
        wtot = 4104 if ps_ == 0 else 4112
        chunks = [(j * 1024, 1024) for j in range(4)] + [(4096, wtot - 4096)]
        for kc in (range(8) if do_in else []):
            for (cj, w) in chunks:
                i = cvt_i[0]
                cvt_i[0] += 1
                stg = STGS[i % len(STGS)]
                dma(A(stg, 0, [1, w]), win_d.ap()[kc * 128:(kc + 1) * 128, c0 + cj:c0 + cj + w], [], [stg])
                dst = A("WIN", kc * WC + cj, [1, w])
                if i % 2 == 0:
                    act(dst, A(stg, 0, [1, w]), AF.Copy, [stg, "GW"], ["WIN"], scale=A("GW", kc, [1, 1]))
                else:
                    ts("dve", dst, A(stg, 0, [1, w]), A("GW", kc, [1, 1]), None, ALU.mult, None, [stg, "GW"], ["WIN"])
        if not do_out:
            return
        r0 = ps_ * 1024
        for kc in range(8):
            i = cvt_i[0]
            cvt_i[0] += 1
            stg = STGS[i % len(STGS)]
            dma(A(stg, 0, [1, 1024]), wout_d.ap()[r0 + kc * 128:r0 + (kc + 1) * 128, :], [], [stg])
            cp(("act", "dve")[i % 2], A("WOUT", kc * 1024, [1, 1024]), A(stg, 0, [1, 1024]), [stg], ["WOUT"])

    ctx = {"HT": "HT0", "BETA": "BETA0"}

    def setp(p):
        ctx["BETA"] = "BETA%d" % p

    def HTk(kc):
        return A(ctx["HT"], kc * 128, [1, 128])

    def proj_tm(b, c0, n):
        mmg(PA(b, 0, [1, n]), [(HTk(kc), A("WIN", kc * WC + c0, [1, n])) for kc in range(8)], [ctx["HT"], "WIN"], b)

    store_ops = []
    RUN = lambda g: [None for _ in g]
    hk = lambda nm: [(nm, 0), (nm, 1)]
    H = lambda nm, h: A(nm, h * 128, [1, 128])

    def phase_a(s, t, p, X=None):
        X = X or ("X%d" % p)
        r0 = s * SEQ + t * L
        dma(A(X, 0, [1, 1024]), x_d.ap()[r0:r0 + L, :], [], [X])
        act(A("JUNK", 0, [1, 1024]), A(X, 0, [1, 1024]), AF.Square, [X], ["JUNK"])
        P.op("dve", lambda e: e.tensor_reduce(A("SS", 0, [1, 1]), A("JUNK", 0, [1, 1024]), AX.X, ALU.add),
             ["JUNK"], ["SS"], dur=1200.0)
        rsq(A("RSTD", 0, [1, 1]), A("SS", 0, [1, 1]), 1.0 / D, ["SS"], "RSTD")
        ts("dve", A("AQK", 0, [1, 1024]), A(X, 0, [1, 1024]), A("RSTD", 0, [1, 1]), None, ALU.mult, None,
           [X, "RSTD"], ["AQK"])
        yield
        b = nb()
        for kc in range(8):
            tr(PB(b, kc * 128, [1, 128]), A("AQK", kc * 128, [1, 128]), IDB(), ["AQK", "IDB"], b)
        cp(CH["ht"], A(ctx["HT"], 0, [1, 1024]), PB(b, 0, [1, 1024]), [bk(b)], [ctx["HT"]])
        yield

    def phase_d(ps_, s, t, p):
        MX = "MIX%d" % p
        r0 = s * SEQ + t * L
        src_d = x_d if ps_ == 0 else part_d
        dma(A("RES", 0, [1, 1024]), src_d.ap()[r0:r0 + L, :], [("pd", r0)] if ps_ == 1 else [], ["RES"])
        b = nb()
        for kc in range(8):
            tr(PB(b, kc * 128, [1, 128]), A(MX, kc * 128, [1, 128]), IDB(), [MX, "IDB"], b)
        cp(CH["mixt"], A("MIXT", 0, [1, 1024]), PB(b, 0, [1, 1024]), [bk(b)], ["MIXT"])
        yield
        for n in range(2):
            b = nb()
            mmg(PA(b, 0, [1, 512]),
                [(A("MIXT", kc * 128, [1, 128]), A("WOUT", kc * 1024 + n * 512, [1, 512])) for kc in range(8)],
                ["MIXT", "WOUT"], b)
            tt("dve", A("RES", n * 512, [1, 512]), PA(b, 0, [1, 512]), A("RES", n * 512, [1, 512]), ALU.add,
               [bk(b), "RES"], ["RES"])
            yield
        if ps_ == 0:
            store_ops.append(dma(part_d.ap()[r0:r0 + L, :], A("RES", 0, [1, 1024]), ["RES"], [("pd", r0)],
                                 q="pool"))
        else:
            act(A("MIXT", 0, [1, 1024]), A("RES", 0, [1, 1024]), AF.Square, ["RES"], ["MIXT"])
            P.op("dve", lambda e: e.tensor_reduce(A("SS2", 0, [1, 1]), A("MIXT", 0, [1, 1024]), AX.X, ALU.add),
                 ["MIXT"], ["SS2"], dur=1200.0)
            rsq(A("RSTD2", 0, [1, 1]), A("SS2", 0, [1, 1]), 1.0 / D, ["SS2"], "RSTD2")
            stt(A("RES", 0, [1, 1024]), A("RES", 0, [1, 1024]), A("RSTD2", 0, [1, 1]), A("GAINM", 0, [1, 1024]),
                ALU.mult, ALU.mult, ["RES", "RSTD2", "GAINM"], ["RES"])
            store_ops.append(dma(out_d.ap()[r0:r0 + L, :], A("RES", 0, [1, 1024]), ["RES"], ["out_dram"], q="pool"))
        yield

    def ml_stage1(s, t, p, first):
        QE = ("QN", "Z")[p]
        K2 = ("QE2", "TT")[p]
        K3 = ("KN", "RP")[p]
        VA = "KBEG%d" % p
        GG = "GG2%d" % p
        EC = "ECRT%d" % p
        setp(p)
        b = nb()
        mmg(PA(b, 0, [1, 8]), [(HTk(kc), A("WIN", kc * WC + 4096, [1, 8])) for kc in range(8)], [ctx["HT"], "WIN"], b)
        tt("dve", A("G8", 0, [1, 8]), PA(b, 0, [1, 8]), A("BIASM", 0, [1, 8]), ALU.add, [bk(b), "BIASM"], ["G8"])
        act(A("E1", 0, [1, 4]), A("G8", 4, [1, 4]), AF.Exp, ["G8"], ["E1"], scale=-1.0)
        act(A("LFP", 0, [1, 4]), A("E1", 0, [1, 4]), AF.Ln, ["E1"], ["LFP"], bias=1.0)
        ts("dve", A("LA", 0, [1, 4]), A("LFP", 0, [1, 4]), -1.0, None, ALU.mult, None, ["LFP"], ["LA"])
        yield
        b = nb()
        for i, cm in enumerate([C_MLE, C_MGT, C_ONE]):
            mmg(PA(b, i * 4, [1, 4]), [(CA(cm, [1, 128]), A("LA", 0, [1, 4]))], ["CONST", "LA"], b, f32=True)
        cp("dve", A("CRT", 0, [1, 12]), PA(b, 0, [1, 12]), [bk(b)], ["CRT"])
        act(A(EC, 0, [1, 12]), A("CRT", 0, [1, 12]), AF.Exp, ["CRT"], [EC])
        tt("dve", A("TA", 0, [1, 4]), A("G8", 0, [1, 4]), A("CRT", 0, [1, 4]), ALU.subtract, ["G8", "CRT"], ["TA"])
        tt("dve", A("TA", 4, [1, 4]), A("G8", 0, [1, 4]), A("CRT", 4, [1, 4]), ALU.add, ["G8", "CRT"], ["TA"])
        act(A("EIC", 0, [1, 8]), A("TA", 0, [1, 8]), AF.Exp, ["TA"], ["EIC"])
        ts("dve", A("QSC", 0, [1, 4]), A(EC, 0, [1, 4]), 128.0 ** -0.5, None, ALU.mult, None, [EC], ["QSC"])
        yield

    def ml_qk(s, t, p, first):
        QE = ("QN", "Z")[p]
        K2 = ("QE2", "TT")[p]
        K3 = ("KN", "RP")[p]
        VA = "KBEG%d" % p
        GG = "GG2%d" % p
        EC = "ECRT%d" % p
        setp(p)
        b = nb()
        proj_tm(b, 0, 512)
        tt("dve", A(QE, 0, [128, 4], [1, 128]), PA(b, 0, [128, 4], [1, 128]), A("QSC", 0, [1, 4], [0, 128]),
           ALU.mult, [bk(b), "QSC"], hk(QE))
        yield
        b = nb()
        proj_tm(b, 512, 512)
        tt("dve", A(K2, 0, [128, 4], [1, 128]), PA(b, 0, [128, 4], [1, 128]), A("EIC", 0, [1, 4], [0, 128]),
           ALU.mult, [bk(b), "EIC"], hk(K2))
        tt("dve", A(K3, 0, [128, 4], [1, 128]), PA(b, 0, [128, 4], [1, 128]), A("EIC", 4, [1, 4], [0, 128]),
           ALU.mult, [bk(b), "EIC"], hk(K3))
        yield

    def ml_voz(s, t, p, first):
        QE = ("QN", "Z")[p]
        K2 = ("QE2", "TT")[p]
        K3 = ("KN", "RP")[p]
        VA = "KBEG%d" % p
        GG = "GG2%d" % p
        EC = "ECRT%d" % p
        setp(p)
        for n in range(2):
            b = nb()
            proj_tm(b, 1024 + n * 512, 512)
            cp(CH["vaug"], A(VA, n * 514, [257, 2], [1, 256]), PA(b, 0, [256, 2], [1, 256]), [bk(b)], [VA])
            yield
        for n in range(2):
            b = nb()
            proj_tm(b, 3072 + n * 512, 512)
            act(A("GS", n * 512, [1, 512]), PA(b, 0, [1, 512]), AF.Silu, [bk(b)], ["GS"])
            yield
        tt("pool", A("GS", 0, [1, 1024]), A("GS", 0, [1, 1024]), A("GAINM", 0, [1, 1024]), ALU.mult,
           ["GS", "GAINM"], ["GS"])
        for n in range(2):
            b = nb()
            proj_tm(b, 2048 + n * 512, 512)
            act(A(GG, n * 512, [1, 512]), PA(b, 0, [1, 512]), AF.Tanh, [bk(b)], [GG], scale=0.5)
            yield
        stt(A(GG, 0, [1, 1024]), A(GG, 0, [1, 1024]), 1.0, A("GS", 0, [1, 1024]), ALU.add, ALU.mult,
            [GG, "GS"], [GG])
        yield

    def ml_stage2(s, t, p, first):
        QE = ("QN", "Z")[p]
        K2 = ("QE2", "TT")[p]
        K3 = ("KN", "RP")[p]
        VA = "KBEG%d" % p
        GG = "GG2%d" % p
        EC = "ECRT%d" % p
        if first:
            mset("pool", A("SG", 0, [1, 1028]), 0.0, ["SG"])
            mset("pool", A("SGB", 0, [1, 1040]), 0.0, ["SGB"])
        b = 0
        for h in range(4):
            tr(PB(b, h * 128, [1, 128]), A(QE, h * 128, [1, 128]), IDB(), hk(QE) + ["IDB"], b)
        for h in range(4):
            tr(PB(b, 512 + h * 128, [1, 128]), A(K2, h * 128, [1, 128]), IDB(), hk(K2) + ["IDB"], b)
        cp(CH["mqnt"], A("QNT", 0, [1, 1024]), PB(b, 0, [1, 1024]), [bk(b)], ["QNT"])
        yield
        b = 1
        for h in range(4):
            mmg(PA(b, h * 128, [1, 128]), [(A("QNT", 512 + h * 128, [1, 128]), A("QNT", h * 128, [1, 128]))],
                ["QNT"], b)
        tt("dve", A("QET0", 0, [128, 4], [1, 128]), PA(b, 0, [128, 4], [1, 128]), CA(C_MLE, [0, 4], [1, 128]),
           ALU.mult, [bk(b), "CONST"], ["QET0"])
        yield
        for h in range(4):
            mmg(PA(h, 0, [1, 257]),
                [(A("QNT", h * 128, [1, 128]), A("SGB", h * 257, [1, 257])),
                 (A("QET0", h * 128, [1, 128]), A(VA, h * 257, [1, 257]))], ["QNT", "SGB", "QET0", VA], h)
        allb = [bk(h) for h in range(4)]
        rr4 = A("RR", 0, [1, 4])
        act(rr4, PA(0, 256, [512, 4]), AF.Abs, allb, ["RR"])
        ts("dve", rr4, rr4, 1.0, None, ALU.max, None, ["RR"], ["RR"])
        P.op("dve", lambda e, rr4=rr4: e.reciprocal(rr4, rr4), ["RR"], ["RR"])
        yield
        for h in range(4):
            act(A("GAM", h * 256, [1, 256]), PA(h, 0, [1, 256]), AF.Square, [bk(h), "RR"], ["GAM"],
                scale=A("RR", h, [1, 1]))
        P.op("dve", lambda e: e.tensor_reduce(A("SSH", 0, [1, 4]), A("GAM", 0, [256, 4], [1, 256]), AX.X, ALU.add),
             ["GAM"], ["SSH"], dur=1200.0)
        rsq(A("RS", 0, [1, 4]), A("SSH", 0, [1, 4]), 1.0 / 256, ["SSH"], "RS")
        tt("dve", A("RS", 0, [1, 4]), A("RS", 0, [1, 4]), rr4, ALU.mult, ["RS", "RR"], ["RS"])
        yield
        for h in range(4):
            stt(A("MIX%d" % p, h * 256, [1, 256]), PA(h, 0, [1, 256]), A("RS", h, [1, 1]), A(GG, h * 256, [1, 256]),
                ALU.mult, ALU.mult, [bk(h), "RS", GG], ["MIX%d" % p])
        yield
        for h in range(4):
            mmg(PA(h, 0, [1, 257]), [(A(K3, h * 128, [1, 128]), A(VA, h * 257, [1, 257]))], hk(K3) + [VA], h)
            stt(A("SG", h * 257, [1, 257]), A("SG", h * 257, [1, 257]), A(EC, 8 + h, [1, 1]),
                PA(h, 0, [1, 257]), ALU.mult, ALU.add, ["SG", EC, bk(h)], ["SG"])
        cp(CH["msgb"], A("SGB", 0, [1, 1028]), A("SG", 0, [1, 1028]), ["SG"], ["SGB"])
        yield

    def dwslot(j, g):
        i = j * 3 + g
        if i < 6:
            nm = ("ACC", "TMPC", "X1")[i // 2]
            return nm, (i % 2) * 1024, T[nm + "_bf"], 2048
        return "DWA", (i - 6) * 1024, T["DWA"], 6 * 1024

    def dwap(j, g, c):
        nm, off, th, rs = dwslot(j, g)
        return nm, bass.AP(th, off + c * 128, [[rs, 128], [1, 128]])

    def build_dw():
        i = 0
        for j in range(4):
            for g in range(3):
                for c in range(8):
                    nm, ap = dwap(j, g, c)
                    ts(("dve", "pool")[i % 2], ap, IDB(), A("CW", j * 24 + g * 8 + c, [1, 1]), None, ALU.mult, None,
                       ["IDB", "CW"], [nm])
                    i += 1

    def gdn_group(g, p, first):
        KB, KR, VBn, QE_, MMn = "KBEG%d" % p, "KREV%d" % p, "VB%d" % p, "QET%d" % p, "MM%d" % p
        EC = "ECRT%d" % p
        XB = ("XB1", "XB2", "XB0")[g]
        S = ("QN", "KN", VBn)[g]
        SK = hk(S) if g < 2 else [S]
        HL = "HALO%d" % g
        if first:
            mset("pool", A("HALO", g * 24, [1, 24]), 0.0, [HL])
        cp("pool", A(XB, 0, [131, 8], [1, 3]), A("HALO", g * 24, [3, 8], [1, 3]), [HL], [XB])
        dwn = sorted(set(dwslot(j, g)[0] for j in range(4)))
        for n in range(2):
            b = nb()
            for c in range(n * 4, n * 4 + 4):
                mmg(PA(b, (c % 4) * 128, [1, 128]),
                    [(A("WIN", kc * WC + g * 1024 + c * 128, [1, 128]), HTk(kc)) for kc in range(8)],
                    ["WIN", ctx["HT"]], b)
            cp(CH["xb%d" % n], A(XB, n * 4 * 131 + 3, [131, 4], [1, 128]), PA(b, 0, [128, 4], [1, 128]), [bk(b)], [XB])
            yield
            if n == 1:
                cp("pool", A("HALO", g * 24, [3, 8], [1, 3]), A(XB, 128, [131, 8], [1, 3]), [XB], [HL])
            b2 = nb()
            for c in range(n * 4, n * 4 + 4):
                mmg(PA(b2, (c % 4) * 128, [1, 128]),
                    [(A(XB, c * 131 + j, [1, 128]), dwap(j, g, c)[1]) for j in range(4)], [XB] + dwn, b2)
            act(A(S, n * 512, [1, 512]), PA(b2, 0, [1, 512]), AF.Silu, [bk(b2)], [SK[n]] if g < 2 else [S])
            yield
        sall = A(S, 0, [128, 8], [1, 128])
        if g < 2:
            SQ, RQ = ("SSQ", "RQ") if g == 0 else ("SSQK", "RQK")
            for n in range(2):
                b = nb()
                act(PA(b, 0, [1, 512]), A(S, n * 512, [1, 512]), AF.Square, [SK[n]], [bk(b)])
                P.op("dve", lambda e, b=b, n=n: e.tensor_reduce(A(SQ, n * 4, [1, 4]), PA(b, 0, [128, 4], [1, 128]),
                                                                AX.X, ALU.add), [bk(b)], [SQ], dur=700.0)
            rsq(A(RQ, 0, [1, 8]), A(SQ, 0, [1, 8]), 1.0, [SQ], RQ)
        if g == 0:
            ts("dve", A("QS1", 0, [1, 8]), A("RQ", 0, [1, 8]), 128.0 ** -0.5, None, ALU.mult, None, ["RQ"], ["QS1"])
            tt("dve", A("QS2", 0, [1, 8]), A("QS1", 0, [1, 8]), A(EC, 0, [1, 8]), ALU.mult, ["QS1", EC], ["QS2"])
            tt("dve", A("QE2", 0, [128, 8], [1, 128]), sall, A("QS2", 0, [1, 8], [0, 128]), ALU.mult,
               SK + ["QS2"], hk("QE2"))
            tt("dve", sall, sall, A("QS1", 0, [1, 8], [0, 128]), ALU.mult, SK + ["QS1"], SK)
            yield
            for src, dst in (("QN", "QNT"), ("QE2", QE_)):
                b2 = nb()
                for c in range(8):
                    tr(PB(b2, c * 128, [1, 128]), A(src, c * 128, [1, 128]), IDB(), hk(src) + ["IDB"], b2)
                cp(CH["q_" + dst[:3]], A(dst, 0, [1, 1024]), PB(b2, 0, [1, 1024]), [bk(b2)], [dst])
                yield
        elif g == 1:
            tt("dve", A("KSB", 0, [1, 8]), A("RQK", 0, [1, 8]), A("BE", 0, [1, 8]), ALU.mult, ["RQK", "BE"], ["KSB"])
            tt("dve", A("KSR", 0, [1, 8]), A("RQK", 0, [1, 8]), A(EC, 8, [1, 8]), ALU.mult, ["RQK", EC], ["KSR"])
            tt("dve", A(KB, 0, [128, 8], [1, 128]), sall, A("KSB", 0, [1, 8], [0, 128]), ALU.mult,
               SK + ["KSB"], [KB])
            tt("dve", A(KR, 0, [128, 8], [1, 128]), sall, A("KSR", 0, [1, 8], [0, 128]), ALU.mult,
               SK + ["KSR"], [KR])
            tt("dve", sall, sall, A("RQK", 0, [1, 8], [0, 128]), ALU.mult, SK + ["RQK"], SK)
            yield
            b2 = nb()
            for c in range(8):
                tr(PB(b2, c * 128, [1, 128]), A("KN", c * 128, [1, 128]), IDB(), hk("KN") + ["IDB"], b2)
            cp(CH["knt"], A("KNT", 0, [1, 1024]), PB(b2, 0, [1, 1024]), [bk(b2)], ["KNT"])
            yield
        else:
            tt("dve", sall, sall, A(ctx["BETA"], 0, [1, 8], [0, 128]), ALU.mult, [S, ctx["BETA"]], [S])
            yield

    def per_head(pairs_fn, reads, evac):
        for n in range(2):
            b = nb()
            for hh in range(4):
                h = n * 4 + hh
                mmg(PA(b, hh * 128, [1, 128]), pairs_fn(h), reads(n) if callable(reads) else reads, b)
            evac(n, b)
            yield

    def gdn_a(s, t, p, first):
        KB, KR, VBn, QE_, MMn, AQ = "KBEG%d" % p, "KREV%d" % p, "VB%d" % p, "QET%d" % p, "MM%d" % p, "AQKT%d" % p
        EC = "ECRT%d" % p
        GG = "GG2%d" % p
        setp(p)
        b = nb()
        mmg(PA(b, 0, [1, 16]), [(HTk(kc), A("WIN", kc * WC + 4096, [1, 16])) for kc in range(8)], [ctx["HT"], "WIN"], b)
        act(A(ctx["BETA"], 0, [1, 8]), PA(b, 0, [1, 8]), AF.Exp, [bk(b)], [ctx["BETA"]], scale=-1.0)
        ts("dve", A(ctx["BETA"], 0, [1, 8]), A(ctx["BETA"], 0, [1, 8]), 1.0, None, ALU.add, None, [ctx["BETA"]], [ctx["BETA"]])
        bt_ = A(ctx["BETA"], 0, [1, 8])
        P.op("dve", lambda e, bt_=bt_: e.reciprocal(bt_, bt_), [ctx["BETA"]], [ctx["BETA"]])
        tt("dve", A("TA", 0, [1, 8]), PA(b, 8, [1, 8]), A("DTB", 0, [1, 8]), ALU.add, [bk(b), "DTB"], ["TA"])
        act(A("E1", 0, [1, 8]), A("TA", 0, [1, 8]), AF.Exp, ["TA"], ["E1"])
        act(A("LFP", 0, [1, 8]), A("E1", 0, [1, 8]), AF.Ln, ["E1"], ["LFP"], bias=1.0)
        tt("dve", A("LA", 0, [1, 8]), A("LFP", 0, [1, 8]), A("NEGA", 0, [1, 8]), ALU.mult, ["LFP", "NEGA"], ["LA"])
        yield
        b = nb()
        for i, cm in enumerate([C_MLE, C_MGT, C_ONE]):
            mmg(PA(b, i * 8, [1, 8]), [(CA(cm, [1, 128]), A("LA", 0, [1, 8]))], ["CONST", "LA"], b, f32=True)
        cp("dve", A("CRT", 0, [1, 24]), PA(b, 0, [1, 24]), [bk(b)], ["CRT"])
        act(A(EC, 0, [1, 24]), A("CRT", 0, [1, 24]), AF.Exp, ["CRT"], [EC])
        tt("dve", A("BE", 0, [1, 8]), A(ctx["BETA"], 0, [1, 8]), A(EC, 0, [1, 8]), ALU.mult, [ctx["BETA"], EC], ["BE"])
        tt("dve", A("JUNK", 0, [128, 8], [1, 128]), CA(C_MGT, [0, 8], [1, 128]), A("LA", 0, [1, 8], [0, 128]),
           ALU.mult, ["CONST", "LA"], ["JUNK"])
        yield
        for n in range(2):
            b = nb()
            mmg(PA(b, 0, [1, 512]), [(CA(C_MLE, [1, 128]), A("JUNK", n * 512, [1, 512]))], ["CONST", "JUNK"], b, f32=True)
            act(A("GAM", n * 512, [1, 512]), PA(b, 0, [1, 512]), AF.Exp, [bk(b)], ["GAM"])
            yield
        tt("pool", A("GS", 0, [128, 8], [1, 128]), A("GAM", 0, [128, 8], [1, 128]), CA(C_MGT, [0, 8], [1, 128]),
           ALU.mult, ["GAM", "CONST"], ["GS"])
        tt("pool", A("GS", 0, [128, 8], [1, 128]), A("GS", 0, [128, 8], [1, 128]), A(ctx["BETA"], 0, [1, 8], [0, 128]),
           ALU.mult, ["GS", ctx["BETA"]], ["GS"])
        tt("pool", A("GAM", 0, [128, 8], [1, 128]), A("GAM", 0, [128, 8], [1, 128]), CA(C_MGE, [0, 8], [1, 128]),
           ALU.mult, ["GAM", "CONST"], ["GAM"])
        yield

    def gdn_k(s, t, p, first):
        KB, KR, VBn, QE_, MMn, AQ = "KBEG%d" % p, "KREV%d" % p, "VB%d" % p, "QET%d" % p, "MM%d" % p, "AQKT%d" % p
        EC = "ECRT%d" % p
        GG = "GG2%d" % p
        setp(p)
        yield from gdn_group(1, p, first)
        yield from per_head(lambda h: [(H("KNT", h), H("KNT", h))], ["KNT"],
                            lambda n, b: tt("dve", A(MMn, n * 512, [1, 512]), PA(b, 0, [1, 512]),
                                            A("GS", n * 512, [1, 512]), ALU.mult, [bk(b), "GS"], [(MMn, n)]))
        for n in range(2):
            tt("pool", A(MMn, n * 512, [128, 4], [1, 128]), A(MMn, n * 512, [128, 4], [1, 128]),
               A("IDB", 0, [0, 4], [1, 128]), ALU.add, [(MMn, n), "IDB"], [(MMn, n)])
        yield

    def gdn_q(s, t, p, first):
        KB, KR, VBn, QE_, MMn, AQ = "KBEG%d" % p, "KREV%d" % p, "VB%d" % p, "QET%d" % p, "MM%d" % p, "AQKT%d" % p
        EC = "ECRT%d" % p
        GG = "GG2%d" % p
        setp(p)
        for n in range(2):
            b = nb()
            proj_tm(b, 3072 + n * 512, 512)
            act(A(GG, n * 512, [1, 512]), PA(b, 0, [1, 512]), AF.Silu, [bk(b)], [GG])
            yield
        tt("pool", A(GG, 0, [128, 8], [1, 128]), A(GG, 0, [128, 8], [1, 128]), A("GAING", 0, [0, 8], [1, 128]),
           ALU.mult, [GG, "GAING"], [GG])
        yield from gdn_group(0, p, first)
        yield from per_head(lambda h: [(H("QNT", h), H("KNT", h))], ["QNT", "KNT"],
                            lambda n, b: tt("dve", A("AQK", n * 512, [1, 512]), PA(b, 0, [1, 512]),
                                            A("GAM", n * 512, [1, 512]), ALU.mult, [bk(b), "GAM"], ["AQK"]))
        b = nb()
        for c in range(8):
            tr(PB(b, c * 128, [1, 128]), A("AQK", c * 128, [1, 128]), IDB(), ["AQK", "IDB"], b)
        cp(CH["aqkt"], A(AQ, 0, [1, 1024]), PB(b, 0, [1, 1024]), [bk(b)], [AQ])
        yield

    def gdn_v(s, t, p, first):
        setp(p)
        yield from gdn_group(2, p, first)

    def gdn_stage2(s, t, p, first):
        KB, KR, VBn, QE_, MMn, AQ = "KBEG%d" % p, "KREV%d" % p, "VB%d" % p, "QET%d" % p, "MM%d" % p, "AQKT%d" % p
        EC = "ECRT%d" % p
        GG = "GG2%d" % p
        r0 = s * SEQ + t * L
        if first:
            mset("pool", A("SG", 0, [1, 1028]), 0.0, ["SG"])
            mset("pool", A("SGB", 0, [1, 1040]), 0.0, ["SGB"])
        for n in range(2):
            b = nb()
            for hh in range(4):
                mmg(PA(b, hh * 128, [1, 128]), [(H(MMn, n * 4 + hh), IDB())], [(MMn, n), "IDB"], b)
            tt("dve", A("Z", n * 512, [128, 4], [1, 128]), PA(b, 0, [128, 4], [1, 128]),
               A("LEVB", 0, [0, 4], [1, 128]), ALU.mult, [bk(b), "LEVB"], [("Z", n)])
            tt("pool", A("TT", n * 512, [128, 4], [1, 128]), A(MMn, n * 512, [128, 4], [1, 128]),
               A("LEVB", 0, [0, 4], [1, 128]), ALU.mult, [(MMn, n), "LEVB"], [("TT", n)])
            yield
        for k in range(1, 7):
            for n in range(2):
                b = nb()
                for hh in range(4):
                    h = n * 4 + hh
                    mmg(PA(b, hh * 128, [1, 128]), [(H(MMn, h), H("Z", h))], [(MMn, n), ("Z", n)], b)
                tt("dve", A("RP", n * 512, [128, 4], [1, 128]), PA(b, 0, [128, 4], [1, 128]),
                   A("LEVB", k * 128, [0, 4], [1, 128]), ALU.mult, [bk(b), "LEVB"], [("RP", n)])
                yield
            for n in range(2):
                zb_ = nb()
                for hh in range(4):
                    h = n * 4 + hh
                    mmg(PA(zb_, hh * 128, [1, 128]), [(H("TT", h), H("RP", h))], [("TT", n), ("RP", n)], zb_)
                tb_ = None
                if k < 6:
                    tb_ = nb()
                    for hh in range(4):
                        h = n * 4 + hh
                        mmg(PA(tb_, hh * 128, [1, 128]), [(H("RP", h), H("TT", h))], [("TT", n), ("RP", n)], tb_)
                cp(CH["z%d" % n], A("Z", n * 512, [1, 512]), PA(zb_, 0, [1, 512]), [bk(zb_)], [("Z", n)])
                if k < 6:
                    cp(CH["tt%d" % n], A("TT", n * 512, [1, 512]), PA(tb_, 0, [1, 512]),
                       [bk(tb_)], [("TT", n)])
                yield
        yield from per_head(lambda h: [(H(KB, h), H("Z", h))], lambda n: [KB, ("Z", n)],
                            lambda n, b: act(A("NWT", n * 512, [1, 512]), PA(b, 0, [1, 512]), AF.Copy, [bk(b)],
                                             [("NWT", n)], scale=-1.0))
        yield from per_head(lambda h: [(H("Z", h), H(VBn, h)), (H("NWT", h), H("SGB", h))],
                            lambda n: [("Z", n), VBn, ("NWT", n), "SGB"],
                            lambda n, b: cp(CH["vnew%d" % n], A("VNEW", n * 512, [1, 512]), PA(b, 0, [1, 512]), [bk(b)],
                                            [("VNEW", n)]))
        ob = []

        def o_evac(n, b):
            ob.append(b)
            act(A("RP", n * 512, [1, 512]), PA(b, 0, [1, 512]), AF.Square, [bk(b)], [("RP", n)])
        yield from per_head(lambda h: [(H(QE_, h), H("SGB", h)), (H(AQ, h), H("VNEW", h))],
                            lambda n: [QE_, "SGB", AQ, ("VNEW", n)], o_evac)
        P.op("dve", lambda e: e.tensor_reduce(A("SSO", 0, [1, 8]), A("RP", 0, [128, 8], [1, 128]), AX.X, ALU.add),
             hk("RP"), ["SSO"], dur=1200.0)
        rsq(A("RSO", 0, [1, 8]), A("SSO", 0, [1, 8]), 1.0 / 128, ["SSO"], "RSO")
        tt("dve", A(GG, 0, [128, 8], [1, 128]), A(GG, 0, [128, 8], [1, 128]), A("RSO", 0, [1, 8], [0, 128]),
           ALU.mult, [GG, "RSO"], [GG])
        for n in range(2):
            tt("dve", A("MIX%d" % p, n * 512, [1, 512]), PA(ob[n], 0, [1, 512]), A(GG, n * 512, [1, 512]), ALU.mult,
               [bk(ob[n]), GG], ["MIX%d" % p])
        yield
        tt("pool", A("SG", 0, [128, 8], [1, 128]), A("SG", 0, [128, 8], [1, 128]), A(EC, 16, [1, 8], [0, 128]),
           ALU.mult, ["SG", EC], ["SG"])
        yield from per_head(lambda h: [(H(KR, h), H("VNEW", h))], lambda n: [KR, ("VNEW", n)],
                            lambda n, b: tt("dve", A("SG", n * 512, [1, 512]), PA(b, 0, [1, 512]),
                                            A("SG", n * 512, [1, 512]), ALU.add, [bk(b), "SG"], ["SG"]))
        cp(CH["sgb"], A("SGB", 0, [1, 1024]), A("SG", 0, [1, 1024]), ["SG"], ["SGB"])
        yield

    def collect(gen, banks):
        if gen is None:
            return []
        P.defer = []
        bset[0] = banks
        for _ in gen:
            pass
        l = P.defer
        P.defer = None
        bset[0] = list(range(8))
        return l

    def collect(gen, banks):
        P.defer = []
        bset[0] = banks
        for _ in gen:
            pass
        l = P.defer
        P.defer = None
        bset[0] = list(range(8))
        return l

    def prep_pass1():
        load_weights(1, ["ACC", "TMPC", "X1"], do_out=False)
        dma(A("GAINM", 0, [1, 1024]), bc(fn_d, 1024), [], ["GAINM"])
        build_dw()
        yield

    for ps_ in range(2):
        if ps_ == 0:
            load_weights(ps_)
        else:
            load_weights(1, ["RES", "JUNK", "X0", "GAM"], do_in=False)
        if ps_ == 0:
            mset("pool", A("KBEG0", 0, [1, 1040]), 1.0, ["KBEG0"])
            mset("pool", A("KBEG1", 0, [1, 1040]), 1.0, ["KBEG1"])
        tiles = [(s, t) for s in range(NSEQ) for t in range(NT_RUN)]
        prev = None
        prev2 = None
        hts = lambda j: "HT%d" % (j % 3)
        ctx["HT"] = hts(0)
        P.merge([collect(phase_a(tiles[0][0], tiles[0][1], 0, X="X0"), [4])])
        for i, cur in enumerate(tiles + [None, None]):
            lists = []
            pd = None
            if prev2 is not None:
                pd = (ps_, prev2[0], prev2[1], (i - 2) % 2)
            if prev is not None:
                pa = (prev[0], prev[1], (i - 1) % 2, prev[1] == 0)
                ctx["HT"] = hts(i - 1)
                if ps_ == 0:
                    lists.append(collect(ml_stage2(*pa), [0, 1, 2, 3]))
                else:
                    lists.append(collect(gdn_v(*pa), [3]))
                    lists.append(collect(gdn_stage2(*pa), [0, 1, 2]))
            if ps_ == 1 and pd is not None:
                lists.append(collect(phase_d(*pd), [3]))
            if cur is not None:
                ca = (cur[0], cur[1], i % 2, cur[1] == 0)
                ctx["HT"] = hts(i)
                if ps_ == 0:
                    lists.append(collect(ml_stage1(*ca), [4]))
                    lists.append(collect(ml_qk(*ca), [5]))
                    lists.append(collect(ml_voz(*ca), [6, 7]))
                else:
                    lists.append(collect(gdn_a(*ca), [4, 5]))
                    lists.append(collect(gdn_k(*ca), [4, 5]))
                    lists.append(collect(gdn_q(*ca), [6, 7]))
            if ps_ == 0 and pd is not None:
                lists.append(collect(phase_d(*pd), [4]))
            if ps_ == 0 and i == len(tiles):
                lists.append(collect(prep_pass1(), [5]))
            if i + 1 < len(tiles):
                nx = tiles[i + 1]
                ctx["HT"] = hts(i + 1)
                lists.insert(0, collect(phase_a(nx[0], nx[1], 0, X="X0"), [5] if ps_ == 0 else [3]))
            P.merge(lists)
            prev2 = prev
            prev = cur
    P.op("pool", None, ["out_dram"], [])
    fin = P.ops[-1]
    for so in store_ops:
        fin["deps"][so[0]] = True


    P.finalize()
    sems = {}
    for e, ng in P.ngen.items():
        for g in range(ng):
            sems[(e, g)] = es.enter_context(nc.semaphore("s_%s_%d" % (e, g)))
    for i in range(NDMA):
        sems[("dma", i)] = es.enter_context(nc.semaphore("s_dma_%d" % i))
    for i in range(NDMAP):
        sems[("dmap", i)] = es.enter_context(nc.semaphore("s_dmap_%d" % i))
    with nc.Block() as block:
        @block.sync
        def _(e):
            P.emit("sp", e, sems)

        @block.tensor
        def _(e):
            P.emit("pe", e, sems)

        @block.scalar
        def _(e):
            P.emit("act", e, sems)

        @block.vector
        def _(e):
            P.emit("dve", e, sems)

        @block.gpsimd
        def _(e):
            P.emit("pool", e, sems)
    es.close()
    return nc


NT_RUN = NT
ALPHA = 0.0
CH = {"ht": "dve", "mixt": "dve", "vaug": "act", "mqnt": "dve", "msgb": "act", "xb0": "dve", "xb1": "act",
      "q_QNT": "act", "q_QET": "act", "knt": "act", "aqkt": "dve", "z0": "dve", "z1": "dve", "tt0": "dve",
      "tt1": "dve", "vnew0": "act", "vnew1": "act", "sgb": "act"}
ZENG = ("act", "dve")
TTENG = ("act", "dve")
TBL_NS = 1300.0
TBL_PEN = 1300.0
_CACHE = {}


def kernel(x, attn_norm, w_in, m_i_bias, m_f_bias, m_out_norm, g_conv, g_a_log, g_dt_bias, g_out_norm, w_out,
           final_norm):
    f = lambda a: np.ascontiguousarray(np.asarray(a, dtype=np.float32))
    if "nc" not in _CACHE:
        _CACHE["nc"] = build_program()
    nc = _CACHE["nc"]
    x = f(x)
    shared = {
        "w_in": f(w_in).reshape(D, 8216),
        "w_out": f(w_out).reshape(2048, D),
        "attn_norm": f(attn_norm).reshape(8, 128),
        "m_i_bias": f(m_i_bias).reshape(1, 4),
        "m_f_bias": f(m_f_bias).reshape(1, 4),
        "m_out_norm": f(m_out_norm).reshape(1, 1024),
        "g_conv": f(g_conv).reshape(96, 128),
        "g_a_log": f(g_a_log).reshape(1, 8),
        "g_dt_bias": f(g_dt_bias).reshape(1, 8),
        "g_out_norm": f(g_out_norm).reshape(1, 128),
        "final_norm": f(final_norm).reshape(1, 1024),
        "consts": make_consts(),
    }
    in_maps = []
    for c in range(NCORES):
        m = dict(shared)
        m["x"] = x[c * NSEQ:(c + 1) * NSEQ].reshape(TOK, D)
        in_maps.append(m)
    res = run_bass_kernel_spmd(nc, in_maps, core_ids=list(range(NCORES)))
    outs = [np.asarray(r["out"]).reshape(NSEQ, SEQ, D) for r in res.results]
    return np.concatenate(outs, axis=0).astype(np.float32)
```

```python
import numpy as np
from contextlib import ExitStack
import concourse.bass as bass
import concourse.mybir as mybir
from concourse.bass_utils import run_bass_kernel_spmd

F32 = mybir.dt.float32
BF = mybir.dt.bfloat16
AF = mybir.ActivationFunctionType
ALU = mybir.AluOpType
AX = mybir.AxisListType

NCORES = 8
SEQ = 2048
NSEQ = 2
TOK = NSEQ * SEQ
L = 128
NT = SEQ // L
D = 1024
WC = 4112
EPS = 1e-6
GEN = 3000
NDMA = 24
NDMAP = 8
C_MLE, C_MGT, C_MGE, C_ONE, C_ID, C_LEV = 0, 128, 256, 384, 512, 640
NCONST = 640 + 7 * 128


def make_consts():
    idx = np.arange(L)
    c = np.zeros((L, NCONST), np.float32)
    c[:, C_MLE:C_MLE + L] = idx[:, None] <= idx[None, :]
    c[:, C_MGT:C_MGT + L] = idx[:, None] > idx[None, :]
    c[:, C_MGE:C_MGE + L] = idx[:, None] >= idx[None, :]
    c[:, C_ONE:C_ONE + L] = 1.0
    c[:, C_ID:C_ID + L] = np.eye(L)
    for k in range(7):
        b2 = 2 << k
        m = np.where((idx[:, None] // b2) == (idx[None, :] // b2), -1.0, 0.0)
        m[idx, idx] = 1.0
        c[:, C_LEV + k * L:C_LEV + (k + 1) * L] = m
    return c


class Prog:
    def __init__(self):
        self.ops = []
        self.lastw = {}
        self.readers = {}
        self.defer = None
        self.cur_tbl = None
        self.efree = {}
        self.wdone = {}
        self.rdone = {}

    def _norm(self, keys):
        return ["JUNK" if k == "GS" else k for k in keys]

    def op(self, eng, fn, reads=(), writes=(), dma=False, dur=300.0, tbl=None):
        reads = self._norm(reads)
        writes = self._norm(writes)
        if self.defer is not None:
            h = [None]
            self.defer.append(dict(eng=eng, fn=fn, reads=list(reads), writes=list(writes), dma=dma, dur=dur, h=h,
                                   tbl=tbl))
            return h
        self._sim(eng, reads, writes, dma, dur, tbl)
        return [self._op(eng, fn, reads, writes, dma)]

    def _est(self, eng, reads, writes, tbl=None, pen=None):
        t = self.efree.get(eng, 0.0)
        if tbl is not None and tbl != self.cur_tbl:
            t += TBL_NS if pen is None else pen
        for r in reads:
            t = max(t, self.wdone.get(r, 0.0))
        for w in writes:
            t = max(t, self.wdone.get(w, 0.0), self.rdone.get(w, 0.0))
        return t

    def _sim(self, eng, reads, writes, dma, dur, tbl=None):
        st = self._est(eng, reads, writes, tbl)
        if tbl is not None:
            self.cur_tbl = tbl
        if dma:
            self.efree[eng] = st + 100.0
            end = st + 3000.0
        else:
            end = st + dur
            self.efree[eng] = end
        for r in reads:
            self.rdone[r] = max(self.rdone.get(r, 0.0), end)
        for w in writes:
            self.wdone[w] = end + 150.0
        return st

    def merge(self, lists):
        lists = [l for l in lists if l]
        n = len(lists)
        accs = []
        for l in lists:
            wl, rl = {}, {}
            for i, d in enumerate(l):
                for r in d["reads"]:
                    rl[r] = i
                for w in d["writes"]:
                    wl[w] = i
            accs.append((wl, rl))
        pos = [0] * n
        rems = []
        for l in lists:
            r_ = [0.0] * (len(l) + 1)
            for i in range(len(l) - 1, -1, -1):
                r_[i] = r_[i + 1] + l[i]["dur"]
            rems.append(r_)

        def blocked(j, d):
            for i in range(j):
                pi = pos[i]
                if pi >= len(lists[i]):
                    continue
                wl, rl = accs[i]
                for r in d["reads"]:
                    if wl.get(r, -1) >= pi:
                        return True
                for w in d["writes"]:
                    if wl.get(w, -1) >= pi or rl.get(w, -1) >= pi:
                        return True
            return False

        while True:
            best = None
            for i, l in enumerate(lists):
                if pos[i] < len(l):
                    d = l[pos[i]]
                    if blocked(i, d):
                        continue
                    st = self._est(d["eng"], d["reads"], d["writes"], d.get("tbl"), TBL_PEN) - ALPHA * rems[i][pos[i]]
                    if best is None or st < best[0]:
                        best = (st, i)
            if best is None:
                break
            i = best[1]
            d = lists[i][pos[i]]
            pos[i] += 1
            self._sim(d["eng"], d["reads"], d["writes"], d["dma"], d["dur"], d.get("tbl"))
            d["h"][0] = self._op(d["eng"], d["fn"], d["reads"], d["writes"], d["dma"])
        assert all(pos[i] == len(lists[i]) for i in range(n))

    def _op(self, eng, fn, reads=(), writes=(), dma=False):
        idx = len(self.ops)
        deps = {}
        for r in reads:
            if r in self.lastw:
                deps[self.lastw[r]] = True
        for w in writes:
            if w in self.lastw:
                deps.setdefault(self.lastw[w], False)
            for rd in self.readers.get(w, ()):
                deps.setdefault(rd, False)
        self.ops.append(dict(eng=eng, fn=fn, deps=deps, dma=dma))
        for r in reads:
            self.readers.setdefault(r, []).append(idx)
        for w in writes:
            self.lastw[w] = idx
            self.readers[w] = []
        return idx

    def finalize(self):
        ops = self.ops
        last_dma_on_sem = {}
        ndma = 0
        npool = 0
        for i, o in enumerate(ops):
            nd = {}
            for d, raw in o["deps"].items():
                od = ops[d]
                if od["dma"]:
                    nd[d] = raw
                elif od["eng"] == o["eng"]:
                    if o["eng"] != "pe" and raw and not o["dma"]:
                        nd[d] = raw
                    elif o["dma"]:
                        nd[d] = raw
                else:
                    nd[d] = raw
            if o["dma"]:
                if o["eng"] == "pool":
                    s = ("dmap", npool % NDMAP)
                    npool += 1
                else:
                    s = ("dma", ndma % NDMA)
                    ndma += 1
                if s in last_dma_on_sem:
                    nd[last_dma_on_sem[s]] = False
                last_dma_on_sem[s] = i
                o["dsem"] = s
            o["deps"] = nd
        needed = set()
        for o in ops:
            needed.update(o["deps"].keys())
        cnt = {}
        dcnt = {}
        for i, o in enumerate(ops):
            if o["dma"]:
                dcnt[o["dsem"]] = dcnt.get(o["dsem"], 0) + 16
                o["tok"] = (o["dsem"], dcnt[o["dsem"]])
            elif i in needed:
                c = cnt.get(o["eng"], 0)
                cnt[o["eng"]] = c + 1
                o["tok"] = ((o["eng"], c // GEN), c % GEN + 1)
            else:
                o["tok"] = None
        self.ngen = {e: (c + GEN - 1) // GEN for e, c in cnt.items()}

    def emit(self, eng_name, eng, sems):
        waited = {}
        for o in self.ops:
            if o["eng"] != eng_name:
                continue
            for d in sorted(o["deps"].keys()):
                key, val = self.ops[d]["tok"]
                if waited.get(key, 0) >= val:
                    continue
                waited[key] = val
                eng.wait_ge(sems[key], val)
            if o["fn"] is None:
                continue
            ins = o["fn"](eng)
            if o["tok"] is not None:
                key, _ = o["tok"]
                ins.then_inc(sems[key], 16 if o["dma"] else 1)


def build_program():
    nc = bass.Bass("TRN2", target_bir_lowering=False)
    dt_in = lambda n, s: nc.dram_tensor(n, list(s), F32, kind="ExternalInput")
    x_d = dt_in("x", [TOK, D])
    win_d = dt_in("w_in", [D, 8216])
    wout_d = dt_in("w_out", [2048, D])
    an_d = dt_in("attn_norm", [8, 128])
    mib_d = dt_in("m_i_bias", [1, 4])
    mfb_d = dt_in("m_f_bias", [1, 4])
    mon_d = dt_in("m_out_norm", [1, 1024])
    gcv_d = dt_in("g_conv", [96, 128])
    gal_d = dt_in("g_a_log", [1, 8])
    gdt_d = dt_in("g_dt_bias", [1, 8])
    gon_d = dt_in("g_out_norm", [1, 128])
    fn_d = dt_in("final_norm", [1, 1024])
    cst_d = dt_in("consts", [L, NCONST])
    out_d = nc.dram_tensor("out", [TOK, D], F32, kind="ExternalOutput")
    part_d = nc.dram_tensor("partial", [TOK, D], F32, kind="Internal")

    P = Prog()
    es = ExitStack()
    T = {}

    def sb(name, cols, dt, parts=128):
        T[name] = es.enter_context(nc.sbuf_tensor(name, [parts, cols], dt))
        return T[name]

    sb("WIN", 8 * WC, BF)
    sb("WOUT", 8 * 1024, BF)
    sb("CONST", 640, F32)
    sb("IDB", 128, BF)
    sb("LEVB", 7 * 128, BF)
    sb("GAINM", 1024, F32)
    sb("GAING", 128, F32)
    sb("CW", 96, F32)
    sb("CWROW", 128, F32, parts=96)
    sb("AN8", 128, F32, parts=8)
    sb("GW", 8, F32)
    sb("BIASM", 8, F32)
    sb("DTB", 8, F32)
    sb("NEGA", 8, F32)
    sb("X0", 1024, F32)
    sb("X1", 1024, F32)
    sb("JUNK", 1028, F32)
    sb("HT0", 1024, BF)
    sb("HT1", 1024, BF)
    sb("HT2", 1024, BF)
    sb("MIX0", 1024, BF)
    sb("MIX1", 1024, BF)
    sb("MIXT", 1024, BF)
    sb("RES", 1028, F32)
    for n in ["SS", "RSTD", "SS2", "RSTD2"]:
        sb(n, 1, F32)
    for n in ["G8", "E1", "LFP", "LA", "CRT", "ECRT0", "ECRT1", "TA", "TB", "EIC", "EIR", "QSC", "RR", "SSH", "RS",
              "BETA0", "BETA1", "BE", "SSQ", "RQ", "SSQK", "RQK", "QS1", "QS2", "KSB", "KSR", "SSO", "RSO"]:
        sb(n, 24 if n in ("CRT", "ECRT0", "ECRT1") else 16, F32)
    sb("TMPC", 1024, F32)
    sb("HALO", 24 * 3, BF)
    sb("ACC", 1024, F32)
    sb("XB0", 8 * 131, BF)
    sb("XB1", 8 * 131, BF)
    sb("XB2", 8 * 131, BF)
    sb("DWA", 6 * 1024, BF)
    sb("QN", 1024, BF)
    sb("QE2", 1024, BF)
    sb("KN", 1024, BF)
    sb("KBEG0", 1040, BF)
    sb("KBEG1", 1040, BF)
    sb("KREV0", 1024, BF)
    sb("KREV1", 1024, BF)
    sb("VB0", 1024, BF)
    sb("VB1", 1024, BF)
    sb("QNT", 1024, BF)
    sb("QET0", 1024, BF)
    sb("QET1", 1024, BF)
    sb("KNT", 1024, BF)
    sb("GAM", 1024, F32)
    sb("MM0", 1024, BF)
    sb("MM1", 1024, BF)
    sb("AQK", 1024, BF)
    sb("AQKT0", 1024, BF)
    sb("AQKT1", 1024, BF)
    sb("Z", 1024, BF)
    sb("TT", 1024, BF)
    sb("RP", 1024, BF)
    sb("NWT", 1024, BF)
    sb("VNEW", 1024, BF)
    sb("GG20", 1024, BF)
    sb("GG21", 1024, BF)
    sb("SG", 1028, F32)
    sb("SGB", 1040, BF)
    PS = es.enter_context(nc.psum_tensor("ps", [128, 4096], F32))
    for nm_ in ("ACC", "TMPC", "X1"):
        T[nm_ + "_bf"] = T[nm_].bitcast(BF)
    PSB = PS.bitcast(BF)

    def A(name, off=0, *dims, p=128):
        if name == "GS":
            name = "JUNK"
        t = T[name]
        return bass.AP(t, off, [[t.shape[1], p]] + [list(d) for d in dims])

    def PA(b, off=0, *dims, p=128):
        return bass.AP(PS, b * 512 + off, [[4096, p]] + [list(d) for d in dims])

    def PB(b, off=0, *dims, p=128):
        return bass.AP(PSB, b * 1024 + off, [[8192, p]] + [list(d) for d in dims])

    def CA(off, *dims, p=128):
        return A("CONST", off, *dims, p=p)

    bank = [0]

    bset = [list(range(8))]
    bctr = {}

    def nb():
        key = tuple(bset[0])
        c = bctr.get(key, 0)
        bctr[key] = c + 1
        return bset[0][c % len(bset[0])]

    def nel(ap):
        n = 1
        for st_, cn in list(ap.ap)[1:]:
            n *= cn
        return n

    def bk(b):
        return ("ps", b)

    dq = [0]

    def dma(out, in_, reads, writes, q=None, slow=False):
        if q is None:
            q = "sp"
        if slow:
            fn = lambda e: e.dma_start(out=out, in_=in_, allow_slow_non_contiguous=True)
        else:
            fn = lambda e: e.dma_start(out=out, in_=in_)
        return P.op(q, fn, reads, writes, dma=True)

    def mmg(out, pairs, reads, b, f32=False):
        n = len(pairs)
        for i, (l, r) in enumerate(pairs):
            d = max(60.0, nel(r) * 0.45) * (4.0 if f32 else 1.0)
            P.op("pe", lambda e, l=l, r=r, i=i: e.matmul(out, l, r, start=(i == 0), stop=(i == n - 1)),
                 reads, [bk(b)], dur=d)

    def tr(out, in_, ident, reads, b):
        P.op("pe", lambda e: e.transpose(out, in_, ident), reads, [bk(b)], dur=60.0)

    def act(out, in_, func, reads, writes, scale=None, bias=None):
        kw = {}
        if scale is not None:
            kw["scale"] = scale
        if bias is not None:
            kw["bias"] = bias
        tbl = "silu" if func in (AF.Silu, AF.Tanh) else ("exp" if func in (AF.Exp, AF.Ln) else None)
        P.op("act", lambda e: e.activation(out, in_, func, **kw), reads, writes, dur=250.0 + 0.85 * nel(out), tbl=tbl)

    def edur(eng, out):
        return (150.0 + 2.1 * nel(out)) if eng == "pool" else (160.0 + 1.02 * nel(out))

    def tt(eng, out, in0, in1, op, reads, writes):
        P.op(eng, lambda e: e.tensor_tensor(out, in0, in1, op), reads, writes, dur=edur(eng, out))

    def ts(eng, out, in0, s1, s2, op0, op1, reads, writes):
        if op1 is None:
            P.op(eng, lambda e: e.tensor_scalar(out, in0, s1, None, op0), reads, writes, dur=edur(eng, out))
        else:
            P.op(eng, lambda e: e.tensor_scalar(out, in0, s1, s2, op0, op1), reads, writes, dur=edur(eng, out))

    def stt(out, in0, scalar, in1, op0, op1, reads, writes):
        P.op("dve", lambda e: e.scalar_tensor_tensor(out, in0, scalar, in1, op0, op1), reads, writes,
             dur=edur("dve", out))

    def cp(eng, out, in_, reads, writes):
        if eng == "act":
            act(out, in_, AF.Copy, reads, writes)
        else:
            P.op(eng, lambda e: e.tensor_copy(out, in_), reads, writes, dur=edur(eng, out))

    def mset(eng, ap, val, writes):
        P.op(eng, lambda e: e.memset(ap, val), [], writes)

    def rsq(out, in_, mul, reads, w):
        act(out, in_, AF.Ln, reads, [w], scale=float(mul), bias=EPS)
        act(out, out, AF.Exp, [w], [w], scale=-0.5)

    IDB = lambda n=128: A("IDB", 0, [1, n], p=n)

    dma(A("CONST", 0, [1, 640]), cst_d.ap()[:, 0:640], [], ["CONST"])
    dma(A("JUNK", 0, [1, 896]), cst_d.ap()[:, 640:NCONST], [], ["JUNK"])
    cp("dve", A("IDB", 0, [1, 128]), CA(C_ID, [1, 128]), ["CONST"], ["IDB"])
    cp("dve", A("LEVB", 0, [1, 896]), A("JUNK", 0, [1, 896]), ["JUNK"], ["LEVB"])
    bc = lambda d, n: bass.AP(d, 0, [[0, 128], [1, n]])
    dma(A("GAINM", 0, [1, 1024]), bc(mon_d, 1024), [], ["GAINM"])
    dma(A("GAING", 0, [1, 128]), bc(gon_d, 128), [], ["GAING"])
    ts("pool", A("GAINM", 0, [1, 1024]), A("GAINM", 0, [1, 1024]), 0.5, None, ALU.mult, None, ["GAINM"], ["GAINM"])
    dma(A("BIASM", 0, [1, 4]), bc(mib_d, 4), [], ["BIASM"])
    dma(A("BIASM", 4, [1, 4]), bc(mfb_d, 4), [], ["BIASM"])
    dma(A("DTB", 0, [1, 8]), bc(gdt_d, 8), [], ["DTB"])
    dma(A("NEGA", 0, [1, 8]), bc(gal_d, 8), [], ["NEGA"])
    act(A("NEGA", 0, [1, 8]), A("NEGA", 0, [1, 8]), AF.Exp, ["NEGA"], ["NEGA"])
    ts("dve", A("NEGA", 0, [1, 8]), A("NEGA", 0, [1, 8]), -1.0, None, ALU.mult, None, ["NEGA"], ["NEGA"])
    dma(A("AN8", 0, [1, 128], p=8), an_d.ap(), [], ["AN8"])
    dma(A("CWROW", 0, [1, 128], p=96), gcv_d.ap(), [], ["CWROW"])
    b = nb()
    tr(PA(b, 0, [1, 8]), A("AN8", 0, [1, 128], p=8), CA(C_ID, [1, 8], p=8), ["AN8", "CONST"], b)
    cp("dve", A("GW", 0, [1, 8]), PA(b, 0, [1, 8]), [bk(b)], ["GW"])
    b = nb()
    tr(PA(b, 0, [1, 96]), A("CWROW", 0, [1, 128], p=96), CA(C_ID, [1, 96], p=96), ["CWROW", "CONST"], b)
    cp("dve", A("CW", 0, [1, 96]), PA(b, 0, [1, 96]), [bk(b)], ["CW"])

    cvt_i = [0]

    STGS = ["RES", "JUNK", "ACC", "TMPC", "X0", "X1", "GAM"]

    def load_weights(ps_, STGS=STGS, do_in=True, do_out=True):
        c0 = 0 if ps_ == 0 else 4104
        wtot = 4104 if ps_ == 0 else 4112
        chunks = [(j * 1024, 1024) for j in range(4)] + [(4096, wtot - 4096)]
        for kc in (range(8) if do_in else []):
            for (cj, w) in chunks:
                i = cvt_i[0]
                cvt_i[0] += 1
                stg = STGS[i % len(STGS)]
                dma(A(stg, 0, [1, w]), win_d.ap()[kc * 128:(kc + 1) * 128, c0 + cj:c0 + cj + w], [], [stg])
                dst = A("WIN", kc * WC + cj, [1, w])
                if i % 2 == 0:
                    act(dst, A(stg, 0, [1, w]), AF.Copy, [stg, "GW"], ["WIN"], scale=A("GW", kc, [1, 1]))
                else:
                    ts("dve", dst, A(stg, 0, [1, w]), A("GW", kc, [1, 1]), None, ALU.mult, None, [stg, "GW"], ["WIN"])
        if not do_out:
            return
        r0 = ps_ * 1024
        for kc in range(8):
            i = cvt_i[0]
            cvt_i[0] += 1
            stg = STGS[i % len(STGS)]
            dma(A(stg, 0, [1, 1024]), wout_d.ap()[r0 + kc * 128:r0 + (kc + 1) * 128, :], [], [stg])
            cp(("act", "dve")[i % 2], A("WOUT", kc * 1024, [1, 1024]), A(stg, 0, [1, 1024]), [stg], ["WOUT"])

    ctx = {"HT": "HT0", "BETA": "BETA0"}

    def setp(p):
        ctx["BETA"] = "BETA%d" % p

    def HTk(kc):
        return A(ctx["HT"], kc * 128, [1, 128])

    def proj_tm(b, c0, n):
        mmg(PA(b, 0, [1, n]), [(HTk(kc), A("WIN", kc * WC + c0, [1, n])) for kc in range(8)], [ctx["HT"], "WIN"], b)

    store_ops = []
    RUN = lambda g: [None for _ in g]
    hk = lambda nm: [(nm, 0), (nm, 1)]
    H = lambda nm, h: A(nm, h * 128, [1, 128])

    def phase_a(s, t, p, X=None):
        X = X or ("X%d" % p)
        r0 = s * SEQ + t * L
        dma(A(X, 0, [1, 1024]), x_d.ap()[r0:r0 + L, :], [], [X])
        act(A("JUNK", 0, [1, 1024]), A(X, 0, [1, 1024]), AF.Square, [X], ["JUNK"])
        P.op("dve", lambda e: e.tensor_reduce(A("SS", 0, [1, 1]), A("JUNK", 0, [1, 1024]), AX.X, ALU.add),
             ["JUNK"], ["SS"], dur=1200.0)
        rsq(A("RSTD", 0, [1, 1]), A("SS", 0, [1, 1]), 1.0 / D, ["SS"], "RSTD")
        ts(CH["xn"], A("AQK", 0, [1, 1024]), A(X, 0, [1, 1024]), A("RSTD", 0, [1, 1]), None, ALU.mult, None,
           [X, "RSTD"], ["AQK"])
        yield
        b = nb()
        for kc in range(8):
            tr(PB(b, kc * 128, [1, 128]), A("AQK", kc * 128, [1, 128]), IDB(), ["AQK", "IDB"], b)
        cp(CH["ht"], A(ctx["HT"], 0, [1, 1024]), PB(b, 0, [1, 1024]), [bk(b)], [ctx["HT"]])
        yield

    def phase_d(ps_, s, t, p):
        MX = "MIX%d" % p
        r0 = s * SEQ + t * L
        src_d = x_d if ps_ == 0 else part_d
        dma(A("RES", 0, [1, 1024]), src_d.ap()[r0:r0 + L, :], [("pd", r0)] if ps_ == 1 else [], ["RES"])
        b = nb()
        for kc in range(8):
            tr(PB(b, kc * 128, [1, 128]), A(MX, kc * 128, [1, 128]), IDB(), [MX, "IDB"], b)
        cp(CH["mixt"], A("MIXT", 0, [1, 1024]), PB(b, 0, [1, 1024]), [bk(b)], ["MIXT"])
        yield
        for n in range(2):
            b = nb()
            mmg(PA(b, 0, [1, 512]),
                [(A("MIXT", kc * 128, [1, 128]), A("WOUT", kc * 1024 + n * 512, [1, 512])) for kc in range(8)],
                ["MIXT", "WOUT"], b)
            tt("dve", A("RES", n * 512, [1, 512]), PA(b, 0, [1, 512]), A("RES", n * 512, [1, 512]), ALU.add,
               [bk(b), "RES"], ["RES"])
            yield
        if ps_ == 0:
            store_ops.append(dma(part_d.ap()[r0:r0 + L, :], A("RES", 0, [1, 1024]), ["RES"], [("pd", r0)],
                                 q="pool"))
        else:
            act(A("MIXT", 0, [1, 1024]), A("RES", 0, [1, 1024]), AF.Square, ["RES"], ["MIXT"])
            P.op("dve", lambda e: e.tensor_reduce(A("SS2", 0, [1, 1]), A("MIXT", 0, [1, 1024]), AX.X, ALU.add),
                 ["MIXT"], ["SS2"], dur=1200.0)
            rsq(A("RSTD2", 0, [1, 1]), A("SS2", 0, [1, 1]), 1.0 / D, ["SS2"], "RSTD2")
            stt(A("RES", 0, [1, 1024]), A("RES", 0, [1, 1024]), A("RSTD2", 0, [1, 1]), A("GAINM", 0, [1, 1024]),
                ALU.mult, ALU.mult, ["RES", "RSTD2", "GAINM"], ["RES"])
            store_ops.append(dma(out_d.ap()[r0:r0 + L, :], A("RES", 0, [1, 1024]), ["RES"], ["out_dram"], q="pool"))
        yield

    def ml_stage1(s, t, p, first):
        QE = ("QN", "Z")[p]
        K2 = ("QE2", "TT")[p]
        K3 = ("KN", "RP")[p]
        VA = "KBEG%d" % p
        GG = "GG2%d" % p
        EC = "ECRT%d" % p
        setp(p)
        b = nb()
        mmg(PA(b, 0, [1, 8]), [(HTk(kc), A("WIN", kc * WC + 4096, [1, 8])) for kc in range(8)], [ctx["HT"], "WIN"], b)
        tt("dve", A("G8", 0, [1, 8]), PA(b, 0, [1, 8]), A("BIASM", 0, [1, 8]), ALU.add, [bk(b), "BIASM"], ["G8"])
        act(A("E1", 0, [1, 4]), A("G8", 4, [1, 4]), AF.Exp, ["G8"], ["E1"], scale=-1.0)
        act(A("LFP", 0, [1, 4]), A("E1", 0, [1, 4]), AF.Ln, ["E1"], ["LFP"], bias=1.0)
        ts("dve", A("LA", 0, [1, 4]), A("LFP", 0, [1, 4]), -1.0, None, ALU.mult, None, ["LFP"], ["LA"])
        yield
        b = nb()
        for i, cm in enumerate([C_MLE, C_MGT, C_ONE]):
            mmg(PA(b, i * 4, [1, 4]), [(CA(cm, [1, 128]), A("LA", 0, [1, 4]))], ["CONST", "LA"], b, f32=True)
        cp("dve", A("CRT", 0, [1, 12]), PA(b, 0, [1, 12]), [bk(b)], ["CRT"])
        act(A(EC, 0, [1, 12]), A("CRT", 0, [1, 12]), AF.Exp, ["CRT"], [EC])
        tt("dve", A("TA", 0, [1, 4]), A("G8", 0, [1, 4]), A("CRT", 0, [1, 4]), ALU.subtract, ["G8", "CRT"], ["TA"])
        tt("dve", A("TA", 4, [1, 4]), A("G8", 0, [1, 4]), A("CRT", 4, [1, 4]), ALU.add, ["G8", "CRT"], ["TA"])
        act(A("EIC", 0, [1, 8]), A("TA", 0, [1, 8]), AF.Exp, ["TA"], ["EIC"])
        ts("dve", A("QSC", 0, [1, 4]), A(EC, 0, [1, 4]), 128.0 ** -0.5, None, ALU.mult, None, [EC], ["QSC"])
        yield

    def ml_qk(s, t, p, first):
        QE = ("QN", "Z")[p]
        K2 = ("QE2", "TT")[p]
        K3 = ("KN", "RP")[p]
        VA = "KBEG%d" % p
        GG = "GG2%d" % p
        EC = "ECRT%d" % p
        setp(p)
        b = nb()
        proj_tm(b, 0, 512)
        tt("dve", A(QE, 0, [128, 4], [1, 128]), PA(b, 0, [128, 4], [1, 128]), A("QSC", 0, [1, 4], [0, 128]),
           ALU.mult, [bk(b), "QSC"], hk(QE))
        yield
        b = nb()
        proj_tm(b, 512, 512)
        tt("dve", A(K2, 0, [128, 4], [1, 128]), PA(b, 0, [128, 4], [1, 128]), A("EIC", 0, [1, 4], [0, 128]),
           ALU.mult, [bk(b), "EIC"], hk(K2))
        tt("dve", A(K3, 0, [128, 4], [1, 128]), PA(b, 0, [128, 4], [1, 128]), A("EIC", 4, [1, 4], [0, 128]),
           ALU.mult, [bk(b), "EIC"], hk(K3))
        yield

    def ml_voz(s, t, p, first):
        QE = ("QN", "Z")[p]
        K2 = ("QE2", "TT")[p]
        K3 = ("KN", "RP")[p]
        VA = "KBEG%d" % p
        GG = "GG2%d" % p
        EC = "ECRT%d" % p
        setp(p)
        for n in range(2):
            b = nb()
            proj_tm(b, 1024 + n * 512, 512)
            cp(CH["vaug"], A(VA, n * 514, [257, 2], [1, 256]), PA(b, 0, [256, 2], [1, 256]), [bk(b)], [VA])
            yield
        for n in range(2):
            b = nb()
            proj_tm(b, 3072 + n * 512, 512)
            act(A("GS", n * 512, [1, 512]), PA(b, 0, [1, 512]), AF.Silu, [bk(b)], ["GS"])
            yield
        tt(CH["mgs"], A("GS", 0, [1, 1024]), A("GS", 0, [1, 1024]), A("GAINM", 0, [1, 1024]), ALU.mult,
           ["GS", "GAINM"], ["GS"])
        for n in range(2):
            b = nb()
            proj_tm(b, 2048 + n * 512, 512)
            act(A(GG, n * 512, [1, 512]), PA(b, 0, [1, 512]), AF.Tanh, [bk(b)], [GG], scale=0.5)
            yield
        stt(A(GG, 0, [1, 1024]), A(GG, 0, [1, 1024]), 1.0, A("GS", 0, [1, 1024]), ALU.add, ALU.mult,
            [GG, "GS"], [GG])
        yield

    def ml_stage2(s, t, p, first):
        QE = ("QN", "Z")[p]
        K2 = ("QE2", "TT")[p]
        K3 = ("KN", "RP")[p]
        VA = "KBEG%d" % p
        GG = "GG2%d" % p
        EC = "ECRT%d" % p
        if first:
            mset("pool", A("SG", 0, [1, 1028]), 0.0, ["SG"])
            mset("pool", A("SGB", 0, [1, 1040]), 0.0, ["SGB"])
        b = 0
        for h in range(4):
            tr(PB(b, h * 128, [1, 128]), A(QE, h * 128, [1, 128]), IDB(), hk(QE) + ["IDB"], b)
        for h in range(4):
            tr(PB(b, 512 + h * 128, [1, 128]), A(K2, h * 128, [1, 128]), IDB(), hk(K2) + ["IDB"], b)
        cp(CH["mqnt"], A("QNT", 0, [1, 1024]), PB(b, 0, [1, 1024]), [bk(b)], ["QNT"])
        yield
        b = 1
        for h in range(4):
            mmg(PA(b, h * 128, [1, 128]), [(A("QNT", 512 + h * 128, [1, 128]), A("QNT", h * 128, [1, 128]))],
                ["QNT"], b)
        tt("dve", A("QET0", 0, [128, 4], [1, 128]), PA(b, 0, [128, 4], [1, 128]), CA(C_MLE, [0, 4], [1, 128]),
           ALU.mult, [bk(b), "CONST"], ["QET0"])
        yield
        for h in range(4):
            mmg(PA(h, 0, [1, 257]),
                [(A("QNT", h * 128, [1, 128]), A("SGB", h * 257, [1, 257])),
                 (A("QET0", h * 128, [1, 128]), A(VA, h * 257, [1, 257]))], ["QNT", "SGB", "QET0", VA], h)
        allb = [bk(h) for h in range(4)]
        rr4 = A("RR", 0, [1, 4])
        act(rr4, PA(0, 256, [512, 4]), AF.Abs, allb, ["RR"])
        ts("dve", rr4, rr4, 1.0, None, ALU.max, None, ["RR"], ["RR"])
        P.op("dve", lambda e, rr4=rr4: e.reciprocal(rr4, rr4), ["RR"], ["RR"])
        yield
        for h in range(4):
            act(A("GAM", h * 256, [1, 256]), PA(h, 0, [1, 256]), AF.Square, [bk(h), "RR"], ["GAM"],
                scale=A("RR", h, [1, 1]))
        P.op("dve", lambda e: e.tensor_reduce(A("SSH", 0, [1, 4]), A("GAM", 0, [256, 4], [1, 256]), AX.X, ALU.add),
             ["GAM"], ["SSH"], dur=1200.0)
        rsq(A("RS", 0, [1, 4]), A("SSH", 0, [1, 4]), 1.0 / 256, ["SSH"], "RS")
        tt("dve", A("RS", 0, [1, 4]), A("RS", 0, [1, 4]), rr4, ALU.mult, ["RS", "RR"], ["RS"])
        yield
        for h in range(4):
            stt(A("MIX%d" % p, h * 256, [1, 256]), PA(h, 0, [1, 256]), A("RS", h, [1, 1]), A(GG, h * 256, [1, 256]),
                ALU.mult, ALU.mult, [bk(h), "RS", GG], ["MIX%d" % p])
        yield
        for h in range(4):
            mmg(PA(h, 0, [1, 257]), [(A(K3, h * 128, [1, 128]), A(VA, h * 257, [1, 257]))], hk(K3) + [VA], h)
            stt(A("SG", h * 257, [1, 257]), A("SG", h * 257, [1, 257]), A(EC, 8 + h, [1, 1]),
                PA(h, 0, [1, 257]), ALU.mult, ALU.add, ["SG", EC, bk(h)], ["SG"])
        cp(CH["msgb"], A("SGB", 0, [1, 1028]), A("SG", 0, [1, 1028]), ["SG"], ["SGB"])
        yield

    def dwslot(j, g):
        i = j * 3 + g
        if i < 6:
            nm = ("ACC", "TMPC", "X1")[i // 2]
            return nm, (i % 2) * 1024, T[nm + "_bf"], 2048
        return "DWA", (i - 6) * 1024, T["DWA"], 6 * 1024

    def dwap(j, g, c):
        nm, off, th, rs = dwslot(j, g)
        return nm, bass.AP(th, off + c * 128, [[rs, 128], [1, 128]])

    def build_dw():
        i = 0
        for j in range(4):
            for g in range(3):
                for c in range(8):
                    nm, ap = dwap(j, g, c)
                    ts(("dve", "pool")[i % 2], ap, IDB(), A("CW", j * 24 + g * 8 + c, [1, 1]), None, ALU.mult, None,
                       ["IDB", "CW"], [nm])
                    i += 1

    def gdn_group(g, p, first):
        KB, KR, VBn, QE_, MMn = "KBEG%d" % p, "KREV%d" % p, "VB%d" % p, "QET%d" % p, "MM%d" % p
        EC = "ECRT%d" % p
        XB = ("XB1", "XB2", "XB0")[g]
        S = ("QN", "KN", VBn)[g]
        SK = hk(S) if g < 2 else [S]
        HL = "HALO%d" % g
        if first:
            mset("pool", A("HALO", g * 24, [1, 24]), 0.0, [HL])
        cp("pool", A(XB, 0, [131, 8], [1, 3]), A("HALO", g * 24, [3, 8], [1, 3]), [HL], [XB])
        dwn = sorted(set(dwslot(j, g)[0] for j in range(4)))
        for n in range(2):
            b = nb()
            for c in range(n * 4, n * 4 + 4):
                mmg(PA(b, (c % 4) * 128, [1, 128]),
                    [(A("WIN", kc * WC + g * 1024 + c * 128, [1, 128]), HTk(kc)) for kc in range(8)],
                    ["WIN", ctx["HT"]], b)
            cp(CH["xb%d" % n], A(XB, n * 4 * 131 + 3, [131, 4], [1, 128]), PA(b, 0, [128, 4], [1, 128]), [bk(b)], [XB])
            yield
            if n == 1:
                cp("pool", A("HALO", g * 24, [3, 8], [1, 3]), A(XB, 128, [131, 8], [1, 3]), [XB], [HL])
            b2 = nb()
            for c in range(n * 4, n * 4 + 4):
                mmg(PA(b2, (c % 4) * 128, [1, 128]),
                    [(A(XB, c * 131 + j, [1, 128]), dwap(j, g, c)[1]) for j in range(4)], [XB] + dwn, b2)
            act(A(S, n * 512, [1, 512]), PA(b2, 0, [1, 512]), AF.Silu, [bk(b2)], [SK[n]] if g < 2 else [S])
            yield
        sall = A(S, 0, [128, 8], [1, 128])
        if g < 2:
            SQ, RQ = ("SSQ", "RQ") if g == 0 else ("SSQK", "RQK")
            for n in range(2):
                b = nb()
                act(PA(b, 0, [1, 512]), A(S, n * 512, [1, 512]), AF.Square, [SK[n]], [bk(b)])
                P.op("dve", lambda e, b=b, n=n: e.tensor_reduce(A(SQ, n * 4, [1, 4]), PA(b, 0, [128, 4], [1, 128]),
                                                                AX.X, ALU.add), [bk(b)], [SQ], dur=700.0)
            rsq(A(RQ, 0, [1, 8]), A(SQ, 0, [1, 8]), 1.0, [SQ], RQ)
        if g == 0:
            ts("dve", A("QS1", 0, [1, 8]), A("RQ", 0, [1, 8]), 128.0 ** -0.5, None, ALU.mult, None, ["RQ"], ["QS1"])
            tt("dve", A("QS2", 0, [1, 8]), A("QS1", 0, [1, 8]), A(EC, 0, [1, 8]), ALU.mult, ["QS1", EC], ["QS2"])
            tt(CH["qe2"], A("QE2", 0, [128, 8], [1, 128]), sall, A("QS2", 0, [1, 8], [0, 128]), ALU.mult,
               SK + ["QS2"], hk("QE2"))
            tt(CH["qn"], sall, sall, A("QS1", 0, [1, 8], [0, 128]), ALU.mult, SK + ["QS1"], SK)
            yield
            for src, dst in (("QN", "QNT"), ("QE2", QE_)):
                b2 = nb()
                for c in range(8):
                    tr(PB(b2, c * 128, [1, 128]), A(src, c * 128, [1, 128]), IDB(), hk(src) + ["IDB"], b2)
                cp(CH["q_" + dst[:3]], A(dst, 0, [1, 1024]), PB(b2, 0, [1, 1024]), [bk(b2)], [dst])
                yield
        elif g == 1:
            tt("dve", A("KSB", 0, [1, 8]), A("RQK", 0, [1, 8]), A("BE", 0, [1, 8]), ALU.mult, ["RQK", "BE"], ["KSB"])
            tt("dve", A("KSR", 0, [1, 8]), A("RQK", 0, [1, 8]), A(EC, 8, [1, 8]), ALU.mult, ["RQK", EC], ["KSR"])
            tt(CH["kb"], A(KB, 0, [128, 8], [1, 128]), sall, A("KSB", 0, [1, 8], [0, 128]), ALU.mult,
               SK + ["KSB"], [KB])
            tt(CH["kr"], A(KR, 0, [128, 8], [1, 128]), sall, A("KSR", 0, [1, 8], [0, 128]), ALU.mult,
               SK + ["KSR"], [KR])
            tt(CH["kn"], sall, sall, A("RQK", 0, [1, 8], [0, 128]), ALU.mult, SK + ["RQK"], SK)
            yield
            b2 = nb()
            for c in range(8):
                tr(PB(b2, c * 128, [1, 128]), A("KN", c * 128, [1, 128]), IDB(), hk("KN") + ["IDB"], b2)
            cp(CH["knt"], A("KNT", 0, [1, 1024]), PB(b2, 0, [1, 1024]), [bk(b2)], ["KNT"])
            yield
        else:
            tt(CH["vb"], sall, sall, A(ctx["BETA"], 0, [1, 8], [0, 128]), ALU.mult, [S, ctx["BETA"]], [S])
            yield

    def per_head(pairs_fn, reads, evac):
        for n in range(2):
            b = nb()
            for hh in range(4):
                h = n * 4 + hh
                mmg(PA(b, hh * 128, [1, 128]), pairs_fn(h), reads(n) if callable(reads) else reads, b)
            evac(n, b)
            yield

    def gdn_a(s, t, p, first):
        KB, KR, VBn, QE_, MMn, AQ = "KBEG%d" % p, "KREV%d" % p, "VB%d" % p, "QET%d" % p, "MM%d" % p, "AQKT%d" % p
        EC = "ECRT%d" % p
        GG = "GG2%d" % p
        setp(p)
        b = nb()
        mmg(PA(b, 0, [1, 16]), [(HTk(kc), A("WIN", kc * WC + 4096, [1, 16])) for kc in range(8)], [ctx["HT"], "WIN"], b)
        act(A(ctx["BETA"], 0, [1, 8]), PA(b, 0, [1, 8]), AF.Exp, [bk(b)], [ctx["BETA"]], scale=-1.0)
        ts("dve", A(ctx["BETA"], 0, [1, 8]), A(ctx["BETA"], 0, [1, 8]), 1.0, None, ALU.add, None, [ctx["BETA"]], [ctx["BETA"]])
        bt_ = A(ctx["BETA"], 0, [1, 8])
        P.op("dve", lambda e, bt_=bt_: e.reciprocal(bt_, bt_), [ctx["BETA"]], [ctx["BETA"]])
        tt("dve", A("TA", 0, [1, 8]), PA(b, 8, [1, 8]), A("DTB", 0, [1, 8]), ALU.add, [bk(b), "DTB"], ["TA"])
        act(A("E1", 0, [1, 8]), A("TA", 0, [1, 8]), AF.Exp, ["TA"], ["E1"])
        act(A("LFP", 0, [1, 8]), A("E1", 0, [1, 8]), AF.Ln, ["E1"], ["LFP"], bias=1.0)
        tt("dve", A("LA", 0, [1, 8]), A("LFP", 0, [1, 8]), A("NEGA", 0, [1, 8]), ALU.mult, ["LFP", "NEGA"], ["LA"])
        yield
        b = nb()
        for i, cm in enumerate([C_MLE, C_MGT, C_ONE]):
            mmg(PA(b, i * 8, [1, 8]), [(CA(cm, [1, 128]), A("LA", 0, [1, 8]))], ["CONST", "LA"], b, f32=True)
        cp("dve", A("CRT", 0, [1, 24]), PA(b, 0, [1, 24]), [bk(b)], ["CRT"])
        act(A(EC, 0, [1, 24]), A("CRT", 0, [1, 24]), AF.Exp, ["CRT"], [EC])
        tt("dve", A("BE", 0, [1, 8]), A(ctx["BETA"], 0, [1, 8]), A(EC, 0, [1, 8]), ALU.mult, [ctx["BETA"], EC], ["BE"])
        tt(CH["rgt"], A("JUNK", 0, [128, 8], [1, 128]), CA(C_MGT, [0, 8], [1, 128]), A("LA", 0, [1, 8], [0, 128]),
           ALU.mult, ["CONST", "LA"], ["JUNK"])
        yield
        for n in range(2):
            b = nb()
            mmg(PA(b, 0, [1, 512]), [(CA(C_MLE, [1, 128]), A("JUNK", n * 512, [1, 512]))], ["CONST", "JUNK"], b, f32=True)
            act(A("GAM", n * 512, [1, 512]), PA(b, 0, [1, 512]), AF.Exp, [bk(b)], ["GAM"])
            yield
        tt(CH["gs1"], A("GS", 0, [128, 8], [1, 128]), A("GAM", 0, [128, 8], [1, 128]), CA(C_MGT, [0, 8], [1, 128]),
           ALU.mult, ["GAM", "CONST"], ["GS"])
        tt(CH["gs2"], A("GS", 0, [128, 8], [1, 128]), A("GS", 0, [128, 8], [1, 128]), A(ctx["BETA"], 0, [1, 8], [0, 128]),
           ALU.mult, ["GS", ctx["BETA"]], ["GS"])
        tt(CH["gam"], A("GAM", 0, [128, 8], [1, 128]), A("GAM", 0, [128, 8], [1, 128]), CA(C_MGE, [0, 8], [1, 128]),
           ALU.mult, ["GAM", "CONST"], ["GAM"])
        yield

    def gdn_k(s, t, p, first):
        KB, KR, VBn, QE_, MMn, AQ = "KBEG%d" % p, "KREV%d" % p, "VB%d" % p, "QET%d" % p, "MM%d" % p, "AQKT%d" % p
        EC = "ECRT%d" % p
        GG = "GG2%d" % p
        setp(p)
        yield from gdn_group(1, p, first)
        yield from per_head(lambda h: [(H("KNT", h), H("KNT", h))], ["KNT"],
                            lambda n, b: tt("dve", A(MMn, n * 512, [1, 512]), PA(b, 0, [1, 512]),
                                            A("GS", n * 512, [1, 512]), ALU.mult, [bk(b), "GS"], [(MMn, n)]))
        for n in range(2):
            tt(CH["mmi"], A(MMn, n * 512, [128, 4], [1, 128]), A(MMn, n * 512, [128, 4], [1, 128]),
               A("IDB", 0, [0, 4], [1, 128]), ALU.add, [(MMn, n), "IDB"], [(MMn, n)])
        yield

    def gdn_q(s, t, p, first):
        KB, KR, VBn, QE_, MMn, AQ = "KBEG%d" % p, "KREV%d" % p, "VB%d" % p, "QET%d" % p, "MM%d" % p, "AQKT%d" % p
        EC = "ECRT%d" % p
        GG = "GG2%d" % p
        setp(p)
        for n in range(2):
            b = nb()
            proj_tm(b, 3072 + n * 512, 512)
            act(A(GG, n * 512, [1, 512]), PA(b, 0, [1, 512]), AF.Silu, [bk(b)], [GG])
            yield
        tt(CH["ggain"], A(GG, 0, [128, 8], [1, 128]), A(GG, 0, [128, 8], [1, 128]), A("GAING", 0, [0, 8], [1, 128]),
           ALU.mult, [GG, "GAING"], [GG])
        yield from gdn_group(0, p, first)
        yield from per_head(lambda h: [(H("QNT", h), H("KNT", h))], ["QNT", "KNT"],
                            lambda n, b: tt("dve", A("AQK", n * 512, [1, 512]), PA(b, 0, [1, 512]),
                                            A("GAM", n * 512, [1, 512]), ALU.mult, [bk(b), "GAM"], ["AQK"]))
        b = nb()
        for c in range(8):
            tr(PB(b, c * 128, [1, 128]), A("AQK", c * 128, [1, 128]), IDB(), ["AQK", "IDB"], b)
        cp(CH["aqkt"], A(AQ, 0, [1, 1024]), PB(b, 0, [1, 1024]), [bk(b)], [AQ])
        yield

    def gdn_v(s, t, p, first):
        setp(p)
        yield from gdn_group(2, p, first)

    def gdn_stage2(s, t, p, first):
        KB, KR, VBn, QE_, MMn, AQ = "KBEG%d" % p, "KREV%d" % p, "VB%d" % p, "QET%d" % p, "MM%d" % p, "AQKT%d" % p
        EC = "ECRT%d" % p
        GG = "GG2%d" % p
        r0 = s * SEQ + t * L
        if first:
            mset("pool", A("SG", 0, [1, 1028]), 0.0, ["SG"])
            mset("pool", A("SGB", 0, [1, 1040]), 0.0, ["SGB"])
        for n in range(2):
            b = nb()
            for hh in range(4):
                mmg(PA(b, hh * 128, [1, 128]), [(H(MMn, n * 4 + hh), IDB())], [(MMn, n), "IDB"], b)
            tt("dve", A("Z", n * 512, [128, 4], [1, 128]), PA(b, 0, [128, 4], [1, 128]),
               A("LEVB", 0, [0, 4], [1, 128]), ALU.mult, [bk(b), "LEVB"], [("Z", n)])
            tt(CH["ttl0"], A("TT", n * 512, [128, 4], [1, 128]), A(MMn, n * 512, [128, 4], [1, 128]),
               A("LEVB", 0, [0, 4], [1, 128]), ALU.mult, [(MMn, n), "LEVB"], [("TT", n)])
            yield
        for k in range(1, 7):
            for n in range(2):
                b = nb()
                for hh in range(4):
                    h = n * 4 + hh
                    mmg(PA(b, hh * 128, [1, 128]), [(H(MMn, h), H("Z", h))], [(MMn, n), ("Z", n)], b)
                tt("dve", A("RP", n * 512, [128, 4], [1, 128]), PA(b, 0, [128, 4], [1, 128]),
                   A("LEVB", k * 128, [0, 4], [1, 128]), ALU.mult, [bk(b), "LEVB"], [("RP", n)])
                yield
            for n in range(2):
                zb_ = nb()
                for hh in range(4):
                    h = n * 4 + hh
                    mmg(PA(zb_, hh * 128, [1, 128]), [(H("TT", h), H("RP", h))], [("TT", n), ("RP", n)], zb_)
                tb_ = None
                if k < 6:
                    tb_ = nb()
                    for hh in range(4):
                        h = n * 4 + hh
                        mmg(PA(tb_, hh * 128, [1, 128]), [(H("RP", h), H("TT", h))], [("TT", n), ("RP", n)], tb_)
                cp(CH["z%d" % n], A("Z", n * 512, [1, 512]), PA(zb_, 0, [1, 512]), [bk(zb_)], [("Z", n)])
                if k < 6:
                    cp(CH["tt%d" % n], A("TT", n * 512, [1, 512]), PA(tb_, 0, [1, 512]),
                       [bk(tb_)], [("TT", n)])
                yield
        yield from per_head(lambda h: [(H(KB, h), H("Z", h))], lambda n: [KB, ("Z", n)],
                            lambda n, b: act(A("NWT", n * 512, [1, 512]), PA(b, 0, [1, 512]), AF.Copy, [bk(b)],
                                             [("NWT", n)], scale=-1.0))
        yield from per_head(lambda h: [(H("Z", h), H(VBn, h)), (H("NWT", h), H("SGB", h))],
                            lambda n: [("Z", n), VBn, ("NWT", n), "SGB"],
                            lambda n, b: cp(CH["vnew%d" % n], A("VNEW", n * 512, [1, 512]), PA(b, 0, [1, 512]), [bk(b)],
                                            [("VNEW", n)]))
        ob = []

        def o_evac(n, b):
            ob.append(b)
            act(A("RP", n * 512, [1, 512]), PA(b, 0, [1, 512]), AF.Square, [bk(b)], [("RP", n)])
        yield from per_head(lambda h: [(H(QE_, h), H("SGB", h)), (H(AQ, h), H("VNEW", h))],
                            lambda n: [QE_, "SGB", AQ, ("VNEW", n)], o_evac)
        P.op("dve", lambda e: e.tensor_reduce(A("SSO", 0, [1, 8]), A("RP", 0, [128, 8], [1, 128]), AX.X, ALU.add),
             hk("RP"), ["SSO"], dur=1200.0)
        rsq(A("RSO", 0, [1, 8]), A("SSO", 0, [1, 8]), 1.0 / 128, ["SSO"], "RSO")
        tt(CH["grso"], A(GG, 0, [128, 8], [1, 128]), A(GG, 0, [128, 8], [1, 128]), A("RSO", 0, [1, 8], [0, 128]),
           ALU.mult, [GG, "RSO"], [GG])
        for n in range(2):
            tt("dve", A("MIX%d" % p, n * 512, [1, 512]), PA(ob[n], 0, [1, 512]), A(GG, n * 512, [1, 512]), ALU.mult,
               [bk(ob[n]), GG], ["MIX%d" % p])
        yield
        tt(CH["sgd"], A("SG", 0, [128, 8], [1, 128]), A("SG", 0, [128, 8], [1, 128]), A(EC, 16, [1, 8], [0, 128]),
           ALU.mult, ["SG", EC], ["SG"])
        yield from per_head(lambda h: [(H(KR, h), H("VNEW", h))], lambda n: [KR, ("VNEW", n)],
                            lambda n, b: tt("dve", A("SG", n * 512, [1, 512]), PA(b, 0, [1, 512]),
                                            A("SG", n * 512, [1, 512]), ALU.add, [bk(b), "SG"], ["SG"]))
        cp(CH["sgb"], A("SGB", 0, [1, 1024]), A("SG", 0, [1, 1024]), ["SG"], ["SGB"])
        yield

    def collect(gen, banks):
        if gen is None:
            return []
        P.defer = []
        bset[0] = banks
        for _ in gen:
            pass
        l = P.defer
        P.defer = None
        bset[0] = list(range(8))
        return l

    def collect(gen, banks):
        P.defer = []
        bset[0] = banks
        for _ in gen:
            pass
        l = P.defer
        P.defer = None
        bset[0] = list(range(8))
        return l

    def prep_pass1():
        load_weights(1, ["ACC", "TMPC", "X1"], do_out=False)
        dma(A("GAINM", 0, [1, 1024]), bc(fn_d, 1024), [], ["GAINM"])
        build_dw()
        yield

    for ps_ in range(2):
        if ps_ == 0:
            load_weights(ps_)
        else:
            load_weights(1, ["RES", "JUNK", "X0", "GAM"], do_in=False)
        if ps_ == 0:
            mset("pool", A("KBEG0", 0, [1, 1040]), 1.0, ["KBEG0"])
            mset("pool", A("KBEG1", 0, [1, 1040]), 1.0, ["KBEG1"])
        tiles = [(s, t) for s in range(NSEQ) for t in range(NT_RUN)]
        prev = None
        prev2 = None
        hts = lambda j: "HT%d" % (j % 3)
        ctx["HT"] = hts(0)
        P.merge([collect(phase_a(tiles[0][0], tiles[0][1], 0, X="X0"), [4])])
        for i, cur in enumerate(tiles + [None, None]):
            lists = []
            pd = None
            if prev2 is not None:
                pd = (ps_, prev2[0], prev2[1], (i - 2) % 2)
            if prev is not None:
                pa = (prev[0], prev[1], (i - 1) % 2, prev[1] == 0)
                ctx["HT"] = hts(i - 1)
                if ps_ == 0:
                    lists.append(collect(ml_stage2(*pa), [0, 1, 2, 3]))
                else:
                    lists.append(collect(gdn_v(*pa), [3]))
                    lists.append(collect(gdn_stage2(*pa), [0, 1, 2]))
            if ps_ == 1 and pd is not None:
                lists.append(collect(phase_d(*pd), [3]))
            if cur is not None:
                ca = (cur[0], cur[1], i % 2, cur[1] == 0)
                ctx["HT"] = hts(i)
                if ps_ == 0:
                    lists.append(collect(ml_stage1(*ca), [4]))
                    lists.append(collect(ml_qk(*ca), [5]))
                    lists.append(collect(ml_voz(*ca), [6, 7]))
                else:
                    lists.append(collect(gdn_a(*ca), [4, 5]))
                    lists.append(collect(gdn_k(*ca), [4, 5]))
                    lists.append(collect(gdn_q(*ca), [6, 7]))
            if ps_ == 0 and pd is not None:
                lists.append(collect(phase_d(*pd), [4]))
            if ps_ == 0 and i == len(tiles):
                lists.append(collect(prep_pass1(), [5]))
            if i + 1 < len(tiles):
                nx = tiles[i + 1]
                ctx["HT"] = hts(i + 1)
                lists.insert(0, collect(phase_a(nx[0], nx[1], 0, X="X0"), [5] if ps_ == 0 else [3]))
            P.merge(lists)
            prev2 = prev
            prev = cur
    P.op("pool", None, ["out_dram"], [])
    fin = P.ops[-1]
    for so in store_ops:
        fin["deps"][so[0]] = True


    P.finalize()
    sems = {}
    for e, ng in P.ngen.items():
        for g in range(ng):
            sems[(e, g)] = es.enter_context(nc.semaphore("s_%s_%d" % (e, g)))
    for i in range(NDMA):
        sems[("dma", i)] = es.enter_context(nc.semaphore("s_dma_%d" % i))
    for i in range(NDMAP):
        sems[("dmap", i)] = es.enter_context(nc.semaphore("s_dmap_%d" % i))
    with nc.Block() as block:
        @block.sync
        def _(e):
            P.emit("sp", e, sems)

        @block.tensor
        def _(e):
            P.emit("pe", e, sems)

        @block.scalar
        def _(e):
            P.emit("act", e, sems)

        @block.vector
        def _(e):
            P.emit("dve", e, sems)

        @block.gpsimd
        def _(e):
            P.emit("pool", e, sems)
    es.close()
    return nc


NT_RUN = NT
ALPHA = 0.0
CH = {"ht": "dve", "mixt": "dve", "vaug": "act", "mqnt": "dve", "msgb": "act", "xb0": "dve", "xb1": "act",
      "q_QNT": "act", "q_QET": "act", "knt": "act", "aqkt": "dve", "z0": "dve", "z1": "dve", "tt0": "act",
      "tt1": "dve", "vnew0": "dve", "vnew1": "act", "sgb": "act", "mgs": "pool", "gs1": "pool", "gs2": "pool",
      "gam": "pool", "mmi": "dve", "ggain": "pool", "ttl0": "pool", "sgd": "pool", "xn": "dve", "qe2": "pool",
      "qn": "pool", "kb": "dve", "kr": "pool", "kn": "dve", "vb": "dve", "rgt": "pool", "grso": "pool"}
ZENG = ("act", "dve")
TTENG = ("act", "dve")
TBL_NS = 1300.0
TBL_PEN = 1300.0
_CACHE = {}


def kernel(x, attn_norm, w_in, m_i_bias, m_f_bias, m_out_norm, g_conv, g_a_log, g_dt_bias, g_out_norm, w_out,
           final_norm):
    f = lambda a: np.ascontiguousarray(np.asarray(a, dtype=np.float32))
    if "nc" not in _CACHE:
        _CACHE["nc"] = build_program()
    nc = _CACHE["nc"]
    x = f(x)
    shared = {
        "w_in": f(w_in).reshape(D, 8216),
        "w_out": f(w_out).reshape(2048, D),
        "attn_norm": f(attn_norm).reshape(8, 128),
        "m_i_bias": f(m_i_bias).reshape(1, 4),
        "m_f_bias": f(m_f_bias).reshape(1, 4),
        "m_out_norm": f(m_out_norm).reshape(1, 1024),
        "g_conv": f(g_conv).reshape(96, 128),
        "g_a_log": f(g_a_log).reshape(1, 8),
        "g_dt_bias": f(g_dt_bias).reshape(1, 8),
        "g_out_norm": f(g_out_norm).reshape(1, 128),
        "final_norm": f(final_norm).reshape(1, 1024),
        "consts": make_consts(),
    }
    in_maps = []
    for c in range(NCORES):
        m = dict(shared)
        m["x"] = x[c * NSEQ:(c + 1) * NSEQ].reshape(TOK, D)
        in_maps.append(m)
    res = run_bass_kernel_spmd(nc, in_maps, core_ids=list(range(NCORES)))
    outs = [np.asarray(r["out"]).reshape(NSEQ, SEQ, D) for r in res.results]
    return np.concatenate(outs, axis=0).astype(np.float32)
```
